# Optimizing a Trainium2 kernel written in Bass

```python
import functools
import jax, jax.numpy as jnp
from jax import lax
import numpy as np

D_MODEL = 1024
BATCH = 1
SEQ = 16384
DEPTH = 1

CHUNK = 64
N_META = 16
LEAD = CHUNK - N_META
Q_BLOCK = 128

FOX_HEADS = 8
FOX_HEAD_DIM = 64
FOX_WIDTH = FOX_HEADS * FOX_HEAD_DIM

GDN_HEADS = 8
GDN_KEY_DIM = 128
GDN_VAL_DIM = 128
GDN_K_WIDTH = GDN_HEADS * GDN_KEY_DIM
GDN_V_WIDTH = GDN_HEADS * GDN_VAL_DIM
GDN_CONV_CH = 2 * GDN_K_WIDTH + GDN_V_WIDTH
CONV_WIDTH = 4

D_FF = ((8 * D_MODEL + 3 * 256 - 1) // (3 * 256)) * 256

IN_SPLIT_SIZES = (FOX_WIDTH, FOX_WIDTH, FOX_WIDTH, FOX_HEADS,
                  GDN_K_WIDTH, GDN_K_WIDTH, GDN_V_WIDTH,
                  GDN_HEADS, GDN_HEADS, GDN_V_WIDTH,
                  D_MODEL, D_MODEL)
IN_WIDTH = sum(IN_SPLIT_SIZES)

ALPHA = (2.0 * DEPTH) ** 0.25
BETA = (8.0 * DEPTH) ** -0.25
LN_EPS = 1e-5
NORM_EPS = 1e-6

kernel_name = 'hybrid_fox_gdn_deepnorm_block'


def _layer_norm(x, g, b):
    xf = x.astype(jnp.float32)
    mu = jnp.mean(xf, axis=-1, keepdims=True)
    var = jnp.mean(jnp.square(xf - mu), axis=-1, keepdims=True)
    return ((xf - mu) * lax.rsqrt(var + LN_EPS) * g.astype(jnp.float32) + b.astype(jnp.float32)).astype(x.dtype)


def _l2norm(x):
    return x * lax.rsqrt(jnp.sum(x * x, axis=-1, keepdims=True) + NORM_EPS)


def _causal_depthwise_conv(x, w):
    return lax.conv_general_dilated(
        x, w[:, None, :].astype(x.dtype), window_strides=(1,),
        padding=[(CONV_WIDTH - 1, 0)], dimension_numbers=('NWC', 'WIO', 'NWC'),
        feature_group_count=x.shape[-1])


def _forgetting_attention(q, k, v, log_f):
    length, head_dim = q.shape[2], q.shape[3]
    cum = jnp.cumsum(log_f, axis=-1)
    scale = head_dim ** -0.5
    k_pos = jnp.arange(length)

    def query_block(i):
        start = i * Q_BLOCK
        q_blk = lax.dynamic_slice_in_dim(q, start, Q_BLOCK, axis=2)
        c_blk = lax.dynamic_slice_in_dim(cum, start, Q_BLOCK, axis=2)
        q_pos = start + jnp.arange(Q_BLOCK)
        logits = jnp.einsum('bhqd,bhkd->bhqk', q_blk, k).astype(jnp.float32) * scale
        logits = logits + c_blk[..., :, None] - cum[..., None, :]
        visible = (k_pos[None, :] <= q_pos[:, None]) & (
            (k_pos[None, :] >= LEAD) | (k_pos[None, :] == q_pos[:, None]))
        probs = jax.nn.softmax(jnp.where(visible, logits, -jnp.inf), axis=-1)
        return jnp.einsum('bhqk,bhkd->bhqd', probs.astype(v.dtype), v)

    out = lax.map(query_block, jnp.arange(length // Q_BLOCK))
    return jnp.moveaxis(out, 0, 2).reshape(q.shape[0], q.shape[1], length, v.shape[-1])


def _gated_delta_rule(q, k, v, g, beta):
    b, h, length, dk = q.shape
    dv = v.shape[-1]
    n = length // CHUNK
    q = q * dk ** -0.5
    q, k, v = (t.reshape(b, h, n, CHUNK, t.shape[-1]) for t in (q, k, v))
    g, beta = (t.reshape(b, h, n, CHUNK) for t in (g, beta))
    gc = jnp.cumsum(g, axis=-1)
    idx = jnp.arange(CHUNK)
    incl = idx[:, None] >= idx[None, :]
    strict = idx[:, None] > idx[None, :]
    decay = jnp.exp(jnp.where(incl, gc[..., :, None] - gc[..., None, :], -jnp.inf))
    k_beta = k * beta[..., None]
    a_strict = jnp.where(strict, jnp.einsum('bhnid,bhnjd->bhnij', k_beta, k) * decay, 0.0)
    solve = functools.partial(lax.linalg.triangular_solve, left_side=True, lower=True, unit_diagonal=True)
    u = solve(a_strict, v * beta[..., None])
    w = solve(a_strict, k_beta * jnp.exp(gc)[..., None])
    attn = jnp.einsum('bhnid,bhnjd->bhnij', q, k) * decay
    q_dec = q * jnp.exp(gc)[..., None]
    k_dec = k * jnp.exp(gc[..., -1:] - gc)[..., None]
    g_last = jnp.exp(gc[..., -1])

    def step(state, xs):
        u_c, w_c, q_c, k_c, a_c, gl = xs
        v_new = u_c - jnp.einsum('bhcd,bhde->bhce', w_c, state)
        o_c = jnp.einsum('bhcd,bhde->bhce', q_c, state) + jnp.einsum('bhij,bhje->bhie', a_c, v_new)
        state = state * gl[..., None, None] + jnp.einsum('bhcd,bhce->bhde', k_c, v_new)
        return state, o_c

    xs = tuple(jnp.moveaxis(t, 2, 0) for t in (u, w, q_dec, k_dec, attn, g_last))
    state0 = jnp.zeros((b, h, dk, dv), jnp.float32)
    _, o = lax.scan(step, state0, xs)
    return jnp.moveaxis(o, 0, 2).reshape(b, h, length, dv)


def _hybrid_layer(h, w_in, b_f, conv_w, a_log, dt_bias, gdn_norm_w, w_out_a, w_out_b, w_o,
                  ln1_g, ln1_b, w_gate, w_up, w_down, ln2_g, ln2_b):
    f32 = jnp.float32
    bsz, length, _ = h.shape
    slot_mask = (jnp.arange(length) >= LEAD).astype(h.dtype)
    proj = (h @ w_in) * slot_mask[None, :, None]
    (q_a, k_a, v_a, f_a, q_b, k_b, v_b, a_b, b_b, z_b, gate_a, gate_b) = jnp.split(
        proj, np.cumsum(IN_SPLIT_SIZES)[:-1].tolist(), axis=-1)

    def heads(t, n_heads, dim):
        return t.reshape(bsz, length, n_heads, dim).transpose(0, 2, 1, 3)

    log_f = jax.nn.log_sigmoid(f_a.astype(f32) + b_f.astype(f32)).transpose(0, 2, 1)
    y_a = _forgetting_attention(heads(q_a, FOX_HEADS, FOX_HEAD_DIM), heads(k_a, FOX_HEADS, FOX_HEAD_DIM),
                                heads(v_a, FOX_HEADS, FOX_HEAD_DIM), log_f)
    y_a = y_a.transpose(0, 2, 1, 3).reshape(bsz, length, FOX_WIDTH)

    qkv = jax.nn.silu(_causal_depthwise_conv(jnp.concatenate([q_b, k_b, v_b], axis=-1), conv_w))
    q_c, k_c, v_c = jnp.split(qkv, [GDN_K_WIDTH, 2 * GDN_K_WIDTH], axis=-1)
    q_c = _l2norm(heads(q_c, GDN_HEADS, GDN_KEY_DIM).astype(f32))
    k_c = _l2norm(heads(k_c, GDN_HEADS, GDN_KEY_DIM).astype(f32))
    v_c = heads(v_c, GDN_HEADS, GDN_VAL_DIM).astype(f32)
    beta = jax.nn.sigmoid(b_b.astype(f32)).transpose(0, 2, 1)
    g = (-jnp.exp(a_log.astype(f32)) * jax.nn.softplus(a_b.astype(f32) + dt_bias.astype(f32))).transpose(0, 2, 1)
    o = _gated_delta_rule(q_c, k_c, v_c, g, beta).transpose(0, 2, 1, 3)
    o = o * lax.rsqrt(jnp.mean(jnp.square(o), axis=-1, keepdims=True) + NORM_EPS) * gdn_norm_w.astype(f32)
    o = o * jax.nn.silu(z_b.astype(f32).reshape(bsz, length, GDN_HEADS, GDN_VAL_DIM))
    y_b = o.reshape(bsz, length, GDN_V_WIDTH).astype(h.dtype)

    mixed = jax.nn.sigmoid(gate_a) * (y_a @ w_out_a) + jax.nn.sigmoid(gate_b) * (y_b @ w_out_b)
    h = _layer_norm(ALPHA * h + mixed @ w_o, ln1_g, ln1_b)

    ffn = (jax.nn.silu(h @ w_gate) * (h @ w_up)) @ w_down
    return _layer_norm(ALPHA * h + ffn, ln2_g, ln2_b)


def setup_inputs(seed: int = 0) -> dict:
    key = jax.random.key(seed)
    ks = jax.random.split(key, 20)

    def nrm(k, shape, scale):
        return jax.random.normal(k, shape, jnp.float32) * scale

    dt = jnp.exp(jax.random.uniform(ks[8], (DEPTH, GDN_HEADS), jnp.float32,
                                    minval=float(np.log(1e-3)), maxval=float(np.log(1e-1))))
    return {
        'x': nrm(ks[0], (BATCH, SEQ, D_MODEL), 1.0),
        'meta_tokens': nrm(ks[1], (N_META, D_MODEL), 1.0),
        'ln_in_g': 1.0 + nrm(ks[2], (D_MODEL,), 0.02),
        'ln_in_b': nrm(ks[3], (D_MODEL,), 0.02),
        'w_in': nrm(ks[4], (DEPTH, D_MODEL, IN_WIDTH), D_MODEL ** -0.5),
        'b_f': 3.0 + nrm(ks[5], (DEPTH, FOX_HEADS), 0.1),
        'conv_w': nrm(ks[6], (DEPTH, CONV_WIDTH, GDN_CONV_CH), CONV_WIDTH ** -0.5),
        'a_log': jnp.log(jax.random.uniform(ks[7], (DEPTH, GDN_HEADS), jnp.float32, minval=1.0, maxval=16.0)),
        'dt_bias': dt + jnp.log(-jnp.expm1(-dt)),
        'gdn_norm_w': 1.0 + nrm(ks[9], (DEPTH, GDN_VAL_DIM), 0.02),
        'w_out_a': nrm(ks[10], (DEPTH, FOX_WIDTH, D_MODEL), FOX_WIDTH ** -0.5),
        'w_out_b': nrm(ks[11], (DEPTH, GDN_V_WIDTH, D_MODEL), GDN_V_WIDTH ** -0.5),
        'w_o': nrm(ks[12], (DEPTH, D_MODEL, D_MODEL), D_MODEL ** -0.5 * BETA),
        'ln1_g': 1.0 + nrm(ks[13], (DEPTH, D_MODEL), 0.02),
        'ln1_b': nrm(ks[14], (DEPTH, D_MODEL), 0.02),
        'w_gate': nrm(ks[15], (DEPTH, D_MODEL, D_FF), D_MODEL ** -0.5),
        'w_up': nrm(ks[16], (DEPTH, D_MODEL, D_FF), D_MODEL ** -0.5),
        'w_down': nrm(ks[17], (DEPTH, D_FF, D_MODEL), D_FF ** -0.5 * BETA),
        'ln2_g': 1.0 + nrm(ks[18], (DEPTH, D_MODEL), 0.02),
        'ln2_b': nrm(ks[19], (DEPTH, D_MODEL), 0.02),
    }


def reference(x, meta_tokens, ln_in_g, ln_in_b, w_in, b_f, conv_w, a_log, dt_bias, gdn_norm_w,
              w_out_a, w_out_b, w_o, ln1_g, ln1_b, w_gate, w_up, w_down, ln2_g, ln2_b):
    bsz, seq, d = x.shape
    length = ((CHUNK + seq + Q_BLOCK - 1) // Q_BLOCK) * Q_BLOCK
    tail = length - CHUNK - seq
    h = jnp.concatenate([
        jnp.zeros((bsz, LEAD, d), x.dtype),
        jnp.broadcast_to(meta_tokens.astype(x.dtype)[None], (bsz, N_META, d)),
        x,
        jnp.zeros((bsz, tail, d), x.dtype)], axis=1)
    h = _layer_norm(h, ln_in_g, ln_in_b)
    for l in range(DEPTH):
        h = _hybrid_layer(h, w_in[l], b_f[l], conv_w[l], a_log[l], dt_bias[l], gdn_norm_w[l],
                          w_out_a[l], w_out_b[l], w_o[l], ln1_g[l], ln1_b[l],
                          w_gate[l], w_up[l], w_down[l], ln2_g[l], ln2_b[l])
    return h[:, CHUNK:CHUNK + seq]
```

```python
import contextlib
import numpy as np
import concourse.bass as bass
import concourse.mybir as mybir
from concourse.bass_utils import run_bass_kernel_spmd

F32 = mybir.dt.float32
BF16 = mybir.dt.bfloat16
ALU = mybir.AluOpType
AF = mybir.ActivationFunctionType

NCORES = 8
D = 1024
KC = 8
DFF = 2816
FC = 22
ALPHA = 2.0 ** 0.25
LN_EPS = 1e-5
NORM_EPS = 1e-6
NEG = -30000.0
G_QA, G_KA, G_V, G_QB, G_KB, G_VB, G_Z = 0, 64, 128, 200, 328, 456, 584
W1COLS = 712
GROUPS = [(G_QA, 64), (G_KA, 64), (G_V, 72), (G_QB, 128), (G_KB, 128), (G_VB, 128), (G_Z, 128)]
PV_G0, PV_B0 = 0, 8
PV_CW = 16
PV_BF, PV_ALOG, PV_DT, PV_GNW = 28, 29, 30, 31
PV_G1, PV_B1, PV_G2, PV_B2 = 32, 40, 48, 56
PV_N = 64
C_ID, C_LT2, C_BLK, C_MSL, C_MIU, C_SEL, C_MQ, C_MK = 0, 128, 256, 384, 512, 640, 642, 646
C_N = 650

ENGS = ("pe", "act", "dve", "pool", "sp")
EPOCH = 16000


class Buf:
    __slots__ = ("w", "r", "name", "excl")

    def __init__(self, name="", excl=False):
        self.w = None
        self.r = []
        self.name = name
        self.excl = excl


class Op:
    __slots__ = ("eng", "fn", "deps", "sig", "val", "key", "inc", "ep")


class Prog:
    def __init__(self):
        self.ops = {e: [] for e in ENGS}
        self.pending = {e: [] for e in ENGS}
        self.dmas = {}

    def add(self, eng, fn, reads=(), writes=(), key=None, inc=16):
        op = Op()
        op.eng, op.fn, op.sig, op.val, op.key, op.inc = eng, fn, False, 0, key, inc
        deps = list(self.pending[eng])
        self.pending[eng] = []
        ex = [b for b in reads if b.excl]
        if ex:
            writes = list(writes) + ex
            reads = [b for b in reads if not b.excl]
        raw = set()
        for b in reads:
            if b.w is not None:
                deps.append(b.w)
                raw.add(id(b.w))
        for b in writes:
            if b.w is not None:
                deps.append(b.w)
                if b.excl:
                    raw.add(id(b.w))
            deps.extend(b.r)
        need, seen = [], set()
        for d in deps:
            if id(d) in seen:
                continue
            seen.add(id(d))
            if d.key is None and key is None and d.eng == eng and (eng == "pe" or id(d) not in raw):
                continue
            if d.key is None:
                d.sig = True
            need.append(d)
        op.deps = need
        for b in reads:
            b.r.append(op)
        for b in writes:
            b.w = op
            b.r = []
        self.ops[eng].append(op)
        if key is not None:
            self.dmas[id(key)] = op
        return op

    def barrier(self):
        last = []
        for e in ENGS:
            if self.ops[e]:
                o = self.ops[e][-1]
                if o.key is None:
                    o.sig = True
                last.append(o)
        last.extend(self.dmas.values())
        for e in ENGS:
            self.pending[e] = list(last)

    def finalize(self):
        keycnt = {}
        for e in ENGS:
            cnt = 0
            for op in self.ops[e]:
                if op.key is not None:
                    k = id(op.key)
                    keycnt[k] = keycnt.get(k, 0) + op.inc
                    op.val = keycnt[k]
                elif op.sig:
                    cnt += 1
                    op.ep = (cnt - 1) // EPOCH
                    op.val = (cnt - 1) % EPOCH + 1
            self.nep = getattr(self, "nep", {})
            self.nep[e] = (cnt - 1) // EPOCH + 1 if cnt else 1

    def emit(self, eng, h, engsem, keysem):
        waited = {}
        for op in self.ops[eng]:
            for d in op.deps:
                s = keysem[id(d.key)] if d.key is not None else engsem[d.eng][d.ep]
                sid = id(s)
                if waited.get(sid, 0) < d.val:
                    h.wait_ge(s, d.val)
                    waited[sid] = d.val
            ins = op.fn(h)
            if op.key is not None:
                ins.then_inc(keysem[id(op.key)], op.inc)
            elif op.sig:
                ins.then_inc(engsem[eng][op.ep], 1)


def build(S, mode="fused"):
    L = ((64 + S + 127) // 128) * 128
    NBLK = L // 128
    tiles = []
    p = 0
    while p < L:
        w = min(512, L - p)
        tiles.append((p, w))
        p += w
    FS = S // NCORES
    W2 = min(512, FS)
    NT2 = FS // W2

    nc = bass.Bass("TRN2", target_bir_lowering=False)
    dt_in = lambda n, shp: nc.dram_tensor(n, shp, F32, kind="ExternalInput").ap()
    xT = dt_in("xT", [D, L])
    xown = dt_in("xown", [D, FS])
    w1 = dt_in("w1", [D, W1COLS])
    pv_d = dt_in("pv", [128, PV_N])
    cst_d = dt_in("cst", [128, C_N])
    w2g = dt_in("w2g", [D, 2048])
    woa = dt_in("woa", [512, D])
    wob = dt_in("wob", [D, D])
    wo = dt_in("wo", [D, D])
    wg = dt_in("wg", [D, DFF])
    wu = dt_in("wu", [D, DFF])
    wd = dt_in("wd", [DFF, D])
    if mode != "p1":
        outT = nc.dram_tensor("outT", [D, FS], F32, kind="ExternalOutput").ap()
    if mode == "p1":
        ysrc = nc.dram_tensor("ysrc", [NCORES * 192, FS], BF16, kind="ExternalOutput").ap()
    else:
        ysrc = nc.dram_tensor("ysrc", [NCORES * 192, FS], BF16).ap()
    DBG = 0
    if mode == "p2":
        yin = nc.dram_tensor("yin", [NCORES * 192, FS], BF16, kind="ExternalInput").ap()
    agout = nc.dram_tensor("agout", [16 * NCORES * 96, FS], BF16).ap()

    P = Prog()
    es = contextlib.ExitStack()
    with es:
        ARENA_F = 50 * 1024
        arena = es.enter_context(nc.sbuf_tensor("arena", [128, ARENA_F], F32))
        psum = es.enter_context(nc.psum_tensor("ps", [128, 4096], F32))
        bank = [psum[:, b * 512:(b + 1) * 512] for b in range(8)]
        bankbuf = [Buf("bank%d" % b, excl=True) for b in range(8)]

        class Arena:
            def __init__(self):
                self.off = 0

            def f32(self, n):
                a = arena[:, self.off:self.off + n]
                self.off += n
                assert self.off <= ARENA_F, "SBUF arena overflow %d" % self.off
                return a

            def bf16(self, n):
                m = (n + 1) // 2
                a = arena[:, self.off:self.off + m].bitcast(BF16)
                self.off += m
                assert self.off <= ARENA_F, "SBUF arena overflow %d" % self.off
                return a[:, 0:n]

        A = Arena()
        cst = A.f32(C_N)
        pv = A.f32(PV_N)
        ones_f = A.f32(128)
        identb = A.bf16(128)
        onesb = A.bf16(128)
        B_cst, B_pv, B_misc = Buf("cst"), Buf("pv"), Buf("misc")
        ident = cst[:, C_ID:C_ID + 128]
        P.add("sp", lambda e: e.dma_start(out=cst, in_=cst_d), writes=[B_cst], key=B_cst)
        P.add("sp", lambda e: e.dma_start(out=pv, in_=pv_d), writes=[B_pv], key=B_pv)
        P.add("pool", lambda e: e.memset(ones_f, 1.0), writes=[B_misc])
        P.add("pool", lambda e: e.memset(onesb, 1.0 / 1024.0), writes=[B_misc])
        P.add("pool", lambda e: e.tensor_copy(out=identb, in_=ident), reads=[B_cst], writes=[B_misc])
        CONSTS = [B_cst, B_pv, B_misc]
        base_off = A.off

        gp = [5]

        def gbank():
            b = gp[0]
            gp[0] = 5 + (gp[0] - 5 + 1) % 3
            return b

        DO_P1 = mode != "p2"
        Wc = A.bf16(KC * W1COLS).rearrange("p (k c) -> p k c", k=KC)
        B_Wc = Buf("Wc")
        bW = A.f32(8)
        prep_off = A.off
        csum = A.f32(W1COLS)
        stg = [A.f32(W1COLS), A.f32(W1COLS)]
        B_stg = [Buf("stg0"), Buf("stg1")]
        wgt = A.f32(W1COLS)
        B_wgt = Buf("wgt")
        B_bW, B_csum = Buf("bW"), Buf("csum")
        w1v = w1.rearrange("(k p) c -> p k c", p=128)
        psc = [bank[5], bank[6]]
        for k in range(KC if DO_P1 else 0):
            s = stg[k % 2]
            P.add("sp", lambda e, s=s, k=k: e.dma_start(out=s, in_=w1v[:, k, :]), writes=[B_stg[k % 2]], key=B_stg[k % 2])
            P.add("dve", lambda e, s=s, k=k: e.tensor_scalar(out=wgt, in0=s, scalar1=pv[:, PV_G0 + k:PV_G0 + k + 1], scalar2=None, op0=ALU.mult),
                  reads=[B_stg[k % 2], B_pv], writes=[B_wgt])
            P.add("pe", lambda e, k=k: e.matmul(psc[0][:, 0:512], ones_f, wgt[:, 0:512], start=(k == 0), stop=(k == KC - 1)),
                  reads=[B_wgt, B_misc], writes=[bankbuf[5]])
            P.add("pe", lambda e, k=k: e.matmul(psc[1][:, 0:W1COLS - 512], ones_f, wgt[:, 512:W1COLS], start=(k == 0), stop=(k == KC - 1)),
                  reads=[B_wgt, B_misc], writes=[bankbuf[6]])
            order = [3, 0, 1, 2, 4, 5, 6]
            for oi, gi in enumerate(order):
                go, gm = GROUPS[gi]
                P.add("pe", lambda e, s=s, k=k, gi=gi, go=go, gm=gm, oi=oi: e.matmul(bank[7][0:gm, gi:gi + 1], s[:, go:go + gm], pv[:, PV_B0 + k:PV_B0 + k + 1],
                                                                                    start=(k == 0 and oi == 0), stop=(k == KC - 1 and oi == 6), skip_group_check=True),
                      reads=[B_stg[k % 2], B_pv], writes=[bankbuf[7]])
        if DO_P1:
            P.add("act", lambda e: e.mul(csum[:, 0:512], psc[0][:, 0:512], 1.0 / 1024.0), reads=[bankbuf[5]], writes=[B_csum])
            P.add("act", lambda e: e.mul(csum[:, 512:W1COLS], psc[1][:, 0:W1COLS - 512], 1.0 / 1024.0), reads=[bankbuf[6]], writes=[B_csum])
            P.add("dve", lambda e: e.tensor_copy(out=bW[:, 0:7], in_=bank[7][:, 0:7]), reads=[bankbuf[7]], writes=[B_bW])
        for k in range(KC if DO_P1 else 0):
            s = stg[k % 2]
            P.add("sp", lambda e, s=s, k=k: e.dma_start(out=s, in_=w1v[:, k, :]), writes=[B_stg[k % 2]], key=B_stg[k % 2])
            P.add("dve", lambda e, s=s, k=k: e.scalar_tensor_tensor(out=Wc[:, k, :], in0=s, scalar=pv[:, PV_G0 + k:PV_G0 + k + 1], in1=csum,
                                                                    op0=ALU.mult, op1=ALU.subtract),
                  reads=[B_stg[k % 2], B_pv, B_csum], writes=[B_Wc])
        P.barrier()
        A.off = prep_off
        sc1 = A.f32(8)
        B_sc1 = Buf("sc1")
        if DO_P1:
          P.add("act", lambda e: e.activation(out=sc1[:, 0:1], in_=pv[:, PV_ALOG:PV_ALOG + 1], func=AF.Exp), reads=[B_pv], writes=[B_sc1])
          P.add("dve", lambda e: e.tensor_scalar(out=sc1[:, 0:1], in0=sc1[:, 0:1], scalar1=-1.0, scalar2=None, op0=ALU.mult), reads=[B_sc1], writes=[B_sc1])
          P.add("dve", lambda e: e.tensor_scalar(out=sc1[:, 1:2], in0=bW[:, 0:1], scalar1=0.125, scalar2=None, op0=ALU.mult), reads=[B_bW], writes=[B_sc1])

        Kaug = A.bf16(L)
        Vc = A.bf16(NBLK * 65).rearrange("p (b c) -> p b c", c=65)
        B_K = [Buf("K%d" % i) for i in range(len(tiles))]
        B_V = [Buf("V%d" % i) for i in range(len(tiles))]
        B_Vinit = Buf("Vinit")
        if DO_P1:
            P.add("pool", lambda e: e.memset(Vc[:, :, 64:65], 1.0), writes=[B_Vinit])

        xb = [A.bf16(KC * 512).rearrange("p (k w) -> p k w", k=KC) for _ in range(2)]
        B_xb = [Buf("xb0"), Buf("xb1")]
        sq = A.bf16(KC * 512).rearrange("p (k w) -> p k w", k=KC)
        B_sq = Buf("sq")

        def T32(name):
            return A.f32(512), Buf(name)

        def T16(name):
            return A.bf16(512), Buf(name)

        mean_sb, B_mean = T32("mean")
        rstd, B_rstd = T32("rstd")
        tmpA, B_tmpA = T32("tmpA")
        tmpB, B_tmpB = T32("tmpB")
        Qaug, B_Q = T16("Qaug")
        VG, B_VG = T32("VG")
        Xq = A.f32(516); Xk = A.f32(516); Xv = A.f32(516)
        B_Xq, B_Xk, B_Xv = Buf("Xq"), Buf("Xk"), Buf("Xv")
        zs, B_zs = T32("zs")
        ctile, B_c = T32("c")
        lrow, B_lrow = T32("lrow")
        lrow2, B_lrow2 = T32("lrow2")
        hiB, B_hi = T16("hi"); loB, B_lo = T16("lo"); lo2B, B_lo2 = T16("lo2")
        r1, B_r1 = T32("r1")
        accr, B_accr = T32("accr")
        onesrow, B_onesrow = T32("onesrow")
        bigm, B_bigm = T32("bigm")
        carry = A.f32(2)
        B_carry = Buf("carry")
        PT = [A.bf16(1024).rearrange("p (j w) -> p j w", j=2) for _ in range(2)]
        B_PT = [Buf("PT0"), Buf("PT1")]
        O_sb, B_Osb = T32("Osb")
        rden, B_rden = T32("rden")
        YA, B_YA = T16("YA")
        YB, B_YB = T16("YB")
        sctm = A.f32(32).rearrange("p (j c) -> p j c", c=8)
        B_sctm = Buf("sctm")
        cq, B_cq = T32("cq"); ck, B_ck = T32("ck"); cv, B_cv = T32("cv")
        qs, B_qs = T32("qs"); ks, B_ks = T32("ks")
        sqq, B_sqq = T32("sqq"); sqk, B_sqk = T32("sqk")
        rnq, B_rnq = T32("rnq"); rnk, B_rnk = T32("rnk")
        qnT, B_qnT = T16("qnT"); knT, B_knT = T16("knT"); vsT, B_vsT = T16("vsT")
        tms = A.f32(64)
        B_tms = Buf("tms")
        t_beta, t_nbeta, t_x, t_ax, t_e, t_l, t_g, t_gc, t_ngc, t_e1, t_e2, t_d = [tms[:, 4 * i:4 * i + 4] for i in range(12)]
        rhs8 = A.f32(8)
        glbc = A.f32(8)
        B_glbc = Buf("glbc")
        Xuw = A.bf16(4 * 256).rearrange("p (j c) -> p j c", c=256)
        B_Xuw = Buf("Xuw")
        kdec = A.bf16(4 * 128).rearrange("p (j c) -> p j c", c=128)
        B_kdec = Buf("kdec")
        UW = A.bf16(4 * 256).rearrange("p (j c) -> p j c", c=256)
        B_UW = [Buf("UW%d" % j) for j in range(4)]
        diag = [A.f32(128) for _ in range(2)]
        B_diag = [Buf("diag0"), Buf("diag1")]
        T1 = [A.f32(128) for _ in range(2)]
        B_T1 = [Buf("T1a"), Buf("T1b")]
        T3 = [A.f32(128) for _ in range(2)]
        B_T3 = [Buf("T3a"), Buf("T3b")]
        Dsl = [A.f32(128) for _ in range(2)]
        B_Dsl = [Buf("Dsl0"), Buf("Dsl1")]
        Diu = [A.f32(128) for _ in range(2)]
        B_Diu = [Buf("Diu0"), Buf("Diu1")]
        EGR = [A.f32(128) for _ in range(4)]
        B_EGR = [Buf("EGR%d" % j) for j in range(4)]
        attnT = [A.bf16(128) for _ in range(4)]
        B_attnT = [Buf("attnT%d" % j) for j in range(4)]
        PP = [[A.bf16(256) for _ in range(2)] for _ in range(4)]
        B_PP = [[Buf("PP%d_%d" % (j, q)) for q in range(2)] for j in range(4)]
        RR = [[A.bf16(128) for _ in range(2)] for _ in range(4)]
        B_RR = [[Buf("RR%d_%d" % (j, q)) for q in range(2)] for j in range(4)]
        qdecT, B_qdecT = T32("qdecT")
        QeffT, B_QeffT = T32("QeffT")
        MT = [A.f32(128) for _ in range(8)]
        B_MT = [Buf("MT%d" % j) for j in range(8)]
        Bn = [A.f32(128) for _ in range(8)]
        B_Bn = [Buf("Bn%d" % j) for j in range(8)]
        Sst = [A.f32(128) for _ in range(9)]
        B_S = [Buf("S%d" % j) for j in range(9)]
        sqo, B_sqo = T32("sqo")
        rno, B_rno = T32("rno")
        p1_end = A.off

        if DO_P1:
            P.add("pool", lambda e: e.memset(Sst[0], 0.0), writes=[B_S[0]])
            P.add("pool", lambda e: e.memset(onesrow, 1.0), writes=[B_onesrow])
            P.add("pool", lambda e: e.memset(bigm, 0.0), writes=[B_bigm])
            P.add("pool", lambda e: e.memset(bigm[64:70, 0:48], 30000.0), writes=[B_bigm])
            P.add("pool", lambda e: e.memset(Xq[:, 0:3], 0.0), writes=[B_Xq])
            P.add("pool", lambda e: e.memset(Xk[:, 0:3], 0.0), writes=[B_Xk])
            P.add("pool", lambda e: e.memset(Xv[:, 0:3], 0.0), writes=[B_Xv])

        xTv = xT.rearrange("(k p) l -> p k l", p=128)
        ysv = ysrc.rearrange("(s f) t -> s f t", f=192)
        B_ysrc = Buf("ysrc")
        RS = slice(64, 70)
        cwq = lambda j: pv[:, PV_CW + j:PV_CW + j + 1]
        s_cur = [0]

        def y_out(src, rows0, nrows, c0, W, Bsrc):
            p = max(c0, 64)
            end = min(c0 + W, 64 + S)
            while p < end:
                f = p - 64
                sh = f // FS
                fe = min(end - 64, (sh + 1) * FS)
                n = fe - f
                P.add("sp", lambda e, sh=sh, f=f, n=n, p=p: e.dma_start(out=ysv[sh, rows0:rows0 + nrows, f - sh * FS:f - sh * FS + n],
                                                                        in_=src[0:nrows, p - c0:p - c0 + n]),
                      reads=[Bsrc], writes=[B_ysrc], key=B_ysrc)
                p += n

        STAGE = 99

        def p1_tile(ti, c0, W):
            if STAGE <= 0:
                return
            nb = W // 128
            blk0 = c0 // 128
            par = ti % 2
            xbt = xb[par]
            P.add("pool", lambda e, xbt=xbt, c0=c0, W=W: e.dma_start(out=xbt[:, :, 0:W], in_=xTv[:, :, c0:c0 + W]), writes=[B_xb[par]], key=B_xb[par])
            P.add("pool", lambda e, xbt=xbt, W=W: e.tensor_tensor(out=sq[:, :, 0:W], in0=xbt[:, :, 0:W], in1=xbt[:, :, 0:W], op=ALU.mult),
                  reads=[B_xb[par]], writes=[B_sq])
            b1, b2 = gbank(), gbank()
            for k in range(KC):
                P.add("pe", lambda e, k=k, b1=b1, xbt=xbt, W=W: e.matmul(bank[b1][:, 0:W], onesb, xbt[:, k, 0:W], start=(k == 0), stop=(k == KC - 1)),
                      reads=[B_xb[par], B_misc], writes=[bankbuf[b1]])
            for k in range(KC):
                P.add("pe", lambda e, k=k, b2=b2, W=W: e.matmul(bank[b2][:, 0:W], onesb, sq[:, k, 0:W], start=(k == 0), stop=(k == KC - 1)),
                      reads=[B_sq, B_misc], writes=[bankbuf[b2]])
            P.add("act", lambda e, b1=b1, W=W: e.copy(mean_sb[:, 0:W], bank[b1][:, 0:W]), reads=[bankbuf[b1]], writes=[B_mean])
            P.add("dve", lambda e, W=W: e.tensor_tensor(out=tmpA[:, 0:W], in0=mean_sb[:, 0:W], in1=mean_sb[:, 0:W], op=ALU.mult), reads=[B_mean], writes=[B_tmpA])
            P.add("dve", lambda e, b2=b2, W=W: e.tensor_tensor(out=tmpA[:, 0:W], in0=bank[b2][:, 0:W], in1=tmpA[:, 0:W], op=ALU.subtract),
                  reads=[bankbuf[b2], B_tmpA], writes=[B_tmpA])
            P.add("dve", lambda e, W=W: e.tensor_scalar(out=tmpA[:, 0:W], in0=tmpA[:, 0:W], scalar1=0.0, scalar2=LN_EPS, op0=ALU.max, op1=ALU.add),
                  reads=[B_tmpA], writes=[B_tmpA])
            P.add("act", lambda e, W=W: e.activation(out=tmpA[:, 0:W], in_=tmpA[:, 0:W], func=AF.Sqrt), reads=[B_tmpA], writes=[B_tmpA])
            P.add("dve", lambda e, W=W: e.reciprocal(out=rstd[:, 0:W], in_=tmpA[:, 0:W]), reads=[B_tmpA], writes=[B_rstd])

            if STAGE <= 1:
                return
            def proj(go, gm):
                b = gbank()
                for k in range(KC):
                    P.add("pe", lambda e, k=k, b=b: e.matmul(bank[b][0:gm, 0:W], Wc[:, k, go:go + gm], xbt[:, k, 0:W], start=(k == 0), stop=(k == KC - 1)),
                          reads=[B_Wc, B_xb[par]], writes=[bankbuf[b]])
                return b

            def evac(b, gm, gi, out_ap, Bout, func=AF.Identity, scale=1.0, bias_ap=None, tmp=None, Btmp=None):
                tmp_, Bt = (tmpB, B_tmpB) if tmp is None else (tmp, Btmp)
                P.add("dve", lambda e: e.tensor_tensor(out=tmp_[0:gm, 0:W], in0=bank[b][0:gm, 0:W], in1=rstd[0:gm, 0:W], op=ALU.mult),
                      reads=[bankbuf[b], B_rstd], writes=[Bt])
                bia = bW[0:gm, gi:gi + 1] if bias_ap is None else bias_ap
                P.add("act", lambda e: e.activation(out=out_ap, in_=tmp_[0:gm, 0:W], func=func, bias=bia, scale=scale),
                      reads=[Bt, B_bW, B_sc1], writes=[Bout])

            b = proj(G_QA, 64)
            evac(b, 64, 0, Qaug[0:64, 0:W], B_Q, scale=0.125, bias_ap=sc1[0:64, 1:2])
            b = proj(G_KA, 64)
            evac(b, 64, 1, Kaug[0:64, c0:c0 + W], B_K[ti])
            b = proj(G_V, 72)
            evac(b, 72, 2, VG[0:72, 0:W], B_VG)
            b = proj(G_QB, 128)
            evac(b, 128, 3, Xq[:, 3:3 + W], B_Xq)
            b = proj(G_KB, 128)
            evac(b, 128, 4, Xk[:, 3:3 + W], B_Xk)
            b = proj(G_VB, 128)
            evac(b, 128, 5, Xv[:, 3:3 + W], B_Xv)
            b = proj(G_Z, 128)
            evac(b, 128, 6, zs[:, 0:W], B_zs, func=AF.Silu)
            if ti == 0:
                for X_, B_ in ((Xq, B_Xq), (Xk, B_Xk), (Xv, B_Xv)):
                    P.add("pool", lambda e, X_=X_: e.memset(X_[:, 3:3 + 48], 0.0), writes=[B_])

            if STAGE <= 2:
                return
            for j in range(nb):
                b = gbank()
                P.add("pe", lambda e, j=j, b=b: e.transpose(bank[b][:, 0:72], VG[0:72, j * 128:(j + 1) * 128], ident[0:72, 0:72]),
                      reads=[B_VG, B_cst], writes=[bankbuf[b]])
                P.add("act", lambda e, j=j, b=b: e.copy(Vc[:, blk0 + j, 0:64], bank[b][:, 0:64]), reads=[bankbuf[b], B_Vinit], writes=[B_V[ti]])
                P.add("dve", lambda e, j=j, b=b: e.tensor_copy(out=sctm[:, j, :], in_=bank[b][:, 64:72]), reads=[bankbuf[b]], writes=[B_sctm])

            if STAGE <= 3:
                return
            P.add("dve", lambda e: e.tensor_scalar(out=lrow[RS, 0:W], in0=VG[RS, 0:W], scalar1=pv[RS, PV_BF:PV_BF + 1], scalar2=-1.0, op0=ALU.add, op1=ALU.mult),
                  reads=[B_VG, B_pv], writes=[B_lrow])
            P.add("dve", lambda e: e.tensor_scalar(out=lrow2[RS, 0:W], in0=lrow[RS, 0:W], scalar1=-1.0, scalar2=None, op0=ALU.mult), reads=[B_lrow], writes=[B_lrow2])
            P.add("dve", lambda e: e.tensor_tensor(out=lrow2[RS, 0:W], in0=lrow2[RS, 0:W], in1=lrow[RS, 0:W], op=ALU.max), reads=[B_lrow, B_lrow2], writes=[B_lrow2])
            P.add("act", lambda e: e.activation(out=lrow2[RS, 0:W], in_=lrow2[RS, 0:W], func=AF.Exp, scale=-1.0), reads=[B_lrow2], writes=[B_lrow2])
            P.add("act", lambda e: e.activation(out=lrow2[RS, 0:W], in_=lrow2[RS, 0:W], func=AF.Ln, bias=1.0), reads=[B_lrow2], writes=[B_lrow2])
            P.add("dve", lambda e: e.scalar_tensor_tensor(out=lrow[RS, 0:W], in0=lrow[RS, 0:W], scalar=0.0, in1=lrow2[RS, 0:W], op0=ALU.max, op1=ALU.add),
                  reads=[B_lrow, B_lrow2], writes=[B_lrow])
            init = 0.0 if ti == 0 else carry[RS, 0:1]
            P.add("dve", lambda e, init=init: e.tensor_tensor_scan(out=ctile[RS, 0:W], data0=onesrow[RS, 0:W], data1=lrow[RS, 0:W], initial=init,
                                                                  op0=ALU.mult, op1=ALU.subtract),
                  reads=[B_lrow, B_onesrow, B_carry], writes=[B_c])
            P.add("dve", lambda e: e.tensor_copy(out=carry[RS, 0:1], in_=ctile[RS, W - 1:W]), reads=[B_c], writes=[B_carry])

            def split3(src, Bsrc):
                P.add("dve", lambda e: e.tensor_copy(out=hiB[RS, 0:W], in_=src[RS, 0:W]), reads=[Bsrc], writes=[B_hi])
                P.add("dve", lambda e: e.tensor_tensor(out=r1[RS, 0:W], in0=src[RS, 0:W], in1=hiB[RS, 0:W], op=ALU.subtract), reads=[Bsrc, B_hi], writes=[B_r1])
                P.add("dve", lambda e: e.tensor_copy(out=loB[RS, 0:W], in_=r1[RS, 0:W]), reads=[B_r1], writes=[B_lo])
                P.add("dve", lambda e: e.tensor_tensor(out=r1[RS, 0:W], in0=r1[RS, 0:W], in1=loB[RS, 0:W], op=ALU.subtract), reads=[B_r1, B_lo], writes=[B_r1])
                P.add("dve", lambda e: e.tensor_copy(out=lo2B[RS, 0:W], in_=r1[RS, 0:W]), reads=[B_r1], writes=[B_lo2])

            def augrows(mc, out_ap, Bout):
                m = lambda i: cst[RS, mc + i:mc + i + 1]
                P.add("dve", lambda e: e.tensor_scalar(out=accr[RS, 0:W], in0=hiB[RS, 0:W], scalar1=m(0), scalar2=m(3), op0=ALU.mult, op1=ALU.add),
                      reads=[B_hi, B_cst], writes=[B_accr])
                P.add("dve", lambda e: e.scalar_tensor_tensor(out=accr[RS, 0:W], in0=loB[RS, 0:W], scalar=m(1), in1=accr[RS, 0:W], op0=ALU.mult, op1=ALU.add),
                      reads=[B_lo, B_accr, B_cst], writes=[B_accr])
                P.add("dve", lambda e: e.scalar_tensor_tensor(out=out_ap, in0=lo2B[RS, 0:W], scalar=m(2), in1=accr[RS, 0:W], op0=ALU.mult, op1=ALU.add),
                      reads=[B_lo2, B_accr, B_cst], writes=[Bout])

            split3(ctile, B_c)
            augrows(C_MQ, Qaug[RS, 0:W], B_Q)
            if ti == 0:
                P.add("dve", lambda e: e.tensor_tensor(out=lrow2[RS, 0:W], in0=ctile[RS, 0:W], in1=bigm[RS, 0:W], op=ALU.add), reads=[B_c, B_bigm], writes=[B_lrow2])
                split3(lrow2, B_lrow2)
            augrows(C_MK, Kaug[RS, c0:c0 + W], B_K[ti])

            if STAGE <= 4:
                return
            nkb = blk0 + nb
            grp = 0
            kb = 0
            while kb < nkb:
                n2 = min(2, nkb - kb)
                g = grp % 2
                grp += 1
                rd = [B_K[(kb + j) * 128 // 512] for j in range(n2)]
                for j in range(n2):
                    P.add("pe", lambda e, g=g, j=j, kb=kb: e.matmul(bank[2 * g + j][:, 0:W], Kaug[0:70, (kb + j) * 128:(kb + j + 1) * 128], Qaug[0:70, 0:W], start=True, stop=True),
                          reads=[B_Q, rd[j]], writes=[bankbuf[2 * g + j]])
                if W == 512:
                    P.add("act", lambda e, g=g, n2=n2: e.activation(out=PT[g][:, 0:n2, :], in_=psum[:, g * 1024:g * 1024 + n2 * 512].rearrange("p (j w) -> p j w", w=512), func=AF.Exp),
                          reads=[bankbuf[2 * g + j] for j in range(n2)], writes=[B_PT[g]])
                else:
                    for j in range(n2):
                        P.add("act", lambda e, g=g, j=j: e.activation(out=PT[g][:, j, 0:W], in_=bank[2 * g + j][:, 0:W], func=AF.Exp),
                              reads=[bankbuf[2 * g + j]], writes=[B_PT[g]])
                for j in range(n2):
                    jj = kb + j - blk0
                    if jj >= 0:
                        P.add("pool", lambda e, g=g, j=j, jj=jj: e.affine_select(out=PT[g][:, j, 0:W], in_=PT[g][:, j, 0:W], pattern=[[1, W]], compare_op=ALU.is_ge,
                                                                                 fill=0.0, base=-128 * jj, channel_multiplier=-1),
                              reads=[B_PT[g]], writes=[B_PT[g]])
                for j in range(n2):
                    kk = kb + j
                    P.add("pe", lambda e, g=g, j=j, kk=kk: e.matmul(bank[4][0:65, 0:W], Vc[:, kk, 0:65], PT[g][:, j, 0:W], start=(kk == 0), stop=(kk == nkb - 1)),
                          reads=[B_PT[g], B_V[kk * 128 // 512], B_Vinit], writes=[bankbuf[4]])
                kb += n2
            P.add("act", lambda e: e.copy(O_sb[0:65, 0:W], bank[4][0:65, 0:W]), reads=[bankbuf[4]], writes=[B_Osb])
            b = gbank()
            P.add("pe", lambda e, b=b: e.matmul(bank[b][0:64, 0:W], ones_f[64:65, 0:64], O_sb[64:65, 0:W], start=True, stop=True),
                  reads=[B_Osb, B_misc], writes=[bankbuf[b]])
            P.add("dve", lambda e, b=b: e.reciprocal(out=rden[0:64, 0:W], in_=bank[b][0:64, 0:W]), reads=[bankbuf[b]], writes=[B_rden])
            P.add("dve", lambda e: e.tensor_tensor(out=YA[0:64, 0:W], in0=O_sb[0:64, 0:W], in1=rden[0:64, 0:W], op=ALU.mult), reads=[B_Osb, B_rden], writes=[B_YA])
            y_out(YA, 0, 64, c0, W, B_YA)

            if STAGE <= 5:
                return
            def conv(X_, B_X, off, out_, B_out):
                P.add("dve", lambda e: e.tensor_scalar(out=out_[:, 0:W], in0=X_[:, 0:W], scalar1=cwq(off), scalar2=None, op0=ALU.mult), reads=[B_X, B_pv], writes=[B_out])
                for j in range(1, 4):
                    P.add("dve", lambda e, j=j: e.scalar_tensor_tensor(out=out_[:, 0:W], in0=X_[:, j:j + W], scalar=cwq(off + j), in1=out_[:, 0:W], op0=ALU.mult, op1=ALU.add),
                          reads=[B_X, B_pv, B_out], writes=[B_out])
                P.add("pool", lambda e: e.tensor_copy(out=X_[:, 0:3], in_=X_[:, W:W + 3]), reads=[B_X], writes=[B_X])

            conv(Xq, B_Xq, 0, cq, B_cq)
            conv(Xk, B_Xk, 4, ck, B_ck)
            conv(Xv, B_Xv, 8, cv, B_cv)
            P.add("act", lambda e: e.activation(out=qs[:, 0:W], in_=cq[:, 0:W], func=AF.Silu), reads=[B_cq], writes=[B_qs])
            P.add("act", lambda e: e.activation(out=ks[:, 0:W], in_=ck[:, 0:W], func=AF.Silu), reads=[B_ck], writes=[B_ks])
            P.add("act", lambda e: e.activation(out=vsT[:, 0:W], in_=cv[:, 0:W], func=AF.Silu), reads=[B_cv], writes=[B_vsT])
            for (src, Bs, sq_, Bsq, rn, Brn, outb, Bo, sc) in ((qs, B_qs, sqq, B_sqq, rnq, B_rnq, qnT, B_qnT, 128.0), (ks, B_ks, sqk, B_sqk, rnk, B_rnk, knT, B_knT, 1.0)):
                P.add("act", lambda e, src=src, sq_=sq_: e.activation(out=sq_[:, 0:W], in_=src[:, 0:W], func=AF.Square), reads=[Bs], writes=[Bsq])
                b = gbank()
                P.add("pe", lambda e, b=b, sq_=sq_: e.matmul(bank[b][:, 0:W], ones_f, sq_[:, 0:W], start=True, stop=True), reads=[Bsq, B_misc], writes=[bankbuf[b]])
                P.add("dve", lambda e, b=b, rn=rn, sc=sc: e.tensor_scalar(out=rn[:, 0:W], in0=bank[b][:, 0:W], scalar1=NORM_EPS, scalar2=sc, op0=ALU.add, op1=ALU.mult),
                      reads=[bankbuf[b]], writes=[Brn])
                P.add("act", lambda e, rn=rn: e.activation(out=rn[:, 0:W], in_=rn[:, 0:W], func=AF.Sqrt), reads=[Brn], writes=[Brn])
                P.add("dve", lambda e, rn=rn: e.reciprocal(out=rn[:, 0:W], in_=rn[:, 0:W]), reads=[Brn], writes=[Brn])
                P.add("dve", lambda e, src=src, rn=rn, outb=outb: e.tensor_tensor(out=outb[:, 0:W], in0=src[:, 0:W], in1=rn[:, 0:W], op=ALU.mult), reads=[Bs, Brn], writes=[Bo])

            if STAGE <= 6:
                return
            a_in = sctm[:, 0:nb, 6]
            b_in = sctm[:, 0:nb, 7]
            nbs = slice(0, nb)
            P.add("act", lambda e: e.activation(out=t_beta[:, nbs], in_=b_in, func=AF.Sigmoid), reads=[B_sctm], writes=[B_tms])
            P.add("dve", lambda e: e.tensor_scalar(out=t_nbeta[:, nbs], in0=t_beta[:, nbs], scalar1=-1.0, scalar2=None, op0=ALU.mult), reads=[B_tms], writes=[B_tms])
            P.add("dve", lambda e: e.tensor_scalar(out=t_x[:, nbs], in0=a_in, scalar1=pv[:, PV_DT:PV_DT + 1], scalar2=None, op0=ALU.add), reads=[B_sctm, B_pv], writes=[B_tms])
            P.add("dve", lambda e: e.tensor_scalar(out=t_ax[:, nbs], in0=t_x[:, nbs], scalar1=-1.0, scalar2=None, op0=ALU.mult), reads=[B_tms], writes=[B_tms])
            P.add("dve", lambda e: e.tensor_tensor(out=t_ax[:, nbs], in0=t_ax[:, nbs], in1=t_x[:, nbs], op=ALU.max), reads=[B_tms], writes=[B_tms])
            P.add("act", lambda e: e.activation(out=t_e[:, nbs], in_=t_ax[:, nbs], func=AF.Exp, scale=-1.0), reads=[B_tms], writes=[B_tms])
            P.add("act", lambda e: e.activation(out=t_l[:, nbs], in_=t_e[:, nbs], func=AF.Ln, bias=1.0), reads=[B_tms], writes=[B_tms])
            P.add("dve", lambda e: e.scalar_tensor_tensor(out=t_g[:, nbs], in0=t_x[:, nbs], scalar=0.0, in1=t_l[:, nbs], op0=ALU.max, op1=ALU.add), reads=[B_tms], writes=[B_tms])
            P.add("dve", lambda e: e.tensor_scalar(out=t_g[:, nbs], in0=t_g[:, nbs], scalar1=sc1[:, 0:1], scalar2=None, op0=ALU.mult), reads=[B_tms, B_sc1], writes=[B_tms])
            bg = gbank()
            P.add("pe", lambda e, bg=bg: e.matmul(bank[bg][:, 0:nb], cst[:, C_LT2:C_LT2 + 128], t_g[:, nbs], start=True, stop=True), reads=[B_tms, B_cst], writes=[bankbuf[bg]])
            P.add("pe", lambda e, bg=bg: e.matmul(bank[bg][:, 8:8 + nb], cst[:, C_BLK:C_BLK + 128], t_g[:, nbs], start=True, stop=True), reads=[B_tms, B_cst], writes=[bankbuf[bg]])
            for c_ in range(2):
                P.add("dve", lambda e, c_=c_: e.tensor_scalar(out=rhs8[:, 0:2 * nb].rearrange("p (j c) -> p j c", c=2)[:, :, c_], in0=t_g[:, nbs],
                                                               scalar1=cst[:, C_SEL + c_:C_SEL + c_ + 1], scalar2=None, op0=ALU.mult),
                      reads=[B_tms, B_cst], writes=[B_glbc])
            P.add("pe", lambda e, bg=bg: e.matmul(bank[bg][:, 16:16 + 2 * nb], ones_f, rhs8[:, 0:2 * nb], start=True, stop=True), reads=[B_glbc, B_misc], writes=[bankbuf[bg]])
            P.add("dve", lambda e, bg=bg: e.tensor_copy(out=t_gc[:, nbs], in_=bank[bg][:, 0:nb]), reads=[bankbuf[bg]], writes=[B_tms])
            P.add("dve", lambda e: e.tensor_scalar(out=t_ngc[:, nbs], in0=t_gc[:, nbs], scalar1=-1.0, scalar2=None, op0=ALU.mult), reads=[B_tms], writes=[B_tms])
            P.add("dve", lambda e, bg=bg: e.tensor_tensor(out=t_d[:, nbs], in0=bank[bg][:, 8:8 + nb], in1=t_gc[:, nbs], op=ALU.subtract), reads=[bankbuf[bg], B_tms], writes=[B_tms])
            P.add("act", lambda e: e.activation(out=t_e2[:, nbs], in_=t_d[:, nbs], func=AF.Exp), reads=[B_tms], writes=[B_tms])
            P.add("act", lambda e: e.activation(out=t_e1[:, nbs], in_=t_gc[:, nbs], func=AF.Exp), reads=[B_tms], writes=[B_tms])
            P.add("dve", lambda e: e.tensor_tensor(out=t_e1[:, nbs], in0=t_e1[:, nbs], in1=t_beta[:, nbs], op=ALU.mult), reads=[B_tms], writes=[B_tms])
            P.add("act", lambda e, bg=bg: e.activation(out=glbc[:, 0:2 * nb], in_=bank[bg][:, 16:16 + 2 * nb], func=AF.Exp), reads=[bankbuf[bg]], writes=[B_glbc])

            if STAGE <= 7:
                return
            for j in range(nb):
                b = gbank()
                pb = bank[b].bitcast(BF16)
                P.add("pe", lambda e, j=j, pb=pb: e.transpose(pb[:, 0:128], knT[:, j * 128:(j + 1) * 128], identb), reads=[B_knT, B_misc], writes=[bankbuf[b]])
                P.add("pe", lambda e, j=j, pb=pb: e.transpose(pb[:, 128:256], vsT[:, j * 128:(j + 1) * 128], identb), reads=[B_vsT, B_misc], writes=[bankbuf[b]])
                P.add("dve", lambda e, j=j, pb=pb: e.tensor_scalar(out=Xuw[:, j, 128:256], in0=pb[:, 0:128], scalar1=t_e1[:, j:j + 1], scalar2=None, op0=ALU.mult),
                      reads=[bankbuf[b], B_tms], writes=[B_Xuw])
                P.add("dve", lambda e, j=j, pb=pb: e.tensor_scalar(out=kdec[:, j, :], in0=pb[:, 0:128], scalar1=t_e2[:, j:j + 1], scalar2=None, op0=ALU.mult),
                      reads=[bankbuf[b], B_tms], writes=[B_kdec])
                P.add("dve", lambda e, j=j, pb=pb: e.tensor_scalar(out=Xuw[:, j, 0:128], in0=pb[:, 128:256], scalar1=t_beta[:, j:j + 1], scalar2=None, op0=ALU.mult),
                      reads=[bankbuf[b], B_tms], writes=[B_Xuw])

            if STAGE <= 8:
                return
            for j in range(nb):
                q2 = j % 2
                cs = slice(j * 128, (j + 1) * 128)
                P.add("dve", lambda e, j=j, q2=q2: e.tensor_scalar(out=diag[q2], in0=ident, scalar1=t_gc[:, j:j + 1], scalar2=None, op0=ALU.mult),
                      reads=[B_cst, B_tms], writes=[B_diag[q2]])
                b = gbank()
                P.add("pe", lambda e, b=b, q2=q2: e.matmul(bank[b][:, 0:128], ones_f, diag[q2], start=True, stop=True), reads=[B_diag[q2], B_misc], writes=[bankbuf[b]])
                P.add("dve", lambda e, b=b, q2=q2: e.scalar_tensor_tensor(out=T1[q2], in0=bank[b][:, 0:128], scalar=-1.0, in1=cst[:, C_MSL:C_MSL + 128], op0=ALU.mult, op1=ALU.add),
                      reads=[bankbuf[b], B_cst], writes=[B_T1[q2]])
                P.add("act", lambda e, j=j, q2=q2: e.activation(out=Dsl[q2], in_=T1[q2], func=AF.Exp, bias=t_gc[:, j:j + 1]), reads=[B_T1[q2], B_tms], writes=[B_Dsl[q2]])
                P.add("dve", lambda e, b=b, q2=q2: e.tensor_tensor(out=T3[q2], in0=bank[b][:, 0:128], in1=cst[:, C_MIU:C_MIU + 128], op=ALU.add),
                      reads=[bankbuf[b], B_cst], writes=[B_T3[q2]])
                P.add("act", lambda e, j=j, q2=q2: e.activation(out=Diu[q2], in_=T3[q2], func=AF.Exp, bias=t_ngc[:, j:j + 1]), reads=[B_T3[q2], B_tms], writes=[B_Diu[q2]])
                P.add("act", lambda e, b=b, j=j: e.activation(out=EGR[j], in_=bank[b][:, 0:128], func=AF.Exp), reads=[bankbuf[b]], writes=[B_EGR[j]])
                b2_ = gbank()
                P.add("pe", lambda e, b2_=b2_, cs=cs: e.matmul(bank[b2_][:, 0:128], knT[:, cs], knT[:, cs], start=True, stop=True), reads=[B_knT], writes=[bankbuf[b2_]])
                P.add("pe", lambda e, b2_=b2_, cs=cs: e.matmul(bank[b2_][:, 128:256], knT[:, cs], qnT[:, cs], start=True, stop=True), reads=[B_knT, B_qnT], writes=[bankbuf[b2_]])
                P.add("dve", lambda e, b2_=b2_, j=j, q2=q2: e.scalar_tensor_tensor(out=PP[j][0][:, 0:128], in0=bank[b2_][:, 0:128], scalar=t_nbeta[:, j:j + 1], in1=Dsl[q2],
                                                                                  op0=ALU.mult, op1=ALU.mult),
                      reads=[bankbuf[b2_], B_tms, B_Dsl[q2]], writes=[B_PP[j][0]])
                P.add("dve", lambda e, b2_=b2_, j=j, q2=q2: e.tensor_tensor(out=attnT[j], in0=bank[b2_][:, 128:256], in1=Diu[q2], op=ALU.mult),
                      reads=[bankbuf[b2_], B_Diu[q2]], writes=[B_attnT[j]])
                b3 = gbank()
                pb3 = bank[b3].bitcast(BF16)
                P.add("pe", lambda e, pb3=pb3, j=j: e.transpose(pb3[:, 0:128], PP[j][0][:, 0:128], identb), reads=[B_PP[j][0], B_misc], writes=[bankbuf[b3]])
                P.add("act", lambda e, pb3=pb3, j=j: e.copy(PP[j][0][:, 128:256], pb3[:, 0:128]), reads=[bankbuf[b3]], writes=[B_PP[j][0]])
                P.add("dve", lambda e, pb3=pb3, j=j: e.tensor_tensor(out=RR[j][0], in0=pb3[:, 0:128], in1=identb, op=ALU.add), reads=[bankbuf[b3], B_misc], writes=[B_RR[j][0]])
            if STAGE <= 9:
                return
            for m in range(1, 6):
                src, dst = (m - 1) % 2, m % 2
                for j in range(nb):
                    b = gbank()
                    pbf = bank[b]
                    P.add("pe", lambda e, b=b, j=j, src=src: e.matmul(bank[b][:, 0:128], PP[j][src][:, 128:256], PP[j][src][:, 0:128], start=True, stop=True),
                          reads=[B_PP[j][src]], writes=[bankbuf[b]])
                    if m < 5:
                        P.add("pe", lambda e, b=b, j=j, src=src: e.matmul(bank[b][:, 128:256], PP[j][src][:, 0:128], PP[j][src][:, 128:256], start=True, stop=True),
                              reads=[B_PP[j][src]], writes=[bankbuf[b]])
                    wcols = 256 if m < 5 else 128
                    P.add("act", lambda e, b=b, j=j, dst=dst, wcols=wcols: e.copy(PP[j][dst][:, 0:wcols], bank[b][:, 0:wcols]), reads=[bankbuf[b]], writes=[B_PP[j][dst]])
                    b2_ = gbank()
                    P.add("pe", lambda e, b2_=b2_, j=j, src=src, dst=dst: e.matmul(bank[b2_][:, 0:128], PP[j][dst][:, 0:128], RR[j][src], start=True, stop=True),
                          reads=[B_PP[j][dst], B_RR[j][src]], writes=[bankbuf[b2_]])
                    P.add("dve", lambda e, b2_=b2_, j=j, src=src, dst=dst: e.tensor_tensor(out=RR[j][dst], in0=bank[b2_][:, 0:128], in1=RR[j][src], op=ALU.add),
                          reads=[bankbuf[b2_], B_RR[j][src]], writes=[B_RR[j][dst]])
            RF = 1
            if STAGE <= 10:
                return
            for j in range(nb):
                b = gbank()
                P.add("pe", lambda e, b=b, j=j: e.matmul(bank[b][:, 0:256], RR[j][RF], Xuw[:, j, :], start=True, stop=True), reads=[B_RR[j][RF], B_Xuw], writes=[bankbuf[b]])
                P.add("act", lambda e, b=b, j=j: e.copy(UW[:, j, :], bank[b][:, 0:256]), reads=[bankbuf[b]], writes=[B_UW[j]])
                for h in range(2):
                    n = 2 * j + h
                    rs_ = slice(64 * h, 64 * h + 64)
                    b2_ = gbank()
                    P.add("pe", lambda e, b2_=b2_, j=j, rs_=rs_: e.matmul(bank[b2_][:, 0:128], UW[rs_, j, 128:256], kdec[rs_, j, :], start=True, stop=True),
                          reads=[B_UW[j], B_kdec], writes=[bankbuf[b2_]])
                    P.add("pe", lambda e, b2_=b2_, j=j, rs_=rs_: e.matmul(bank[b2_][:, 128:256], kdec[rs_, j, :], UW[rs_, j, 0:128], start=True, stop=True),
                          reads=[B_UW[j], B_kdec], writes=[bankbuf[b2_]])
                    P.add("dve", lambda e, b2_=b2_, n=n: e.scalar_tensor_tensor(out=MT[n], in0=ident, scalar=glbc[:, n:n + 1], in1=bank[b2_][:, 0:128], op0=ALU.mult, op1=ALU.subtract),
                          reads=[bankbuf[b2_], B_glbc, B_cst], writes=[B_MT[n]])
                    P.add("act", lambda e, b2_=b2_, n=n: e.copy(Bn[n], bank[b2_][:, 128:256]), reads=[bankbuf[b2_]], writes=[B_Bn[n]])
                cs = slice(j * 128, (j + 1) * 128)
                P.add("dve", lambda e, j=j, cs=cs: e.tensor_tensor(out=qdecT[:, cs], in0=qnT[:, cs], in1=EGR[j], op=ALU.mult), reads=[B_qnT, B_EGR[j]], writes=[B_qdecT])
                b3 = gbank()
                P.add("pe", lambda e, b3=b3, j=j: e.matmul(bank[b3][:, 0:128], UW[:, j, 128:256], attnT[j], start=True, stop=True), reads=[B_UW[j], B_attnT[j]], writes=[bankbuf[b3]])
                P.add("dve", lambda e, b3=b3, cs=cs: e.tensor_tensor(out=QeffT[:, cs], in0=qdecT[:, cs], in1=bank[b3][:, 0:128], op=ALU.subtract),
                      reads=[bankbuf[b3], B_qdecT], writes=[B_QeffT])
            if STAGE <= 11:
                return
            bo = gbank()
            for n in range(2 * nb):
                j, h = n // 2, n % 2
                rs_ = slice(64 * h, 64 * h + 64)
                si = s_cur[0]
                sn = (si + 1) % 9
                col = slice(j * 128 + 64 * h, j * 128 + 64 * h + 64)
                P.add("pe", lambda e, j=j, rs_=rs_, col=col, h=h: e.matmul(bank[bo][:, col], UW[rs_, j, 0:128], attnT[j][rs_, 64 * h:64 * h + 64], start=True, stop=False),
                      reads=[B_UW[j], B_attnT[j]], writes=[bankbuf[bo]])
                P.add("pe", lambda e, si=si, col=col: e.matmul(bank[bo][:, col], Sst[si], QeffT[:, col], start=False, stop=True),
                      reads=[B_S[si], B_QeffT], writes=[bankbuf[bo]])
                bs = gbank()
                if bs == bo:
                    bs = gbank()
                P.add("pe", lambda e, bs=bs, n=n, si=si: e.matmul(bank[bs][:, 0:128], MT[n], Sst[si], start=True, stop=True), reads=[B_MT[n], B_S[si]], writes=[bankbuf[bs]])
                P.add("dve", lambda e, bs=bs, n=n, sn=sn: e.tensor_tensor(out=Sst[sn], in0=bank[bs][:, 0:128], in1=Bn[n], op=ALU.add), reads=[bankbuf[bs], B_Bn[n]], writes=[B_S[sn]])
                s_cur[0] = sn
            if STAGE <= 12:
                return
            P.add("act", lambda e: e.activation(out=sqo[:, 0:W], in_=bank[bo][:, 0:W], func=AF.Square), reads=[bankbuf[bo]], writes=[B_sqo])
            b = gbank()
            if b == bo:
                b = gbank()
            P.add("pe", lambda e, b=b: e.matmul(bank[b][:, 0:W], ones_f, sqo[:, 0:W], start=True, stop=True), reads=[B_sqo, B_misc], writes=[bankbuf[b]])
            P.add("dve", lambda e, b=b: e.tensor_scalar(out=rno[:, 0:W], in0=bank[b][:, 0:W], scalar1=1.0 / 128.0, scalar2=NORM_EPS, op0=ALU.mult, op1=ALU.add),
                  reads=[bankbuf[b]], writes=[B_rno])
            P.add("act", lambda e: e.activation(out=rno[:, 0:W], in_=rno[:, 0:W], func=AF.Sqrt), reads=[B_rno], writes=[B_rno])
            P.add("dve", lambda e: e.reciprocal(out=rno[:, 0:W], in_=rno[:, 0:W]), reads=[B_rno], writes=[B_rno])
            P.add("dve", lambda e: e.scalar_tensor_tensor(out=sqo[:, 0:W], in0=bank[bo][:, 0:W], scalar=pv[:, PV_GNW:PV_GNW + 1], in1=rno[:, 0:W], op0=ALU.mult, op1=ALU.mult),
                  reads=[bankbuf[bo], B_rno, B_pv, B_sqo], writes=[B_sqo])
            P.add("dve", lambda e: e.tensor_tensor(out=YB[:, 0:W], in0=sqo[:, 0:W], in1=zs[:, 0:W], op=ALU.mult), reads=[B_sqo, B_zs], writes=[B_YB])
            y_out(YB, 64, 128, c0, W, B_YB)

        KNT = 999
        for ti_, (c0_t, W_t) in enumerate(tiles[:KNT] if DO_P1 else []):
            p1_tile(ti_, c0_t, W_t)

        STOP = {"fused": 0, "p1": 1, "p2": 0}[mode]
        B_ag = Buf("ag")
        if mode == "fused":
          for pc in range(16):
            P.add("pool", lambda e, pc=pc: e.collective_compute("AllGather", ALU.bypass, replica_groups=[list(range(NCORES))],
                                                                 ins=[ysrc[pc * 96:(pc + 1) * 96, :].opt()], outs=[agout[pc * 768:(pc + 1) * 768, :].opt()]),
                  reads=[B_ysrc], writes=[B_ag], key=B_ag, inc=1)
        P.barrier()

        A.off = base_off
        bufA = A.f32(KC * 512).rearrange("p (k w) -> p k w", k=KC)
        bufH = A.f32(KC * 512).rearrange("p (k w) -> p k w", k=KC)
        bufH1 = A.f32(KC * 512).rearrange("p (k w) -> p k w", k=KC)
        hb = A.bf16(KC * 512).rearrange("p (k w) -> p k w", k=KC)
        h1b = A.bf16(KC * 512).rearrange("p (k w) -> p k w", k=KC)
        yb16 = A.bf16(12 * FS).rearrange("p (k w) -> p k w", k=12)
        mixb = A.bf16(KC * 512).rearrange("p (k w) -> p k w", k=KC)
        actb = A.bf16(FC * 512).rearrange("p (k w) -> p k w", k=FC)
        ring = [A.bf16(KC * 512).rearrange("p (k w) -> p k w", k=KC) for _ in range(2)]
        B_ring = [Buf("ring%d" % i) for i in range(2)]
        dpan = A.bf16(FC * 512).rearrange("p (k w) -> p k w", k=FC)
        B_dpan = Buf("dpan")
        sq2 = actb[:, 0:8, :]
        xb2 = actb[:, 8:16, :]
        l_mean = A.f32(512); l_rstd = A.f32(512); l_t = A.f32(512); l_t2 = A.f32(512); l_t3 = A.f32(512)
        B_bufA, B_bufH, B_bufH1, B_hb, B_h1b, B_yb16, B_mixb, B_actb = [Buf(n) for n in ("bufA", "bufH", "bufH1", "hb", "h1b", "yb16", "mixb", "actb")]
        B_lmean, B_lrstd, B_lt, B_lt2, B_lt3 = [Buf(n) for n in ("lmean", "lrstd", "lt", "lt2", "lt3")]
        B_sq2 = B_actb
        B_xb2 = B_actb
        gp2 = [0]

        def gb2():
            b = gp2[0]
            gp2[0] = (gp2[0] + 1) % 8
            return b

        rp = [0]

        def load_panel(src_ap, kk, ncols):
            i = rp[0]
            rp[0] = (rp[0] + 1) % 2
            P.add("pool", lambda e: e.dma_start(out=ring[i][:, 0:kk, 0:ncols], in_=src_ap.rearrange("(k p) c -> p k c", p=128)), writes=[B_ring[i]], key=B_ring[i])
            return ring[i], B_ring[i]

        def layer_norm(src, Bsrc, dst32, Bdst32, dstb, Bdstb, gcol, bcol, W):
            P.add("pool", lambda e: e.tensor_copy(out=xb2[:, :, 0:W], in_=src[:, :, 0:W]), reads=[Bsrc], writes=[B_xb2])
            P.add("pool", lambda e: e.tensor_tensor(out=sq2[:, :, 0:W], in0=src[:, :, 0:W], in1=src[:, :, 0:W], op=ALU.mult), reads=[Bsrc], writes=[B_sq2])
            b1, b2 = gb2(), gb2()
            for k in range(KC):
                P.add("pe", lambda e, k=k: e.matmul(bank[b1][:, 0:W], onesb, xb2[:, k, 0:W], start=(k == 0), stop=(k == KC - 1)), reads=[B_xb2, B_misc], writes=[bankbuf[b1]])
            for k in range(KC):
                P.add("pe", lambda e, k=k: e.matmul(bank[b2][:, 0:W], onesb, sq2[:, k, 0:W], start=(k == 0), stop=(k == KC - 1)), reads=[B_sq2, B_misc], writes=[bankbuf[b2]])
            P.add("act", lambda e: e.copy(l_mean[:, 0:W], bank[b1][:, 0:W]), reads=[bankbuf[b1]], writes=[B_lmean])
            P.add("dve", lambda e: e.tensor_tensor(out=l_t[:, 0:W], in0=l_mean[:, 0:W], in1=l_mean[:, 0:W], op=ALU.mult), reads=[B_lmean], writes=[B_lt])
            P.add("dve", lambda e: e.tensor_tensor(out=l_t[:, 0:W], in0=bank[b2][:, 0:W], in1=l_t[:, 0:W], op=ALU.subtract), reads=[bankbuf[b2], B_lt], writes=[B_lt])
            P.add("dve", lambda e: e.tensor_scalar(out=l_t[:, 0:W], in0=l_t[:, 0:W], scalar1=0.0, scalar2=LN_EPS, op0=ALU.max, op1=ALU.add), reads=[B_lt], writes=[B_lt])
            P.add("act", lambda e: e.activation(out=l_t[:, 0:W], in_=l_t[:, 0:W], func=AF.Sqrt), reads=[B_lt], writes=[B_lt])
            P.add("dve", lambda e: e.reciprocal(out=l_rstd[:, 0:W], in_=l_t[:, 0:W]), reads=[B_lt], writes=[B_lrstd])
            for k in range(KC):
                P.add("dve", lambda e, k=k: e.tensor_tensor(out=l_t2[:, 0:W], in0=src[:, k, 0:W], in1=l_mean[:, 0:W], op=ALU.subtract), reads=[Bsrc, B_lmean], writes=[B_lt2])
                P.add("dve", lambda e, k=k: e.tensor_tensor(out=l_t2[:, 0:W], in0=l_t2[:, 0:W], in1=l_rstd[:, 0:W], op=ALU.mult), reads=[B_lt2, B_lrstd], writes=[B_lt2])
                P.add("act", lambda e, k=k: e.activation(out=dst32[:, k, 0:W], in_=l_t2[:, 0:W], func=AF.Identity, bias=pv[:, bcol + k:bcol + k + 1], scale=pv[:, gcol + k:gcol + k + 1]),
                      reads=[B_lt2, B_pv], writes=[Bdst32])
                if dstb is not None:
                    P.add("pool", lambda e, k=k: e.tensor_copy(out=dstb[:, k, 0:W], in_=dst32[:, k, 0:W]), reads=[Bdst32], writes=[Bdstb])

        xov = xown.rearrange("(k p) t -> p k t", p=128)
        outv = outT.rearrange("(k p) t -> p k t", p=128) if mode != "p1" else None
        B_out = Buf("out")
        pid_cache = {}

        def pid_of(e):
            return e.partition_id()

        agv5 = agout.rearrange("(s hf r f) t -> s hf r f t", s=8, hf=2, r=8)
        agv6 = agout.rearrange("(s hf q h f) t -> s hf q h f t", s=8, hf=2, q=4, h=2)

        if mode == "p2":
            for r in range(NCORES):
                P.add("sp", lambda e, r=r: e.dma_start(out=yb16[:, 4 + r, 0:FS], in_=yin[r * 192 + 64:r * 192 + 192, :]), writes=[B_yb16], key=B_yb16)
                P.add("sp", lambda e, r=r: e.dma_start(out=yb16[64 * (r % 2):64 * (r % 2) + 64, r // 2, 0:FS], in_=yin[r * 192:r * 192 + 64, :]), writes=[B_yb16], key=B_yb16)
        elif mode == "fused":
            def ldb1(e):
                pid = e.partition_id()
                src = agv5[bass.ds(pid, 1), 0, :, 64:96, 0:FS].rearrange("s r f t -> f (s r) t")
                return e.dma_start(out=yb16[0:32, 4:12, 0:FS], in_=src)
            P.add("sp", ldb1, reads=[B_ag], writes=[B_yb16], key=B_yb16)

            def ldb2(e):
                pid = e.partition_id()
                src = agv5[bass.ds(pid, 1), 1, :, 0:96, 0:FS].rearrange("s r f t -> f (s r) t")
                return e.dma_start(out=yb16[32:128, 4:12, 0:FS], in_=src)
            P.add("sp", ldb2, reads=[B_ag], writes=[B_yb16], key=B_yb16)
            for hh in range(2):
                def lda(e, hh=hh):
                    pid = e.partition_id()
                    src = agv6[bass.ds(pid, 1), 0, :, hh, 0:64, 0:FS].rearrange("s q f t -> f (s q) t")
                    return e.dma_start(out=yb16[64 * hh:64 * hh + 64, 0:4, 0:FS], in_=src)
                P.add("sp", lda, reads=[B_ag], writes=[B_yb16], key=B_yb16)


        def p2_tile(t2):
            t0 = t2 * W2
            W = W2
            P.add("sp", lambda e, t0=t0: e.dma_start(out=bufA[:, :, 0:W], in_=xov[:, :, t0:t0 + W]), writes=[B_bufA], key=B_bufA)
            layer_norm(bufA, B_bufA, bufH, B_bufH, hb, B_hb, PV_G0, PV_B0, W)
            for g4 in range(2):
                cs = slice(g4 * 512, g4 * 512 + 512)
                pga, Bga = load_panel(w2g[:, g4 * 512:g4 * 512 + 512], 8, 512)
                pa, Bpa = load_panel(woa[:, cs], 4, 512)
                for half in range(2):
                    if half == 0:
                        pg_, Bg_, pw_, Bw_, nk, yoff = pga, Bga, pa, Bpa, 4, 0
                    else:
                        pg_, Bg_ = load_panel(w2g[:, 1024 + g4 * 512:1024 + g4 * 512 + 512], 8, 512)
                        pw_, Bw_ = load_panel(wob[:, cs], 8, 512)
                        nk, yoff = 8, 4
                    for o in range(4):
                        oc = g4 * 4 + o
                        osl = slice(o * 128, o * 128 + 128)
                        bg_, bw_ = gb2(), gb2()
                        for k in range(KC):
                            P.add("pe", lambda e, k=k, bg_=bg_, pg_=pg_, osl=osl: e.matmul(bank[bg_][:, 0:W], pg_[:, k, osl], hb[:, k, 0:W], start=(k == 0), stop=(k == KC - 1)),
                                  reads=[Bg_, B_hb], writes=[bankbuf[bg_]])
                        for k in range(nk):
                            P.add("pe", lambda e, k=k, bw_=bw_, pw_=pw_, osl=osl, yoff=yoff, nk=nk: e.matmul(bank[bw_][:, 0:W], pw_[:, k, osl], yb16[:, yoff + k, t0:t0 + W], start=(k == 0), stop=(k == nk - 1)),
                                  reads=[Bw_, B_yb16], writes=[bankbuf[bw_]])
                        P.add("act", lambda e, bg_=bg_: e.activation(out=l_t[:, 0:W], in_=bank[bg_][:, 0:W], func=AF.Sigmoid), reads=[bankbuf[bg_]], writes=[B_lt])
                        if half == 0:
                            P.add("dve", lambda e, bw_=bw_, oc=oc: e.tensor_tensor(out=bufA[:, oc, 0:W], in0=bank[bw_][:, 0:W], in1=l_t[:, 0:W], op=ALU.mult),
                                  reads=[bankbuf[bw_], B_lt], writes=[B_bufA])
                        else:
                            P.add("dve", lambda e, bw_=bw_: e.tensor_tensor(out=l_t3[:, 0:W], in0=bank[bw_][:, 0:W], in1=l_t[:, 0:W], op=ALU.mult),
                                  reads=[bankbuf[bw_], B_lt], writes=[B_lt3])
                            P.add("dve", lambda e, oc=oc: e.tensor_tensor(out=mixb[:, oc, 0:W], in0=l_t3[:, 0:W], in1=bufA[:, oc, 0:W], op=ALU.add),
                                  reads=[B_lt3, B_bufA], writes=[B_mixb])
            for g4 in range(2):
                pw_, Bw_ = load_panel(wo[:, g4 * 512:g4 * 512 + 512], 8, 512)
                for o in range(4):
                    oc = g4 * 4 + o
                    osl = slice(o * 128, o * 128 + 128)
                    b = gb2()
                    for k in range(KC):
                        P.add("pe", lambda e, k=k, b=b, pw_=pw_, osl=osl: e.matmul(bank[b][:, 0:W], pw_[:, k, osl], mixb[:, k, 0:W], start=(k == 0), stop=(k == KC - 1)),
                              reads=[Bw_, B_mixb], writes=[bankbuf[b]])
                    P.add("dve", lambda e, b=b, oc=oc: e.scalar_tensor_tensor(out=bufA[:, oc, 0:W], in0=bufH[:, oc, 0:W], scalar=ALPHA, in1=bank[b][:, 0:W], op0=ALU.mult, op1=ALU.add),
                          reads=[bankbuf[b], B_bufH], writes=[B_bufA])
            layer_norm(bufA, B_bufA, bufH1, B_bufH1, h1b, B_h1b, PV_G1, PV_B1, W)
            for c0_ in range(0, DFF, 512):
                ncol = min(512, DFF - c0_)
                pg_, Bg_ = load_panel(wg[:, c0_:c0_ + ncol], 8, ncol)
                pu_, Bu_ = load_panel(wu[:, c0_:c0_ + ncol], 8, ncol)
                for o in range(ncol // 128):
                    fc = c0_ // 128 + o
                    osl = slice(o * 128, o * 128 + 128)
                    bg_, bu_ = gb2(), gb2()
                    for k in range(KC):
                        P.add("pe", lambda e, k=k, bg_=bg_, pg_=pg_, osl=osl: e.matmul(bank[bg_][:, 0:W], pg_[:, k, osl], h1b[:, k, 0:W], start=(k == 0), stop=(k == KC - 1)),
                              reads=[Bg_, B_h1b], writes=[bankbuf[bg_]])
                    for k in range(KC):
                        P.add("pe", lambda e, k=k, bu_=bu_, pu_=pu_, osl=osl: e.matmul(bank[bu_][:, 0:W], pu_[:, k, osl], h1b[:, k, 0:W], start=(k == 0), stop=(k == KC - 1)),
                              reads=[Bu_, B_h1b], writes=[bankbuf[bu_]])
                    P.add("act", lambda e, bg_=bg_: e.activation(out=l_t[:, 0:W], in_=bank[bg_][:, 0:W], func=AF.Silu), reads=[bankbuf[bg_]], writes=[B_lt])
                    P.add("dve", lambda e, bu_=bu_, fc=fc: e.tensor_tensor(out=actb[:, fc, 0:W], in0=bank[bu_][:, 0:W], in1=l_t[:, 0:W], op=ALU.mult),
                          reads=[bankbuf[bu_], B_lt], writes=[B_actb])
            for g4 in range(2):
                P.add("pool", lambda e, g4=g4: e.dma_start(out=dpan[:, :, :], in_=wd[:, g4 * 512:g4 * 512 + 512].rearrange("(k p) c -> p k c", p=128)), writes=[B_dpan], key=B_dpan)
                for o in range(4):
                    oc = g4 * 4 + o
                    osl = slice(o * 128, o * 128 + 128)
                    b = gb2()
                    for k in range(FC):
                        P.add("pe", lambda e, k=k, b=b, osl=osl: e.matmul(bank[b][:, 0:W], dpan[:, k, osl], actb[:, k, 0:W], start=(k == 0), stop=(k == FC - 1)),
                              reads=[B_dpan, B_actb], writes=[bankbuf[b]])
                    P.add("dve", lambda e, b=b, oc=oc: e.scalar_tensor_tensor(out=bufA[:, oc, 0:W], in0=bufH1[:, oc, 0:W], scalar=ALPHA, in1=bank[b][:, 0:W], op0=ALU.mult, op1=ALU.add),
                          reads=[bankbuf[b], B_bufH1], writes=[B_bufA])
            layer_norm(bufA, B_bufA, bufH, B_bufH, None, None, PV_G2, PV_B2, W)
            P.add("sp", lambda e, t0=t0: e.dma_start(out=outv[:, :, t0:t0 + W], in_=bufH[:, :, 0:W]), reads=[B_bufH], writes=[B_out], key=B_out)
        if STOP == 0:
            for t2_ in range(NT2):
                p2_tile(t2_)
        else:
            P.add("sp", lambda e: e.nop(), reads=[B_ysrc])
        P.add("sp", lambda e: e.nop(), reads=[B_out])

        P.finalize()
        engsem = {e: [es.enter_context(nc.semaphore("sem_%s_%d" % (e, i))) for i in range(P.nep[e])] for e in ENGS}
        print("ops per engine", {e: len(P.ops[e]) for e in ENGS}, "epochs", P.nep)
        keysem = {}
        for e in ENGS:
            for op in P.ops[e]:
                if op.key is not None and id(op.key) not in keysem:
                    keysem[id(op.key)] = es.enter_context(nc.semaphore("k_" + op.key.name))
        block = es.enter_context(nc.Block())

        @block.tensor
        def _(h):
            P.emit("pe", h, engsem, keysem)

        @block.scalar
        def _(h):
            P.emit("act", h, engsem, keysem)

        @block.vector
        def _(h):
            P.emit("dve", h, engsem, keysem)

        @block.gpsimd
        def _(h):
            P.emit("pool", h, engsem, keysem)

        @block.sync
        def _(h):
            P.emit("sp", h, engsem, keysem)
    return nc


def make_consts():
    c = np.zeros((128, C_N), np.float32)
    i = np.arange(128)
    same = (i[:, None] // 64) == (i[None, :] // 64)
    c[:, C_ID:C_ID + 128] = np.eye(128, dtype=np.float32)
    c[:, C_LT2:C_LT2 + 128] = (same & (i[:, None] <= i[None, :])).astype(np.float32)
    c[:, C_BLK:C_BLK + 128] = same.astype(np.float32)
    c[:, C_MSL:C_MSL + 128] = np.where(same & (i[:, None] > i[None, :]), 0.0, NEG)
    c[:, C_MIU:C_MIU + 128] = np.where(same & (i[:, None] <= i[None, :]), 0.0, NEG)
    c[:, C_SEL + 0] = (i < 64)
    c[:, C_SEL + 1] = (i >= 64)
    for r in range(3):
        c[64 + r, C_MQ + r] = 1.0
        c[67 + r, C_MQ + 3] = 1.0
        c[67 + r, C_MK + r] = -1.0
        c[64 + r, C_MK + 3] = 1.0
    return c


def shard_inputs(inp):
    x = np.asarray(inp["x"], np.float32)[0]
    S = x.shape[0]
    L = ((64 + S + 127) // 128) * 128
    FS = S // NCORES
    xT = np.zeros((D, L), np.float32)
    xT[:, 48:64] = np.asarray(inp["meta_tokens"], np.float32).T
    xT[:, 64:64 + S] = x.T
    w_in = np.asarray(inp["w_in"], np.float32)[0]
    conv_w = np.asarray(inp["conv_w"], np.float32)[0]
    cst = make_consts()
    col = lambda v: np.ascontiguousarray(np.asarray(v, np.float32).reshape(KC, 128).T)
    maps = []
    for c in range(NCORES):
        w1 = np.zeros((D, W1COLS), np.float32)
        w1[:, G_QA:G_QA + 64] = w_in[:, c * 64:(c + 1) * 64]
        w1[:, G_KA:G_KA + 64] = w_in[:, 512 + c * 64:512 + (c + 1) * 64]
        w1[:, G_V:G_V + 64] = w_in[:, 1024 + c * 64:1024 + (c + 1) * 64]
        for r in range(6):
            w1[:, G_V + 64 + r] = w_in[:, 1536 + c]
        w1[:, G_V + 70] = w_in[:, 4616 + c]
        w1[:, G_V + 71] = w_in[:, 4624 + c]
        w1[:, G_QB:G_QB + 128] = w_in[:, 1544 + c * 128:1544 + (c + 1) * 128]
        w1[:, G_KB:G_KB + 128] = w_in[:, 2568 + c * 128:2568 + (c + 1) * 128]
        w1[:, G_VB:G_VB + 128] = w_in[:, 3592 + c * 128:3592 + (c + 1) * 128]
        w1[:, G_Z:G_Z + 128] = w_in[:, 4632 + c * 128:4632 + (c + 1) * 128]
        pv = np.zeros((128, PV_N), np.float32)
        pv[:, PV_G0:PV_G0 + 8] = col(inp["ln_in_g"])
        pv[:, PV_B0:PV_B0 + 8] = col(inp["ln_in_b"])
        for t, base in enumerate((0, 1024, 2048)):
            pv[:, PV_CW + 4 * t:PV_CW + 4 * t + 4] = conv_w[:, base + c * 128:base + (c + 1) * 128].T
        pv[:, PV_BF] = np.asarray(inp["b_f"], np.float32)[0, c]
        pv[:, PV_ALOG] = np.asarray(inp["a_log"], np.float32)[0, c]
        pv[:, PV_DT] = np.asarray(inp["dt_bias"], np.float32)[0, c]
        pv[:, PV_GNW] = np.asarray(inp["gdn_norm_w"], np.float32)[0]
        pv[:, PV_G1:PV_G1 + 8] = col(inp["ln1_g"])
        pv[:, PV_B1:PV_B1 + 8] = col(inp["ln1_b"])
        pv[:, PV_G2:PV_G2 + 8] = col(inp["ln2_g"])
        pv[:, PV_B2:PV_B2 + 8] = col(inp["ln2_b"])
        maps.append({
            "xT": xT, "xown": np.ascontiguousarray(xT[:, 64 + c * FS:64 + (c + 1) * FS]), "w1": w1, "pv": pv, "cst": cst,
            "w2g": np.ascontiguousarray(w_in[:, 5656:7704]),
            "woa": np.asarray(inp["w_out_a"], np.float32)[0], "wob": np.asarray(inp["w_out_b"], np.float32)[0],
            "wo": np.asarray(inp["w_o"], np.float32)[0], "wg": np.asarray(inp["w_gate"], np.float32)[0],
            "wu": np.asarray(inp["w_up"], np.float32)[0], "wd": np.asarray(inp["w_down"], np.float32)[0],
        })
    return maps, S


P1_KEYS = ("xT", "w1", "pv", "cst")
P2_KEYS = ("xown", "pv", "cst", "w2g", "woa", "wob", "wo", "wg", "wu", "wd")


def kernel(**inputs):
    maps, S = shard_inputs(inputs)
    FS = S // NCORES
    nc1 = build(S, "p1")
    r1 = run_bass_kernel_spmd(nc1, maps, core_ids=list(range(NCORES)))
    ys = [np.asarray(r1.results[c]["ysrc"]).reshape(NCORES, 192, FS) for c in range(NCORES)]
    maps2 = []
    for c in range(NCORES):
        m2 = dict(maps[c])
        m2["yin"] = np.ascontiguousarray(np.concatenate([ys[r][c] for r in range(NCORES)], axis=0))
        maps2.append(m2)
    nc2 = build(S, "p2")
    res = run_bass_kernel_spmd(nc2, maps2, core_ids=list(range(NCORES)))
    out = np.concatenate([np.asarray(res.results[c]["outT"], np.float32).T for c in range(NCORES)], axis=0)
    return out[None].astype(np.float32)
```

```python
import contextlib
import numpy as np
import concourse.bass as bass
import concourse.mybir as mybir
from concourse.bass_utils import run_bass_kernel_spmd

F32 = mybir.dt.float32
BF16 = mybir.dt.bfloat16
ALU = mybir.AluOpType
AF = mybir.ActivationFunctionType

NCORES = 8
D = 1024
KC = 8
DFF = 2816
FC = 22
ALPHA = 2.0 ** 0.25
LN_EPS = 1e-5
NORM_EPS = 1e-6
NEG = -30000.0
G_QA, G_KA, G_V, G_QB, G_KB, G_VB, G_Z = 0, 64, 128, 200, 328, 456, 584
W1COLS = 712
GROUPS = [(G_QA, 64), (G_KA, 64), (G_V, 72), (G_QB, 128), (G_KB, 128), (G_VB, 128), (G_Z, 128)]
PV_G0, PV_B0 = 0, 8
PV_CW = 16
PV_BF, PV_ALOG, PV_DT, PV_GNW = 28, 29, 30, 31
PV_G1, PV_B1, PV_G2, PV_B2 = 32, 40, 48, 56
PV_N = 64
C_ID, C_LT2, C_BLK, C_MSL, C_MIU, C_SEL, C_MQ, C_MK = 0, 128, 256, 384, 512, 640, 642, 646
C_N = 650

ENGS = ("pe", "act", "dve", "pool", "sp")
EPOCH = 16000


class Buf:
    __slots__ = ("w", "r", "name", "excl")

    def __init__(self, name="", excl=False):
        self.w = None
        self.r = []
        self.name = name
        self.excl = excl


class Op:
    __slots__ = ("eng", "fn", "deps", "sig", "val", "key", "inc", "ep")


class Prog:
    def __init__(self):
        self.ops = {e: [] for e in ENGS}
        self.pending = {e: [] for e in ENGS}
        self.dmas = {}
        self.cap = None

    def begin_capture(self):
        self.cap = []

    def end_capture(self):
        c, self.cap = self.cap, None
        return c

    def add_merged(self, lists):
        idx = [0] * len(lists)
        tot = [max(1, len(l)) for l in lists]
        while True:
            best, bf = -1, 2.0
            for i, l in enumerate(lists):
                if idx[i] < len(l):
                    f = idx[i] / tot[i]
                    if f < bf:
                        best, bf = i, f
            if best < 0:
                break
            self.add(*lists[best][idx[best]])
            idx[best] += 1

    def add(self, eng, fn, reads=(), writes=(), key=None, inc=16):
        if self.cap is not None:
            self.cap.append((eng, fn, reads, writes, key, inc))
            return None
        op = Op()
        op.eng, op.fn, op.sig, op.val, op.key, op.inc = eng, fn, False, 0, key, inc
        deps = list(self.pending[eng])
        self.pending[eng] = []
        ex = [b for b in reads if b.excl]
        if ex:
            writes = list(writes) + ex
            reads = [b for b in reads if not b.excl]
        raw = set()
        for b in reads:
            if b.w is not None:
                deps.append(b.w)
                raw.add(id(b.w))
        for b in writes:
            if b.w is not None:
                deps.append(b.w)
                if b.excl:
                    raw.add(id(b.w))
            deps.extend(b.r)
        need, seen = [], set()
        for d in deps:
            if id(d) in seen:
                continue
            seen.add(id(d))
            if d.key is None and key is None and d.eng == eng and (eng == "pe" or id(d) not in raw):
                continue
            if d.key is None:
                d.sig = True
            need.append(d)
        op.deps = need
        for b in reads:
            b.r.append(op)
        for b in writes:
            b.w = op
            b.r = []
        self.ops[eng].append(op)
        if key is not None:
            self.dmas[id(key)] = op
        return op

    def barrier(self):
        last = []
        for e in ENGS:
            if self.ops[e]:
                o = self.ops[e][-1]
                if o.key is None:
                    o.sig = True
                last.append(o)
        last.extend(self.dmas.values())
        for e in ENGS:
            self.pending[e] = list(last)

    def finalize(self):
        keycnt = {}
        for e in ENGS:
            cnt = 0
            for op in self.ops[e]:
                if op.key is not None:
                    k = id(op.key)
                    keycnt[k] = keycnt.get(k, 0) + op.inc
                    op.val = keycnt[k]
                elif op.sig:
                    cnt += 1
                    op.ep = (cnt - 1) // EPOCH
                    op.val = (cnt - 1) % EPOCH + 1
            self.nep = getattr(self, "nep", {})
            self.nep[e] = (cnt - 1) // EPOCH + 1 if cnt else 1

    def emit(self, eng, h, engsem, keysem):
        waited = {}
        for op in self.ops[eng]:
            for d in op.deps:
                s = keysem[id(d.key)] if d.key is not None else engsem[d.eng][d.ep]
                sid = id(s)
                if waited.get(sid, 0) < d.val:
                    h.wait_ge(s, d.val)
                    waited[sid] = d.val
            ins = op.fn(h)
            if op.key is not None:
                ins.then_inc(keysem[id(op.key)], op.inc)
            elif op.sig:
                ins.then_inc(engsem[eng][op.ep], 1)


def build(S, mode="fused"):
    L = ((64 + S + 127) // 128) * 128
    NBLK = L // 128
    tiles = []
    p = 0
    while p < L:
        w = min(512, L - p)
        tiles.append((p, w))
        p += w
    FS = S // NCORES
    W2 = min(512, FS)
    NT2 = FS // W2

    nc = bass.Bass("TRN2", target_bir_lowering=False)
    dt_in = lambda n, shp: nc.dram_tensor(n, shp, F32, kind="ExternalInput").ap()
    xT = dt_in("xT", [D, L])
    xown = dt_in("xown", [D, FS])
    w1 = dt_in("w1", [D, W1COLS])
    pv_d = dt_in("pv", [128, PV_N])
    cst_d = dt_in("cst", [128, C_N])
    w2g = dt_in("w2g", [D, 2048])
    woa = dt_in("woa", [512, D])
    wob = dt_in("wob", [D, D])
    wo = dt_in("wo", [D, D])
    wg = dt_in("wg", [D, DFF])
    wu = dt_in("wu", [D, DFF])
    wd = dt_in("wd", [DFF, D])
    if mode != "p1":
        outT = nc.dram_tensor("outT", [D, FS], F32, kind="ExternalOutput").ap()
    if mode == "p1":
        ysrc = nc.dram_tensor("ysrc", [NCORES * 192, FS], BF16, kind="ExternalOutput").ap()
    else:
        ysrc = nc.dram_tensor("ysrc", [NCORES * 192, FS], BF16).ap()
    DBG = 0
    if mode == "p2":
        yin = nc.dram_tensor("yin", [NCORES * 192, FS], BF16, kind="ExternalInput").ap()
    agout = nc.dram_tensor("agout", [16 * NCORES * 96, FS], BF16).ap()

    P = Prog()
    es = contextlib.ExitStack()
    with es:
        ARENA_F = 50 * 1024
        arena = es.enter_context(nc.sbuf_tensor("arena", [128, ARENA_F], F32))
        psum = es.enter_context(nc.psum_tensor("ps", [128, 4096], F32))
        bank = [psum[:, b * 512:(b + 1) * 512] for b in range(8)]
        bankbuf = [Buf("bank%d" % b, excl=True) for b in range(8)]

        class Arena:
            def __init__(self):
                self.off = 0

            def f32(self, n):
                a = arena[:, self.off:self.off + n]
                self.off += n
                assert self.off <= ARENA_F, "SBUF arena overflow %d" % self.off
                return a

            def bf16(self, n):
                m = (n + 1) // 2
                a = arena[:, self.off:self.off + m].bitcast(BF16)
                self.off += m
                assert self.off <= ARENA_F, "SBUF arena overflow %d" % self.off
                return a[:, 0:n]

        A = Arena()
        cst = A.f32(C_N)
        pv = A.f32(PV_N)
        ones_f = A.f32(128)
        identb = A.bf16(128)
        onesb = A.bf16(128)
        B_cst, B_pv, B_misc = Buf("cst"), Buf("pv"), Buf("misc")
        ident = cst[:, C_ID:C_ID + 128]
        P.add("sp", lambda e: e.dma_start(out=cst, in_=cst_d), writes=[B_cst], key=B_cst)
        P.add("sp", lambda e: e.dma_start(out=pv, in_=pv_d), writes=[B_pv], key=B_pv)
        P.add("pool", lambda e: e.memset(ones_f, 1.0), writes=[B_misc])
        P.add("pool", lambda e: e.memset(onesb, 1.0 / 1024.0), writes=[B_misc])
        P.add("pool", lambda e: e.tensor_copy(out=identb, in_=ident), reads=[B_cst], writes=[B_misc])
        CONSTS = [B_cst, B_pv, B_misc]
        base_off = A.off

        gp = [5]
        gmode = [0]

        def gbank():
            if gmode[0] == 1:
                gp[0] = 6 if gp[0] == 5 else 5
                return gp[0]
            b = gp[0]
            gp[0] = 5 + (gp[0] - 5 + 1) % 3
            return b

        DO_P1 = mode != "p2"
        Wc = A.bf16(KC * W1COLS).rearrange("p (k c) -> p k c", k=KC)
        B_Wc = Buf("Wc")
        bW = A.f32(8)
        prep_off = A.off
        csum = A.f32(W1COLS)
        stg = [A.f32(W1COLS), A.f32(W1COLS)]
        B_stg = [Buf("stg0"), Buf("stg1")]
        wgt = A.f32(W1COLS)
        B_wgt = Buf("wgt")
        B_bW, B_csum = Buf("bW"), Buf("csum")
        w1v = w1.rearrange("(k p) c -> p k c", p=128)
        psc = [bank[5], bank[6]]
        for k in range(KC if DO_P1 else 0):
            s = stg[k % 2]
            P.add("sp", lambda e, s=s, k=k: e.dma_start(out=s, in_=w1v[:, k, :]), writes=[B_stg[k % 2]], key=B_stg[k % 2])
            P.add("dve", lambda e, s=s, k=k: e.tensor_scalar(out=wgt, in0=s, scalar1=pv[:, PV_G0 + k:PV_G0 + k + 1], scalar2=None, op0=ALU.mult),
                  reads=[B_stg[k % 2], B_pv], writes=[B_wgt])
            P.add("pe", lambda e, k=k: e.matmul(psc[0][:, 0:512], ones_f, wgt[:, 0:512], start=(k == 0), stop=(k == KC - 1)),
                  reads=[B_wgt, B_misc], writes=[bankbuf[5]])
            P.add("pe", lambda e, k=k: e.matmul(psc[1][:, 0:W1COLS - 512], ones_f, wgt[:, 512:W1COLS], start=(k == 0), stop=(k == KC - 1)),
                  reads=[B_wgt, B_misc], writes=[bankbuf[6]])
            order = [3, 0, 1, 2, 4, 5, 6]
            for oi, gi in enumerate(order):
                go, gm = GROUPS[gi]
                P.add("pe", lambda e, s=s, k=k, gi=gi, go=go, gm=gm, oi=oi: e.matmul(bank[7][0:gm, gi:gi + 1], s[:, go:go + gm], pv[:, PV_B0 + k:PV_B0 + k + 1],
                                                                                    start=(k == 0 and oi == 0), stop=(k == KC - 1 and oi == 6), skip_group_check=True),
                      reads=[B_stg[k % 2], B_pv], writes=[bankbuf[7]])
        if DO_P1:
            P.add("act", lambda e: e.mul(csum[:, 0:512], psc[0][:, 0:512], 1.0 / 1024.0), reads=[bankbuf[5]], writes=[B_csum])
            P.add("act", lambda e: e.mul(csum[:, 512:W1COLS], psc[1][:, 0:W1COLS - 512], 1.0 / 1024.0), reads=[bankbuf[6]], writes=[B_csum])
            P.add("dve", lambda e: e.tensor_copy(out=bW[:, 0:7], in_=bank[7][:, 0:7]), reads=[bankbuf[7]], writes=[B_bW])
        for k in range(KC if DO_P1 else 0):
            s = stg[k % 2]
            P.add("sp", lambda e, s=s, k=k: e.dma_start(out=s, in_=w1v[:, k, :]), writes=[B_stg[k % 2]], key=B_stg[k % 2])
            P.add("dve", lambda e, s=s, k=k: e.scalar_tensor_tensor(out=Wc[:, k, :], in0=s, scalar=pv[:, PV_G0 + k:PV_G0 + k + 1], in1=csum,
                                                                    op0=ALU.mult, op1=ALU.subtract),
                  reads=[B_stg[k % 2], B_pv, B_csum], writes=[B_Wc])
        P.barrier()
        A.off = prep_off
        sc1 = A.f32(8)
        B_sc1 = Buf("sc1")
        if DO_P1:
          P.add("act", lambda e: e.activation(out=sc1[:, 0:1], in_=pv[:, PV_ALOG:PV_ALOG + 1], func=AF.Exp), reads=[B_pv], writes=[B_sc1])
          P.add("dve", lambda e: e.tensor_scalar(out=sc1[:, 0:1], in0=sc1[:, 0:1], scalar1=-1.0, scalar2=None, op0=ALU.mult), reads=[B_sc1], writes=[B_sc1])
          P.add("dve", lambda e: e.tensor_scalar(out=sc1[:, 1:2], in0=bW[:, 0:1], scalar1=0.125, scalar2=None, op0=ALU.mult), reads=[B_bW], writes=[B_sc1])

        Kaug = A.bf16(L)
        Vc = A.bf16(NBLK * 65).rearrange("p (b c) -> p b c", c=65)
        B_K = [Buf("K%d" % i) for i in range(len(tiles))]
        B_V = [Buf("V%d" % i) for i in range(len(tiles))]
        B_Vinit = Buf("Vinit")
        if DO_P1:
            P.add("pool", lambda e: e.memset(Vc[:, :, 64:65], 1.0), writes=[B_Vinit])

        xb = [A.bf16(KC * 512).rearrange("p (k w) -> p k w", k=KC) for _ in range(2)]
        B_xb = [Buf("xb0"), Buf("xb1")]
        sq = A.bf16(KC * 512).rearrange("p (k w) -> p k w", k=KC)
        B_sq = Buf("sq")

        def T32(name):
            return A.f32(512), Buf(name)

        def T16(name):
            return A.bf16(512), Buf(name)

        mean_sb, B_mean = T32("mean")
        rstd, B_rstd = T32("rstd")
        tmpA, B_tmpA = T32("tmpA")
        tmpB, B_tmpB = T32("tmpB")
        Qaug, B_Q = T16("Qaug")
        VG, B_VG = T32("VG")
        Xq = A.f32(516); Xk = A.f32(516); Xv = A.f32(516)
        B_Xq, B_Xk, B_Xv = Buf("Xq"), Buf("Xk"), Buf("Xv")
        zs, B_zs = T32("zs")
        ctile, B_c = T32("c")
        lrow, B_lrow = T32("lrow")
        lrow2, B_lrow2 = T32("lrow2")
        hiB, B_hi = T16("hi"); loB, B_lo = T16("lo"); lo2B, B_lo2 = T16("lo2")
        r1, B_r1 = T32("r1")
        accr, B_accr = T32("accr")
        onesrow, B_onesrow = T32("onesrow")
        bigm, B_bigm = T32("bigm")
        carry = A.f32(2)
        B_carry = Buf("carry")
        PT = [A.bf16(1024).rearrange("p (j w) -> p j w", j=2) for _ in range(2)]
        B_PT = [Buf("PT0"), Buf("PT1")]
        O_sb, B_Osb = T32("Osb")
        rden, B_rden = T32("rden")
        YA, B_YA = T16("YA")
        YB, B_YB = T16("YB")
        sctm = A.f32(32).rearrange("p (j c) -> p j c", c=8)
        B_sctm = Buf("sctm")
        cq, B_cq = T32("cq"); ck, B_ck = T32("ck"); cv, B_cv = T32("cv")
        qs, B_qs = T32("qs"); ks, B_ks = T32("ks")
        sqq, B_sqq = T32("sqq"); sqk, B_sqk = T32("sqk")
        rnq, B_rnq = T32("rnq"); rnk, B_rnk = T32("rnk")
        qnT, B_qnT = T16("qnT"); knT, B_knT = T16("knT"); vsT, B_vsT = T16("vsT")
        tms = A.f32(64)
        B_tms = Buf("tms")
        t_beta, t_nbeta, t_x, t_ax, t_e, t_l, t_g, t_gc, t_ngc, t_e1, t_e2, t_d = [tms[:, 4 * i:4 * i + 4] for i in range(12)]
        rhs8 = A.f32(8)
        glbc = A.f32(8)
        B_glbc = Buf("glbc")
        Xuw = A.bf16(4 * 256).rearrange("p (j c) -> p j c", c=256)
        B_Xuw = Buf("Xuw")
        kdec = A.bf16(4 * 128).rearrange("p (j c) -> p j c", c=128)
        B_kdec = Buf("kdec")
        UW = A.bf16(4 * 256).rearrange("p (j c) -> p j c", c=256)
        B_UW = [Buf("UW%d" % j) for j in range(4)]
        diag = [A.f32(128) for _ in range(2)]
        B_diag = [Buf("diag0"), Buf("diag1")]
        T1 = [A.f32(128) for _ in range(2)]
        B_T1 = [Buf("T1a"), Buf("T1b")]
        T3 = [A.f32(128) for _ in range(2)]
        B_T3 = [Buf("T3a"), Buf("T3b")]
        Dsl = [A.f32(128) for _ in range(2)]
        B_Dsl = [Buf("Dsl0"), Buf("Dsl1")]
        Diu = [A.f32(128) for _ in range(2)]
        B_Diu = [Buf("Diu0"), Buf("Diu1")]
        EGR = [A.f32(128) for _ in range(4)]
        B_EGR = [Buf("EGR%d" % j) for j in range(4)]
        attnT = [A.bf16(128) for _ in range(4)]
        B_attnT = [Buf("attnT%d" % j) for j in range(4)]
        PP = [[A.bf16(256) for _ in range(2)] for _ in range(4)]
        B_PP = [[Buf("PP%d_%d" % (j, q)) for q in range(2)] for j in range(4)]
        RR = [[A.bf16(128) for _ in range(2)] for _ in range(4)]
        B_RR = [[Buf("RR%d_%d" % (j, q)) for q in range(2)] for j in range(4)]
        qdecT, B_qdecT = T32("qdecT")
        QeffT, B_QeffT = T32("QeffT")
        MT = [A.f32(128) for _ in range(8)]
        B_MT = [Buf("MT%d" % j) for j in range(8)]
        Bn = [A.f32(128) for _ in range(8)]
        B_Bn = [Buf("Bn%d" % j) for j in range(8)]
        Sst = [A.f32(128) for _ in range(9)]
        B_S = [Buf("S%d" % j) for j in range(9)]
        sqo, B_sqo = T32("sqo")
        rno, B_rno = T32("rno")
        p1_end = A.off

        if DO_P1:
            P.add("pool", lambda e: e.memset(Sst[0], 0.0), writes=[B_S[0]])
            P.add("pool", lambda e: e.memset(onesrow, 1.0), writes=[B_onesrow])
            P.add("pool", lambda e: e.memset(bigm, 0.0), writes=[B_bigm])
            P.add("pool", lambda e: e.memset(bigm[64:70, 0:48], 30000.0), writes=[B_bigm])
            P.add("pool", lambda e: e.memset(Xq[:, 0:3], 0.0), writes=[B_Xq])
            P.add("pool", lambda e: e.memset(Xk[:, 0:3], 0.0), writes=[B_Xk])
            P.add("pool", lambda e: e.memset(Xv[:, 0:3], 0.0), writes=[B_Xv])

        xTv = xT.rearrange("(k p) l -> p k l", p=128)
        ysv = ysrc.rearrange("(s f) t -> s f t", f=192)
        B_ysrc = Buf("ysrc")
        RS = slice(64, 70)
        cwq = lambda j: pv[:, PV_CW + j:PV_CW + j + 1]
        s_cur = [0]

        def y_out(src, rows0, nrows, c0, W, Bsrc):
            p = max(c0, 64)
            end = min(c0 + W, 64 + S)
            while p < end:
                f = p - 64
                sh = f // FS
                fe = min(end - 64, (sh + 1) * FS)
                n = fe - f
                P.add("sp", lambda e, sh=sh, f=f, n=n, p=p: e.dma_start(out=ysv[sh, rows0:rows0 + nrows, f - sh * FS:f - sh * FS + n],
                                                                        in_=src[0:nrows, p - c0:p - c0 + n]),
                      reads=[Bsrc], writes=[B_ysrc], key=B_ysrc)
                p += n

        STAGE = 99

        def p1_tile(ti, c0, W):
            nb = W // 128
            blk0 = c0 // 128
            par = ti % 2
            xbt = xb[par]
            P.add("pool", lambda e, xbt=xbt, c0=c0, W=W: e.dma_start(out=xbt[:, :, 0:W], in_=xTv[:, :, c0:c0 + W]), writes=[B_xb[par]], key=B_xb[par])
            P.add("pool", lambda e, xbt=xbt, W=W: e.tensor_tensor(out=sq[:, :, 0:W], in0=xbt[:, :, 0:W], in1=xbt[:, :, 0:W], op=ALU.mult),
                  reads=[B_xb[par]], writes=[B_sq])
            b1, b2 = gbank(), gbank()
            for k in range(KC):
                P.add("pe", lambda e, k=k, b1=b1, xbt=xbt, W=W: e.matmul(bank[b1][:, 0:W], onesb, xbt[:, k, 0:W], start=(k == 0), stop=(k == KC - 1)),
                      reads=[B_xb[par], B_misc], writes=[bankbuf[b1]])
            for k in range(KC):
                P.add("pe", lambda e, k=k, b2=b2, W=W: e.matmul(bank[b2][:, 0:W], onesb, sq[:, k, 0:W], start=(k == 0), stop=(k == KC - 1)),
                      reads=[B_sq, B_misc], writes=[bankbuf[b2]])
            P.add("act", lambda e, b1=b1, W=W: e.copy(mean_sb[:, 0:W], bank[b1][:, 0:W]), reads=[bankbuf[b1]], writes=[B_mean])
            P.add("dve", lambda e, W=W: e.tensor_tensor(out=tmpA[:, 0:W], in0=mean_sb[:, 0:W], in1=mean_sb[:, 0:W], op=ALU.mult), reads=[B_mean], writes=[B_tmpA])
            P.add("dve", lambda e, b2=b2, W=W: e.tensor_tensor(out=tmpA[:, 0:W], in0=bank[b2][:, 0:W], in1=tmpA[:, 0:W], op=ALU.subtract),
                  reads=[bankbuf[b2], B_tmpA], writes=[B_tmpA])
            P.add("dve", lambda e, W=W: e.tensor_scalar(out=tmpA[:, 0:W], in0=tmpA[:, 0:W], scalar1=0.0, scalar2=LN_EPS, op0=ALU.max, op1=ALU.add),
                  reads=[B_tmpA], writes=[B_tmpA])
            P.add("act", lambda e, W=W: e.activation(out=tmpA[:, 0:W], in_=tmpA[:, 0:W], func=AF.Sqrt), reads=[B_tmpA], writes=[B_tmpA])
            P.add("dve", lambda e, W=W: e.reciprocal(out=rstd[:, 0:W], in_=tmpA[:, 0:W]), reads=[B_tmpA], writes=[B_rstd])

            def proj(go, gm):
                b = gbank()
                for k in range(KC):
                    P.add("pe", lambda e, k=k, b=b: e.matmul(bank[b][0:gm, 0:W], Wc[:, k, go:go + gm], xbt[:, k, 0:W], start=(k == 0), stop=(k == KC - 1)),
                          reads=[B_Wc, B_xb[par]], writes=[bankbuf[b]])
                return b

            def evac(b, gm, gi, out_ap, Bout, func=AF.Identity, scale=1.0, bias_ap=None, tmp=None, Btmp=None):
                tmp_, Bt = (tmpB, B_tmpB) if tmp is None else (tmp, Btmp)
                P.add("dve", lambda e: e.tensor_tensor(out=tmp_[0:gm, 0:W], in0=bank[b][0:gm, 0:W], in1=rstd[0:gm, 0:W], op=ALU.mult),
                      reads=[bankbuf[b], B_rstd], writes=[Bt])
                bia = bW[0:gm, gi:gi + 1] if bias_ap is None else bias_ap
                P.add("act", lambda e: e.activation(out=out_ap, in_=tmp_[0:gm, 0:W], func=func, bias=bia, scale=scale),
                      reads=[Bt, B_bW, B_sc1], writes=[Bout])

            b = proj(G_QA, 64)
            evac(b, 64, 0, Qaug[0:64, 0:W], B_Q, scale=0.125, bias_ap=sc1[0:64, 1:2])
            b = proj(G_KA, 64)
            evac(b, 64, 1, Kaug[0:64, c0:c0 + W], B_K[ti])
            b = proj(G_V, 72)
            evac(b, 72, 2, VG[0:72, 0:W], B_VG)
            b = proj(G_QB, 128)
            evac(b, 128, 3, Xq[:, 3:3 + W], B_Xq)
            b = proj(G_KB, 128)
            evac(b, 128, 4, Xk[:, 3:3 + W], B_Xk)
            b = proj(G_VB, 128)
            evac(b, 128, 5, Xv[:, 3:3 + W], B_Xv)
            b = proj(G_Z, 128)
            evac(b, 128, 6, zs[:, 0:W], B_zs, func=AF.Silu)
            if ti == 0:
                for X_, B_ in ((Xq, B_Xq), (Xk, B_Xk), (Xv, B_Xv)):
                    P.add("pool", lambda e, X_=X_: e.memset(X_[:, 3:3 + 48], 0.0), writes=[B_])

            for j in range(nb):
                b = gbank()
                P.add("pe", lambda e, j=j, b=b: e.transpose(bank[b][:, 0:72], VG[0:72, j * 128:(j + 1) * 128], ident[0:72, 0:72]),
                      reads=[B_VG, B_cst], writes=[bankbuf[b]])
                P.add("act", lambda e, j=j, b=b: e.copy(Vc[:, blk0 + j, 0:64], bank[b][:, 0:64]), reads=[bankbuf[b], B_Vinit], writes=[B_V[ti]])
                P.add("dve", lambda e, j=j, b=b: e.tensor_copy(out=sctm[:, j, :], in_=bank[b][:, 64:72]), reads=[bankbuf[b]], writes=[B_sctm])

            P.add("dve", lambda e: e.tensor_scalar(out=lrow[RS, 0:W], in0=VG[RS, 0:W], scalar1=pv[RS, PV_BF:PV_BF + 1], scalar2=-1.0, op0=ALU.add, op1=ALU.mult),
                  reads=[B_VG, B_pv], writes=[B_lrow])
            P.add("dve", lambda e: e.tensor_scalar(out=lrow2[RS, 0:W], in0=lrow[RS, 0:W], scalar1=-1.0, scalar2=None, op0=ALU.mult), reads=[B_lrow], writes=[B_lrow2])
            P.add("dve", lambda e: e.tensor_tensor(out=lrow2[RS, 0:W], in0=lrow2[RS, 0:W], in1=lrow[RS, 0:W], op=ALU.max), reads=[B_lrow, B_lrow2], writes=[B_lrow2])
            P.add("act", lambda e: e.activation(out=lrow2[RS, 0:W], in_=lrow2[RS, 0:W], func=AF.Exp, scale=-1.0), reads=[B_lrow2], writes=[B_lrow2])
            P.add("act", lambda e: e.activation(out=lrow2[RS, 0:W], in_=lrow2[RS, 0:W], func=AF.Ln, bias=1.0), reads=[B_lrow2], writes=[B_lrow2])
            P.add("dve", lambda e: e.scalar_tensor_tensor(out=lrow[RS, 0:W], in0=lrow[RS, 0:W], scalar=0.0, in1=lrow2[RS, 0:W], op0=ALU.max, op1=ALU.add),
                  reads=[B_lrow, B_lrow2], writes=[B_lrow])
            init = 0.0 if ti == 0 else carry[RS, 0:1]
            P.add("dve", lambda e, init=init: e.tensor_tensor_scan(out=ctile[RS, 0:W], data0=onesrow[RS, 0:W], data1=lrow[RS, 0:W], initial=init,
                                                                  op0=ALU.mult, op1=ALU.subtract),
                  reads=[B_lrow, B_onesrow, B_carry], writes=[B_c])
            P.add("dve", lambda e: e.tensor_copy(out=carry[RS, 0:1], in_=ctile[RS, W - 1:W]), reads=[B_c], writes=[B_carry])

            def split3(src, Bsrc):
                P.add("dve", lambda e: e.tensor_copy(out=hiB[RS, 0:W], in_=src[RS, 0:W]), reads=[Bsrc], writes=[B_hi])
                P.add("dve", lambda e: e.tensor_tensor(out=r1[RS, 0:W], in0=src[RS, 0:W], in1=hiB[RS, 0:W], op=ALU.subtract), reads=[Bsrc, B_hi], writes=[B_r1])
                P.add("dve", lambda e: e.tensor_copy(out=loB[RS, 0:W], in_=r1[RS, 0:W]), reads=[B_r1], writes=[B_lo])
                P.add("dve", lambda e: e.tensor_tensor(out=r1[RS, 0:W], in0=r1[RS, 0:W], in1=loB[RS, 0:W], op=ALU.subtract), reads=[B_r1, B_lo], writes=[B_r1])
                P.add("dve", lambda e: e.tensor_copy(out=lo2B[RS, 0:W], in_=r1[RS, 0:W]), reads=[B_r1], writes=[B_lo2])

            def augrows(mc, out_ap, Bout):
                m = lambda i: cst[RS, mc + i:mc + i + 1]
                P.add("dve", lambda e: e.tensor_scalar(out=accr[RS, 0:W], in0=hiB[RS, 0:W], scalar1=m(0), scalar2=m(3), op0=ALU.mult, op1=ALU.add),
                      reads=[B_hi, B_cst], writes=[B_accr])
                P.add("dve", lambda e: e.scalar_tensor_tensor(out=accr[RS, 0:W], in0=loB[RS, 0:W], scalar=m(1), in1=accr[RS, 0:W], op0=ALU.mult, op1=ALU.add),
                      reads=[B_lo, B_accr, B_cst], writes=[B_accr])
                P.add("dve", lambda e: e.scalar_tensor_tensor(out=out_ap, in0=lo2B[RS, 0:W], scalar=m(2), in1=accr[RS, 0:W], op0=ALU.mult, op1=ALU.add),
                      reads=[B_lo2, B_accr, B_cst], writes=[Bout])

            split3(ctile, B_c)
            augrows(C_MQ, Qaug[RS, 0:W], B_Q)
            if ti == 0:
                P.add("dve", lambda e: e.tensor_tensor(out=lrow2[RS, 0:W], in0=ctile[RS, 0:W], in1=bigm[RS, 0:W], op=ALU.add), reads=[B_c, B_bigm], writes=[B_lrow2])
                split3(lrow2, B_lrow2)
            augrows(C_MK, Kaug[RS, c0:c0 + W], B_K[ti])

            P.begin_capture()
            nkb = blk0 + nb
            grp = 0
            kb = 0
            while kb < nkb:
                n2 = min(2, nkb - kb)
                g = grp % 2
                grp += 1
                rd = [B_K[(kb + j) * 128 // 512] for j in range(n2)]
                for j in range(n2):
                    P.add("pe", lambda e, g=g, j=j, kb=kb: e.matmul(bank[2 * g + j][:, 0:W], Kaug[0:70, (kb + j) * 128:(kb + j + 1) * 128], Qaug[0:70, 0:W], start=True, stop=True),
                          reads=[B_Q, rd[j]], writes=[bankbuf[2 * g + j]])
                if W == 512:
                    P.add("act", lambda e, g=g, n2=n2: e.activation(out=PT[g][:, 0:n2, :], in_=psum[:, g * 1024:g * 1024 + n2 * 512].rearrange("p (j w) -> p j w", w=512), func=AF.Exp),
                          reads=[bankbuf[2 * g + j] for j in range(n2)], writes=[B_PT[g]])
                else:
                    for j in range(n2):
                        P.add("act", lambda e, g=g, j=j: e.activation(out=PT[g][:, j, 0:W], in_=bank[2 * g + j][:, 0:W], func=AF.Exp),
                              reads=[bankbuf[2 * g + j]], writes=[B_PT[g]])
                for j in range(n2):
                    jj = kb + j - blk0
                    if jj >= 0:
                        P.add("pool", lambda e, g=g, j=j, jj=jj: e.affine_select(out=PT[g][:, j, 0:W], in_=PT[g][:, j, 0:W], pattern=[[1, W]], compare_op=ALU.is_ge,
                                                                                 fill=0.0, base=-128 * jj, channel_multiplier=-1),
                              reads=[B_PT[g]], writes=[B_PT[g]])
                for j in range(n2):
                    kk = kb + j
                    P.add("pe", lambda e, g=g, j=j, kk=kk: e.matmul(bank[4][0:65, 0:W], Vc[:, kk, 0:65], PT[g][:, j, 0:W], start=(kk == 0), stop=(kk == nkb - 1)),
                          reads=[B_PT[g], B_V[kk * 128 // 512], B_Vinit], writes=[bankbuf[4]])
                kb += n2
            P.add("act", lambda e: e.copy(O_sb[0:65, 0:W], bank[4][0:65, 0:W]), reads=[bankbuf[4]], writes=[B_Osb])
            b = 4
            P.add("pe", lambda e, b=b: e.matmul(bank[b][0:64, 0:W], ones_f[64:65, 0:64], O_sb[64:65, 0:W], start=True, stop=True),
                  reads=[B_Osb, B_misc], writes=[bankbuf[b]])
            P.add("dve", lambda e, b=b: e.reciprocal(out=rden[0:64, 0:W], in_=bank[b][0:64, 0:W]), reads=[bankbuf[b]], writes=[B_rden])
            P.add("dve", lambda e: e.tensor_tensor(out=YA[0:64, 0:W], in0=O_sb[0:64, 0:W], in1=rden[0:64, 0:W], op=ALU.mult), reads=[B_Osb, B_rden], writes=[B_YA])
            y_out(YA, 0, 64, c0, W, B_YA)

            att_ops = P.end_capture()
            P.begin_capture()
            gmode[0] = 1
            def conv(X_, B_X, off, out_, B_out):
                P.add("dve", lambda e: e.tensor_scalar(out=out_[:, 0:W], in0=X_[:, 0:W], scalar1=cwq(off), scalar2=None, op0=ALU.mult), reads=[B_X, B_pv], writes=[B_out])
                for j in range(1, 4):
                    P.add("dve", lambda e, j=j: e.scalar_tensor_tensor(out=out_[:, 0:W], in0=X_[:, j:j + W], scalar=cwq(off + j), in1=out_[:, 0:W], op0=ALU.mult, op1=ALU.add),
                          reads=[B_X, B_pv, B_out], writes=[B_out])
                P.add("pool", lambda e: e.tensor_copy(out=X_[:, 0:3], in_=X_[:, W:W + 3]), reads=[B_X], writes=[B_X])

            conv(Xq, B_Xq, 0, cq, B_cq)
            conv(Xk, B_Xk, 4, ck, B_ck)
            conv(Xv, B_Xv, 8, cv, B_cv)
            P.add("act", lambda e: e.activation(out=qs[:, 0:W], in_=cq[:, 0:W], func=AF.Silu), reads=[B_cq], writes=[B_qs])
            P.add("act", lambda e: e.activation(out=ks[:, 0:W], in_=ck[:, 0:W], func=AF.Silu), reads=[B_ck], writes=[B_ks])
            P.add("act", lambda e: e.activation(out=vsT[:, 0:W], in_=cv[:, 0:W], func=AF.Silu), reads=[B_cv], writes=[B_vsT])
            for (src, Bs, sq_, Bsq, rn, Brn, outb, Bo, sc) in ((qs, B_qs, sqq, B_sqq, rnq, B_rnq, qnT, B_qnT, 128.0), (ks, B_ks, sqk, B_sqk, rnk, B_rnk, knT, B_knT, 1.0)):
                P.add("act", lambda e, src=src, sq_=sq_: e.activation(out=sq_[:, 0:W], in_=src[:, 0:W], func=AF.Square), reads=[Bs], writes=[Bsq])
                b = gbank()
                P.add("pe", lambda e, b=b, sq_=sq_: e.matmul(bank[b][:, 0:W], ones_f, sq_[:, 0:W], start=True, stop=True), reads=[Bsq, B_misc], writes=[bankbuf[b]])
                P.add("dve", lambda e, b=b, rn=rn, sc=sc: e.tensor_scalar(out=rn[:, 0:W], in0=bank[b][:, 0:W], scalar1=NORM_EPS, scalar2=sc, op0=ALU.add, op1=ALU.mult),
                      reads=[bankbuf[b]], writes=[Brn])
                P.add("act", lambda e, rn=rn: e.activation(out=rn[:, 0:W], in_=rn[:, 0:W], func=AF.Sqrt), reads=[Brn], writes=[Brn])
                P.add("dve", lambda e, rn=rn: e.reciprocal(out=rn[:, 0:W], in_=rn[:, 0:W]), reads=[Brn], writes=[Brn])
                P.add("dve", lambda e, src=src, rn=rn, outb=outb: e.tensor_tensor(out=outb[:, 0:W], in0=src[:, 0:W], in1=rn[:, 0:W], op=ALU.mult), reads=[Bs, Brn], writes=[Bo])

            a_in = sctm[:, 0:nb, 6]
            b_in = sctm[:, 0:nb, 7]
            nbs = slice(0, nb)
            P.add("act", lambda e: e.activation(out=t_beta[:, nbs], in_=b_in, func=AF.Sigmoid), reads=[B_sctm], writes=[B_tms])
            P.add("dve", lambda e: e.tensor_scalar(out=t_nbeta[:, nbs], in0=t_beta[:, nbs], scalar1=-1.0, scalar2=None, op0=ALU.mult), reads=[B_tms], writes=[B_tms])
            P.add("dve", lambda e: e.tensor_scalar(out=t_x[:, nbs], in0=a_in, scalar1=pv[:, PV_DT:PV_DT + 1], scalar2=None, op0=ALU.add), reads=[B_sctm, B_pv], writes=[B_tms])
            P.add("dve", lambda e: e.tensor_scalar(out=t_ax[:, nbs], in0=t_x[:, nbs], scalar1=-1.0, scalar2=None, op0=ALU.mult), reads=[B_tms], writes=[B_tms])
            P.add("dve", lambda e: e.tensor_tensor(out=t_ax[:, nbs], in0=t_ax[:, nbs], in1=t_x[:, nbs], op=ALU.max), reads=[B_tms], writes=[B_tms])
            P.add("act", lambda e: e.activation(out=t_e[:, nbs], in_=t_ax[:, nbs], func=AF.Exp, scale=-1.0), reads=[B_tms], writes=[B_tms])
            P.add("act", lambda e: e.activation(out=t_l[:, nbs], in_=t_e[:, nbs], func=AF.Ln, bias=1.0), reads=[B_tms], writes=[B_tms])
            P.add("dve", lambda e: e.scalar_tensor_tensor(out=t_g[:, nbs], in0=t_x[:, nbs], scalar=0.0, in1=t_l[:, nbs], op0=ALU.max, op1=ALU.add), reads=[B_tms], writes=[B_tms])
            P.add("dve", lambda e: e.tensor_scalar(out=t_g[:, nbs], in0=t_g[:, nbs], scalar1=sc1[:, 0:1], scalar2=None, op0=ALU.mult), reads=[B_tms, B_sc1], writes=[B_tms])
            bg = gbank()
            P.add("pe", lambda e, bg=bg: e.matmul(bank[bg][:, 0:nb], cst[:, C_LT2:C_LT2 + 128], t_g[:, nbs], start=True, stop=True), reads=[B_tms, B_cst], writes=[bankbuf[bg]])
            P.add("pe", lambda e, bg=bg: e.matmul(bank[bg][:, 8:8 + nb], cst[:, C_BLK:C_BLK + 128], t_g[:, nbs], start=True, stop=True), reads=[B_tms, B_cst], writes=[bankbuf[bg]])
            for c_ in range(2):
                P.add("dve", lambda e, c_=c_: e.tensor_scalar(out=rhs8[:, 0:2 * nb].rearrange("p (j c) -> p j c", c=2)[:, :, c_], in0=t_g[:, nbs],
                                                               scalar1=cst[:, C_SEL + c_:C_SEL + c_ + 1], scalar2=None, op0=ALU.mult),
                      reads=[B_tms, B_cst], writes=[B_glbc])
            P.add("pe", lambda e, bg=bg: e.matmul(bank[bg][:, 16:16 + 2 * nb], ones_f, rhs8[:, 0:2 * nb], start=True, stop=True), reads=[B_glbc, B_misc], writes=[bankbuf[bg]])
            P.add("dve", lambda e, bg=bg: e.tensor_copy(out=t_gc[:, nbs], in_=bank[bg][:, 0:nb]), reads=[bankbuf[bg]], writes=[B_tms])
            P.add("dve", lambda e: e.tensor_scalar(out=t_ngc[:, nbs], in0=t_gc[:, nbs], scalar1=-1.0, scalar2=None, op0=ALU.mult), reads=[B_tms], writes=[B_tms])
            P.add("dve", lambda e, bg=bg: e.tensor_tensor(out=t_d[:, nbs], in0=bank[bg][:, 8:8 + nb], in1=t_gc[:, nbs], op=ALU.subtract), reads=[bankbuf[bg], B_tms], writes=[B_tms])
            P.add("act", lambda e: e.activation(out=t_e2[:, nbs], in_=t_d[:, nbs], func=AF.Exp), reads=[B_tms], writes=[B_tms])
            P.add("act", lambda e: e.activation(out=t_e1[:, nbs], in_=t_gc[:, nbs], func=AF.Exp), reads=[B_tms], writes=[B_tms])
            P.add("dve", lambda e: e.tensor_tensor(out=t_e1[:, nbs], in0=t_e1[:, nbs], in1=t_beta[:, nbs], op=ALU.mult), reads=[B_tms], writes=[B_tms])
            P.add("act", lambda e, bg=bg: e.activation(out=glbc[:, 0:2 * nb], in_=bank[bg][:, 16:16 + 2 * nb], func=AF.Exp), reads=[bankbuf[bg]], writes=[B_glbc])

            for j in range(nb):
                b = gbank()
                pb = bank[b].bitcast(BF16)
                P.add("pe", lambda e, j=j, pb=pb: e.transpose(pb[:, 0:128], knT[:, j * 128:(j + 1) * 128], identb), reads=[B_knT, B_misc], writes=[bankbuf[b]])
                P.add("pe", lambda e, j=j, pb=pb: e.transpose(pb[:, 128:256], vsT[:, j * 128:(j + 1) * 128], identb), reads=[B_vsT, B_misc], writes=[bankbuf[b]])
                P.add("dve", lambda e, j=j, pb=pb: e.tensor_scalar(out=Xuw[:, j, 128:256], in0=pb[:, 0:128], scalar1=t_e1[:, j:j + 1], scalar2=None, op0=ALU.mult),
                      reads=[bankbuf[b], B_tms], writes=[B_Xuw])
                P.add("dve", lambda e, j=j, pb=pb: e.tensor_scalar(out=kdec[:, j, :], in0=pb[:, 0:128], scalar1=t_e2[:, j:j + 1], scalar2=None, op0=ALU.mult),
                      reads=[bankbuf[b], B_tms], writes=[B_kdec])
                P.add("dve", lambda e, j=j, pb=pb: e.tensor_scalar(out=Xuw[:, j, 0:128], in0=pb[:, 128:256], scalar1=t_beta[:, j:j + 1], scalar2=None, op0=ALU.mult),
                      reads=[bankbuf[b], B_tms], writes=[B_Xuw])

            for j in range(nb):
                q2 = j % 2
                cs = slice(j * 128, (j + 1) * 128)
                P.add("dve", lambda e, j=j, q2=q2: e.tensor_scalar(out=diag[q2], in0=ident, scalar1=t_gc[:, j:j + 1], scalar2=None, op0=ALU.mult),
                      reads=[B_cst, B_tms], writes=[B_diag[q2]])
                b = gbank()
                P.add("pe", lambda e, b=b, q2=q2: e.matmul(bank[b][:, 0:128], ones_f, diag[q2], start=True, stop=True), reads=[B_diag[q2], B_misc], writes=[bankbuf[b]])
                P.add("dve", lambda e, b=b, q2=q2: e.scalar_tensor_tensor(out=T1[q2], in0=bank[b][:, 0:128], scalar=-1.0, in1=cst[:, C_MSL:C_MSL + 128], op0=ALU.mult, op1=ALU.add),
                      reads=[bankbuf[b], B_cst], writes=[B_T1[q2]])
                P.add("act", lambda e, j=j, q2=q2: e.activation(out=Dsl[q2], in_=T1[q2], func=AF.Exp, bias=t_gc[:, j:j + 1]), reads=[B_T1[q2], B_tms], writes=[B_Dsl[q2]])
                P.add("dve", lambda e, b=b, q2=q2: e.tensor_tensor(out=T3[q2], in0=bank[b][:, 0:128], in1=cst[:, C_MIU:C_MIU + 128], op=ALU.add),
                      reads=[bankbuf[b], B_cst], writes=[B_T3[q2]])
                P.add("act", lambda e, j=j, q2=q2: e.activation(out=Diu[q2], in_=T3[q2], func=AF.Exp, bias=t_ngc[:, j:j + 1]), reads=[B_T3[q2], B_tms], writes=[B_Diu[q2]])
                P.add("act", lambda e, b=b, j=j: e.activation(out=EGR[j], in_=bank[b][:, 0:128], func=AF.Exp), reads=[bankbuf[b]], writes=[B_EGR[j]])
                b2_ = gbank()
                P.add("pe", lambda e, b2_=b2_, cs=cs: e.matmul(bank[b2_][:, 0:128], knT[:, cs], knT[:, cs], start=True, stop=True), reads=[B_knT], writes=[bankbuf[b2_]])
                P.add("pe", lambda e, b2_=b2_, cs=cs: e.matmul(bank[b2_][:, 128:256], knT[:, cs], qnT[:, cs], start=True, stop=True), reads=[B_knT, B_qnT], writes=[bankbuf[b2_]])
                P.add("dve", lambda e, b2_=b2_, j=j, q2=q2: e.scalar_tensor_tensor(out=PP[j][0][:, 0:128], in0=bank[b2_][:, 0:128], scalar=t_nbeta[:, j:j + 1], in1=Dsl[q2],
                                                                                  op0=ALU.mult, op1=ALU.mult),
                      reads=[bankbuf[b2_], B_tms, B_Dsl[q2]], writes=[B_PP[j][0]])
                P.add("dve", lambda e, b2_=b2_, j=j, q2=q2: e.tensor_tensor(out=attnT[j], in0=bank[b2_][:, 128:256], in1=Diu[q2], op=ALU.mult),
                      reads=[bankbuf[b2_], B_Diu[q2]], writes=[B_attnT[j]])
                b3 = gbank()
                pb3 = bank[b3].bitcast(BF16)
                P.add("pe", lambda e, pb3=pb3, j=j: e.transpose(pb3[:, 0:128], PP[j][0][:, 0:128], identb), reads=[B_PP[j][0], B_misc], writes=[bankbuf[b3]])
                P.add("act", lambda e, pb3=pb3, j=j: e.copy(PP[j][0][:, 128:256], pb3[:, 0:128]), reads=[bankbuf[b3]], writes=[B_PP[j][0]])
                P.add("dve", lambda e, pb3=pb3, j=j: e.tensor_tensor(out=RR[j][0], in0=pb3[:, 0:128], in1=identb, op=ALU.add), reads=[bankbuf[b3], B_misc], writes=[B_RR[j][0]])
            for m in range(1, 6):
                src, dst = (m - 1) % 2, m % 2
                for j in range(nb):
                    b = gbank()
                    pbf = bank[b]
                    P.add("pe", lambda e, b=b, j=j, src=src: e.matmul(bank[b][:, 0:128], PP[j][src][:, 128:256], PP[j][src][:, 0:128], start=True, stop=True),
                          reads=[B_PP[j][src]], writes=[bankbuf[b]])
                    if m < 5:
                        P.add("pe", lambda e, b=b, j=j, src=src: e.matmul(bank[b][:, 128:256], PP[j][src][:, 0:128], PP[j][src][:, 128:256], start=True, stop=True),
                              reads=[B_PP[j][src]], writes=[bankbuf[b]])
                    wcols = 256 if m < 5 else 128
                    P.add("act", lambda e, b=b, j=j, dst=dst, wcols=wcols: e.copy(PP[j][dst][:, 0:wcols], bank[b][:, 0:wcols]), reads=[bankbuf[b]], writes=[B_PP[j][dst]])
                    b2_ = gbank()
                    P.add("pe", lambda e, b2_=b2_, j=j, src=src, dst=dst: e.matmul(bank[b2_][:, 0:128], PP[j][dst][:, 0:128], RR[j][src], start=True, stop=True),
                          reads=[B_PP[j][dst], B_RR[j][src]], writes=[bankbuf[b2_]])
                    P.add("dve", lambda e, b2_=b2_, j=j, src=src, dst=dst: e.tensor_tensor(out=RR[j][dst], in0=bank[b2_][:, 0:128], in1=RR[j][src], op=ALU.add),
                          reads=[bankbuf[b2_], B_RR[j][src]], writes=[B_RR[j][dst]])
            RF = 1
            for j in range(nb):
                b = gbank()
                P.add("pe", lambda e, b=b, j=j: e.matmul(bank[b][:, 0:256], RR[j][RF], Xuw[:, j, :], start=True, stop=True), reads=[B_RR[j][RF], B_Xuw], writes=[bankbuf[b]])
                P.add("act", lambda e, b=b, j=j: e.copy(UW[:, j, :], bank[b][:, 0:256]), reads=[bankbuf[b]], writes=[B_UW[j]])
                for h in range(2):
                    n = 2 * j + h
                    rs_ = slice(64 * h, 64 * h + 64)
                    b2_ = gbank()
                    P.add("pe", lambda e, b2_=b2_, j=j, rs_=rs_: e.matmul(bank[b2_][:, 0:128], UW[rs_, j, 128:256], kdec[rs_, j, :], start=True, stop=True),
                          reads=[B_UW[j], B_kdec], writes=[bankbuf[b2_]])
                    P.add("pe", lambda e, b2_=b2_, j=j, rs_=rs_: e.matmul(bank[b2_][:, 128:256], kdec[rs_, j, :], UW[rs_, j, 0:128], start=True, stop=True),
                          reads=[B_UW[j], B_kdec], writes=[bankbuf[b2_]])
                    P.add("dve", lambda e, b2_=b2_, n=n: e.scalar_tensor_tensor(out=MT[n], in0=ident, scalar=glbc[:, n:n + 1], in1=bank[b2_][:, 0:128], op0=ALU.mult, op1=ALU.subtract),
                          reads=[bankbuf[b2_], B_glbc, B_cst], writes=[B_MT[n]])
                    P.add("act", lambda e, b2_=b2_, n=n: e.copy(Bn[n], bank[b2_][:, 128:256]), reads=[bankbuf[b2_]], writes=[B_Bn[n]])
                cs = slice(j * 128, (j + 1) * 128)
                P.add("dve", lambda e, j=j, cs=cs: e.tensor_tensor(out=qdecT[:, cs], in0=qnT[:, cs], in1=EGR[j], op=ALU.mult), reads=[B_qnT, B_EGR[j]], writes=[B_qdecT])
                b3 = gbank()
                P.add("pe", lambda e, b3=b3, j=j: e.matmul(bank[b3][:, 0:128], UW[:, j, 128:256], attnT[j], start=True, stop=True), reads=[B_UW[j], B_attnT[j]], writes=[bankbuf[b3]])
                P.add("dve", lambda e, b3=b3, cs=cs: e.tensor_tensor(out=QeffT[:, cs], in0=qdecT[:, cs], in1=bank[b3][:, 0:128], op=ALU.subtract),
                      reads=[bankbuf[b3], B_qdecT], writes=[B_QeffT])
            bo = 7
            for n in range(2 * nb):
                j, h = n // 2, n % 2
                rs_ = slice(64 * h, 64 * h + 64)
                si = s_cur[0]
                sn = (si + 1) % 9
                col = slice(j * 128 + 64 * h, j * 128 + 64 * h + 64)
                P.add("pe", lambda e, j=j, rs_=rs_, col=col, h=h: e.matmul(bank[bo][:, col], UW[rs_, j, 0:128], attnT[j][rs_, 64 * h:64 * h + 64], start=True, stop=False),
                      reads=[B_UW[j], B_attnT[j]], writes=[bankbuf[bo]])
                P.add("pe", lambda e, si=si, col=col: e.matmul(bank[bo][:, col], Sst[si], QeffT[:, col], start=False, stop=True),
                      reads=[B_S[si], B_QeffT], writes=[bankbuf[bo]])
                bs = gbank()
                if bs == bo:
                    bs = gbank()
                P.add("pe", lambda e, bs=bs, n=n, si=si: e.matmul(bank[bs][:, 0:128], MT[n], Sst[si], start=True, stop=True), reads=[B_MT[n], B_S[si]], writes=[bankbuf[bs]])
                P.add("dve", lambda e, bs=bs, n=n, sn=sn: e.tensor_tensor(out=Sst[sn], in0=bank[bs][:, 0:128], in1=Bn[n], op=ALU.add), reads=[bankbuf[bs], B_Bn[n]], writes=[B_S[sn]])
                s_cur[0] = sn
            P.add("act", lambda e: e.activation(out=sqo[:, 0:W], in_=bank[bo][:, 0:W], func=AF.Square), reads=[bankbuf[bo]], writes=[B_sqo])
            b = gbank()
            if b == bo:
                b = gbank()
            P.add("pe", lambda e, b=b: e.matmul(bank[b][:, 0:W], ones_f, sqo[:, 0:W], start=True, stop=True), reads=[B_sqo, B_misc], writes=[bankbuf[b]])
            P.add("dve", lambda e, b=b: e.tensor_scalar(out=rno[:, 0:W], in0=bank[b][:, 0:W], scalar1=1.0 / 128.0, scalar2=NORM_EPS, op0=ALU.mult, op1=ALU.add),
                  reads=[bankbuf[b]], writes=[B_rno])
            P.add("act", lambda e: e.activation(out=rno[:, 0:W], in_=rno[:, 0:W], func=AF.Sqrt), reads=[B_rno], writes=[B_rno])
            P.add("dve", lambda e: e.reciprocal(out=rno[:, 0:W], in_=rno[:, 0:W]), reads=[B_rno], writes=[B_rno])
            P.add("dve", lambda e: e.scalar_tensor_tensor(out=sqo[:, 0:W], in0=bank[bo][:, 0:W], scalar=pv[:, PV_GNW:PV_GNW + 1], in1=rno[:, 0:W], op0=ALU.mult, op1=ALU.mult),
                  reads=[bankbuf[bo], B_rno, B_pv, B_sqo], writes=[B_sqo])
            P.add("dve", lambda e: e.tensor_tensor(out=YB[:, 0:W], in0=sqo[:, 0:W], in1=zs[:, 0:W], op=ALU.mult), reads=[B_sqo, B_zs], writes=[B_YB])
            y_out(YB, 64, 128, c0, W, B_YB)
            gdn_ops = P.end_capture()
            gmode[0] = 0
            P.add_merged([att_ops, gdn_ops])

        KNT = 999
        for ti_, (c0_t, W_t) in enumerate(tiles[:KNT] if DO_P1 else []):
            p1_tile(ti_, c0_t, W_t)

        STOP = {"fused": 0, "p1": 1, "p2": 0}[mode]
        B_ag = Buf("ag")
        if mode == "fused":
          for pc in range(16):
            P.add("pool", lambda e, pc=pc: e.collective_compute("AllGather", ALU.bypass, replica_groups=[list(range(NCORES))],
                                                                 ins=[ysrc[pc * 96:(pc + 1) * 96, :].opt()], outs=[agout[pc * 768:(pc + 1) * 768, :].opt()]),
                  reads=[B_ysrc], writes=[B_ag], key=B_ag, inc=1)
        P.barrier()

        A.off = base_off
        bufA = A.f32(KC * 512).rearrange("p (k w) -> p k w", k=KC)
        bufH = A.f32(KC * 512).rearrange("p (k w) -> p k w", k=KC)
        bufH1 = A.f32(KC * 512).rearrange("p (k w) -> p k w", k=KC)
        hb = A.bf16(KC * 512).rearrange("p (k w) -> p k w", k=KC)
        h1b = A.bf16(KC * 512).rearrange("p (k w) -> p k w", k=KC)
        yb16 = A.bf16(12 * FS).rearrange("p (k w) -> p k w", k=12)
        mixb = A.bf16(KC * 512).rearrange("p (k w) -> p k w", k=KC)
        actb = A.bf16(FC * 512).rearrange("p (k w) -> p k w", k=FC)
        ring = [A.bf16(KC * 512).rearrange("p (k w) -> p k w", k=KC) for _ in range(2)]
        B_ring = [Buf("ring%d" % i) for i in range(2)]
        dpan = A.bf16(FC * 512).rearrange("p (k w) -> p k w", k=FC)
        B_dpan = Buf("dpan")
        sq2 = actb[:, 0:8, :]
        xb2 = actb[:, 8:16, :]
        l_mean = A.f32(512); l_rstd = A.f32(512); l_t = A.f32(512); l_t2 = A.f32(512); l_t3 = A.f32(512)
        B_bufA, B_bufH, B_bufH1, B_hb, B_h1b, B_yb16, B_mixb, B_actb = [Buf(n) for n in ("bufA", "bufH", "bufH1", "hb", "h1b", "yb16", "mixb", "actb")]
        B_lmean, B_lrstd, B_lt, B_lt2, B_lt3 = [Buf(n) for n in ("lmean", "lrstd", "lt", "lt2", "lt3")]
        B_sq2 = B_actb
        B_xb2 = B_actb
        gp2 = [0]

        def gb2():
            b = gp2[0]
            gp2[0] = (gp2[0] + 1) % 8
            return b

        rp = [0]

        def load_panel(src_ap, kk, ncols):
            i = rp[0]
            rp[0] = (rp[0] + 1) % 2
            P.add("pool", lambda e: e.dma_start(out=ring[i][:, 0:kk, 0:ncols], in_=src_ap.rearrange("(k p) c -> p k c", p=128)), writes=[B_ring[i]], key=B_ring[i])
            return ring[i], B_ring[i]

        def layer_norm(src, Bsrc, dst32, Bdst32, dstb, Bdstb, gcol, bcol, W):
            P.add("pool", lambda e: e.tensor_copy(out=xb2[:, :, 0:W], in_=src[:, :, 0:W]), reads=[Bsrc], writes=[B_xb2])
            P.add("pool", lambda e: e.tensor_tensor(out=sq2[:, :, 0:W], in0=src[:, :, 0:W], in1=src[:, :, 0:W], op=ALU.mult), reads=[Bsrc], writes=[B_sq2])
            b1, b2 = gb2(), gb2()
            for k in range(KC):
                P.add("pe", lambda e, k=k: e.matmul(bank[b1][:, 0:W], onesb, xb2[:, k, 0:W], start=(k == 0), stop=(k == KC - 1)), reads=[B_xb2, B_misc], writes=[bankbuf[b1]])
            for k in range(KC):
                P.add("pe", lambda e, k=k: e.matmul(bank[b2][:, 0:W], onesb, sq2[:, k, 0:W], start=(k == 0), stop=(k == KC - 1)), reads=[B_sq2, B_misc], writes=[bankbuf[b2]])
            P.add("act", lambda e: e.copy(l_mean[:, 0:W], bank[b1][:, 0:W]), reads=[bankbuf[b1]], writes=[B_lmean])
            P.add("dve", lambda e: e.tensor_tensor(out=l_t[:, 0:W], in0=l_mean[:, 0:W], in1=l_mean[:, 0:W], op=ALU.mult), reads=[B_lmean], writes=[B_lt])
            P.add("dve", lambda e: e.tensor_tensor(out=l_t[:, 0:W], in0=bank[b2][:, 0:W], in1=l_t[:, 0:W], op=ALU.subtract), reads=[bankbuf[b2], B_lt], writes=[B_lt])
            P.add("dve", lambda e: e.tensor_scalar(out=l_t[:, 0:W], in0=l_t[:, 0:W], scalar1=0.0, scalar2=LN_EPS, op0=ALU.max, op1=ALU.add), reads=[B_lt], writes=[B_lt])
            P.add("act", lambda e: e.activation(out=l_t[:, 0:W], in_=l_t[:, 0:W], func=AF.Sqrt), reads=[B_lt], writes=[B_lt])
            P.add("dve", lambda e: e.reciprocal(out=l_rstd[:, 0:W], in_=l_t[:, 0:W]), reads=[B_lt], writes=[B_lrstd])
            for k in range(KC):
                P.add("dve", lambda e, k=k: e.tensor_tensor(out=l_t2[:, 0:W], in0=src[:, k, 0:W], in1=l_mean[:, 0:W], op=ALU.subtract), reads=[Bsrc, B_lmean], writes=[B_lt2])
                P.add("dve", lambda e, k=k: e.tensor_tensor(out=l_t2[:, 0:W], in0=l_t2[:, 0:W], in1=l_rstd[:, 0:W], op=ALU.mult), reads=[B_lt2, B_lrstd], writes=[B_lt2])
                P.add("act", lambda e, k=k: e.activation(out=dst32[:, k, 0:W], in_=l_t2[:, 0:W], func=AF.Identity, bias=pv[:, bcol + k:bcol + k + 1], scale=pv[:, gcol + k:gcol + k + 1]),
                      reads=[B_lt2, B_pv], writes=[Bdst32])
                if dstb is not None:
                    P.add("pool", lambda e, k=k: e.tensor_copy(out=dstb[:, k, 0:W], in_=dst32[:, k, 0:W]), reads=[Bdst32], writes=[Bdstb])

        xov = xown.rearrange("(k p) t -> p k t", p=128)
        outv = outT.rearrange("(k p) t -> p k t", p=128) if mode != "p1" else None
        B_out = Buf("out")
        pid_cache = {}

        def pid_of(e):
            return e.partition_id()

        agv5 = agout.rearrange("(s hf r f) t -> s hf r f t", s=8, hf=2, r=8)
        agv6 = agout.rearrange("(s hf q h f) t -> s hf q h f t", s=8, hf=2, q=4, h=2)

        if mode == "p2":
            for r in range(NCORES):
                P.add("sp", lambda e, r=r: e.dma_start(out=yb16[:, 4 + r, 0:FS], in_=yin[r * 192 + 64:r * 192 + 192, :]), writes=[B_yb16], key=B_yb16)
                P.add("sp", lambda e, r=r: e.dma_start(out=yb16[64 * (r % 2):64 * (r % 2) + 64, r // 2, 0:FS], in_=yin[r * 192:r * 192 + 64, :]), writes=[B_yb16], key=B_yb16)
        elif mode == "fused":
            def ldb1(e):
                pid = e.partition_id()
                src = agv5[bass.ds(pid, 1), 0, :, 64:96, 0:FS].rearrange("s r f t -> f (s r) t")
                return e.dma_start(out=yb16[0:32, 4:12, 0:FS], in_=src)
            P.add("sp", ldb1, reads=[B_ag], writes=[B_yb16], key=B_yb16)

            def ldb2(e):
                pid = e.partition_id()
                src = agv5[bass.ds(pid, 1), 1, :, 0:96, 0:FS].rearrange("s r f t -> f (s r) t")
                return e.dma_start(out=yb16[32:128, 4:12, 0:FS], in_=src)
            P.add("sp", ldb2, reads=[B_ag], writes=[B_yb16], key=B_yb16)
            for hh in range(2):
                def lda(e, hh=hh):
                    pid = e.partition_id()
                    src = agv6[bass.ds(pid, 1), 0, :, hh, 0:64, 0:FS].rearrange("s q f t -> f (s q) t")
                    return e.dma_start(out=yb16[64 * hh:64 * hh + 64, 0:4, 0:FS], in_=src)
                P.add("sp", lda, reads=[B_ag], writes=[B_yb16], key=B_yb16)


        def p2_tile(t2):
            t0 = t2 * W2
            W = W2
            P.add("sp", lambda e, t0=t0: e.dma_start(out=bufA[:, :, 0:W], in_=xov[:, :, t0:t0 + W]), writes=[B_bufA], key=B_bufA)
            layer_norm(bufA, B_bufA, bufH, B_bufH, hb, B_hb, PV_G0, PV_B0, W)
            for g4 in range(2):
                cs = slice(g4 * 512, g4 * 512 + 512)
                pga, Bga = load_panel(w2g[:, g4 * 512:g4 * 512 + 512], 8, 512)
                pa, Bpa = load_panel(woa[:, cs], 4, 512)
                for half in range(2):
                    if half == 0:
                        pg_, Bg_, pw_, Bw_, nk, yoff = pga, Bga, pa, Bpa, 4, 0
                    else:
                        pg_, Bg_ = load_panel(w2g[:, 1024 + g4 * 512:1024 + g4 * 512 + 512], 8, 512)
                        pw_, Bw_ = load_panel(wob[:, cs], 8, 512)
                        nk, yoff = 8, 4
                    for o in range(4):
                        oc = g4 * 4 + o
                        osl = slice(o * 128, o * 128 + 128)
                        bg_, bw_ = gb2(), gb2()
                        for k in range(KC):
                            P.add("pe", lambda e, k=k, bg_=bg_, pg_=pg_, osl=osl: e.matmul(bank[bg_][:, 0:W], pg_[:, k, osl], hb[:, k, 0:W], start=(k == 0), stop=(k == KC - 1)),
                                  reads=[Bg_, B_hb], writes=[bankbuf[bg_]])
                        for k in range(nk):
                            P.add("pe", lambda e, k=k, bw_=bw_, pw_=pw_, osl=osl, yoff=yoff, nk=nk: e.matmul(bank[bw_][:, 0:W], pw_[:, k, osl], yb16[:, yoff + k, t0:t0 + W], start=(k == 0), stop=(k == nk - 1)),
                                  reads=[Bw_, B_yb16], writes=[bankbuf[bw_]])
                        P.add("act", lambda e, bg_=bg_: e.activation(out=l_t[:, 0:W], in_=bank[bg_][:, 0:W], func=AF.Sigmoid), reads=[bankbuf[bg_]], writes=[B_lt])
                        if half == 0:
                            P.add("dve", lambda e, bw_=bw_, oc=oc: e.tensor_tensor(out=bufA[:, oc, 0:W], in0=bank[bw_][:, 0:W], in1=l_t[:, 0:W], op=ALU.mult),
                                  reads=[bankbuf[bw_], B_lt], writes=[B_bufA])
                        else:
                            P.add("dve", lambda e, bw_=bw_: e.tensor_tensor(out=l_t3[:, 0:W], in0=bank[bw_][:, 0:W], in1=l_t[:, 0:W], op=ALU.mult),
                                  reads=[bankbuf[bw_], B_lt], writes=[B_lt3])
                            P.add("dve", lambda e, oc=oc: e.tensor_tensor(out=mixb[:, oc, 0:W], in0=l_t3[:, 0:W], in1=bufA[:, oc, 0:W], op=ALU.add),
                                  reads=[B_lt3, B_bufA], writes=[B_mixb])
            for g4 in range(2):
                pw_, Bw_ = load_panel(wo[:, g4 * 512:g4 * 512 + 512], 8, 512)
                for o in range(4):
                    oc = g4 * 4 + o
                    osl = slice(o * 128, o * 128 + 128)
                    b = gb2()
                    for k in range(KC):
                        P.add("pe", lambda e, k=k, b=b, pw_=pw_, osl=osl: e.matmul(bank[b][:, 0:W], pw_[:, k, osl], mixb[:, k, 0:W], start=(k == 0), stop=(k == KC - 1)),
                              reads=[Bw_, B_mixb], writes=[bankbuf[b]])
                    P.add("dve", lambda e, b=b, oc=oc: e.scalar_tensor_tensor(out=bufA[:, oc, 0:W], in0=bufH[:, oc, 0:W], scalar=ALPHA, in1=bank[b][:, 0:W], op0=ALU.mult, op1=ALU.add),
                          reads=[bankbuf[b], B_bufH], writes=[B_bufA])
            layer_norm(bufA, B_bufA, bufH1, B_bufH1, h1b, B_h1b, PV_G1, PV_B1, W)
            for c0_ in range(0, DFF, 512):
                ncol = min(512, DFF - c0_)
                pg_, Bg_ = load_panel(wg[:, c0_:c0_ + ncol], 8, ncol)
                pu_, Bu_ = load_panel(wu[:, c0_:c0_ + ncol], 8, ncol)
                for o in range(ncol // 128):
                    fc = c0_ // 128 + o
                    osl = slice(o * 128, o * 128 + 128)
                    bg_, bu_ = gb2(), gb2()
                    for k in range(KC):
                        P.add("pe", lambda e, k=k, bg_=bg_, pg_=pg_, osl=osl: e.matmul(bank[bg_][:, 0:W], pg_[:, k, osl], h1b[:, k, 0:W], start=(k == 0), stop=(k == KC - 1)),
                              reads=[Bg_, B_h1b], writes=[bankbuf[bg_]])
                    for k in range(KC):
                        P.add("pe", lambda e, k=k, bu_=bu_, pu_=pu_, osl=osl: e.matmul(bank[bu_][:, 0:W], pu_[:, k, osl], h1b[:, k, 0:W], start=(k == 0), stop=(k == KC - 1)),
                              reads=[Bu_, B_h1b], writes=[bankbuf[bu_]])
                    P.add("act", lambda e, bg_=bg_: e.activation(out=l_t[:, 0:W], in_=bank[bg_][:, 0:W], func=AF.Silu), reads=[bankbuf[bg_]], writes=[B_lt])
                    P.add("dve", lambda e, bu_=bu_, fc=fc: e.tensor_tensor(out=actb[:, fc, 0:W], in0=bank[bu_][:, 0:W], in1=l_t[:, 0:W], op=ALU.mult),
                          reads=[bankbuf[bu_], B_lt], writes=[B_actb])
            for g4 in range(2):
                P.add("pool", lambda e, g4=g4: e.dma_start(out=dpan[:, :, :], in_=wd[:, g4 * 512:g4 * 512 + 512].rearrange("(k p) c -> p k c", p=128)), writes=[B_dpan], key=B_dpan)
                for o in range(4):
                    oc = g4 * 4 + o
                    osl = slice(o * 128, o * 128 + 128)
                    b = gb2()
                    for k in range(FC):
                        P.add("pe", lambda e, k=k, b=b, osl=osl: e.matmul(bank[b][:, 0:W], dpan[:, k, osl], actb[:, k, 0:W], start=(k == 0), stop=(k == FC - 1)),
                              reads=[B_dpan, B_actb], writes=[bankbuf[b]])
                    P.add("dve", lambda e, b=b, oc=oc: e.scalar_tensor_tensor(out=bufA[:, oc, 0:W], in0=bufH1[:, oc, 0:W], scalar=ALPHA, in1=bank[b][:, 0:W], op0=ALU.mult, op1=ALU.add),
                          reads=[bankbuf[b], B_bufH1], writes=[B_bufA])
            layer_norm(bufA, B_bufA, bufH, B_bufH, None, None, PV_G2, PV_B2, W)
            P.add("sp", lambda e, t0=t0: e.dma_start(out=outv[:, :, t0:t0 + W], in_=bufH[:, :, 0:W]), reads=[B_bufH], writes=[B_out], key=B_out)
        if STOP == 0:
            for t2_ in range(NT2):
                p2_tile(t2_)
        else:
            P.add("sp", lambda e: e.nop(), reads=[B_ysrc])
        P.add("sp", lambda e: e.nop(), reads=[B_out])

        P.finalize()
        engsem = {e: [es.enter_context(nc.semaphore("sem_%s_%d" % (e, i))) for i in range(P.nep[e])] for e in ENGS}
        print("ops per engine", {e: len(P.ops[e]) for e in ENGS}, "epochs", P.nep)
        keysem = {}
        for e in ENGS:
            for op in P.ops[e]:
                if op.key is not None and id(op.key) not in keysem:
                    keysem[id(op.key)] = es.enter_context(nc.semaphore("k_" + op.key.name))
        block = es.enter_context(nc.Block())

        @block.tensor
        def _(h):
            P.emit("pe", h, engsem, keysem)

        @block.scalar
        def _(h):
            P.emit("act", h, engsem, keysem)

        @block.vector
        def _(h):
            P.emit("dve", h, engsem, keysem)

        @block.gpsimd
        def _(h):
            P.emit("pool", h, engsem, keysem)

        @block.sync
        def _(h):
            P.emit("sp", h, engsem, keysem)
    return nc


def make_consts():
    c = np.zeros((128, C_N), np.float32)
    i = np.arange(128)
    same = (i[:, None] // 64) == (i[None, :] // 64)
    c[:, C_ID:C_ID + 128] = np.eye(128, dtype=np.float32)
    c[:, C_LT2:C_LT2 + 128] = (same & (i[:, None] <= i[None, :])).astype(np.float32)
    c[:, C_BLK:C_BLK + 128] = same.astype(np.float32)
    c[:, C_MSL:C_MSL + 128] = np.where(same & (i[:, None] > i[None, :]), 0.0, NEG)
    c[:, C_MIU:C_MIU + 128] = np.where(same & (i[:, None] <= i[None, :]), 0.0, NEG)
    c[:, C_SEL + 0] = (i < 64)
    c[:, C_SEL + 1] = (i >= 64)
    for r in range(3):
        c[64 + r, C_MQ + r] = 1.0
        c[67 + r, C_MQ + 3] = 1.0
        c[67 + r, C_MK + r] = -1.0
        c[64 + r, C_MK + 3] = 1.0
    return c


def shard_inputs(inp):
    x = np.asarray(inp["x"], np.float32)[0]
    S = x.shape[0]
    L = ((64 + S + 127) // 128) * 128
    FS = S // NCORES
    xT = np.zeros((D, L), np.float32)
    xT[:, 48:64] = np.asarray(inp["meta_tokens"], np.float32).T
    xT[:, 64:64 + S] = x.T
    w_in = np.asarray(inp["w_in"], np.float32)[0]
    conv_w = np.asarray(inp["conv_w"], np.float32)[0]
    cst = make_consts()
    col = lambda v: np.ascontiguousarray(np.asarray(v, np.float32).reshape(KC, 128).T)
    maps = []
    for c in range(NCORES):
        w1 = np.zeros((D, W1COLS), np.float32)
        w1[:, G_QA:G_QA + 64] = w_in[:, c * 64:(c + 1) * 64]
        w1[:, G_KA:G_KA + 64] = w_in[:, 512 + c * 64:512 + (c + 1) * 64]
        w1[:, G_V:G_V + 64] = w_in[:, 1024 + c * 64:1024 + (c + 1) * 64]
        for r in range(6):
            w1[:, G_V + 64 + r] = w_in[:, 1536 + c]
        w1[:, G_V + 70] = w_in[:, 4616 + c]
        w1[:, G_V + 71] = w_in[:, 4624 + c]
        w1[:, G_QB:G_QB + 128] = w_in[:, 1544 + c * 128:1544 + (c + 1) * 128]
        w1[:, G_KB:G_KB + 128] = w_in[:, 2568 + c * 128:2568 + (c + 1) * 128]
        w1[:, G_VB:G_VB + 128] = w_in[:, 3592 + c * 128:3592 + (c + 1) * 128]
        w1[:, G_Z:G_Z + 128] = w_in[:, 4632 + c * 128:4632 + (c + 1) * 128]
        pv = np.zeros((128, PV_N), np.float32)
        pv[:, PV_G0:PV_G0 + 8] = col(inp["ln_in_g"])
        pv[:, PV_B0:PV_B0 + 8] = col(inp["ln_in_b"])
        for t, base in enumerate((0, 1024, 2048)):
            pv[:, PV_CW + 4 * t:PV_CW + 4 * t + 4] = conv_w[:, base + c * 128:base + (c + 1) * 128].T
        pv[:, PV_BF] = np.asarray(inp["b_f"], np.float32)[0, c]
        pv[:, PV_ALOG] = np.asarray(inp["a_log"], np.float32)[0, c]
        pv[:, PV_DT] = np.asarray(inp["dt_bias"], np.float32)[0, c]
        pv[:, PV_GNW] = np.asarray(inp["gdn_norm_w"], np.float32)[0]
        pv[:, PV_G1:PV_G1 + 8] = col(inp["ln1_g"])
        pv[:, PV_B1:PV_B1 + 8] = col(inp["ln1_b"])
        pv[:, PV_G2:PV_G2 + 8] = col(inp["ln2_g"])
        pv[:, PV_B2:PV_B2 + 8] = col(inp["ln2_b"])
        maps.append({
            "xT": xT, "xown": np.ascontiguousarray(xT[:, 64 + c * FS:64 + (c + 1) * FS]), "w1": w1, "pv": pv, "cst": cst,
            "w2g": np.ascontiguousarray(w_in[:, 5656:7704]),
            "woa": np.asarray(inp["w_out_a"], np.float32)[0], "wob": np.asarray(inp["w_out_b"], np.float32)[0],
            "wo": np.asarray(inp["w_o"], np.float32)[0], "wg": np.asarray(inp["w_gate"], np.float32)[0],
            "wu": np.asarray(inp["w_up"], np.float32)[0], "wd": np.asarray(inp["w_down"], np.float32)[0],
        })
    return maps, S


P1_KEYS = ("xT", "w1", "pv", "cst")
P2_KEYS = ("xown", "pv", "cst", "w2g", "woa", "wob", "wo", "wg", "wu", "wd")


def kernel(**inputs):
    maps, S = shard_inputs(inputs)
    FS = S // NCORES
    nc1 = build(S, "p1")
    r1 = run_bass_kernel_spmd(nc1, maps, core_ids=list(range(NCORES)))
    ys = [np.asarray(r1.results[c]["ysrc"]).reshape(NCORES, 192, FS) for c in range(NCORES)]
    maps2 = []
    for c in range(NCORES):
        m2 = dict(maps[c])
        m2["yin"] = np.ascontiguousarray(np.concatenate([ys[r][c] for r in range(NCORES)], axis=0))
        maps2.append(m2)
    nc2 = build(S, "p2")
    res = run_bass_kernel_spmd(nc2, maps2, core_ids=list(range(NCORES)))
    out = np.concatenate([np.asarray(res.results[c]["outT"], np.float32).T for c in range(NCORES)], axis=0)
    return out[None].astype(np.float32)
```

```python
import contextlib
import numpy as np
import concourse.bass as bass
import concourse.mybir as mybir
from concourse.bass_utils import run_bass_kernel_spmd

F32 = mybir.dt.float32
BF16 = mybir.dt.bfloat16
ALU = mybir.AluOpType
AF = mybir.ActivationFunctionType

NCORES = 8
D = 1024
KC = 8
DFF = 2816
FC = 22
ALPHA = 2.0 ** 0.25
LN_EPS = 1e-5
NORM_EPS = 1e-6
NEG = -30000.0
G_QA, G_KA, G_V, G_QB, G_KB, G_VB, G_Z = 0, 64, 128, 200, 328, 456, 584
W1COLS = 712
GROUPS = [(G_QA, 64), (G_KA, 64), (G_V, 72), (G_QB, 128), (G_KB, 128), (G_VB, 128), (G_Z, 128)]
PV_G0, PV_B0 = 0, 8
PV_CW = 16
PV_BF, PV_ALOG, PV_DT, PV_GNW = 28, 29, 30, 31
PV_G1, PV_B1, PV_G2, PV_B2 = 32, 40, 48, 56
PV_N = 64
C_ID, C_LT2, C_BLK, C_MSL, C_MIU, C_SEL, C_MQ, C_MK = 0, 128, 256, 384, 512, 640, 642, 646
C_N = 650

ENGS = ("pe", "act", "dve", "pool", "sp")
EPOCH = 16000


class Buf:
    __slots__ = ("w", "r", "name", "excl")

    def __init__(self, name="", excl=False):
        self.w = None
        self.r = []
        self.name = name
        self.excl = excl


class Op:
    __slots__ = ("eng", "fn", "deps", "sig", "val", "key", "inc", "ep")


class Prog:
    def __init__(self):
        self.ops = {e: [] for e in ENGS}
        self.pending = {e: [] for e in ENGS}
        self.dmas = {}
        self.cap = None

    def begin_capture(self):
        self.cap = []

    def end_capture(self):
        c, self.cap = self.cap, None
        return c

    def add_merged(self, lists):
        idx = [0] * len(lists)
        tot = [max(1, len(l)) for l in lists]
        while True:
            best, bf = -1, 2.0
            for i, l in enumerate(lists):
                if idx[i] < len(l):
                    f = idx[i] / tot[i]
                    if f < bf:
                        best, bf = i, f
            if best < 0:
                break
            self.add(*lists[best][idx[best]])
            idx[best] += 1

    def add(self, eng, fn, reads=(), writes=(), key=None, inc=16):
        if self.cap is not None:
            self.cap.append((eng, fn, reads, writes, key, inc))
            return None
        op = Op()
        op.eng, op.fn, op.sig, op.val, op.key, op.inc = eng, fn, False, 0, key, inc
        deps = list(self.pending[eng])
        self.pending[eng] = []
        ex = [b for b in reads if b.excl]
        if ex:
            writes = list(writes) + ex
            reads = [b for b in reads if not b.excl]
        raw = set()
        for b in reads:
            if b.w is not None:
                deps.append(b.w)
                raw.add(id(b.w))
        for b in writes:
            if b.w is not None:
                deps.append(b.w)
                if b.excl:
                    raw.add(id(b.w))
            deps.extend(b.r)
        need, seen = [], set()
        for d in deps:
            if id(d) in seen:
                continue
            seen.add(id(d))
            if d.key is None and key is None and d.eng == eng and (eng == "pe" or id(d) not in raw):
                continue
            if d.key is None:
                d.sig = True
            need.append(d)
        op.deps = need
        for b in reads:
            b.r.append(op)
        for b in writes:
            b.w = op
            b.r = []
        self.ops[eng].append(op)
        if key is not None:
            self.dmas[id(key)] = op
        return op

    def barrier(self):
        last = []
        for e in ENGS:
            if self.ops[e]:
                o = self.ops[e][-1]
                if o.key is None:
                    o.sig = True
                last.append(o)
        last.extend(self.dmas.values())
        for e in ENGS:
            self.pending[e] = list(last)

    def finalize(self):
        keycnt = {}
        for e in ENGS:
            cnt = 0
            for op in self.ops[e]:
                if op.key is not None:
                    k = id(op.key)
                    keycnt[k] = keycnt.get(k, 0) + op.inc
                    op.val = keycnt[k]
                elif op.sig:
                    cnt += 1
                    op.ep = (cnt - 1) // EPOCH
                    op.val = (cnt - 1) % EPOCH + 1
            self.nep = getattr(self, "nep", {})
            self.nep[e] = (cnt - 1) // EPOCH + 1 if cnt else 1

    def emit(self, eng, h, engsem, keysem):
        waited = {}
        for op in self.ops[eng]:
            for d in op.deps:
                s = keysem[id(d.key)] if d.key is not None else engsem[d.eng][d.ep]
                sid = id(s)
                if waited.get(sid, 0) < d.val:
                    h.wait_ge(s, d.val)
                    waited[sid] = d.val
            ins = op.fn(h)
            if op.key is not None:
                ins.then_inc(keysem[id(op.key)], op.inc)
            elif op.sig:
                ins.then_inc(engsem[eng][op.ep], 1)


def build(S, mode="fused"):
    L = ((64 + S + 127) // 128) * 128
    NBLK = L // 128
    tiles = []
    p = 0
    while p < L:
        w = min(512, L - p)
        tiles.append((p, w))
        p += w
    FS = S // NCORES
    W2 = min(512, FS)
    NT2 = FS // W2

    nc = bass.Bass("TRN2", target_bir_lowering=False)
    dt_in = lambda n, shp: nc.dram_tensor(n, shp, F32, kind="ExternalInput").ap()
    xT = dt_in("xT", [D, L])
    xown = dt_in("xown", [D, FS])
    w1 = dt_in("w1", [D, W1COLS])
    pv_d = dt_in("pv", [128, PV_N])
    cst_d = dt_in("cst", [128, C_N])
    w2g = dt_in("w2g", [D, 2048])
    woa = dt_in("woa", [512, D])
    wob = dt_in("wob", [D, D])
    wo = dt_in("wo", [D, D])
    wg = dt_in("wg", [D, DFF])
    wu = dt_in("wu", [D, DFF])
    wd = dt_in("wd", [DFF, D])
    if mode != "p1":
        outT = nc.dram_tensor("outT", [D, FS], F32, kind="ExternalOutput").ap()
    if mode == "p1":
        ysrc = nc.dram_tensor("ysrc", [NCORES * 192, FS], BF16, kind="ExternalOutput").ap()
    else:
        ysrc = nc.dram_tensor("ysrc", [NCORES * 192, FS], BF16).ap()
    DBG = 0
    if mode == "p2":
        yin = nc.dram_tensor("yin", [NCORES * 192, FS], BF16, kind="ExternalInput").ap()
    agout = nc.dram_tensor("ystage", [16 * NCORES * 96, FS], BF16).ap()
    agin = nc.dram_tensor("agin", [96, FS], BF16).ap()
    agbuf = nc.dram_tensor("agbuf", [NCORES * 96, FS], BF16).ap()

    P = Prog()
    es = contextlib.ExitStack()
    with es:
        ARENA_F = 50 * 1024
        arena = es.enter_context(nc.sbuf_tensor("arena", [128, ARENA_F], F32))
        psum = es.enter_context(nc.psum_tensor("ps", [128, 4096], F32))
        bank = [psum[:, b * 512:(b + 1) * 512] for b in range(8)]
        bankbuf = [Buf("bank%d" % b, excl=True) for b in range(8)]

        class Arena:
            def __init__(self):
                self.off = 0

            def f32(self, n):
                a = arena[:, self.off:self.off + n]
                self.off += n
                assert self.off <= ARENA_F, "SBUF arena overflow %d" % self.off
                return a

            def bf16(self, n):
                m = (n + 1) // 2
                a = arena[:, self.off:self.off + m].bitcast(BF16)
                self.off += m
                assert self.off <= ARENA_F, "SBUF arena overflow %d" % self.off
                return a[:, 0:n]

        A = Arena()
        cst = A.f32(C_N)
        pv = A.f32(PV_N)
        ones_f = A.f32(128)
        identb = A.bf16(128)
        onesb = A.bf16(128)
        B_cst, B_pv, B_misc = Buf("cst"), Buf("pv"), Buf("misc")
        ident = cst[:, C_ID:C_ID + 128]
        P.add("sp", lambda e: e.dma_start(out=cst, in_=cst_d), writes=[B_cst], key=B_cst)
        P.add("sp", lambda e: e.dma_start(out=pv, in_=pv_d), writes=[B_pv], key=B_pv)
        P.add("pool", lambda e: e.memset(ones_f, 1.0), writes=[B_misc])
        P.add("pool", lambda e: e.memset(onesb, 1.0 / 1024.0), writes=[B_misc])
        P.add("pool", lambda e: e.tensor_copy(out=identb, in_=ident), reads=[B_cst], writes=[B_misc])
        CONSTS = [B_cst, B_pv, B_misc]
        base_off = A.off

        gp = [5]
        gmode = [0]

        def gbank():
            if gmode[0] == 1:
                gp[0] = 6 if gp[0] == 5 else 5
                return gp[0]
            if gmode[0] == 2:
                gpa[0] = 4 if gpa[0] == 3 else 3
                return gpa[0]
            b = gp[0]
            gp[0] = 5 + (gp[0] - 5 + 1) % 3
            return b

        gpa = [3]

        DO_P1 = mode != "p2"
        Wc = A.bf16(KC * W1COLS).rearrange("p (k c) -> p k c", k=KC)
        B_Wc = Buf("Wc")
        bW = A.f32(8)
        prep_off = A.off
        csum = A.f32(W1COLS)
        stg = [A.f32(W1COLS), A.f32(W1COLS)]
        B_stg = [Buf("stg0"), Buf("stg1")]
        wgt = A.f32(W1COLS)
        B_wgt = Buf("wgt")
        B_bW, B_csum = Buf("bW"), Buf("csum")
        w1v = w1.rearrange("(k p) c -> p k c", p=128)
        psc = [bank[5], bank[6]]
        for k in range(KC if DO_P1 else 0):
            s = stg[k % 2]
            P.add("sp", lambda e, s=s, k=k: e.dma_start(out=s, in_=w1v[:, k, :]), writes=[B_stg[k % 2]], key=B_stg[k % 2])
            P.add("dve", lambda e, s=s, k=k: e.tensor_scalar(out=wgt, in0=s, scalar1=pv[:, PV_G0 + k:PV_G0 + k + 1], scalar2=None, op0=ALU.mult),
                  reads=[B_stg[k % 2], B_pv], writes=[B_wgt])
            P.add("pe", lambda e, k=k: e.matmul(psc[0][:, 0:512], ones_f, wgt[:, 0:512], start=(k == 0), stop=(k == KC - 1)),
                  reads=[B_wgt, B_misc], writes=[bankbuf[5]])
            P.add("pe", lambda e, k=k: e.matmul(psc[1][:, 0:W1COLS - 512], ones_f, wgt[:, 512:W1COLS], start=(k == 0), stop=(k == KC - 1)),
                  reads=[B_wgt, B_misc], writes=[bankbuf[6]])
            order = [3, 0, 1, 2, 4, 5, 6]
            for oi, gi in enumerate(order):
                go, gm = GROUPS[gi]
                P.add("pe", lambda e, s=s, k=k, gi=gi, go=go, gm=gm, oi=oi: e.matmul(bank[7][0:gm, gi:gi + 1], s[:, go:go + gm], pv[:, PV_B0 + k:PV_B0 + k + 1],
                                                                                    start=(k == 0 and oi == 0), stop=(k == KC - 1 and oi == 6), skip_group_check=True),
                      reads=[B_stg[k % 2], B_pv], writes=[bankbuf[7]])
        if DO_P1:
            P.add("act", lambda e: e.mul(csum[:, 0:512], psc[0][:, 0:512], 1.0 / 1024.0), reads=[bankbuf[5]], writes=[B_csum])
            P.add("act", lambda e: e.mul(csum[:, 512:W1COLS], psc[1][:, 0:W1COLS - 512], 1.0 / 1024.0), reads=[bankbuf[6]], writes=[B_csum])
            P.add("dve", lambda e: e.tensor_copy(out=bW[:, 0:7], in_=bank[7][:, 0:7]), reads=[bankbuf[7]], writes=[B_bW])
        for k in range(KC if DO_P1 else 0):
            s = stg[k % 2]
            P.add("sp", lambda e, s=s, k=k: e.dma_start(out=s, in_=w1v[:, k, :]), writes=[B_stg[k % 2]], key=B_stg[k % 2])
            P.add("dve", lambda e, s=s, k=k: e.scalar_tensor_tensor(out=Wc[:, k, :], in0=s, scalar=pv[:, PV_G0 + k:PV_G0 + k + 1], in1=csum,
                                                                    op0=ALU.mult, op1=ALU.subtract),
                  reads=[B_stg[k % 2], B_pv, B_csum], writes=[B_Wc])
        P.barrier()
        A.off = prep_off
        sc1 = A.f32(8)
        B_sc1 = Buf("sc1")
        if DO_P1:
          P.add("act", lambda e: e.activation(out=sc1[:, 0:1], in_=pv[:, PV_ALOG:PV_ALOG + 1], func=AF.Exp), reads=[B_pv], writes=[B_sc1])
          P.add("dve", lambda e: e.tensor_scalar(out=sc1[:, 0:1], in0=sc1[:, 0:1], scalar1=-1.0, scalar2=None, op0=ALU.mult), reads=[B_sc1], writes=[B_sc1])
          P.add("dve", lambda e: e.tensor_scalar(out=sc1[:, 1:2], in0=bW[:, 0:1], scalar1=0.125, scalar2=None, op0=ALU.mult), reads=[B_bW], writes=[B_sc1])

        Kaug = A.bf16(L)
        Vc = A.bf16(NBLK * 65).rearrange("p (b c) -> p b c", c=65)
        B_K = [Buf("K%d" % i) for i in range(len(tiles))]
        B_V = [Buf("V%d" % i) for i in range(len(tiles))]
        B_Vinit = Buf("Vinit")
        if DO_P1:
            P.add("pool", lambda e: e.memset(Vc[:, :, 64:65], 1.0), writes=[B_Vinit])

        xb = [A.bf16(KC * 512).rearrange("p (k w) -> p k w", k=KC) for _ in range(2)]
        B_xb = [Buf("xb0"), Buf("xb1")]
        sq = A.bf16(KC * 512).rearrange("p (k w) -> p k w", k=KC)
        B_sq = Buf("sq")

        def T32(name):
            return A.f32(512), Buf(name)

        def T16(name):
            return A.bf16(512), Buf(name)

        mean_sb, B_mean = T32("mean")
        rstd, B_rstd = T32("rstd")
        tmpA, B_tmpA = T32("tmpA")
        tmpB, B_tmpB = T32("tmpB")
        Qaug_l = [A.bf16(512), A.bf16(512)]
        B_Q_l = [Buf("Qaug0"), Buf("Qaug1")]
        VG, B_VG = T32("VG")
        X_l = [[A.f32(516) for _ in range(3)] for _ in range(2)]
        B_X_l = [[Buf("X%d_%d" % (q, t)) for t in range(3)] for q in range(2)]
        zs_l = [A.f32(512), A.f32(512)]
        B_zs_l = [Buf("zs0"), Buf("zs1")]
        ctile, B_c = T32("c")
        lrow, B_lrow = T32("lrow")
        lrow2, B_lrow2 = T32("lrow2")
        hiB, B_hi = T16("hi"); loB, B_lo = T16("lo"); lo2B, B_lo2 = T16("lo2")
        r1, B_r1 = T32("r1")
        accr, B_accr = T32("accr")
        onesrow, B_onesrow = T32("onesrow")
        bigm, B_bigm = T32("bigm")
        carry = A.f32(2)
        B_carry = Buf("carry")
        PT = [A.bf16(1024).rearrange("p (j w) -> p j w", j=2) for _ in range(2)]
        B_PT = [Buf("PT0"), Buf("PT1")]
        O_sb, B_Osb = T32("Osb")
        rden, B_rden = T32("rden")
        YA, B_YA = T16("YA")
        YB, B_YB = T16("YB")
        halo = [A.f32(12) for _ in range(3)]
        B_halo = [Buf("halo%d" % i) for i in range(3)]
        sctm_l = [A.f32(32).rearrange("p (j c) -> p j c", c=8) for _ in range(2)]
        B_sctm_l = [Buf("sctm0"), Buf("sctm1")]
        cq, B_cq = T32("cq"); ck, B_ck = T32("ck"); cv, B_cv = T32("cv")
        qs, B_qs = cq, B_cq
        ks, B_ks = ck, B_ck
        sqq, B_sqq = T32("sqq"); sqk, B_sqk = sqq, B_sqq
        rnq, B_rnq = T32("rnq"); rnk, B_rnk = rnq, B_rnq
        qnT, B_qnT = T16("qnT"); knT, B_knT = T16("knT"); vsT, B_vsT = T16("vsT")
        tms = A.f32(64)
        B_tms = Buf("tms")
        t_beta, t_nbeta, t_x, t_ax, t_e, t_l, t_g, t_gc, t_ngc, t_e1, t_e2, t_d = [tms[:, 4 * i:4 * i + 4] for i in range(12)]
        rhs8 = A.f32(8)
        glbc = A.f32(8)
        B_glbc = Buf("glbc")
        Xuw = A.bf16(4 * 256).rearrange("p (j c) -> p j c", c=256)
        B_Xuw = Buf("Xuw")
        kdec = A.bf16(4 * 128).rearrange("p (j c) -> p j c", c=128)
        B_kdec = Buf("kdec")
        UW = A.bf16(4 * 256).rearrange("p (j c) -> p j c", c=256)
        B_UW = [Buf("UW%d" % j) for j in range(4)]
        diag = [A.f32(128) for _ in range(2)]
        B_diag = [Buf("diag0"), Buf("diag1")]
        T1 = [A.f32(128) for _ in range(2)]
        B_T1 = [Buf("T1a"), Buf("T1b")]
        T3 = [A.f32(128) for _ in range(2)]
        B_T3 = [Buf("T3a"), Buf("T3b")]
        Dsl = [A.f32(128) for _ in range(2)]
        B_Dsl = [Buf("Dsl0"), Buf("Dsl1")]
        Diu = [A.f32(128) for _ in range(2)]
        B_Diu = [Buf("Diu0"), Buf("Diu1")]
        EGR = [A.f32(128) for _ in range(4)]
        B_EGR = [Buf("EGR%d" % j) for j in range(4)]
        attnT = [A.bf16(128) for _ in range(4)]
        B_attnT = [Buf("attnT%d" % j) for j in range(4)]
        PP = [[A.bf16(256) for _ in range(2)] for _ in range(4)]
        B_PP = [[Buf("PP%d_%d" % (j, q)) for q in range(2)] for j in range(4)]
        RR = [[A.bf16(128) for _ in range(2)] for _ in range(4)]
        B_RR = [[Buf("RR%d_%d" % (j, q)) for q in range(2)] for j in range(4)]
        qdecT, B_qdecT = T32("qdecT")
        QeffT, B_QeffT = T32("QeffT")
        MT = [A.f32(128) for _ in range(8)]
        B_MT = [Buf("MT%d" % j) for j in range(8)]
        Bn = [A.f32(128) for _ in range(8)]
        B_Bn = [Buf("Bn%d" % j) for j in range(8)]
        Sst = [A.f32(128) for _ in range(9)]
        B_S = [Buf("S%d" % j) for j in range(9)]
        sqo, B_sqo = T32("sqo")
        rno, B_rno = T32("rno")
        p1_end = A.off

        if DO_P1:
            P.add("pool", lambda e: e.memset(Sst[0], 0.0), writes=[B_S[0]])
            P.add("pool", lambda e: e.memset(onesrow, 1.0), writes=[B_onesrow])
            P.add("pool", lambda e: e.memset(bigm, 0.0), writes=[B_bigm])
            P.add("pool", lambda e: e.memset(bigm[64:70, 0:48], 30000.0), writes=[B_bigm])
            for t_ in range(3):
                P.add("pool", lambda e, t_=t_: e.memset(X_l[0][t_][:, 0:3], 0.0), writes=[B_X_l[0][t_]])

        xTv = xT.rearrange("(k p) l -> p k l", p=128)
        ysv = ysrc.rearrange("(s f) t -> s f t", f=192)
        B_ysrc = Buf("ysrc")
        RS = slice(64, 70)
        cwq = lambda j: pv[:, PV_CW + j:PV_CW + j + 1]
        s_cur = [0]

        def y_out(src, rows0, nrows, c0, W, Bsrc):
            p = max(c0, 64)
            end = min(c0 + W, 64 + S)
            while p < end:
                f = p - 64
                sh = f // FS
                fe = min(end - 64, (sh + 1) * FS)
                n = fe - f
                P.add("sp", lambda e, sh=sh, f=f, n=n, p=p: e.dma_start(out=ysv[sh, rows0:rows0 + nrows, f - sh * FS:f - sh * FS + n],
                                                                        in_=src[0:nrows, p - c0:p - c0 + n]),
                      reads=[Bsrc], writes=[B_ysrc], key=B_ysrc)
                p += n

        STAGE = 99

        def p1_tile(ti, c0, W):
            nb = W // 128
            blk0 = c0 // 128
            par = ti % 2
            xbt = xb[par]
            Qaug, B_Q = Qaug_l[par], B_Q_l[par]
            zs, B_zs = zs_l[par], B_zs_l[par]
            (Xq, Xk, Xv), (B_Xq, B_Xk, B_Xv) = X_l[par], B_X_l[par]
            sctm, B_sctm = sctm_l[par], B_sctm_l[par]
            gmode[0] = 2
            P.begin_capture()
            P.add("pool", lambda e, xbt=xbt, c0=c0, W=W: e.dma_start(out=xbt[:, :, 0:W], in_=xTv[:, :, c0:c0 + W]), writes=[B_xb[par]], key=B_xb[par])
            P.add("pool", lambda e, xbt=xbt, W=W: e.tensor_tensor(out=sq[:, :, 0:W], in0=xbt[:, :, 0:W], in1=xbt[:, :, 0:W], op=ALU.mult),
                  reads=[B_xb[par]], writes=[B_sq])
            b1, b2 = gbank(), gbank()
            for k in range(KC):
                P.add("pe", lambda e, k=k, b1=b1, xbt=xbt, W=W: e.matmul(bank[b1][:, 0:W], onesb, xbt[:, k, 0:W], start=(k == 0), stop=(k == KC - 1)),
                      reads=[B_xb[par], B_misc], writes=[bankbuf[b1]])
            for k in range(KC):
                P.add("pe", lambda e, k=k, b2=b2, W=W: e.matmul(bank[b2][:, 0:W], onesb, sq[:, k, 0:W], start=(k == 0), stop=(k == KC - 1)),
                      reads=[B_sq, B_misc], writes=[bankbuf[b2]])
            P.add("act", lambda e, b1=b1, W=W: e.copy(mean_sb[:, 0:W], bank[b1][:, 0:W]), reads=[bankbuf[b1]], writes=[B_mean])
            P.add("dve", lambda e, W=W: e.tensor_tensor(out=tmpA[:, 0:W], in0=mean_sb[:, 0:W], in1=mean_sb[:, 0:W], op=ALU.mult), reads=[B_mean], writes=[B_tmpA])
            P.add("dve", lambda e, b2=b2, W=W: e.tensor_tensor(out=tmpA[:, 0:W], in0=bank[b2][:, 0:W], in1=tmpA[:, 0:W], op=ALU.subtract),
                  reads=[bankbuf[b2], B_tmpA], writes=[B_tmpA])
            P.add("dve", lambda e, W=W: e.tensor_scalar(out=tmpA[:, 0:W], in0=tmpA[:, 0:W], scalar1=0.0, scalar2=LN_EPS, op0=ALU.max, op1=ALU.add),
                  reads=[B_tmpA], writes=[B_tmpA])
            P.add("act", lambda e, W=W: e.activation(out=tmpA[:, 0:W], in_=tmpA[:, 0:W], func=AF.Sqrt), reads=[B_tmpA], writes=[B_tmpA])
            P.add("dve", lambda e, W=W: e.reciprocal(out=rstd[:, 0:W], in_=tmpA[:, 0:W]), reads=[B_tmpA], writes=[B_rstd])

            def proj(go, gm):
                b = gbank()
                for k in range(KC):
                    P.add("pe", lambda e, k=k, b=b: e.matmul(bank[b][0:gm, 0:W], Wc[:, k, go:go + gm], xbt[:, k, 0:W], start=(k == 0), stop=(k == KC - 1)),
                          reads=[B_Wc, B_xb[par]], writes=[bankbuf[b]])
                return b

            def evac(b, gm, gi, out_ap, Bout, func=AF.Identity, scale=1.0, bias_ap=None, tmp=None, Btmp=None):
                tmp_, Bt = (tmpB, B_tmpB) if tmp is None else (tmp, Btmp)
                P.add("dve", lambda e: e.tensor_tensor(out=tmp_[0:gm, 0:W], in0=bank[b][0:gm, 0:W], in1=rstd[0:gm, 0:W], op=ALU.mult),
                      reads=[bankbuf[b], B_rstd], writes=[Bt])
                bia = bW[0:gm, gi:gi + 1] if bias_ap is None else bias_ap
                P.add("act", lambda e: e.activation(out=out_ap, in_=tmp_[0:gm, 0:W], func=func, bias=bia, scale=scale),
                      reads=[Bt, B_bW, B_sc1], writes=[Bout])

            b = proj(G_QA, 64)
            evac(b, 64, 0, Qaug[0:64, 0:W], B_Q, scale=0.125, bias_ap=sc1[0:64, 1:2])
            b = proj(G_KA, 64)
            evac(b, 64, 1, Kaug[0:64, c0:c0 + W], B_K[ti])
            b = proj(G_V, 72)
            evac(b, 72, 2, VG[0:72, 0:W], B_VG)
            b = proj(G_QB, 128)
            evac(b, 128, 3, Xq[:, 3:3 + W], B_Xq)
            b = proj(G_KB, 128)
            evac(b, 128, 4, Xk[:, 3:3 + W], B_Xk)
            b = proj(G_VB, 128)
            evac(b, 128, 5, Xv[:, 3:3 + W], B_Xv)
            b = proj(G_Z, 128)
            evac(b, 128, 6, zs[:, 0:W], B_zs, func=AF.Silu)
            if ti == 0:
                for X_, B_ in ((Xq, B_Xq), (Xk, B_Xk), (Xv, B_Xv)):
                    P.add("pool", lambda e, X_=X_: e.memset(X_[:, 3:3 + 48], 0.0), writes=[B_])
            for t_ in range(3):
                P.add("pool", lambda e, t_=t_: e.tensor_copy(out=halo[ti % 3][:, 3 * t_:3 * t_ + 3], in_=X_l[par][t_][:, W:W + 3]), reads=[B_X_l[par][t_]], writes=[B_halo[ti % 3]])

            for j in range(nb):
                b = gbank()
                P.add("pe", lambda e, j=j, b=b: e.transpose(bank[b][:, 0:72], VG[0:72, j * 128:(j + 1) * 128], ident[0:72, 0:72]),
                      reads=[B_VG, B_cst], writes=[bankbuf[b]])
                P.add("act", lambda e, j=j, b=b: e.copy(Vc[:, blk0 + j, 0:64], bank[b][:, 0:64]), reads=[bankbuf[b], B_Vinit], writes=[B_V[ti]])
                P.add("dve", lambda e, j=j, b=b: e.tensor_copy(out=sctm[:, j, :], in_=bank[b][:, 64:72]), reads=[bankbuf[b]], writes=[B_sctm])

            P.add("dve", lambda e: e.tensor_scalar(out=lrow[RS, 0:W], in0=VG[RS, 0:W], scalar1=pv[RS, PV_BF:PV_BF + 1], scalar2=-1.0, op0=ALU.add, op1=ALU.mult),
                  reads=[B_VG, B_pv], writes=[B_lrow])
            P.add("dve", lambda e: e.tensor_scalar(out=lrow2[RS, 0:W], in0=lrow[RS, 0:W], scalar1=-1.0, scalar2=None, op0=ALU.mult), reads=[B_lrow], writes=[B_lrow2])
            P.add("dve", lambda e: e.tensor_tensor(out=lrow2[RS, 0:W], in0=lrow2[RS, 0:W], in1=lrow[RS, 0:W], op=ALU.max), reads=[B_lrow, B_lrow2], writes=[B_lrow2])
            P.add("act", lambda e: e.activation(out=lrow2[RS, 0:W], in_=lrow2[RS, 0:W], func=AF.Exp, scale=-1.0), reads=[B_lrow2], writes=[B_lrow2])
            P.add("act", lambda e: e.activation(out=lrow2[RS, 0:W], in_=lrow2[RS, 0:W], func=AF.Ln, bias=1.0), reads=[B_lrow2], writes=[B_lrow2])
            P.add("dve", lambda e: e.scalar_tensor_tensor(out=lrow[RS, 0:W], in0=lrow[RS, 0:W], scalar=0.0, in1=lrow2[RS, 0:W], op0=ALU.max, op1=ALU.add),
                  reads=[B_lrow, B_lrow2], writes=[B_lrow])
            init = 0.0 if ti == 0 else carry[RS, 0:1]
            P.add("dve", lambda e, init=init: e.tensor_tensor_scan(out=ctile[RS, 0:W], data0=onesrow[RS, 0:W], data1=lrow[RS, 0:W], initial=init,
                                                                  op0=ALU.mult, op1=ALU.subtract),
                  reads=[B_lrow, B_onesrow, B_carry], writes=[B_c])
            P.add("dve", lambda e: e.tensor_copy(out=carry[RS, 0:1], in_=ctile[RS, W - 1:W]), reads=[B_c], writes=[B_carry])

            def split3(src, Bsrc):
                P.add("dve", lambda e: e.tensor_copy(out=hiB[RS, 0:W], in_=src[RS, 0:W]), reads=[Bsrc], writes=[B_hi])
                P.add("dve", lambda e: e.tensor_tensor(out=r1[RS, 0:W], in0=src[RS, 0:W], in1=hiB[RS, 0:W], op=ALU.subtract), reads=[Bsrc, B_hi], writes=[B_r1])
                P.add("dve", lambda e: e.tensor_copy(out=loB[RS, 0:W], in_=r1[RS, 0:W]), reads=[B_r1], writes=[B_lo])
                P.add("dve", lambda e: e.tensor_tensor(out=r1[RS, 0:W], in0=r1[RS, 0:W], in1=loB[RS, 0:W], op=ALU.subtract), reads=[B_r1, B_lo], writes=[B_r1])
                P.add("dve", lambda e: e.tensor_copy(out=lo2B[RS, 0:W], in_=r1[RS, 0:W]), reads=[B_r1], writes=[B_lo2])

            def augrows(mc, out_ap, Bout):
                m = lambda i: cst[RS, mc + i:mc + i + 1]
                P.add("dve", lambda e: e.tensor_scalar(out=accr[RS, 0:W], in0=hiB[RS, 0:W], scalar1=m(0), scalar2=m(3), op0=ALU.mult, op1=ALU.add),
                      reads=[B_hi, B_cst], writes=[B_accr])
                P.add("dve", lambda e: e.scalar_tensor_tensor(out=accr[RS, 0:W], in0=loB[RS, 0:W], scalar=m(1), in1=accr[RS, 0:W], op0=ALU.mult, op1=ALU.add),
                      reads=[B_lo, B_accr, B_cst], writes=[B_accr])
                P.add("dve", lambda e: e.scalar_tensor_tensor(out=out_ap, in0=lo2B[RS, 0:W], scalar=m(2), in1=accr[RS, 0:W], op0=ALU.mult, op1=ALU.add),
                      reads=[B_lo2, B_accr, B_cst], writes=[Bout])

            split3(ctile, B_c)
            augrows(C_MQ, Qaug[RS, 0:W], B_Q)
            if ti == 0:
                P.add("dve", lambda e: e.tensor_tensor(out=lrow2[RS, 0:W], in0=ctile[RS, 0:W], in1=bigm[RS, 0:W], op=ALU.add), reads=[B_c, B_bigm], writes=[B_lrow2])
                split3(lrow2, B_lrow2)
            augrows(C_MK, Kaug[RS, c0:c0 + W], B_K[ti])

            a_ops = P.end_capture()
            P.begin_capture()
            nkb = blk0 + nb
            for kb in range(nkb):
                g = kb % 2
                P.add("pe", lambda e, g=g, kb=kb: e.matmul(bank[g][:, 0:W], Kaug[0:70, kb * 128:(kb + 1) * 128], Qaug[0:70, 0:W], start=True, stop=True),
                      reads=[B_Q, B_K[kb * 128 // 512]], writes=[bankbuf[g]])
                P.add("act", lambda e, g=g: e.activation(out=PT[g][:, 0, 0:W], in_=bank[g][:, 0:W], func=AF.Exp), reads=[bankbuf[g]], writes=[B_PT[g]])
                jj = kb - blk0
                if jj >= 0:
                    P.add("pool", lambda e, g=g, jj=jj: e.affine_select(out=PT[g][:, 0, 0:W], in_=PT[g][:, 0, 0:W], pattern=[[1, W]], compare_op=ALU.is_ge,
                                                                        fill=0.0, base=-128 * jj, channel_multiplier=-1),
                          reads=[B_PT[g]], writes=[B_PT[g]])
                P.add("pe", lambda e, g=g, kb=kb: e.matmul(bank[2][0:65, 0:W], Vc[:, kb, 0:65], PT[g][:, 0, 0:W], start=(kb == 0), stop=(kb == nkb - 1)),
                      reads=[B_PT[g], B_V[kb * 128 // 512], B_Vinit], writes=[bankbuf[2]])
            P.add("act", lambda e: e.copy(O_sb[0:65, 0:W], bank[2][0:65, 0:W]), reads=[bankbuf[2]], writes=[B_Osb])
            b = 2
            P.add("pe", lambda e, b=b: e.matmul(bank[b][0:64, 0:W], ones_f[64:65, 0:64], O_sb[64:65, 0:W], start=True, stop=True),
                  reads=[B_Osb, B_misc], writes=[bankbuf[b]])
            P.add("dve", lambda e, b=b: e.reciprocal(out=rden[0:64, 0:W], in_=bank[b][0:64, 0:W]), reads=[bankbuf[b]], writes=[B_rden])
            P.add("dve", lambda e: e.tensor_tensor(out=YA[0:64, 0:W], in0=O_sb[0:64, 0:W], in1=rden[0:64, 0:W], op=ALU.mult), reads=[B_Osb, B_rden], writes=[B_YA])
            y_out(YA, 0, 64, c0, W, B_YA)

            att_ops = P.end_capture()
            P.begin_capture()
            gmode[0] = 1
            def conv(X_, B_X, off, out_, B_out):
                P.add("dve", lambda e: e.tensor_scalar(out=out_[:, 0:W], in0=X_[:, 0:W], scalar1=cwq(off), scalar2=None, op0=ALU.mult), reads=[B_X, B_pv], writes=[B_out])
                for j in range(1, 4):
                    P.add("dve", lambda e, j=j: e.scalar_tensor_tensor(out=out_[:, 0:W], in0=X_[:, j:j + W], scalar=cwq(off + j), in1=out_[:, 0:W], op0=ALU.mult, op1=ALU.add),
                          reads=[B_X, B_pv, B_out], writes=[B_out])

            if ti > 0:
                for t_ in range(3):
                    P.add("pool", lambda e, t_=t_: e.tensor_copy(out=X_l[par][t_][:, 0:3], in_=halo[(ti - 1) % 3][:, 3 * t_:3 * t_ + 3]), reads=[B_halo[(ti - 1) % 3]], writes=[B_X_l[par][t_]])
            conv(Xq, B_Xq, 0, cq, B_cq)
            conv(Xk, B_Xk, 4, ck, B_ck)
            conv(Xv, B_Xv, 8, cv, B_cv)
            P.add("act", lambda e: e.activation(out=qs[:, 0:W], in_=cq[:, 0:W], func=AF.Silu), reads=[B_cq], writes=[B_qs])
            P.add("act", lambda e: e.activation(out=ks[:, 0:W], in_=ck[:, 0:W], func=AF.Silu), reads=[B_ck], writes=[B_ks])
            P.add("act", lambda e: e.activation(out=vsT[:, 0:W], in_=cv[:, 0:W], func=AF.Silu), reads=[B_cv], writes=[B_vsT])
            for (src, Bs, sq_, Bsq, rn, Brn, outb, Bo, sc) in ((qs, B_qs, sqq, B_sqq, rnq, B_rnq, qnT, B_qnT, 128.0), (ks, B_ks, sqk, B_sqk, rnk, B_rnk, knT, B_knT, 1.0)):
                P.add("act", lambda e, src=src, sq_=sq_: e.activation(out=sq_[:, 0:W], in_=src[:, 0:W], func=AF.Square), reads=[Bs], writes=[Bsq])
                b = gbank()
                P.add("pe", lambda e, b=b, sq_=sq_: e.matmul(bank[b][:, 0:W], ones_f, sq_[:, 0:W], start=True, stop=True), reads=[Bsq, B_misc], writes=[bankbuf[b]])
                P.add("dve", lambda e, b=b, rn=rn, sc=sc: e.tensor_scalar(out=rn[:, 0:W], in0=bank[b][:, 0:W], scalar1=NORM_EPS, scalar2=sc, op0=ALU.add, op1=ALU.mult),
                      reads=[bankbuf[b]], writes=[Brn])
                P.add("act", lambda e, rn=rn: e.activation(out=rn[:, 0:W], in_=rn[:, 0:W], func=AF.Sqrt), reads=[Brn], writes=[Brn])
                P.add("dve", lambda e, rn=rn: e.reciprocal(out=rn[:, 0:W], in_=rn[:, 0:W]), reads=[Brn], writes=[Brn])
                P.add("dve", lambda e, src=src, rn=rn, outb=outb: e.tensor_tensor(out=outb[:, 0:W], in0=src[:, 0:W], in1=rn[:, 0:W], op=ALU.mult), reads=[Bs, Brn], writes=[Bo])

            a_in = sctm[:, 0:nb, 6]
            b_in = sctm[:, 0:nb, 7]
            nbs = slice(0, nb)
            P.add("act", lambda e: e.activation(out=t_beta[:, nbs], in_=b_in, func=AF.Sigmoid), reads=[B_sctm], writes=[B_tms])
            P.add("dve", lambda e: e.tensor_scalar(out=t_nbeta[:, nbs], in0=t_beta[:, nbs], scalar1=-1.0, scalar2=None, op0=ALU.mult), reads=[B_tms], writes=[B_tms])
            P.add("dve", lambda e: e.tensor_scalar(out=t_x[:, nbs], in0=a_in, scalar1=pv[:, PV_DT:PV_DT + 1], scalar2=None, op0=ALU.add), reads=[B_sctm, B_pv], writes=[B_tms])
            P.add("dve", lambda e: e.tensor_scalar(out=t_ax[:, nbs], in0=t_x[:, nbs], scalar1=-1.0, scalar2=None, op0=ALU.mult), reads=[B_tms], writes=[B_tms])
            P.add("dve", lambda e: e.tensor_tensor(out=t_ax[:, nbs], in0=t_ax[:, nbs], in1=t_x[:, nbs], op=ALU.max), reads=[B_tms], writes=[B_tms])
            P.add("act", lambda e: e.activation(out=t_e[:, nbs], in_=t_ax[:, nbs], func=AF.Exp, scale=-1.0), reads=[B_tms], writes=[B_tms])
            P.add("act", lambda e: e.activation(out=t_l[:, nbs], in_=t_e[:, nbs], func=AF.Ln, bias=1.0), reads=[B_tms], writes=[B_tms])
            P.add("dve", lambda e: e.scalar_tensor_tensor(out=t_g[:, nbs], in0=t_x[:, nbs], scalar=0.0, in1=t_l[:, nbs], op0=ALU.max, op1=ALU.add), reads=[B_tms], writes=[B_tms])
            P.add("dve", lambda e: e.tensor_scalar(out=t_g[:, nbs], in0=t_g[:, nbs], scalar1=sc1[:, 0:1], scalar2=None, op0=ALU.mult), reads=[B_tms, B_sc1], writes=[B_tms])
            bg = gbank()
            P.add("pe", lambda e, bg=bg: e.matmul(bank[bg][:, 0:nb], cst[:, C_LT2:C_LT2 + 128], t_g[:, nbs], start=True, stop=True), reads=[B_tms, B_cst], writes=[bankbuf[bg]])
            P.add("pe", lambda e, bg=bg: e.matmul(bank[bg][:, 8:8 + nb], cst[:, C_BLK:C_BLK + 128], t_g[:, nbs], start=True, stop=True), reads=[B_tms, B_cst], writes=[bankbuf[bg]])
            for c_ in range(2):
                P.add("dve", lambda e, c_=c_: e.tensor_scalar(out=rhs8[:, 0:2 * nb].rearrange("p (j c) -> p j c", c=2)[:, :, c_], in0=t_g[:, nbs],
                                                               scalar1=cst[:, C_SEL + c_:C_SEL + c_ + 1], scalar2=None, op0=ALU.mult),
                      reads=[B_tms, B_cst], writes=[B_glbc])
            P.add("pe", lambda e, bg=bg: e.matmul(bank[bg][:, 16:16 + 2 * nb], ones_f, rhs8[:, 0:2 * nb], start=True, stop=True), reads=[B_glbc, B_misc], writes=[bankbuf[bg]])
            P.add("dve", lambda e, bg=bg: e.tensor_copy(out=t_gc[:, nbs], in_=bank[bg][:, 0:nb]), reads=[bankbuf[bg]], writes=[B_tms])
            P.add("dve", lambda e: e.tensor_scalar(out=t_ngc[:, nbs], in0=t_gc[:, nbs], scalar1=-1.0, scalar2=None, op0=ALU.mult), reads=[B_tms], writes=[B_tms])
            P.add("dve", lambda e, bg=bg: e.tensor_tensor(out=t_d[:, nbs], in0=bank[bg][:, 8:8 + nb], in1=t_gc[:, nbs], op=ALU.subtract), reads=[bankbuf[bg], B_tms], writes=[B_tms])
            P.add("act", lambda e: e.activation(out=t_e2[:, nbs], in_=t_d[:, nbs], func=AF.Exp), reads=[B_tms], writes=[B_tms])
            P.add("act", lambda e: e.activation(out=t_e1[:, nbs], in_=t_gc[:, nbs], func=AF.Exp), reads=[B_tms], writes=[B_tms])
            P.add("dve", lambda e: e.tensor_tensor(out=t_e1[:, nbs], in0=t_e1[:, nbs], in1=t_beta[:, nbs], op=ALU.mult), reads=[B_tms], writes=[B_tms])
            P.add("act", lambda e, bg=bg: e.activation(out=glbc[:, 0:2 * nb], in_=bank[bg][:, 16:16 + 2 * nb], func=AF.Exp), reads=[bankbuf[bg]], writes=[B_glbc])

            for j in range(nb):
                b = gbank()
                pb = bank[b].bitcast(BF16)
                P.add("pe", lambda e, j=j, pb=pb: e.transpose(pb[:, 0:128], knT[:, j * 128:(j + 1) * 128], identb), reads=[B_knT, B_misc], writes=[bankbuf[b]])
                P.add("pe", lambda e, j=j, pb=pb: e.transpose(pb[:, 128:256], vsT[:, j * 128:(j + 1) * 128], identb), reads=[B_vsT, B_misc], writes=[bankbuf[b]])
                P.add("dve", lambda e, j=j, pb=pb: e.tensor_scalar(out=Xuw[:, j, 128:256], in0=pb[:, 0:128], scalar1=t_e1[:, j:j + 1], scalar2=None, op0=ALU.mult),
                      reads=[bankbuf[b], B_tms], writes=[B_Xuw])
                P.add("dve", lambda e, j=j, pb=pb: e.tensor_scalar(out=kdec[:, j, :], in0=pb[:, 0:128], scalar1=t_e2[:, j:j + 1], scalar2=None, op0=ALU.mult),
                      reads=[bankbuf[b], B_tms], writes=[B_kdec])
                P.add("dve", lambda e, j=j, pb=pb: e.tensor_scalar(out=Xuw[:, j, 0:128], in0=pb[:, 128:256], scalar1=t_beta[:, j:j + 1], scalar2=None, op0=ALU.mult),
                      reads=[bankbuf[b], B_tms], writes=[B_Xuw])

            for j in range(nb):
                q2 = j % 2
                cs = slice(j * 128, (j + 1) * 128)
                P.add("dve", lambda e, j=j, q2=q2: e.tensor_scalar(out=diag[q2], in0=ident, scalar1=t_gc[:, j:j + 1], scalar2=None, op0=ALU.mult),
                      reads=[B_cst, B_tms], writes=[B_diag[q2]])
                b = gbank()
                P.add("pe", lambda e, b=b, q2=q2: e.matmul(bank[b][:, 0:128], ones_f, diag[q2], start=True, stop=True), reads=[B_diag[q2], B_misc], writes=[bankbuf[b]])
                P.add("dve", lambda e, b=b, q2=q2: e.scalar_tensor_tensor(out=T1[q2], in0=bank[b][:, 0:128], scalar=-1.0, in1=cst[:, C_MSL:C_MSL + 128], op0=ALU.mult, op1=ALU.add),
                      reads=[bankbuf[b], B_cst], writes=[B_T1[q2]])
                P.add("act", lambda e, j=j, q2=q2: e.activation(out=Dsl[q2], in_=T1[q2], func=AF.Exp, bias=t_gc[:, j:j + 1]), reads=[B_T1[q2], B_tms], writes=[B_Dsl[q2]])
                P.add("dve", lambda e, b=b, q2=q2: e.tensor_tensor(out=T3[q2], in0=bank[b][:, 0:128], in1=cst[:, C_MIU:C_MIU + 128], op=ALU.add),
                      reads=[bankbuf[b], B_cst], writes=[B_T3[q2]])
                P.add("act", lambda e, j=j, q2=q2: e.activation(out=Diu[q2], in_=T3[q2], func=AF.Exp, bias=t_ngc[:, j:j + 1]), reads=[B_T3[q2], B_tms], writes=[B_Diu[q2]])
                P.add("act", lambda e, b=b, j=j: e.activation(out=EGR[j], in_=bank[b][:, 0:128], func=AF.Exp), reads=[bankbuf[b]], writes=[B_EGR[j]])
                b2_ = gbank()
                P.add("pe", lambda e, b2_=b2_, cs=cs: e.matmul(bank[b2_][:, 0:128], knT[:, cs], knT[:, cs], start=True, stop=True), reads=[B_knT], writes=[bankbuf[b2_]])
                P.add("pe", lambda e, b2_=b2_, cs=cs: e.matmul(bank[b2_][:, 128:256], knT[:, cs], qnT[:, cs], start=True, stop=True), reads=[B_knT, B_qnT], writes=[bankbuf[b2_]])
                P.add("dve", lambda e, b2_=b2_, j=j, q2=q2: e.scalar_tensor_tensor(out=PP[j][0][:, 0:128], in0=bank[b2_][:, 0:128], scalar=t_nbeta[:, j:j + 1], in1=Dsl[q2],
                                                                                  op0=ALU.mult, op1=ALU.mult),
                      reads=[bankbuf[b2_], B_tms, B_Dsl[q2]], writes=[B_PP[j][0]])
                P.add("dve", lambda e, b2_=b2_, j=j, q2=q2: e.tensor_tensor(out=attnT[j], in0=bank[b2_][:, 128:256], in1=Diu[q2], op=ALU.mult),
                      reads=[bankbuf[b2_], B_Diu[q2]], writes=[B_attnT[j]])
                b3 = gbank()
                pb3 = bank[b3].bitcast(BF16)
                P.add("pe", lambda e, pb3=pb3, j=j: e.transpose(pb3[:, 0:128], PP[j][0][:, 0:128], identb), reads=[B_PP[j][0], B_misc], writes=[bankbuf[b3]])
                P.add("act", lambda e, pb3=pb3, j=j: e.copy(PP[j][0][:, 128:256], pb3[:, 0:128]), reads=[bankbuf[b3]], writes=[B_PP[j][0]])
                P.add("dve", lambda e, pb3=pb3, j=j: e.tensor_tensor(out=RR[j][0], in0=pb3[:, 0:128], in1=identb, op=ALU.add), reads=[bankbuf[b3], B_misc], writes=[B_RR[j][0]])
            for m in range(1, 6):
                src, dst = (m - 1) % 2, m % 2
                for j in range(nb):
                    b = gbank()
                    pbf = bank[b]
                    P.add("pe", lambda e, b=b, j=j, src=src: e.matmul(bank[b][:, 0:128], PP[j][src][:, 128:256], PP[j][src][:, 0:128], start=True, stop=True),
                          reads=[B_PP[j][src]], writes=[bankbuf[b]])
                    if m < 5:
                        P.add("pe", lambda e, b=b, j=j, src=src: e.matmul(bank[b][:, 128:256], PP[j][src][:, 0:128], PP[j][src][:, 128:256], start=True, stop=True),
                              reads=[B_PP[j][src]], writes=[bankbuf[b]])
                    wcols = 256 if m < 5 else 128
                    P.add("act", lambda e, b=b, j=j, dst=dst, wcols=wcols: e.copy(PP[j][dst][:, 0:wcols], bank[b][:, 0:wcols]), reads=[bankbuf[b]], writes=[B_PP[j][dst]])
                    b2_ = gbank()
                    P.add("pe", lambda e, b2_=b2_, j=j, src=src, dst=dst: e.matmul(bank[b2_][:, 0:128], PP[j][dst][:, 0:128], RR[j][src], start=True, stop=True),
                          reads=[B_PP[j][dst], B_RR[j][src]], writes=[bankbuf[b2_]])
                    P.add("dve", lambda e, b2_=b2_, j=j, src=src, dst=dst: e.tensor_tensor(out=RR[j][dst], in0=bank[b2_][:, 0:128], in1=RR[j][src], op=ALU.add),
                          reads=[bankbuf[b2_], B_RR[j][src]], writes=[B_RR[j][dst]])
            RF = 1
            for j in range(nb):
                b = gbank()
                P.add("pe", lambda e, b=b, j=j: e.matmul(bank[b][:, 0:256], RR[j][RF], Xuw[:, j, :], start=True, stop=True), reads=[B_RR[j][RF], B_Xuw], writes=[bankbuf[b]])
                P.add("act", lambda e, b=b, j=j: e.copy(UW[:, j, :], bank[b][:, 0:256]), reads=[bankbuf[b]], writes=[B_UW[j]])
                for h in range(2):
                    n = 2 * j + h
                    rs_ = slice(64 * h, 64 * h + 64)
                    b2_ = gbank()
                    P.add("pe", lambda e, b2_=b2_, j=j, rs_=rs_: e.matmul(bank[b2_][:, 0:128], UW[rs_, j, 128:256], kdec[rs_, j, :], start=True, stop=True),
                          reads=[B_UW[j], B_kdec], writes=[bankbuf[b2_]])
                    P.add("pe", lambda e, b2_=b2_, j=j, rs_=rs_: e.matmul(bank[b2_][:, 128:256], kdec[rs_, j, :], UW[rs_, j, 0:128], start=True, stop=True),
                          reads=[B_UW[j], B_kdec], writes=[bankbuf[b2_]])
                    P.add("dve", lambda e, b2_=b2_, n=n: e.scalar_tensor_tensor(out=MT[n], in0=ident, scalar=glbc[:, n:n + 1], in1=bank[b2_][:, 0:128], op0=ALU.mult, op1=ALU.subtract),
                          reads=[bankbuf[b2_], B_glbc, B_cst], writes=[B_MT[n]])
                    P.add("act", lambda e, b2_=b2_, n=n: e.copy(Bn[n], bank[b2_][:, 128:256]), reads=[bankbuf[b2_]], writes=[B_Bn[n]])
                cs = slice(j * 128, (j + 1) * 128)
                P.add("dve", lambda e, j=j, cs=cs: e.tensor_tensor(out=qdecT[:, cs], in0=qnT[:, cs], in1=EGR[j], op=ALU.mult), reads=[B_qnT, B_EGR[j]], writes=[B_qdecT])
                b3 = gbank()
                P.add("pe", lambda e, b3=b3, j=j: e.matmul(bank[b3][:, 0:128], UW[:, j, 128:256], attnT[j], start=True, stop=True), reads=[B_UW[j], B_attnT[j]], writes=[bankbuf[b3]])
                P.add("dve", lambda e, b3=b3, cs=cs: e.tensor_tensor(out=QeffT[:, cs], in0=qdecT[:, cs], in1=bank[b3][:, 0:128], op=ALU.subtract),
                      reads=[bankbuf[b3], B_qdecT], writes=[B_QeffT])
            bo = 7
            for n in range(2 * nb):
                j, h = n // 2, n % 2
                rs_ = slice(64 * h, 64 * h + 64)
                si = s_cur[0]
                sn = (si + 1) % 9
                col = slice(j * 128 + 64 * h, j * 128 + 64 * h + 64)
                P.add("pe", lambda e, j=j, rs_=rs_, col=col, h=h: e.matmul(bank[bo][:, col], UW[rs_, j, 0:128], attnT[j][rs_, 64 * h:64 * h + 64], start=True, stop=False),
                      reads=[B_UW[j], B_attnT[j]], writes=[bankbuf[bo]])
                P.add("pe", lambda e, si=si, col=col: e.matmul(bank[bo][:, col], Sst[si], QeffT[:, col], start=False, stop=True),
                      reads=[B_S[si], B_QeffT], writes=[bankbuf[bo]])
                bs = gbank()
                if bs == bo:
                    bs = gbank()
                P.add("pe", lambda e, bs=bs, n=n, si=si: e.matmul(bank[bs][:, 0:128], MT[n], Sst[si], start=True, stop=True), reads=[B_MT[n], B_S[si]], writes=[bankbuf[bs]])
                P.add("dve", lambda e, bs=bs, n=n, sn=sn: e.tensor_tensor(out=Sst[sn], in0=bank[bs][:, 0:128], in1=Bn[n], op=ALU.add), reads=[bankbuf[bs], B_Bn[n]], writes=[B_S[sn]])
                s_cur[0] = sn
            P.add("act", lambda e: e.activation(out=sqo[:, 0:W], in_=bank[bo][:, 0:W], func=AF.Square), reads=[bankbuf[bo]], writes=[B_sqo])
            b = gbank()
            if b == bo:
                b = gbank()
            P.add("pe", lambda e, b=b: e.matmul(bank[b][:, 0:W], ones_f, sqo[:, 0:W], start=True, stop=True), reads=[B_sqo, B_misc], writes=[bankbuf[b]])
            P.add("dve", lambda e, b=b: e.tensor_scalar(out=rno[:, 0:W], in0=bank[b][:, 0:W], scalar1=1.0 / 128.0, scalar2=NORM_EPS, op0=ALU.mult, op1=ALU.add),
                  reads=[bankbuf[b]], writes=[B_rno])
            P.add("act", lambda e: e.activation(out=rno[:, 0:W], in_=rno[:, 0:W], func=AF.Sqrt), reads=[B_rno], writes=[B_rno])
            P.add("dve", lambda e: e.reciprocal(out=rno[:, 0:W], in_=rno[:, 0:W]), reads=[B_rno], writes=[B_rno])
            P.add("dve", lambda e: e.scalar_tensor_tensor(out=sqo[:, 0:W], in0=bank[bo][:, 0:W], scalar=pv[:, PV_GNW:PV_GNW + 1], in1=rno[:, 0:W], op0=ALU.mult, op1=ALU.mult),
                  reads=[bankbuf[bo], B_rno, B_pv, B_sqo], writes=[B_sqo])
            P.add("dve", lambda e: e.tensor_tensor(out=YB[:, 0:W], in0=sqo[:, 0:W], in1=zs[:, 0:W], op=ALU.mult), reads=[B_sqo, B_zs], writes=[B_YB])
            y_out(YB, 64, 128, c0, W, B_YB)
            gdn_ops = P.end_capture()
            return a_ops, att_ops, gdn_ops

        KNT = 999
        gmode[0] = 1
        secs = [p1_tile(ti_, c0_t, W_t) for ti_, (c0_t, W_t) in enumerate(tiles if DO_P1 else [])]
        if secs:
            P.add_merged([secs[0][0]])
        for i_ in range(len(secs)):
            lists = [secs[i_][1], secs[i_][2]]
            if i_ + 1 < len(secs):
                lists.append(secs[i_ + 1][0])
            P.add_merged(lists)
        gmode[0] = 0

        STOP = {"fused": 0, "p1": 1, "p2": 0}[mode]
        B_ag = Buf("ag")
        if mode == "fused":
          B_agin, B_agbuf, B_stage = Buf("agin"), Buf("agbuf"), Buf("ystage")
          for pc in range(16):
            P.add("sp", lambda e, pc=pc: e.dma_start(out=agin, in_=ysrc[pc * 96:(pc + 1) * 96, :]), reads=[B_ysrc], writes=[B_agin], key=B_agin)
            P.add("pool", lambda e: e.collective_compute("AllGather", ALU.bypass, replica_groups=[list(range(NCORES))], ins=[agin.opt()], outs=[agbuf.opt()]),
                  reads=[B_agin], writes=[B_agbuf], key=B_agbuf, inc=1)
            P.add("sp", lambda e, pc=pc: e.dma_start(out=agout[pc * 768:(pc + 1) * 768, :], in_=agbuf), reads=[B_agbuf], writes=[B_stage], key=B_stage)
          P.add("sp", lambda e: e.nop(), reads=[B_stage], writes=[B_ag])
        P.barrier()

        A.off = base_off
        bufA = A.f32(KC * 512).rearrange("p (k w) -> p k w", k=KC)
        bufH = A.f32(KC * 512).rearrange("p (k w) -> p k w", k=KC)
        bufH1 = A.f32(KC * 512).rearrange("p (k w) -> p k w", k=KC)
        hb = A.bf16(KC * 512).rearrange("p (k w) -> p k w", k=KC)
        h1b = hb
        yb16 = A.bf16(12 * FS).rearrange("p (k w) -> p k w", k=12)
        mixb = A.bf16(KC * 512).rearrange("p (k w) -> p k w", k=KC)
        actb = A.bf16(FC * 512).rearrange("p (k w) -> p k w", k=FC)
        ring = [A.bf16(KC * 512).rearrange("p (k w) -> p k w", k=KC) for _ in range(3)]
        B_ring = [Buf("ring%d" % i) for i in range(3)]
        dpan = A.bf16(FC * 512).rearrange("p (k w) -> p k w", k=FC)
        B_dpan = Buf("dpan")
        sq2 = actb[:, 0:8, :]
        xb2 = actb[:, 8:16, :]
        l_mean = A.f32(512); l_rstd = A.f32(512); l_t = A.f32(512); l_t2 = A.f32(512); l_t3 = A.f32(512)
        B_bufA, B_bufH, B_bufH1, B_hb, B_h1b, B_yb16, B_mixb, B_actb = [Buf(n) for n in ("bufA", "bufH", "bufH1", "hb", "h1b", "yb16", "mixb", "actb")]
        B_h1b = B_hb
        B_lmean, B_lrstd, B_lt, B_lt2, B_lt3 = [Buf(n) for n in ("lmean", "lrstd", "lt", "lt2", "lt3")]
        B_sq2 = B_actb
        B_xb2 = B_actb
        gp2 = [0]

        def gb2():
            b = gp2[0]
            gp2[0] = (gp2[0] + 1) % 8
            return b

        rp = [0]

        def load_panel(src_ap, kk, ncols):
            i = rp[0]
            rp[0] = (rp[0] + 1) % 3
            P.add("pool", lambda e: e.dma_start(out=ring[i][:, 0:kk, 0:ncols], in_=src_ap.rearrange("(k p) c -> p k c", p=128)), writes=[B_ring[i]], key=B_ring[i])
            return ring[i], B_ring[i]

        def layer_norm(src, Bsrc, dst32, Bdst32, dstb, Bdstb, gcol, bcol, W):
            P.add("act", lambda e: e.copy(xb2[:, :, 0:W], src[:, :, 0:W]), reads=[Bsrc], writes=[B_xb2])
            P.add("act", lambda e: e.activation(out=sq2[:, :, 0:W], in_=src[:, :, 0:W], func=AF.Square), reads=[Bsrc], writes=[B_sq2])
            b1, b2 = gb2(), gb2()
            for k in range(KC):
                P.add("pe", lambda e, k=k: e.matmul(bank[b1][:, 0:W], onesb, xb2[:, k, 0:W], start=(k == 0), stop=(k == KC - 1)), reads=[B_xb2, B_misc], writes=[bankbuf[b1]])
            for k in range(KC):
                P.add("pe", lambda e, k=k: e.matmul(bank[b2][:, 0:W], onesb, sq2[:, k, 0:W], start=(k == 0), stop=(k == KC - 1)), reads=[B_sq2, B_misc], writes=[bankbuf[b2]])
            P.add("act", lambda e: e.copy(l_mean[:, 0:W], bank[b1][:, 0:W]), reads=[bankbuf[b1]], writes=[B_lmean])
            P.add("dve", lambda e: e.tensor_tensor(out=l_t[:, 0:W], in0=l_mean[:, 0:W], in1=l_mean[:, 0:W], op=ALU.mult), reads=[B_lmean], writes=[B_lt])
            P.add("dve", lambda e: e.tensor_tensor(out=l_t[:, 0:W], in0=bank[b2][:, 0:W], in1=l_t[:, 0:W], op=ALU.subtract), reads=[bankbuf[b2], B_lt], writes=[B_lt])
            P.add("dve", lambda e: e.tensor_scalar(out=l_t[:, 0:W], in0=l_t[:, 0:W], scalar1=0.0, scalar2=LN_EPS, op0=ALU.max, op1=ALU.add), reads=[B_lt], writes=[B_lt])
            P.add("act", lambda e: e.activation(out=l_t[:, 0:W], in_=l_t[:, 0:W], func=AF.Sqrt), reads=[B_lt], writes=[B_lt])
            P.add("dve", lambda e: e.reciprocal(out=l_rstd[:, 0:W], in_=l_t[:, 0:W]), reads=[B_lt], writes=[B_lrstd])
            for k in range(KC):
                P.add("dve", lambda e, k=k: e.tensor_tensor(out=l_t2[:, 0:W], in0=src[:, k, 0:W], in1=l_mean[:, 0:W], op=ALU.subtract), reads=[Bsrc, B_lmean], writes=[B_lt2])
                P.add("dve", lambda e, k=k: e.tensor_tensor(out=l_t2[:, 0:W], in0=l_t2[:, 0:W], in1=l_rstd[:, 0:W], op=ALU.mult), reads=[B_lt2, B_lrstd], writes=[B_lt2])
                P.add("act", lambda e, k=k: e.activation(out=dst32[:, k, 0:W], in_=l_t2[:, 0:W], func=AF.Identity, bias=pv[:, bcol + k:bcol + k + 1], scale=pv[:, gcol + k:gcol + k + 1]),
                      reads=[B_lt2, B_pv], writes=[Bdst32])
                if dstb is not None:
                    P.add("act", lambda e, k=k: e.activation(out=dstb[:, k, 0:W], in_=l_t2[:, 0:W], func=AF.Identity, bias=pv[:, bcol + k:bcol + k + 1], scale=pv[:, gcol + k:gcol + k + 1]),
                          reads=[B_lt2, B_pv], writes=[Bdstb])

        xov = xown.rearrange("(k p) t -> p k t", p=128)
        outv = outT.rearrange("(k p) t -> p k t", p=128) if mode != "p1" else None
        B_out = Buf("out")
        pid_cache = {}

        def pid_of(e):
            return e.partition_id()

        agv5 = agout.rearrange("(s hf r f) t -> s hf r f t", s=8, hf=2, r=8)
        agv6 = agout.rearrange("(s hf q h f) t -> s hf q h f t", s=8, hf=2, q=4, h=2)

        if mode == "p2":
            for r in range(NCORES):
                P.add("sp", lambda e, r=r: e.dma_start(out=yb16[:, 4 + r, 0:FS], in_=yin[r * 192 + 64:r * 192 + 192, :]), writes=[B_yb16], key=B_yb16)
                P.add("sp", lambda e, r=r: e.dma_start(out=yb16[64 * (r % 2):64 * (r % 2) + 64, r // 2, 0:FS], in_=yin[r * 192:r * 192 + 64, :]), writes=[B_yb16], key=B_yb16)
        elif mode == "fused":
            def ldb1(e):
                pid = e.partition_id()
                src = agv5[bass.ds(pid, 1), 0, :, 64:96, 0:FS].rearrange("s r f t -> f (s r) t")
                return e.dma_start(out=yb16[0:32, 4:12, 0:FS], in_=src)
            P.add("sp", ldb1, reads=[B_ag], writes=[B_yb16], key=B_yb16)

            def ldb2(e):
                pid = e.partition_id()
                src = agv5[bass.ds(pid, 1), 1, :, 0:96, 0:FS].rearrange("s r f t -> f (s r) t")
                return e.dma_start(out=yb16[32:128, 4:12, 0:FS], in_=src)
            P.add("sp", ldb2, reads=[B_ag], writes=[B_yb16], key=B_yb16)
            for hh in range(2):
                def lda(e, hh=hh):
                    pid = e.partition_id()
                    src = agv6[bass.ds(pid, 1), 0, :, hh, 0:64, 0:FS].rearrange("s q f t -> f (s q) t")
                    return e.dma_start(out=yb16[64 * hh:64 * hh + 64, 0:4, 0:FS], in_=src)
                P.add("sp", lda, reads=[B_ag], writes=[B_yb16], key=B_yb16)


        def p2_tile(t2):
            t0 = t2 * W2
            W = W2
            P.add("sp", lambda e, t0=t0: e.dma_start(out=bufA[:, :, 0:W], in_=xov[:, :, t0:t0 + W]), writes=[B_bufA], key=B_bufA)
            layer_norm(bufA, B_bufA, bufH, B_bufH, hb, B_hb, PV_G0, PV_B0, W)
            for g4 in range(2):
                cs = slice(g4 * 512, g4 * 512 + 512)
                pga, Bga = load_panel(w2g[:, g4 * 512:g4 * 512 + 512], 8, 512)
                pa, Bpa = load_panel(woa[:, cs], 4, 512)
                for half in range(2):
                    if half == 0:
                        pg_, Bg_, pw_, Bw_, nk, yoff = pga, Bga, pa, Bpa, 4, 0
                    else:
                        pg_, Bg_ = load_panel(w2g[:, 1024 + g4 * 512:1024 + g4 * 512 + 512], 8, 512)
                        pw_, Bw_ = load_panel(wob[:, cs], 8, 512)
                        nk, yoff = 8, 4
                    for o in range(4):
                        oc = g4 * 4 + o
                        osl = slice(o * 128, o * 128 + 128)
                        bg_, bw_ = gb2(), gb2()
                        for k in range(KC):
                            P.add("pe", lambda e, k=k, bg_=bg_, pg_=pg_, osl=osl: e.matmul(bank[bg_][:, 0:W], pg_[:, k, osl], hb[:, k, 0:W], start=(k == 0), stop=(k == KC - 1)),
                                  reads=[Bg_, B_hb], writes=[bankbuf[bg_]])
                        for k in range(nk):
                            P.add("pe", lambda e, k=k, bw_=bw_, pw_=pw_, osl=osl, yoff=yoff, nk=nk: e.matmul(bank[bw_][:, 0:W], pw_[:, k, osl], yb16[:, yoff + k, t0:t0 + W], start=(k == 0), stop=(k == nk - 1)),
                                  reads=[Bw_, B_yb16], writes=[bankbuf[bw_]])
                        P.add("act", lambda e, bg_=bg_: e.activation(out=l_t[:, 0:W], in_=bank[bg_][:, 0:W], func=AF.Sigmoid), reads=[bankbuf[bg_]], writes=[B_lt])
                        if half == 0:
                            P.add("dve", lambda e, bw_=bw_, oc=oc: e.tensor_tensor(out=bufA[:, oc, 0:W], in0=bank[bw_][:, 0:W], in1=l_t[:, 0:W], op=ALU.mult),
                                  reads=[bankbuf[bw_], B_lt], writes=[B_bufA])
                        else:
                            P.add("dve", lambda e, bw_=bw_: e.tensor_tensor(out=l_t3[:, 0:W], in0=bank[bw_][:, 0:W], in1=l_t[:, 0:W], op=ALU.mult),
                                  reads=[bankbuf[bw_], B_lt], writes=[B_lt3])
                            P.add("dve", lambda e, oc=oc: e.tensor_tensor(out=mixb[:, oc, 0:W], in0=l_t3[:, 0:W], in1=bufA[:, oc, 0:W], op=ALU.add),
                                  reads=[B_lt3, B_bufA], writes=[B_mixb])
            for g4 in range(2):
                pw_, Bw_ = load_panel(wo[:, g4 * 512:g4 * 512 + 512], 8, 512)
                for o in range(4):
                    oc = g4 * 4 + o
                    osl = slice(o * 128, o * 128 + 128)
                    b = gb2()
                    for k in range(KC):
                        P.add("pe", lambda e, k=k, b=b, pw_=pw_, osl=osl: e.matmul(bank[b][:, 0:W], pw_[:, k, osl], mixb[:, k, 0:W], start=(k == 0), stop=(k == KC - 1)),
                              reads=[Bw_, B_mixb], writes=[bankbuf[b]])
                    P.add("dve", lambda e, b=b, oc=oc: e.scalar_tensor_tensor(out=bufA[:, oc, 0:W], in0=bufH[:, oc, 0:W], scalar=ALPHA, in1=bank[b][:, 0:W], op0=ALU.mult, op1=ALU.add),
                          reads=[bankbuf[b], B_bufH], writes=[B_bufA])
            layer_norm(bufA, B_bufA, bufH1, B_bufH1, h1b, B_h1b, PV_G1, PV_B1, W)
            for c0_ in range(0, DFF, 512):
                ncol = min(512, DFF - c0_)
                pg_, Bg_ = load_panel(wg[:, c0_:c0_ + ncol], 8, ncol)
                pu_, Bu_ = load_panel(wu[:, c0_:c0_ + ncol], 8, ncol)
                for o in range(ncol // 128):
                    fc = c0_ // 128 + o
                    osl = slice(o * 128, o * 128 + 128)
                    bg_, bu_ = gb2(), gb2()
                    for k in range(KC):
                        P.add("pe", lambda e, k=k, bg_=bg_, pg_=pg_, osl=osl: e.matmul(bank[bg_][:, 0:W], pg_[:, k, osl], h1b[:, k, 0:W], start=(k == 0), stop=(k == KC - 1)),
                              reads=[Bg_, B_h1b], writes=[bankbuf[bg_]])
                    for k in range(KC):
                        P.add("pe", lambda e, k=k, bu_=bu_, pu_=pu_, osl=osl: e.matmul(bank[bu_][:, 0:W], pu_[:, k, osl], h1b[:, k, 0:W], start=(k == 0), stop=(k == KC - 1)),
                              reads=[Bu_, B_h1b], writes=[bankbuf[bu_]])
                    P.add("act", lambda e, bg_=bg_: e.activation(out=l_t[:, 0:W], in_=bank[bg_][:, 0:W], func=AF.Silu), reads=[bankbuf[bg_]], writes=[B_lt])
                    P.add("dve", lambda e, bu_=bu_, fc=fc: e.tensor_tensor(out=actb[:, fc, 0:W], in0=bank[bu_][:, 0:W], in1=l_t[:, 0:W], op=ALU.mult),
                          reads=[bankbuf[bu_], B_lt], writes=[B_actb])
            for g4 in range(2):
                P.add("pool", lambda e, g4=g4: e.dma_start(out=dpan[:, :, :], in_=wd[:, g4 * 512:g4 * 512 + 512].rearrange("(k p) c -> p k c", p=128)), writes=[B_dpan], key=B_dpan)
                for o in range(4):
                    oc = g4 * 4 + o
                    osl = slice(o * 128, o * 128 + 128)
                    b = gb2()
                    for k in range(FC):
                        P.add("pe", lambda e, k=k, b=b, osl=osl: e.matmul(bank[b][:, 0:W], dpan[:, k, osl], actb[:, k, 0:W], start=(k == 0), stop=(k == FC - 1)),
                              reads=[B_dpan, B_actb], writes=[bankbuf[b]])
                    P.add("dve", lambda e, b=b, oc=oc: e.scalar_tensor_tensor(out=bufA[:, oc, 0:W], in0=bufH1[:, oc, 0:W], scalar=ALPHA, in1=bank[b][:, 0:W], op0=ALU.mult, op1=ALU.add),
                          reads=[bankbuf[b], B_bufH1], writes=[B_bufA])
            layer_norm(bufA, B_bufA, bufH, B_bufH, None, None, PV_G2, PV_B2, W)
            P.add("sp", lambda e, t0=t0: e.dma_start(out=outv[:, :, t0:t0 + W], in_=bufH[:, :, 0:W]), reads=[B_bufH], writes=[B_out], key=B_out)
        if STOP == 0:
            for t2_ in range(NT2):
                p2_tile(t2_)
        else:
            P.add("sp", lambda e: e.nop(), reads=[B_ysrc])
        P.add("sp", lambda e: e.nop(), reads=[B_out])

        P.finalize()
        engsem = {e: [es.enter_context(nc.semaphore("sem_%s_%d" % (e, i))) for i in range(P.nep[e])] for e in ENGS}
        print("ops per engine", {e: len(P.ops[e]) for e in ENGS}, "epochs", P.nep)
        keysem = {}
        for e in ENGS:
            for op in P.ops[e]:
                if op.key is not None and id(op.key) not in keysem:
                    keysem[id(op.key)] = es.enter_context(nc.semaphore("k_" + op.key.name))
        block = es.enter_context(nc.Block())

        @block.tensor
        def _(h):
            P.emit("pe", h, engsem, keysem)

        @block.scalar
        def _(h):
            P.emit("act", h, engsem, keysem)

        @block.vector
        def _(h):
            P.emit("dve", h, engsem, keysem)

        @block.gpsimd
        def _(h):
            P.emit("pool", h, engsem, keysem)

        @block.sync
        def _(h):
            P.emit("sp", h, engsem, keysem)
    return nc


def make_consts():
    c = np.zeros((128, C_N), np.float32)
    i = np.arange(128)
    same = (i[:, None] // 64) == (i[None, :] // 64)
    c[:, C_ID:C_ID + 128] = np.eye(128, dtype=np.float32)
    c[:, C_LT2:C_LT2 + 128] = (same & (i[:, None] <= i[None, :])).astype(np.float32)
    c[:, C_BLK:C_BLK + 128] = same.astype(np.float32)
    c[:, C_MSL:C_MSL + 128] = np.where(same & (i[:, None] > i[None, :]), 0.0, NEG)
    c[:, C_MIU:C_MIU + 128] = np.where(same & (i[:, None] <= i[None, :]), 0.0, NEG)
    c[:, C_SEL + 0] = (i < 64)
    c[:, C_SEL + 1] = (i >= 64)
    for r in range(3):
        c[64 + r, C_MQ + r] = 1.0
        c[67 + r, C_MQ + 3] = 1.0
        c[67 + r, C_MK + r] = -1.0
        c[64 + r, C_MK + 3] = 1.0
    return c


def shard_inputs(inp):
    x = np.asarray(inp["x"], np.float32)[0]
    S = x.shape[0]
    L = ((64 + S + 127) // 128) * 128
    FS = S // NCORES
    xT = np.zeros((D, L), np.float32)
    xT[:, 48:64] = np.asarray(inp["meta_tokens"], np.float32).T
    xT[:, 64:64 + S] = x.T
    w_in = np.asarray(inp["w_in"], np.float32)[0]
    conv_w = np.asarray(inp["conv_w"], np.float32)[0]
    cst = make_consts()
    col = lambda v: np.ascontiguousarray(np.asarray(v, np.float32).reshape(KC, 128).T)
    maps = []
    for c in range(NCORES):
        w1 = np.zeros((D, W1COLS), np.float32)
        w1[:, G_QA:G_QA + 64] = w_in[:, c * 64:(c + 1) * 64]
        w1[:, G_KA:G_KA + 64] = w_in[:, 512 + c * 64:512 + (c + 1) * 64]
        w1[:, G_V:G_V + 64] = w_in[:, 1024 + c * 64:1024 + (c + 1) * 64]
        for r in range(6):
            w1[:, G_V + 64 + r] = w_in[:, 1536 + c]
        w1[:, G_V + 70] = w_in[:, 4616 + c]
        w1[:, G_V + 71] = w_in[:, 4624 + c]
        w1[:, G_QB:G_QB + 128] = w_in[:, 1544 + c * 128:1544 + (c + 1) * 128]
        w1[:, G_KB:G_KB + 128] = w_in[:, 2568 + c * 128:2568 + (c + 1) * 128]
        w1[:, G_VB:G_VB + 128] = w_in[:, 3592 + c * 128:3592 + (c + 1) * 128]
        w1[:, G_Z:G_Z + 128] = w_in[:, 4632 + c * 128:4632 + (c + 1) * 128]
        pv = np.zeros((128, PV_N), np.float32)
        pv[:, PV_G0:PV_G0 + 8] = col(inp["ln_in_g"])
        pv[:, PV_B0:PV_B0 + 8] = col(inp["ln_in_b"])
        for t, base in enumerate((0, 1024, 2048)):
            pv[:, PV_CW + 4 * t:PV_CW + 4 * t + 4] = conv_w[:, base + c * 128:base + (c + 1) * 128].T
        pv[:, PV_BF] = np.asarray(inp["b_f"], np.float32)[0, c]
        pv[:, PV_ALOG] = np.asarray(inp["a_log"], np.float32)[0, c]
        pv[:, PV_DT] = np.asarray(inp["dt_bias"], np.float32)[0, c]
        pv[:, PV_GNW] = np.asarray(inp["gdn_norm_w"], np.float32)[0]
        pv[:, PV_G1:PV_G1 + 8] = col(inp["ln1_g"])
        pv[:, PV_B1:PV_B1 + 8] = col(inp["ln1_b"])
        pv[:, PV_G2:PV_G2 + 8] = col(inp["ln2_g"])
        pv[:, PV_B2:PV_B2 + 8] = col(inp["ln2_b"])
        maps.append({
            "xT": xT, "xown": np.ascontiguousarray(xT[:, 64 + c * FS:64 + (c + 1) * FS]), "w1": w1, "pv": pv, "cst": cst,
            "w2g": np.ascontiguousarray(w_in[:, 5656:7704]),
            "woa": np.asarray(inp["w_out_a"], np.float32)[0], "wob": np.asarray(inp["w_out_b"], np.float32)[0],
            "wo": np.asarray(inp["w_o"], np.float32)[0], "wg": np.asarray(inp["w_gate"], np.float32)[0],
            "wu": np.asarray(inp["w_up"], np.float32)[0], "wd": np.asarray(inp["w_down"], np.float32)[0],
        })
    return maps, S


P1_KEYS = ("xT", "w1", "pv", "cst")
P2_KEYS = ("xown", "pv", "cst", "w2g", "woa", "wob", "wo", "wg", "wu", "wd")


def kernel_fused(**inputs):
    maps, S = shard_inputs(inputs)
    nc = build(S, "fused")
    res = run_bass_kernel_spmd(nc, maps, core_ids=list(range(NCORES)))
    out = np.concatenate([np.asarray(res.results[c]["outT"], np.float32).T for c in range(NCORES)], axis=0)
    return out[None].astype(np.float32)


def kernel(**inputs):
    maps, S = shard_inputs(inputs)
    FS = S // NCORES
    nc1 = build(S, "p1")
    r1 = run_bass_kernel_spmd(nc1, maps, core_ids=list(range(NCORES)))
    ys = [np.asarray(r1.results[c]["ysrc"]).reshape(NCORES, 192, FS) for c in range(NCORES)]
    maps2 = []
    for c in range(NCORES):
        m2 = dict(maps[c])
        m2["yin"] = np.ascontiguousarray(np.concatenate([ys[r][c] for r in range(NCORES)], axis=0))
        maps2.append(m2)
    nc2 = build(S, "p2")
    res = run_bass_kernel_spmd(nc2, maps2, core_ids=list(range(NCORES)))
    out = np.concatenate([np.asarray(res.results[c]["outT"], np.float32).T for c in range(NCORES)], axis=0)
    return out[None].astype(np.float32)
```

```python
import contextlib
import numpy as np
import concourse.bass as bass
import concourse.mybir as mybir
from concourse.bass_utils import run_bass_kernel_spmd

F32 = mybir.dt.float32
BF16 = mybir.dt.bfloat16
ALU = mybir.AluOpType
AF = mybir.ActivationFunctionType

NCORES = 8
D = 1024
KC = 8
DFF = 2816
FC = 22
ALPHA = 2.0 ** 0.25
LN_EPS = 1e-5
NORM_EPS = 1e-6
NEG = -30000.0
G_QA, G_KA, G_V, G_QB, G_KB, G_VB, G_Z = 0, 64, 128, 200, 328, 456, 584
W1COLS = 712
GROUPS = [(G_QA, 64), (G_KA, 64), (G_V, 72), (G_QB, 128), (G_KB, 128), (G_VB, 128), (G_Z, 128)]
PV_G0, PV_B0 = 0, 8
PV_CW = 16
PV_BF, PV_ALOG, PV_DT, PV_GNW = 28, 29, 30, 31
PV_G1, PV_B1, PV_G2, PV_B2 = 32, 40, 48, 56
PV_N = 64
C_ID, C_LT2, C_BLK, C_MSL, C_MIU, C_SEL, C_MQ, C_MK = 0, 128, 256, 384, 512, 640, 642, 646
C_N = 650

ENGS = ("pe", "act", "dve", "pool", "sp")
EPOCH = 16000


class Buf:
    __slots__ = ("w", "r", "name", "excl")

    def __init__(self, name="", excl=False):
        self.w = None
        self.r = []
        self.name = name
        self.excl = excl


class Op:
    __slots__ = ("eng", "fn", "deps", "sig", "val", "key", "inc", "ep")


class Prog:
    def __init__(self):
        self.ops = {e: [] for e in ENGS}
        self.pending = {e: [] for e in ENGS}
        self.dmas = {}
        self.cap = None

    def begin_capture(self):
        self.cap = []

    def end_capture(self):
        c, self.cap = self.cap, None
        return c

    def add_merged(self, lists):
        idx = [0] * len(lists)
        tot = [max(1, len(l)) for l in lists]
        while True:
            best, bf = -1, 2.0
            for i, l in enumerate(lists):
                if idx[i] < len(l):
                    f = idx[i] / tot[i]
                    if f < bf:
                        best, bf = i, f
            if best < 0:
                break
            self.add(*lists[best][idx[best]])
            idx[best] += 1

    def add(self, eng, fn, reads=(), writes=(), key=None, inc=16):
        if self.cap is not None:
            self.cap.append((eng, fn, reads, writes, key, inc))
            return None
        op = Op()
        op.eng, op.fn, op.sig, op.val, op.key, op.inc = eng, fn, False, 0, key, inc
        deps = list(self.pending[eng])
        self.pending[eng] = []
        ex = [b for b in reads if b.excl]
        if ex:
            writes = list(writes) + ex
            reads = [b for b in reads if not b.excl]
        raw = set()
        for b in reads:
            if b.w is not None:
                deps.append(b.w)
                raw.add(id(b.w))
        for b in writes:
            if b.w is not None:
                deps.append(b.w)
                if b.excl:
                    raw.add(id(b.w))
            deps.extend(b.r)
        need, seen = [], set()
        for d in deps:
            if id(d) in seen:
                continue
            seen.add(id(d))
            if d.key is None and key is None and d.eng == eng and (eng == "pe" or id(d) not in raw):
                continue
            if d.key is None:
                d.sig = True
            need.append(d)
        op.deps = need
        for b in reads:
            b.r.append(op)
        for b in writes:
            b.w = op
            b.r = []
        self.ops[eng].append(op)
        if key is not None:
            self.dmas[id(key)] = op
        return op

    def barrier(self):
        last = []
        for e in ENGS:
            if self.ops[e]:
                o = self.ops[e][-1]
                if o.key is None:
                    o.sig = True
                last.append(o)
        last.extend(self.dmas.values())
        for e in ENGS:
            self.pending[e] = list(last)

    def finalize(self):
        keycnt = {}
        for e in ENGS:
            cnt = 0
            for op in self.ops[e]:
                if op.key is not None:
                    k = id(op.key)
                    keycnt[k] = keycnt.get(k, 0) + op.inc
                    op.val = keycnt[k]
                elif op.sig:
                    cnt += 1
                    op.ep = (cnt - 1) // EPOCH
                    op.val = (cnt - 1) % EPOCH + 1
            self.nep = getattr(self, "nep", {})
            self.nep[e] = (cnt - 1) // EPOCH + 1 if cnt else 1

    def emit(self, eng, h, engsem, keysem):
        waited = {}
        for op in self.ops[eng]:
            for d in op.deps:
                s = keysem[id(d.key)] if d.key is not None else engsem[d.eng][d.ep]
                sid = id(s)
                if waited.get(sid, 0) < d.val:
                    h.wait_ge(s, d.val)
                    waited[sid] = d.val
            ins = op.fn(h)
            if op.key is not None:
                ins.then_inc(keysem[id(op.key)], op.inc)
            elif op.sig:
                ins.then_inc(engsem[eng][op.ep], 1)


def build(S, mode="fused"):
    L = ((64 + S + 127) // 128) * 128
    NBLK = L // 128
    tiles = []
    p = 0
    while p < L:
        w = min(512, L - p)
        tiles.append((p, w))
        p += w
    FS = S // NCORES
    W2 = min(512, FS)
    NT2 = FS // W2

    nc = bass.Bass("TRN2", target_bir_lowering=False)
    dt_in = lambda n, shp: nc.dram_tensor(n, shp, F32, kind="ExternalInput").ap()
    xT = dt_in("xT", [D, L])
    xown = dt_in("xown", [D, FS])
    w1 = dt_in("w1", [D, W1COLS])
    pv_d = dt_in("pv", [128, PV_N])
    cst_d = dt_in("cst", [128, C_N])
    w2g = dt_in("w2g", [D, 2048])
    woa = dt_in("woa", [512, D])
    wob = dt_in("wob", [D, D])
    wo = dt_in("wo", [D, D])
    wg = dt_in("wg", [D, DFF])
    wu = dt_in("wu", [D, DFF])
    wd = dt_in("wd", [DFF, D])
    if mode != "p1":
        outT = nc.dram_tensor("outT", [D, FS], F32, kind="ExternalOutput").ap()
    if mode == "p1":
        ysrc = nc.dram_tensor("ysrc", [NCORES * 192, FS], BF16, kind="ExternalOutput").ap()
    else:
        ysrc = nc.dram_tensor("ysrc", [NCORES * 192, FS], BF16).ap()
    DBG = 0
    if mode == "p2":
        yin = nc.dram_tensor("yin", [NCORES * 192, FS], BF16, kind="ExternalInput").ap()
    agout = nc.dram_tensor("ystage", [16 * NCORES * 96, FS], BF16).ap()
    agin = nc.dram_tensor("agin", [96, FS], BF16).ap()
    agbuf = nc.dram_tensor("agbuf", [NCORES * 96, FS], BF16).ap()

    P = Prog()
    es = contextlib.ExitStack()
    with es:
        ARENA_F = 50 * 1024
        arena = es.enter_context(nc.sbuf_tensor("arena", [128, ARENA_F], F32))
        psum = es.enter_context(nc.psum_tensor("ps", [128, 4096], F32))
        bank = [psum[:, b * 512:(b + 1) * 512] for b in range(8)]
        bankbuf = [Buf("bank%d" % b, excl=True) for b in range(8)]

        class Arena:
            def __init__(self):
                self.off = 0

            def f32(self, n):
                a = arena[:, self.off:self.off + n]
                self.off += n
                assert self.off <= ARENA_F, "SBUF arena overflow %d" % self.off
                return a

            def bf16(self, n):
                m = (n + 1) // 2
                a = arena[:, self.off:self.off + m].bitcast(BF16)
                self.off += m
                assert self.off <= ARENA_F, "SBUF arena overflow %d" % self.off
                return a[:, 0:n]

        A = Arena()
        cst = A.f32(C_N)
        pv = A.f32(PV_N)
        ones_f = A.f32(128)
        identb = A.bf16(128)
        onesb = A.bf16(128)
        B_cst, B_pv, B_misc = Buf("cst"), Buf("pv"), Buf("misc")
        ident = cst[:, C_ID:C_ID + 128]
        P.add("sp", lambda e: e.dma_start(out=cst, in_=cst_d), writes=[B_cst], key=B_cst)
        P.add("sp", lambda e: e.dma_start(out=pv, in_=pv_d), writes=[B_pv], key=B_pv)
        P.add("pool", lambda e: e.memset(ones_f, 1.0), writes=[B_misc])
        P.add("pool", lambda e: e.memset(onesb, 1.0 / 1024.0), writes=[B_misc])
        P.add("pool", lambda e: e.tensor_copy(out=identb, in_=ident), reads=[B_cst], writes=[B_misc])
        CONSTS = [B_cst, B_pv, B_misc]
        base_off = A.off

        gp = [5]
        gmode = [0]

        def gbank():
            if gmode[0] == 1:
                gp[0] = 6 if gp[0] == 5 else 5
                return gp[0]
            if gmode[0] == 2:
                return 4
            b = gp[0]
            gp[0] = 5 + (gp[0] - 5 + 1) % 3
            return b

        gpa = [3]

        DO_P1 = mode != "p2"
        Wc = A.bf16(KC * W1COLS).rearrange("p (k c) -> p k c", k=KC)
        B_Wc = Buf("Wc")
        bW = A.f32(8)
        prep_off = A.off
        csum = A.f32(W1COLS)
        stg = [A.f32(W1COLS), A.f32(W1COLS)]
        B_stg = [Buf("stg0"), Buf("stg1")]
        wgt = A.f32(W1COLS)
        B_wgt = Buf("wgt")
        B_bW, B_csum = Buf("bW"), Buf("csum")
        w1v = w1.rearrange("(k p) c -> p k c", p=128)
        psc = [bank[5], bank[6]]
        for k in range(KC if DO_P1 else 0):
            s = stg[k % 2]
            P.add("sp", lambda e, s=s, k=k: e.dma_start(out=s, in_=w1v[:, k, :]), writes=[B_stg[k % 2]], key=B_stg[k % 2])
            P.add("dve", lambda e, s=s, k=k: e.tensor_scalar(out=wgt, in0=s, scalar1=pv[:, PV_G0 + k:PV_G0 + k + 1], scalar2=None, op0=ALU.mult),
                  reads=[B_stg[k % 2], B_pv], writes=[B_wgt])
            P.add("pe", lambda e, k=k: e.matmul(psc[0][:, 0:512], ones_f, wgt[:, 0:512], start=(k == 0), stop=(k == KC - 1)),
                  reads=[B_wgt, B_misc], writes=[bankbuf[5]])
            P.add("pe", lambda e, k=k: e.matmul(psc[1][:, 0:W1COLS - 512], ones_f, wgt[:, 512:W1COLS], start=(k == 0), stop=(k == KC - 1)),
                  reads=[B_wgt, B_misc], writes=[bankbuf[6]])
            order = [3, 0, 1, 2, 4, 5, 6]
            for oi, gi in enumerate(order):
                go, gm = GROUPS[gi]
                P.add("pe", lambda e, s=s, k=k, gi=gi, go=go, gm=gm, oi=oi: e.matmul(bank[7][0:gm, gi:gi + 1], s[:, go:go + gm], pv[:, PV_B0 + k:PV_B0 + k + 1],
                                                                                    start=(k == 0 and oi == 0), stop=(k == KC - 1 and oi == 6), skip_group_check=True),
                      reads=[B_stg[k % 2], B_pv], writes=[bankbuf[7]])
        if DO_P1:
            P.add("act", lambda e: e.mul(csum[:, 0:512], psc[0][:, 0:512], 1.0 / 1024.0), reads=[bankbuf[5]], writes=[B_csum])
            P.add("act", lambda e: e.mul(csum[:, 512:W1COLS], psc[1][:, 0:W1COLS - 512], 1.0 / 1024.0), reads=[bankbuf[6]], writes=[B_csum])
            P.add("dve", lambda e: e.tensor_copy(out=bW[:, 0:7], in_=bank[7][:, 0:7]), reads=[bankbuf[7]], writes=[B_bW])
        for k in range(KC if DO_P1 else 0):
            s = stg[k % 2]
            P.add("sp", lambda e, s=s, k=k: e.dma_start(out=s, in_=w1v[:, k, :]), writes=[B_stg[k % 2]], key=B_stg[k % 2])
            P.add("dve", lambda e, s=s, k=k: e.scalar_tensor_tensor(out=Wc[:, k, :], in0=s, scalar=pv[:, PV_G0 + k:PV_G0 + k + 1], in1=csum,
                                                                    op0=ALU.mult, op1=ALU.subtract),
                  reads=[B_stg[k % 2], B_pv, B_csum], writes=[B_Wc])
        P.barrier()
        A.off = prep_off
        sc1 = A.f32(8)
        B_sc1 = Buf("sc1")
        if DO_P1:
          P.add("act", lambda e: e.activation(out=sc1[:, 0:1], in_=pv[:, PV_ALOG:PV_ALOG + 1], func=AF.Exp), reads=[B_pv], writes=[B_sc1])
          P.add("dve", lambda e: e.tensor_scalar(out=sc1[:, 0:1], in0=sc1[:, 0:1], scalar1=-1.0, scalar2=None, op0=ALU.mult), reads=[B_sc1], writes=[B_sc1])
          P.add("dve", lambda e: e.tensor_scalar(out=sc1[:, 1:2], in0=bW[:, 0:1], scalar1=0.125, scalar2=None, op0=ALU.mult), reads=[B_bW], writes=[B_sc1])

        Kaug = A.bf16(L)
        Vc = A.bf16(NBLK * 65).rearrange("p (b c) -> p b c", c=65)
        B_K = [Buf("K%d" % i) for i in range(len(tiles))]
        B_V = [Buf("V%d" % i) for i in range(len(tiles))]
        B_Vinit = Buf("Vinit")
        if DO_P1:
            P.add("pool", lambda e: e.memset(Vc[:, :, 64:65], 1.0), writes=[B_Vinit])

        xb = [A.bf16(KC * 512).rearrange("p (k w) -> p k w", k=KC) for _ in range(2)]
        B_xb = [Buf("xb0"), Buf("xb1")]
        sq = A.bf16(KC * 512).rearrange("p (k w) -> p k w", k=KC)
        B_sq = Buf("sq")

        def T32(name):
            return A.f32(512), Buf(name)

        def T16(name):
            return A.bf16(512), Buf(name)

        mean_sb, B_mean = T32("mean")
        rstd, B_rstd = T32("rstd")
        tmpA, B_tmpA = T32("tmpA")
        tmpB, B_tmpB = T32("tmpB")
        Qaug_l = [A.bf16(512), A.bf16(512)]
        B_Q_l = [Buf("Qaug0"), Buf("Qaug1")]
        VG, B_VG = T32("VG")
        X_l = [[A.f32(516) for _ in range(3)]] * 2
        B_X_l = [[Buf("X_%d" % t) for t in range(3)]] * 2
        zs_l = [A.f32(512), A.f32(512)]
        B_zs_l = [Buf("zs0"), Buf("zs1")]
        ctile, B_c = T32("c")
        lrow, B_lrow = T32("lrow")
        lrow2, B_lrow2 = T32("lrow2")
        hiB, B_hi = T16("hi"); loB, B_lo = T16("lo"); lo2B, B_lo2 = T16("lo2")
        r1, B_r1 = T32("r1")
        accr, B_accr = T32("accr")
        onesrow, B_onesrow = T32("onesrow")
        bigm, B_bigm = T32("bigm")
        carry = A.f32(2)
        B_carry = Buf("carry")
        PT = [A.bf16(1024).rearrange("p (j w) -> p j w", j=2) for _ in range(2)]
        B_PT = [Buf("PT0"), Buf("PT1")]
        B_PT4 = [Buf("PT4_%d" % i) for i in range(4)]
        O_sb, B_Osb = T32("Osb")
        rden, B_rden = T32("rden")
        YA, B_YA = T16("YA")
        YB, B_YB = T16("YB")
        halo = [A.f32(12) for _ in range(3)]
        B_halo = [Buf("halo%d" % i) for i in range(3)]
        sctm_l = [A.f32(32).rearrange("p (j c) -> p j c", c=8) for _ in range(2)]
        B_sctm_l = [Buf("sctm0"), Buf("sctm1")]
        cq, B_cq = T32("cq"); ck, B_ck = T32("ck"); cv, B_cv = T32("cv")
        qs, B_qs = cq, B_cq
        ks, B_ks = ck, B_ck
        sqq, B_sqq = T32("sqq"); sqk, B_sqk = sqq, B_sqq
        rnq, B_rnq = T32("rnq"); rnk, B_rnk = rnq, B_rnq
        qnT_l = [T16("qnT0"), T16("qnT1")]; knT_l = [T16("knT0"), T16("knT1")]; vsT_l = [T16("vsT0"), T16("vsT1")]
        tms_l = [A.f32(64), A.f32(64)]
        B_tms_l = [Buf("tms0"), Buf("tms1")]
        rhs8_l = [A.f32(8), A.f32(8)]
        glbc_l = [A.f32(8), A.f32(8)]
        B_glbc_l = [Buf("glbc0"), Buf("glbc1")]
        Xuw_l = [A.bf16(4 * 256).rearrange("p (j c) -> p j c", c=256) for _ in range(2)]
        B_Xuw_l = [Buf("Xuw0"), Buf("Xuw1")]
        kdec_l = [A.bf16(4 * 128).rearrange("p (j c) -> p j c", c=128) for _ in range(2)]
        B_kdec_l = [Buf("kdec0"), Buf("kdec1")]
        UW = A.bf16(4 * 256).rearrange("p (j c) -> p j c", c=256)
        B_UW = [Buf("UW%d" % j) for j in range(4)]
        diag = [A.f32(128) for _ in range(2)]
        B_diag = [Buf("diag0"), Buf("diag1")]
        T1 = [A.f32(128) for _ in range(2)]
        B_T1 = [Buf("T1a"), Buf("T1b")]
        T3 = [A.f32(128) for _ in range(2)]
        B_T3 = [Buf("T3a"), Buf("T3b")]
        Dsl = [A.f32(128) for _ in range(2)]
        B_Dsl = [Buf("Dsl0"), Buf("Dsl1")]
        Diu = [A.f32(128) for _ in range(2)]
        B_Diu = [Buf("Diu0"), Buf("Diu1")]
        EGR = [A.f32(128) for _ in range(4)]
        B_EGR = [Buf("EGR%d" % j) for j in range(4)]
        attnT = [A.bf16(128) for _ in range(4)]
        B_attnT = [Buf("attnT%d" % j) for j in range(4)]
        PP = [[A.bf16(256) for _ in range(2)] for _ in range(4)]
        B_PP = [[Buf("PP%d_%d" % (j, q)) for q in range(2)] for j in range(4)]
        RR = [[A.bf16(128) for _ in range(2)] for _ in range(4)]
        B_RR = [[Buf("RR%d_%d" % (j, q)) for q in range(2)] for j in range(4)]
        qdecT, B_qdecT = T32("qdecT")
        QeffT, B_QeffT = T32("QeffT")
        MT = [A.f32(128) for _ in range(8)]
        B_MT = [Buf("MT%d" % j) for j in range(8)]
        Bn = [A.f32(128) for _ in range(8)]
        B_Bn = [Buf("Bn%d" % j) for j in range(8)]
        Sst = [A.f32(128) for _ in range(9)]
        B_S = [Buf("S%d" % j) for j in range(9)]
        sqo, B_sqo = T32("sqo")
        rno, B_rno = T32("rno")
        p1_end = A.off

        if DO_P1:
            P.add("pool", lambda e: e.memset(Sst[0], 0.0), writes=[B_S[0]])
            P.add("pool", lambda e: e.memset(onesrow, 1.0), writes=[B_onesrow])
            P.add("pool", lambda e: e.memset(bigm, 0.0), writes=[B_bigm])
            P.add("pool", lambda e: e.memset(bigm[64:70, 0:48], 30000.0), writes=[B_bigm])
            for t_ in range(3):
                P.add("pool", lambda e, t_=t_: e.memset(X_l[0][t_][:, 0:3], 0.0), writes=[B_X_l[0][t_]])

        xTv = xT.rearrange("(k p) l -> p k l", p=128)
        ysv = ysrc.rearrange("(s f) t -> s f t", f=192)
        B_ysrc = Buf("ysrc")
        RS = slice(64, 70)
        cwq = lambda j: pv[:, PV_CW + j:PV_CW + j + 1]
        s_cur = [0]

        def y_out(src, rows0, nrows, c0, W, Bsrc):
            p = max(c0, 64)
            end = min(c0 + W, 64 + S)
            while p < end:
                f = p - 64
                sh = f // FS
                fe = min(end - 64, (sh + 1) * FS)
                n = fe - f
                P.add("sp", lambda e, sh=sh, f=f, n=n, p=p: e.dma_start(out=ysv[sh, rows0:rows0 + nrows, f - sh * FS:f - sh * FS + n],
                                                                        in_=src[0:nrows, p - c0:p - c0 + n]),
                      reads=[Bsrc], writes=[B_ysrc], key=B_ysrc)
                p += n

        STAGE = 99

        def p1_tile(ti, c0, W):
            nb = W // 128
            blk0 = c0 // 128
            par = ti % 2
            xbt = xb[par]
            Qaug, B_Q = Qaug_l[par], B_Q_l[par]
            zs, B_zs = zs_l[par], B_zs_l[par]
            (Xq, Xk, Xv), (B_Xq, B_Xk, B_Xv) = X_l[par], B_X_l[par]
            sctm, B_sctm = sctm_l[par], B_sctm_l[par]
            (qnT, B_qnT), (knT, B_knT), (vsT, B_vsT) = qnT_l[par], knT_l[par], vsT_l[par]
            tms, B_tms = tms_l[par], B_tms_l[par]
            t_beta, t_nbeta, t_x, t_ax, t_e, t_l, t_g, t_gc, t_ngc, t_e1, t_e2, t_d = [tms[:, 4 * i_:4 * i_ + 4] for i_ in range(12)]
            rhs8, glbc, B_glbc = rhs8_l[par], glbc_l[par], B_glbc_l[par]
            Xuw, B_Xuw, kdec, B_kdec = Xuw_l[par], B_Xuw_l[par], kdec_l[par], B_kdec_l[par]
            gmode[0] = 2
            P.begin_capture()
            P.add("pool", lambda e, xbt=xbt, c0=c0, W=W: e.dma_start(out=xbt[:, :, 0:W], in_=xTv[:, :, c0:c0 + W]), writes=[B_xb[par]], key=B_xb[par])
            P.add("pool", lambda e, xbt=xbt, W=W: e.tensor_tensor(out=sq[:, :, 0:W], in0=xbt[:, :, 0:W], in1=xbt[:, :, 0:W], op=ALU.mult),
                  reads=[B_xb[par]], writes=[B_sq])
            b1, b2 = gbank(), gbank()
            for k in range(KC):
                P.add("pe", lambda e, k=k, b1=b1, xbt=xbt, W=W: e.matmul(bank[b1][:, 0:W], onesb, xbt[:, k, 0:W], start=(k == 0), stop=(k == KC - 1)),
                      reads=[B_xb[par], B_misc], writes=[bankbuf[b1]])
            P.add("act", lambda e, b1=b1, W=W: e.copy(mean_sb[:, 0:W], bank[b1][:, 0:W]), reads=[bankbuf[b1]], writes=[B_mean])
            for k in range(KC):
                P.add("pe", lambda e, k=k, b2=b2, W=W: e.matmul(bank[b2][:, 0:W], onesb, sq[:, k, 0:W], start=(k == 0), stop=(k == KC - 1)),
                      reads=[B_sq, B_misc], writes=[bankbuf[b2]])
            P.add("dve", lambda e, W=W: e.tensor_tensor(out=tmpA[:, 0:W], in0=mean_sb[:, 0:W], in1=mean_sb[:, 0:W], op=ALU.mult), reads=[B_mean], writes=[B_tmpA])
            P.add("dve", lambda e, b2=b2, W=W: e.tensor_tensor(out=tmpA[:, 0:W], in0=bank[b2][:, 0:W], in1=tmpA[:, 0:W], op=ALU.subtract),
                  reads=[bankbuf[b2], B_tmpA], writes=[B_tmpA])
            P.add("dve", lambda e, W=W: e.tensor_scalar(out=tmpA[:, 0:W], in0=tmpA[:, 0:W], scalar1=0.0, scalar2=LN_EPS, op0=ALU.max, op1=ALU.add),
                  reads=[B_tmpA], writes=[B_tmpA])
            P.add("act", lambda e, W=W: e.activation(out=tmpA[:, 0:W], in_=tmpA[:, 0:W], func=AF.Sqrt), reads=[B_tmpA], writes=[B_tmpA])
            P.add("dve", lambda e, W=W: e.reciprocal(out=rstd[:, 0:W], in_=tmpA[:, 0:W]), reads=[B_tmpA], writes=[B_rstd])

            def proj(go, gm):
                b = gbank()
                for k in range(KC):
                    P.add("pe", lambda e, k=k, b=b: e.matmul(bank[b][0:gm, 0:W], Wc[:, k, go:go + gm], xbt[:, k, 0:W], start=(k == 0), stop=(k == KC - 1)),
                          reads=[B_Wc, B_xb[par]], writes=[bankbuf[b]])
                return b

            def evac(b, gm, gi, out_ap, Bout, func=AF.Identity, scale=1.0, bias_ap=None, tmp=None, Btmp=None):
                tmp_, Bt = (tmpB, B_tmpB) if tmp is None else (tmp, Btmp)
                P.add("dve", lambda e: e.tensor_tensor(out=tmp_[0:gm, 0:W], in0=bank[b][0:gm, 0:W], in1=rstd[0:gm, 0:W], op=ALU.mult),
                      reads=[bankbuf[b], B_rstd], writes=[Bt])
                bia = bW[0:gm, gi:gi + 1] if bias_ap is None else bias_ap
                P.add("act", lambda e: e.activation(out=out_ap, in_=tmp_[0:gm, 0:W], func=func, bias=bia, scale=scale),
                      reads=[Bt, B_bW, B_sc1], writes=[Bout])

            b = proj(G_QA, 64)
            evac(b, 64, 0, Qaug[0:64, 0:W], B_Q, scale=0.125, bias_ap=sc1[0:64, 1:2])
            b = proj(G_KA, 64)
            evac(b, 64, 1, Kaug[0:64, c0:c0 + W], B_K[ti])
            b = proj(G_V, 72)
            evac(b, 72, 2, VG[0:72, 0:W], B_VG)
            b = proj(G_QB, 128)
            evac(b, 128, 3, Xq[:, 3:3 + W], B_Xq)
            b = proj(G_KB, 128)
            evac(b, 128, 4, Xk[:, 3:3 + W], B_Xk)
            b = proj(G_VB, 128)
            evac(b, 128, 5, Xv[:, 3:3 + W], B_Xv)
            b = proj(G_Z, 128)
            evac(b, 128, 6, zs[:, 0:W], B_zs, func=AF.Silu)
            if ti == 0:
                for X_, B_ in ((Xq, B_Xq), (Xk, B_Xk), (Xv, B_Xv)):
                    P.add("pool", lambda e, X_=X_: e.memset(X_[:, 3:3 + 48], 0.0), writes=[B_])

            for j in range(nb):
                b = gbank()
                P.add("pe", lambda e, j=j, b=b: e.transpose(bank[b][:, 0:72], VG[0:72, j * 128:(j + 1) * 128], ident[0:72, 0:72]),
                      reads=[B_VG, B_cst], writes=[bankbuf[b]])
                P.add("act", lambda e, j=j, b=b: e.copy(Vc[:, blk0 + j, 0:64], bank[b][:, 0:64]), reads=[bankbuf[b], B_Vinit], writes=[B_V[ti]])
                P.add("dve", lambda e, j=j, b=b: e.tensor_copy(out=sctm[:, j, :], in_=bank[b][:, 64:72]), reads=[bankbuf[b]], writes=[B_sctm])

            P.add("dve", lambda e: e.tensor_scalar(out=lrow[RS, 0:W], in0=VG[RS, 0:W], scalar1=pv[RS, PV_BF:PV_BF + 1], scalar2=-1.0, op0=ALU.add, op1=ALU.mult),
                  reads=[B_VG, B_pv], writes=[B_lrow])
            P.add("dve", lambda e: e.tensor_scalar(out=lrow2[RS, 0:W], in0=lrow[RS, 0:W], scalar1=-1.0, scalar2=None, op0=ALU.mult), reads=[B_lrow], writes=[B_lrow2])
            P.add("dve", lambda e: e.tensor_tensor(out=lrow2[RS, 0:W], in0=lrow2[RS, 0:W], in1=lrow[RS, 0:W], op=ALU.max), reads=[B_lrow, B_lrow2], writes=[B_lrow2])
            P.add("act", lambda e: e.activation(out=lrow2[RS, 0:W], in_=lrow2[RS, 0:W], func=AF.Exp, scale=-1.0), reads=[B_lrow2], writes=[B_lrow2])
            P.add("act", lambda e: e.activation(out=lrow2[RS, 0:W], in_=lrow2[RS, 0:W], func=AF.Ln, bias=1.0), reads=[B_lrow2], writes=[B_lrow2])
            P.add("dve", lambda e: e.scalar_tensor_tensor(out=lrow[RS, 0:W], in0=lrow[RS, 0:W], scalar=0.0, in1=lrow2[RS, 0:W], op0=ALU.max, op1=ALU.add),
                  reads=[B_lrow, B_lrow2], writes=[B_lrow])
            init = 0.0 if ti == 0 else carry[RS, 0:1]
            P.add("dve", lambda e, init=init: e.tensor_tensor_scan(out=ctile[RS, 0:W], data0=onesrow[RS, 0:W], data1=lrow[RS, 0:W], initial=init,
                                                                  op0=ALU.mult, op1=ALU.subtract),
                  reads=[B_lrow, B_onesrow, B_carry], writes=[B_c])
            P.add("dve", lambda e: e.tensor_copy(out=carry[RS, 0:1], in_=ctile[RS, W - 1:W]), reads=[B_c], writes=[B_carry])

            def split3(src, Bsrc):
                P.add("dve", lambda e: e.tensor_copy(out=hiB[RS, 0:W], in_=src[RS, 0:W]), reads=[Bsrc], writes=[B_hi])
                P.add("dve", lambda e: e.tensor_tensor(out=r1[RS, 0:W], in0=src[RS, 0:W], in1=hiB[RS, 0:W], op=ALU.subtract), reads=[Bsrc, B_hi], writes=[B_r1])
                P.add("dve", lambda e: e.tensor_copy(out=loB[RS, 0:W], in_=r1[RS, 0:W]), reads=[B_r1], writes=[B_lo])
                P.add("dve", lambda e: e.tensor_tensor(out=r1[RS, 0:W], in0=r1[RS, 0:W], in1=loB[RS, 0:W], op=ALU.subtract), reads=[B_r1, B_lo], writes=[B_r1])
                P.add("dve", lambda e: e.tensor_copy(out=lo2B[RS, 0:W], in_=r1[RS, 0:W]), reads=[B_r1], writes=[B_lo2])

            def augrows(mc, out_ap, Bout):
                m = lambda i: cst[RS, mc + i:mc + i + 1]
                P.add("dve", lambda e: e.tensor_scalar(out=accr[RS, 0:W], in0=hiB[RS, 0:W], scalar1=m(0), scalar2=m(3), op0=ALU.mult, op1=ALU.add),
                      reads=[B_hi, B_cst], writes=[B_accr])
                P.add("dve", lambda e: e.scalar_tensor_tensor(out=accr[RS, 0:W], in0=loB[RS, 0:W], scalar=m(1), in1=accr[RS, 0:W], op0=ALU.mult, op1=ALU.add),
                      reads=[B_lo, B_accr, B_cst], writes=[B_accr])
                P.add("dve", lambda e: e.scalar_tensor_tensor(out=out_ap, in0=lo2B[RS, 0:W], scalar=m(2), in1=accr[RS, 0:W], op0=ALU.mult, op1=ALU.add),
                      reads=[B_lo2, B_accr, B_cst], writes=[Bout])

            split3(ctile, B_c)
            augrows(C_MQ, Qaug[RS, 0:W], B_Q)
            if ti == 0:
                P.add("dve", lambda e: e.tensor_tensor(out=lrow2[RS, 0:W], in0=ctile[RS, 0:W], in1=bigm[RS, 0:W], op=ALU.add), reads=[B_c, B_bigm], writes=[B_lrow2])
                split3(lrow2, B_lrow2)
            augrows(C_MK, Kaug[RS, c0:c0 + W], B_K[ti])

            a_ops = P.end_capture()
            P.begin_capture()
            nkb = blk0 + nb
            SB = [0, 1, 3]

            def qk(kb):
                sb_ = SB[kb % 3]
                P.add("pe", lambda e, sb_=sb_, kb=kb: e.matmul(bank[sb_][:, 0:W], Kaug[0:70, kb * 128:(kb + 1) * 128], Qaug[0:70, 0:W], start=True, stop=True),
                      reads=[B_Q, B_K[kb * 128 // 512]], writes=[bankbuf[sb_]])

            qk(0)
            if nkb > 1:
                qk(1)
            for kb in range(nkb):
                if kb + 2 < nkb:
                    qk(kb + 2)
                sb_ = SB[kb % 3]
                pt_ = PT[(kb % 4) // 2][:, kb % 2, 0:W]
                Bpt = B_PT4[kb % 4]
                P.add("act", lambda e, sb_=sb_, pt_=pt_: e.activation(out=pt_, in_=bank[sb_][:, 0:W], func=AF.Exp), reads=[bankbuf[sb_]], writes=[Bpt])
                jj = kb - blk0
                if jj >= 0:
                    P.add("pool", lambda e, pt_=pt_, jj=jj: e.affine_select(out=pt_, in_=pt_, pattern=[[1, W]], compare_op=ALU.is_ge,
                                                                            fill=0.0, base=-128 * jj, channel_multiplier=-1),
                          reads=[Bpt], writes=[Bpt])
                P.add("pe", lambda e, pt_=pt_, kb=kb: e.matmul(bank[2][0:65, 0:W], Vc[:, kb, 0:65], pt_, start=(kb == 0), stop=(kb == nkb - 1)),
                      reads=[Bpt, B_V[kb * 128 // 512], B_Vinit], writes=[bankbuf[2]])
            P.add("act", lambda e: e.copy(O_sb[0:65, 0:W], bank[2][0:65, 0:W]), reads=[bankbuf[2]], writes=[B_Osb])
            b = 2
            P.add("pe", lambda e, b=b: e.matmul(bank[b][0:64, 0:W], ones_f[64:65, 0:64], O_sb[64:65, 0:W], start=True, stop=True),
                  reads=[B_Osb, B_misc], writes=[bankbuf[b]])
            P.add("dve", lambda e, b=b: e.reciprocal(out=rden[0:64, 0:W], in_=bank[b][0:64, 0:W]), reads=[bankbuf[b]], writes=[B_rden])
            P.add("dve", lambda e: e.tensor_tensor(out=YA[0:64, 0:W], in0=O_sb[0:64, 0:W], in1=rden[0:64, 0:W], op=ALU.mult), reads=[B_Osb, B_rden], writes=[B_YA])
            y_out(YA, 0, 64, c0, W, B_YA)

            att_ops = P.end_capture()
            P.begin_capture()
            gmode[0] = 2
            def conv(X_, B_X, off, out_, B_out):
                P.add("dve", lambda e: e.tensor_scalar(out=out_[:, 0:W], in0=X_[:, 0:W], scalar1=cwq(off), scalar2=None, op0=ALU.mult), reads=[B_X, B_pv], writes=[B_out])
                for j in range(1, 4):
                    P.add("dve", lambda e, j=j: e.scalar_tensor_tensor(out=out_[:, 0:W], in0=X_[:, j:j + W], scalar=cwq(off + j), in1=out_[:, 0:W], op0=ALU.mult, op1=ALU.add),
                          reads=[B_X, B_pv, B_out], writes=[B_out])
                P.add("pool", lambda e: e.tensor_copy(out=X_[:, 0:3], in_=X_[:, W:W + 3]), reads=[B_X], writes=[B_X])

            conv(Xq, B_Xq, 0, cq, B_cq)
            conv(Xk, B_Xk, 4, ck, B_ck)
            conv(Xv, B_Xv, 8, cv, B_cv)
            P.add("act", lambda e: e.activation(out=qs[:, 0:W], in_=cq[:, 0:W], func=AF.Silu), reads=[B_cq], writes=[B_qs])
            P.add("act", lambda e: e.activation(out=ks[:, 0:W], in_=ck[:, 0:W], func=AF.Silu), reads=[B_ck], writes=[B_ks])
            P.add("act", lambda e: e.activation(out=vsT[:, 0:W], in_=cv[:, 0:W], func=AF.Silu), reads=[B_cv], writes=[B_vsT])
            for (src, Bs, sq_, Bsq, rn, Brn, outb, Bo, sc) in ((qs, B_qs, sqq, B_sqq, rnq, B_rnq, qnT, B_qnT, 128.0), (ks, B_ks, sqk, B_sqk, rnk, B_rnk, knT, B_knT, 1.0)):
                P.add("act", lambda e, src=src, sq_=sq_: e.activation(out=sq_[:, 0:W], in_=src[:, 0:W], func=AF.Square), reads=[Bs], writes=[Bsq])
                b = gbank()
                P.add("pe", lambda e, b=b, sq_=sq_: e.matmul(bank[b][:, 0:W], ones_f, sq_[:, 0:W], start=True, stop=True), reads=[Bsq, B_misc], writes=[bankbuf[b]])
                P.add("dve", lambda e, b=b, rn=rn, sc=sc: e.tensor_scalar(out=rn[:, 0:W], in0=bank[b][:, 0:W], scalar1=NORM_EPS, scalar2=sc, op0=ALU.add, op1=ALU.mult),
                      reads=[bankbuf[b]], writes=[Brn])
                P.add("act", lambda e, rn=rn: e.activation(out=rn[:, 0:W], in_=rn[:, 0:W], func=AF.Sqrt), reads=[Brn], writes=[Brn])
                P.add("dve", lambda e, rn=rn: e.reciprocal(out=rn[:, 0:W], in_=rn[:, 0:W]), reads=[Brn], writes=[Brn])
                P.add("dve", lambda e, src=src, rn=rn, outb=outb: e.tensor_tensor(out=outb[:, 0:W], in0=src[:, 0:W], in1=rn[:, 0:W], op=ALU.mult), reads=[Bs, Brn], writes=[Bo])

            a_in = sctm[:, 0:nb, 6]
            b_in = sctm[:, 0:nb, 7]
            nbs = slice(0, nb)
            P.add("act", lambda e: e.activation(out=t_beta[:, nbs], in_=b_in, func=AF.Sigmoid), reads=[B_sctm], writes=[B_tms])
            P.add("dve", lambda e: e.tensor_scalar(out=t_nbeta[:, nbs], in0=t_beta[:, nbs], scalar1=-1.0, scalar2=None, op0=ALU.mult), reads=[B_tms], writes=[B_tms])
            P.add("dve", lambda e: e.tensor_scalar(out=t_x[:, nbs], in0=a_in, scalar1=pv[:, PV_DT:PV_DT + 1], scalar2=None, op0=ALU.add), reads=[B_sctm, B_pv], writes=[B_tms])
            P.add("dve", lambda e: e.tensor_scalar(out=t_ax[:, nbs], in0=t_x[:, nbs], scalar1=-1.0, scalar2=None, op0=ALU.mult), reads=[B_tms], writes=[B_tms])
            P.add("dve", lambda e: e.tensor_tensor(out=t_ax[:, nbs], in0=t_ax[:, nbs], in1=t_x[:, nbs], op=ALU.max), reads=[B_tms], writes=[B_tms])
            P.add("act", lambda e: e.activation(out=t_e[:, nbs], in_=t_ax[:, nbs], func=AF.Exp, scale=-1.0), reads=[B_tms], writes=[B_tms])
            P.add("act", lambda e: e.activation(out=t_l[:, nbs], in_=t_e[:, nbs], func=AF.Ln, bias=1.0), reads=[B_tms], writes=[B_tms])
            P.add("dve", lambda e: e.scalar_tensor_tensor(out=t_g[:, nbs], in0=t_x[:, nbs], scalar=0.0, in1=t_l[:, nbs], op0=ALU.max, op1=ALU.add), reads=[B_tms], writes=[B_tms])
            P.add("dve", lambda e: e.tensor_scalar(out=t_g[:, nbs], in0=t_g[:, nbs], scalar1=sc1[:, 0:1], scalar2=None, op0=ALU.mult), reads=[B_tms, B_sc1], writes=[B_tms])
            bg = gbank()
            P.add("pe", lambda e, bg=bg: e.matmul(bank[bg][:, 0:nb], cst[:, C_LT2:C_LT2 + 128], t_g[:, nbs], start=True, stop=True), reads=[B_tms, B_cst], writes=[bankbuf[bg]])
            P.add("pe", lambda e, bg=bg: e.matmul(bank[bg][:, 8:8 + nb], cst[:, C_BLK:C_BLK + 128], t_g[:, nbs], start=True, stop=True), reads=[B_tms, B_cst], writes=[bankbuf[bg]])
            for c_ in range(2):
                P.add("dve", lambda e, c_=c_: e.tensor_scalar(out=rhs8[:, 0:2 * nb].rearrange("p (j c) -> p j c", c=2)[:, :, c_], in0=t_g[:, nbs],
                                                               scalar1=cst[:, C_SEL + c_:C_SEL + c_ + 1], scalar2=None, op0=ALU.mult),
                      reads=[B_tms, B_cst], writes=[B_glbc])
            P.add("pe", lambda e, bg=bg: e.matmul(bank[bg][:, 16:16 + 2 * nb], ones_f, rhs8[:, 0:2 * nb], start=True, stop=True), reads=[B_glbc, B_misc], writes=[bankbuf[bg]])
            P.add("dve", lambda e, bg=bg: e.tensor_copy(out=t_gc[:, nbs], in_=bank[bg][:, 0:nb]), reads=[bankbuf[bg]], writes=[B_tms])
            P.add("dve", lambda e: e.tensor_scalar(out=t_ngc[:, nbs], in0=t_gc[:, nbs], scalar1=-1.0, scalar2=None, op0=ALU.mult), reads=[B_tms], writes=[B_tms])
            P.add("dve", lambda e, bg=bg: e.tensor_tensor(out=t_d[:, nbs], in0=bank[bg][:, 8:8 + nb], in1=t_gc[:, nbs], op=ALU.subtract), reads=[bankbuf[bg], B_tms], writes=[B_tms])
            P.add("act", lambda e: e.activation(out=t_e2[:, nbs], in_=t_d[:, nbs], func=AF.Exp), reads=[B_tms], writes=[B_tms])
            P.add("act", lambda e: e.activation(out=t_e1[:, nbs], in_=t_gc[:, nbs], func=AF.Exp), reads=[B_tms], writes=[B_tms])
            P.add("dve", lambda e: e.tensor_tensor(out=t_e1[:, nbs], in0=t_e1[:, nbs], in1=t_beta[:, nbs], op=ALU.mult), reads=[B_tms], writes=[B_tms])
            P.add("act", lambda e, bg=bg: e.activation(out=glbc[:, 0:2 * nb], in_=bank[bg][:, 16:16 + 2 * nb], func=AF.Exp), reads=[bankbuf[bg]], writes=[B_glbc])

            for j in range(nb):
                b = gbank()
                pb = bank[b].bitcast(BF16)
                P.add("pe", lambda e, j=j, pb=pb: e.transpose(pb[:, 0:128], knT[:, j * 128:(j + 1) * 128], identb), reads=[B_knT, B_misc], writes=[bankbuf[b]])
                P.add("pe", lambda e, j=j, pb=pb: e.transpose(pb[:, 128:256], vsT[:, j * 128:(j + 1) * 128], identb), reads=[B_vsT, B_misc], writes=[bankbuf[b]])
                P.add("dve", lambda e, j=j, pb=pb: e.tensor_scalar(out=Xuw[:, j, 128:256], in0=pb[:, 0:128], scalar1=t_e1[:, j:j + 1], scalar2=None, op0=ALU.mult),
                      reads=[bankbuf[b], B_tms], writes=[B_Xuw])
                P.add("dve", lambda e, j=j, pb=pb: e.tensor_scalar(out=kdec[:, j, :], in0=pb[:, 0:128], scalar1=t_e2[:, j:j + 1], scalar2=None, op0=ALU.mult),
                      reads=[bankbuf[b], B_tms], writes=[B_kdec])
                P.add("dve", lambda e, j=j, pb=pb: e.tensor_scalar(out=Xuw[:, j, 0:128], in0=pb[:, 128:256], scalar1=t_beta[:, j:j + 1], scalar2=None, op0=ALU.mult),
                      reads=[bankbuf[b], B_tms], writes=[B_Xuw])

            a_ops = a_ops + P.end_capture()
            P.begin_capture()
            gmode[0] = 1
            for j in range(nb):
                q2 = j % 2
                cs = slice(j * 128, (j + 1) * 128)
                P.add("dve", lambda e, j=j, q2=q2: e.tensor_scalar(out=diag[q2], in0=ident, scalar1=t_gc[:, j:j + 1], scalar2=None, op0=ALU.mult),
                      reads=[B_cst, B_tms], writes=[B_diag[q2]])
                b = gbank()
                P.add("pe", lambda e, b=b, q2=q2: e.matmul(bank[b][:, 0:128], ones_f, diag[q2], start=True, stop=True), reads=[B_diag[q2], B_misc], writes=[bankbuf[b]])
                P.add("dve", lambda e, b=b, q2=q2: e.scalar_tensor_tensor(out=T1[q2], in0=bank[b][:, 0:128], scalar=-1.0, in1=cst[:, C_MSL:C_MSL + 128], op0=ALU.mult, op1=ALU.add),
                      reads=[bankbuf[b], B_cst], writes=[B_T1[q2]])
                P.add("act", lambda e, j=j, q2=q2: e.activation(out=Dsl[q2], in_=T1[q2], func=AF.Exp, bias=t_gc[:, j:j + 1]), reads=[B_T1[q2], B_tms], writes=[B_Dsl[q2]])
                P.add("dve", lambda e, b=b, q2=q2: e.tensor_tensor(out=T3[q2], in0=bank[b][:, 0:128], in1=cst[:, C_MIU:C_MIU + 128], op=ALU.add),
                      reads=[bankbuf[b], B_cst], writes=[B_T3[q2]])
                P.add("act", lambda e, j=j, q2=q2: e.activation(out=Diu[q2], in_=T3[q2], func=AF.Exp, bias=t_ngc[:, j:j + 1]), reads=[B_T3[q2], B_tms], writes=[B_Diu[q2]])
                P.add("act", lambda e, b=b, j=j: e.activation(out=EGR[j], in_=bank[b][:, 0:128], func=AF.Exp), reads=[bankbuf[b]], writes=[B_EGR[j]])
                b2_ = gbank()
                P.add("pe", lambda e, b2_=b2_, cs=cs: e.matmul(bank[b2_][:, 0:128], knT[:, cs], knT[:, cs], start=True, stop=True), reads=[B_knT], writes=[bankbuf[b2_]])
                P.add("pe", lambda e, b2_=b2_, cs=cs: e.matmul(bank[b2_][:, 128:256], knT[:, cs], qnT[:, cs], start=True, stop=True), reads=[B_knT, B_qnT], writes=[bankbuf[b2_]])
                P.add("dve", lambda e, b2_=b2_, j=j, q2=q2: e.scalar_tensor_tensor(out=PP[j][0][:, 0:128], in0=bank[b2_][:, 0:128], scalar=t_nbeta[:, j:j + 1], in1=Dsl[q2],
                                                                                  op0=ALU.mult, op1=ALU.mult),
                      reads=[bankbuf[b2_], B_tms, B_Dsl[q2]], writes=[B_PP[j][0]])
                P.add("dve", lambda e, b2_=b2_, j=j, q2=q2: e.tensor_tensor(out=attnT[j], in0=bank[b2_][:, 128:256], in1=Diu[q2], op=ALU.mult),
                      reads=[bankbuf[b2_], B_Diu[q2]], writes=[B_attnT[j]])
                b3 = gbank()
                pb3 = bank[b3].bitcast(BF16)
                P.add("pe", lambda e, pb3=pb3, j=j: e.transpose(pb3[:, 0:128], PP[j][0][:, 0:128], identb), reads=[B_PP[j][0], B_misc], writes=[bankbuf[b3]])
                P.add("act", lambda e, pb3=pb3, j=j: e.copy(PP[j][0][:, 128:256], pb3[:, 0:128]), reads=[bankbuf[b3]], writes=[B_PP[j][0]])
                P.add("dve", lambda e, pb3=pb3, j=j: e.tensor_tensor(out=RR[j][0], in0=pb3[:, 0:128], in1=identb, op=ALU.add), reads=[bankbuf[b3], B_misc], writes=[B_RR[j][0]])
            for m in range(1, 6):
                src, dst = (m - 1) % 2, m % 2
                for j in range(nb):
                    b = gbank()
                    pbf = bank[b]
                    P.add("pe", lambda e, b=b, j=j, src=src: e.matmul(bank[b][:, 0:128], PP[j][src][:, 128:256], PP[j][src][:, 0:128], start=True, stop=True),
                          reads=[B_PP[j][src]], writes=[bankbuf[b]])
                    if m < 5:
                        P.add("pe", lambda e, b=b, j=j, src=src: e.matmul(bank[b][:, 128:256], PP[j][src][:, 0:128], PP[j][src][:, 128:256], start=True, stop=True),
                              reads=[B_PP[j][src]], writes=[bankbuf[b]])
                    wcols = 256 if m < 5 else 128
                    P.add("act", lambda e, b=b, j=j, dst=dst, wcols=wcols: e.copy(PP[j][dst][:, 0:wcols], bank[b][:, 0:wcols]), reads=[bankbuf[b]], writes=[B_PP[j][dst]])
                    b2_ = gbank()
                    P.add("pe", lambda e, b2_=b2_, j=j, src=src, dst=dst: e.matmul(bank[b2_][:, 0:128], PP[j][dst][:, 0:128], RR[j][src], start=True, stop=True),
                          reads=[B_PP[j][dst], B_RR[j][src]], writes=[bankbuf[b2_]])
                    P.add("dve", lambda e, b2_=b2_, j=j, src=src, dst=dst: e.tensor_tensor(out=RR[j][dst], in0=bank[b2_][:, 0:128], in1=RR[j][src], op=ALU.add),
                          reads=[bankbuf[b2_], B_RR[j][src]], writes=[B_RR[j][dst]])
            RF = 1
            for j in range(nb):
                b = gbank()
                P.add("pe", lambda e, b=b, j=j: e.matmul(bank[b][:, 0:256], RR[j][RF], Xuw[:, j, :], start=True, stop=True), reads=[B_RR[j][RF], B_Xuw], writes=[bankbuf[b]])
                P.add("act", lambda e, b=b, j=j: e.copy(UW[:, j, :], bank[b][:, 0:256]), reads=[bankbuf[b]], writes=[B_UW[j]])
                for h in range(2):
                    n = 2 * j + h
                    rs_ = slice(64 * h, 64 * h + 64)
                    b2_ = gbank()
                    P.add("pe", lambda e, b2_=b2_, j=j, rs_=rs_: e.matmul(bank[b2_][:, 0:128], UW[rs_, j, 128:256], kdec[rs_, j, :], start=True, stop=True),
                          reads=[B_UW[j], B_kdec], writes=[bankbuf[b2_]])
                    P.add("pe", lambda e, b2_=b2_, j=j, rs_=rs_: e.matmul(bank[b2_][:, 128:256], kdec[rs_, j, :], UW[rs_, j, 0:128], start=True, stop=True),
                          reads=[B_UW[j], B_kdec], writes=[bankbuf[b2_]])
                    P.add("dve", lambda e, b2_=b2_, n=n: e.scalar_tensor_tensor(out=MT[n], in0=ident, scalar=glbc[:, n:n + 1], in1=bank[b2_][:, 0:128], op0=ALU.mult, op1=ALU.subtract),
                          reads=[bankbuf[b2_], B_glbc, B_cst], writes=[B_MT[n]])
                    P.add("act", lambda e, b2_=b2_, n=n: e.copy(Bn[n], bank[b2_][:, 128:256]), reads=[bankbuf[b2_]], writes=[B_Bn[n]])
                cs = slice(j * 128, (j + 1) * 128)
                P.add("dve", lambda e, j=j, cs=cs: e.tensor_tensor(out=qdecT[:, cs], in0=qnT[:, cs], in1=EGR[j], op=ALU.mult), reads=[B_qnT, B_EGR[j]], writes=[B_qdecT])
                b3 = gbank()
                P.add("pe", lambda e, b3=b3, j=j: e.matmul(bank[b3][:, 0:128], UW[:, j, 128:256], attnT[j], start=True, stop=True), reads=[B_UW[j], B_attnT[j]], writes=[bankbuf[b3]])
                P.add("dve", lambda e, b3=b3, cs=cs: e.tensor_tensor(out=QeffT[:, cs], in0=qdecT[:, cs], in1=bank[b3][:, 0:128], op=ALU.subtract),
                      reads=[bankbuf[b3], B_qdecT], writes=[B_QeffT])
            bo = 7
            for n in range(2 * nb):
                j, h = n // 2, n % 2
                rs_ = slice(64 * h, 64 * h + 64)
                si = s_cur[0]
                sn = (si + 1) % 9
                col = slice(j * 128 + 64 * h, j * 128 + 64 * h + 64)
                P.add("pe", lambda e, j=j, rs_=rs_, col=col, h=h: e.matmul(bank[bo][:, col], UW[rs_, j, 0:128], attnT[j][rs_, 64 * h:64 * h + 64], start=True, stop=False),
                      reads=[B_UW[j], B_attnT[j]], writes=[bankbuf[bo]])
                P.add("pe", lambda e, si=si, col=col: e.matmul(bank[bo][:, col], Sst[si], QeffT[:, col], start=False, stop=True),
                      reads=[B_S[si], B_QeffT], writes=[bankbuf[bo]])
                bs = gbank()
                if bs == bo:
                    bs = gbank()
                P.add("pe", lambda e, bs=bs, n=n, si=si: e.matmul(bank[bs][:, 0:128], MT[n], Sst[si], start=True, stop=True), reads=[B_MT[n], B_S[si]], writes=[bankbuf[bs]])
                P.add("dve", lambda e, bs=bs, n=n, sn=sn: e.tensor_tensor(out=Sst[sn], in0=bank[bs][:, 0:128], in1=Bn[n], op=ALU.add), reads=[bankbuf[bs], B_Bn[n]], writes=[B_S[sn]])
                s_cur[0] = sn
            P.add("act", lambda e: e.activation(out=sqo[:, 0:W], in_=bank[bo][:, 0:W], func=AF.Square), reads=[bankbuf[bo]], writes=[B_sqo])
            b = gbank()
            if b == bo:
                b = gbank()
            P.add("pe", lambda e, b=b: e.matmul(bank[b][:, 0:W], ones_f, sqo[:, 0:W], start=True, stop=True), reads=[B_sqo, B_misc], writes=[bankbuf[b]])
            P.add("dve", lambda e, b=b: e.tensor_scalar(out=rno[:, 0:W], in0=bank[b][:, 0:W], scalar1=1.0 / 128.0, scalar2=NORM_EPS, op0=ALU.mult, op1=ALU.add),
                  reads=[bankbuf[b]], writes=[B_rno])
            P.add("act", lambda e: e.activation(out=rno[:, 0:W], in_=rno[:, 0:W], func=AF.Sqrt), reads=[B_rno], writes=[B_rno])
            P.add("dve", lambda e: e.reciprocal(out=rno[:, 0:W], in_=rno[:, 0:W]), reads=[B_rno], writes=[B_rno])
            P.add("dve", lambda e: e.scalar_tensor_tensor(out=sqo[:, 0:W], in0=bank[bo][:, 0:W], scalar=pv[:, PV_GNW:PV_GNW + 1], in1=rno[:, 0:W], op0=ALU.mult, op1=ALU.mult),
                  reads=[bankbuf[bo], B_rno, B_pv, B_sqo], writes=[B_sqo])
            P.add("dve", lambda e: e.tensor_tensor(out=YB[:, 0:W], in0=sqo[:, 0:W], in1=zs[:, 0:W], op=ALU.mult), reads=[B_sqo, B_zs], writes=[B_YB])
            y_out(YB, 64, 128, c0, W, B_YB)
            gdn_ops = P.end_capture()
            return a_ops, att_ops, gdn_ops

        KNT = 999
        gmode[0] = 1
        secs = [p1_tile(ti_, c0_t, W_t) for ti_, (c0_t, W_t) in enumerate(tiles if DO_P1 else [])]
        if secs:
            P.add_merged([secs[0][0]])
        for i_ in range(len(secs)):
            lists = [secs[i_][1], secs[i_][2]]
            if i_ + 1 < len(secs):
                lists.append(secs[i_ + 1][0])
            P.add_merged(lists)
        gmode[0] = 0

        STOP = {"fused": 0, "p1": 1, "p2": 0}[mode]
        B_ag = Buf("ag")
        if mode == "fused":
          B_agin, B_agbuf, B_stage = Buf("agin"), Buf("agbuf"), Buf("ystage")
          for pc in range(16):
            P.add("sp", lambda e, pc=pc: e.dma_start(out=agin, in_=ysrc[pc * 96:(pc + 1) * 96, :]), reads=[B_ysrc], writes=[B_agin], key=B_agin)
            P.add("pool", lambda e: e.collective_compute("AllGather", ALU.bypass, replica_groups=[list(range(NCORES))], ins=[agin.opt()], outs=[agbuf.opt()]),
                  reads=[B_agin], writes=[B_agbuf], key=B_agbuf, inc=1)
            P.add("sp", lambda e, pc=pc: e.dma_start(out=agout[pc * 768:(pc + 1) * 768, :], in_=agbuf), reads=[B_agbuf], writes=[B_stage], key=B_stage)
          P.add("sp", lambda e: e.nop(), reads=[B_stage], writes=[B_ag])
        P.barrier()

        A.off = base_off
        bufA = A.f32(KC * 512).rearrange("p (k w) -> p k w", k=KC)
        bufH = A.f32(KC * 512).rearrange("p (k w) -> p k w", k=KC)
        bufH1 = A.f32(KC * 512).rearrange("p (k w) -> p k w", k=KC)
        hb = A.bf16(KC * 512).rearrange("p (k w) -> p k w", k=KC)
        h1b = hb
        yb16 = A.bf16(12 * FS).rearrange("p (k w) -> p k w", k=12)
        mixb = A.bf16(KC * 512).rearrange("p (k w) -> p k w", k=KC)
        actb = A.bf16(FC * 512).rearrange("p (k w) -> p k w", k=FC)
        ring = [A.bf16(KC * 512).rearrange("p (k w) -> p k w", k=KC) for _ in range(3)]
        B_ring = [Buf("ring%d" % i) for i in range(3)]
        dpan = A.bf16(FC * 512).rearrange("p (k w) -> p k w", k=FC)
        B_dpan = Buf("dpan")
        sq2 = actb[:, 0:8, :]
        xb2 = actb[:, 8:16, :]
        l_mean = A.f32(512); l_rstd = A.f32(512); l_t = A.f32(512); l_t2 = A.f32(512); l_t3 = A.f32(512)
        B_bufA, B_bufH, B_bufH1, B_hb, B_h1b, B_yb16, B_mixb, B_actb = [Buf(n) for n in ("bufA", "bufH", "bufH1", "hb", "h1b", "yb16", "mixb", "actb")]
        B_h1b = B_hb
        B_lmean, B_lrstd, B_lt, B_lt2, B_lt3 = [Buf(n) for n in ("lmean", "lrstd", "lt", "lt2", "lt3")]
        B_sq2 = B_actb
        B_xb2 = B_actb
        gp2 = [0]

        def gb2():
            b = gp2[0]
            gp2[0] = (gp2[0] + 1) % 8
            return b

        rp = [0]

        def load_panel(src_ap, kk, ncols):
            i = rp[0]
            rp[0] = (rp[0] + 1) % 3
            P.add("pool", lambda e: e.dma_start(out=ring[i][:, 0:kk, 0:ncols], in_=src_ap.rearrange("(k p) c -> p k c", p=128)), writes=[B_ring[i]], key=B_ring[i])
            return ring[i], B_ring[i]

        def layer_norm(src, Bsrc, dst32, Bdst32, dstb, Bdstb, gcol, bcol, W):
            P.add("act", lambda e: e.copy(xb2[:, :, 0:W], src[:, :, 0:W]), reads=[Bsrc], writes=[B_xb2])
            P.add("act", lambda e: e.activation(out=sq2[:, :, 0:W], in_=src[:, :, 0:W], func=AF.Square), reads=[Bsrc], writes=[B_sq2])
            b1, b2 = gb2(), gb2()
            for k in range(KC):
                P.add("pe", lambda e, k=k: e.matmul(bank[b1][:, 0:W], onesb, xb2[:, k, 0:W], start=(k == 0), stop=(k == KC - 1)), reads=[B_xb2, B_misc], writes=[bankbuf[b1]])
            for k in range(KC):
                P.add("pe", lambda e, k=k: e.matmul(bank[b2][:, 0:W], onesb, sq2[:, k, 0:W], start=(k == 0), stop=(k == KC - 1)), reads=[B_sq2, B_misc], writes=[bankbuf[b2]])
            P.add("act", lambda e: e.copy(l_mean[:, 0:W], bank[b1][:, 0:W]), reads=[bankbuf[b1]], writes=[B_lmean])
            P.add("dve", lambda e: e.tensor_tensor(out=l_t[:, 0:W], in0=l_mean[:, 0:W], in1=l_mean[:, 0:W], op=ALU.mult), reads=[B_lmean], writes=[B_lt])
            P.add("dve", lambda e: e.tensor_tensor(out=l_t[:, 0:W], in0=bank[b2][:, 0:W], in1=l_t[:, 0:W], op=ALU.subtract), reads=[bankbuf[b2], B_lt], writes=[B_lt])
            P.add("dve", lambda e: e.tensor_scalar(out=l_t[:, 0:W], in0=l_t[:, 0:W], scalar1=0.0, scalar2=LN_EPS, op0=ALU.max, op1=ALU.add), reads=[B_lt], writes=[B_lt])
            P.add("act", lambda e: e.activation(out=l_t[:, 0:W], in_=l_t[:, 0:W], func=AF.Sqrt), reads=[B_lt], writes=[B_lt])
            P.add("dve", lambda e: e.reciprocal(out=l_rstd[:, 0:W], in_=l_t[:, 0:W]), reads=[B_lt], writes=[B_lrstd])
            for k in range(KC):
                P.add("dve", lambda e, k=k: e.tensor_tensor(out=l_t2[:, 0:W], in0=src[:, k, 0:W], in1=l_mean[:, 0:W], op=ALU.subtract), reads=[Bsrc, B_lmean], writes=[B_lt2])
                P.add("dve", lambda e, k=k: e.tensor_tensor(out=l_t2[:, 0:W], in0=l_t2[:, 0:W], in1=l_rstd[:, 0:W], op=ALU.mult), reads=[B_lt2, B_lrstd], writes=[B_lt2])
                P.add("act", lambda e, k=k: e.activation(out=dst32[:, k, 0:W], in_=l_t2[:, 0:W], func=AF.Identity, bias=pv[:, bcol + k:bcol + k + 1], scale=pv[:, gcol + k:gcol + k + 1]),
                      reads=[B_lt2, B_pv], writes=[Bdst32])
                if dstb is not None:
                    P.add("act", lambda e, k=k: e.activation(out=dstb[:, k, 0:W], in_=l_t2[:, 0:W], func=AF.Identity, bias=pv[:, bcol + k:bcol + k + 1], scale=pv[:, gcol + k:gcol + k + 1]),
                          reads=[B_lt2, B_pv], writes=[Bdstb])

        xov = xown.rearrange("(k p) t -> p k t", p=128)
        outv = outT.rearrange("(k p) t -> p k t", p=128) if mode != "p1" else None
        B_out = Buf("out")
        pid_cache = {}

        def pid_of(e):
            return e.partition_id()

        agv5 = agout.rearrange("(s hf r f) t -> s hf r f t", s=8, hf=2, r=8)
        agv6 = agout.rearrange("(s hf q h f) t -> s hf q h f t", s=8, hf=2, q=4, h=2)

        if mode == "p2":
            for r in range(NCORES):
                P.add("sp", lambda e, r=r: e.dma_start(out=yb16[:, 4 + r, 0:FS], in_=yin[r * 192 + 64:r * 192 + 192, :]), writes=[B_yb16], key=B_yb16)
                P.add("sp", lambda e, r=r: e.dma_start(out=yb16[64 * (r % 2):64 * (r % 2) + 64, r // 2, 0:FS], in_=yin[r * 192:r * 192 + 64, :]), writes=[B_yb16], key=B_yb16)
        elif mode == "fused":
            def ldb1(e):
                pid = e.partition_id()
                src = agv5[bass.ds(pid, 1), 0, :, 64:96, 0:FS].rearrange("s r f t -> f (s r) t")
                return e.dma_start(out=yb16[0:32, 4:12, 0:FS], in_=src)
            P.add("sp", ldb1, reads=[B_ag], writes=[B_yb16], key=B_yb16)

            def ldb2(e):
                pid = e.partition_id()
                src = agv5[bass.ds(pid, 1), 1, :, 0:96, 0:FS].rearrange("s r f t -> f (s r) t")
                return e.dma_start(out=yb16[32:128, 4:12, 0:FS], in_=src)
            P.add("sp", ldb2, reads=[B_ag], writes=[B_yb16], key=B_yb16)
            for hh in range(2):
                def lda(e, hh=hh):
                    pid = e.partition_id()
                    src = agv6[bass.ds(pid, 1), 0, :, hh, 0:64, 0:FS].rearrange("s q f t -> f (s q) t")
                    return e.dma_start(out=yb16[64 * hh:64 * hh + 64, 0:4, 0:FS], in_=src)
                P.add("sp", lda, reads=[B_ag], writes=[B_yb16], key=B_yb16)


        def p2_tile(t2):
            t0 = t2 * W2
            W = W2
            P.add("sp", lambda e, t0=t0: e.dma_start(out=bufA[:, :, 0:W], in_=xov[:, :, t0:t0 + W]), writes=[B_bufA], key=B_bufA)
            layer_norm(bufA, B_bufA, bufH, B_bufH, hb, B_hb, PV_G0, PV_B0, W)
            for g4 in range(2):
                cs = slice(g4 * 512, g4 * 512 + 512)
                pga, Bga = load_panel(w2g[:, g4 * 512:g4 * 512 + 512], 8, 512)
                pa, Bpa = load_panel(woa[:, cs], 4, 512)
                for half in range(2):
                    if half == 0:
                        pg_, Bg_, pw_, Bw_, nk, yoff = pga, Bga, pa, Bpa, 4, 0
                    else:
                        pg_, Bg_ = load_panel(w2g[:, 1024 + g4 * 512:1024 + g4 * 512 + 512], 8, 512)
                        pw_, Bw_ = load_panel(wob[:, cs], 8, 512)
                        nk, yoff = 8, 4
                    for o in range(4):
                        oc = g4 * 4 + o
                        osl = slice(o * 128, o * 128 + 128)
                        bg_, bw_ = gb2(), gb2()
                        for k in range(KC):
                            P.add("pe", lambda e, k=k, bg_=bg_, pg_=pg_, osl=osl: e.matmul(bank[bg_][:, 0:W], pg_[:, k, osl], hb[:, k, 0:W], start=(k == 0), stop=(k == KC - 1)),
                                  reads=[Bg_, B_hb], writes=[bankbuf[bg_]])
                        for k in range(nk):
                            P.add("pe", lambda e, k=k, bw_=bw_, pw_=pw_, osl=osl, yoff=yoff, nk=nk: e.matmul(bank[bw_][:, 0:W], pw_[:, k, osl], yb16[:, yoff + k, t0:t0 + W], start=(k == 0), stop=(k == nk - 1)),
                                  reads=[Bw_, B_yb16], writes=[bankbuf[bw_]])
                        P.add("act", lambda e, bg_=bg_: e.activation(out=l_t[:, 0:W], in_=bank[bg_][:, 0:W], func=AF.Sigmoid), reads=[bankbuf[bg_]], writes=[B_lt])
                        if half == 0:
                            P.add("dve", lambda e, bw_=bw_, oc=oc: e.tensor_tensor(out=bufA[:, oc, 0:W], in0=bank[bw_][:, 0:W], in1=l_t[:, 0:W], op=ALU.mult),
                                  reads=[bankbuf[bw_], B_lt], writes=[B_bufA])
                        else:
                            P.add("dve", lambda e, bw_=bw_: e.tensor_tensor(out=l_t3[:, 0:W], in0=bank[bw_][:, 0:W], in1=l_t[:, 0:W], op=ALU.mult),
                                  reads=[bankbuf[bw_], B_lt], writes=[B_lt3])
                            P.add("dve", lambda e, oc=oc: e.tensor_tensor(out=mixb[:, oc, 0:W], in0=l_t3[:, 0:W], in1=bufA[:, oc, 0:W], op=ALU.add),
                                  reads=[B_lt3, B_bufA], writes=[B_mixb])
            for g4 in range(2):
                pw_, Bw_ = load_panel(wo[:, g4 * 512:g4 * 512 + 512], 8, 512)
                for o in range(4):
                    oc = g4 * 4 + o
                    osl = slice(o * 128, o * 128 + 128)
                    b = gb2()
                    for k in range(KC):
                        P.add("pe", lambda e, k=k, b=b, pw_=pw_, osl=osl: e.matmul(bank[b][:, 0:W], pw_[:, k, osl], mixb[:, k, 0:W], start=(k == 0), stop=(k == KC - 1)),
                              reads=[Bw_, B_mixb], writes=[bankbuf[b]])
                    P.add("dve", lambda e, b=b, oc=oc: e.scalar_tensor_tensor(out=bufA[:, oc, 0:W], in0=bufH[:, oc, 0:W], scalar=ALPHA, in1=bank[b][:, 0:W], op0=ALU.mult, op1=ALU.add),
                          reads=[bankbuf[b], B_bufH], writes=[B_bufA])
            layer_norm(bufA, B_bufA, bufH1, B_bufH1, h1b, B_h1b, PV_G1, PV_B1, W)
            for c0_ in range(0, DFF, 512):
                ncol = min(512, DFF - c0_)
                pg_, Bg_ = load_panel(wg[:, c0_:c0_ + ncol], 8, ncol)
                pu_, Bu_ = load_panel(wu[:, c0_:c0_ + ncol], 8, ncol)
                for o in range(ncol // 128):
                    fc = c0_ // 128 + o
                    osl = slice(o * 128, o * 128 + 128)
                    bg_, bu_ = gb2(), gb2()
                    for k in range(KC):
                        P.add("pe", lambda e, k=k, bg_=bg_, pg_=pg_, osl=osl: e.matmul(bank[bg_][:, 0:W], pg_[:, k, osl], h1b[:, k, 0:W], start=(k == 0), stop=(k == KC - 1)),
                              reads=[Bg_, B_h1b], writes=[bankbuf[bg_]])
                    for k in range(KC):
                        P.add("pe", lambda e, k=k, bu_=bu_, pu_=pu_, osl=osl: e.matmul(bank[bu_][:, 0:W], pu_[:, k, osl], h1b[:, k, 0:W], start=(k == 0), stop=(k == KC - 1)),
                              reads=[Bu_, B_h1b], writes=[bankbuf[bu_]])
                    P.add("act", lambda e, bg_=bg_: e.activation(out=l_t[:, 0:W], in_=bank[bg_][:, 0:W], func=AF.Silu), reads=[bankbuf[bg_]], writes=[B_lt])
                    P.add("dve", lambda e, bu_=bu_, fc=fc: e.tensor_tensor(out=actb[:, fc, 0:W], in0=bank[bu_][:, 0:W], in1=l_t[:, 0:W], op=ALU.mult),
                          reads=[bankbuf[bu_], B_lt], writes=[B_actb])
            for g4 in range(2):
                P.add("pool", lambda e, g4=g4: e.dma_start(out=dpan[:, :, :], in_=wd[:, g4 * 512:g4 * 512 + 512].rearrange("(k p) c -> p k c", p=128)), writes=[B_dpan], key=B_dpan)
                for o in range(4):
                    oc = g4 * 4 + o
                    osl = slice(o * 128, o * 128 + 128)
                    b = gb2()
                    for k in range(FC):
                        P.add("pe", lambda e, k=k, b=b, osl=osl: e.matmul(bank[b][:, 0:W], dpan[:, k, osl], actb[:, k, 0:W], start=(k == 0), stop=(k == FC - 1)),
                              reads=[B_dpan, B_actb], writes=[bankbuf[b]])
                    P.add("dve", lambda e, b=b, oc=oc: e.scalar_tensor_tensor(out=bufA[:, oc, 0:W], in0=bufH1[:, oc, 0:W], scalar=ALPHA, in1=bank[b][:, 0:W], op0=ALU.mult, op1=ALU.add),
                          reads=[bankbuf[b], B_bufH1], writes=[B_bufA])
            layer_norm(bufA, B_bufA, bufH, B_bufH, None, None, PV_G2, PV_B2, W)
            P.add("sp", lambda e, t0=t0: e.dma_start(out=outv[:, :, t0:t0 + W], in_=bufH[:, :, 0:W]), reads=[B_bufH], writes=[B_out], key=B_out)
        if STOP == 0:
            for t2_ in range(NT2):
                p2_tile(t2_)
        else:
            P.add("sp", lambda e: e.nop(), reads=[B_ysrc])
        P.add("sp", lambda e: e.nop(), reads=[B_out])

        P.finalize()
        engsem = {e: [es.enter_context(nc.semaphore("sem_%s_%d" % (e, i))) for i in range(P.nep[e])] for e in ENGS}
        print("ops per engine", {e: len(P.ops[e]) for e in ENGS}, "epochs", P.nep)
        keysem = {}
        for e in ENGS:
            for op in P.ops[e]:
                if op.key is not None and id(op.key) not in keysem:
                    keysem[id(op.key)] = es.enter_context(nc.semaphore("k_" + op.key.name))
        block = es.enter_context(nc.Block())

        @block.tensor
        def _(h):
            P.emit("pe", h, engsem, keysem)

        @block.scalar
        def _(h):
            P.emit("act", h, engsem, keysem)

        @block.vector
        def _(h):
            P.emit("dve", h, engsem, keysem)

        @block.gpsimd
        def _(h):
            P.emit("pool", h, engsem, keysem)

        @block.sync
        def _(h):
            P.emit("sp", h, engsem, keysem)
    return nc


def make_consts():
    c = np.zeros((128, C_N), np.float32)
    i = np.arange(128)
    same = (i[:, None] // 64) == (i[None, :] // 64)
    c[:, C_ID:C_ID + 128] = np.eye(128, dtype=np.float32)
    c[:, C_LT2:C_LT2 + 128] = (same & (i[:, None] <= i[None, :])).astype(np.float32)
    c[:, C_BLK:C_BLK + 128] = same.astype(np.float32)
    c[:, C_MSL:C_MSL + 128] = np.where(same & (i[:, None] > i[None, :]), 0.0, NEG)
    c[:, C_MIU:C_MIU + 128] = np.where(same & (i[:, None] <= i[None, :]), 0.0, NEG)
    c[:, C_SEL + 0] = (i < 64)
    c[:, C_SEL + 1] = (i >= 64)
    for r in range(3):
        c[64 + r, C_MQ + r] = 1.0
        c[67 + r, C_MQ + 3] = 1.0
        c[67 + r, C_MK + r] = -1.0
        c[64 + r, C_MK + 3] = 1.0
    return c


def shard_inputs(inp):
    x = np.asarray(inp["x"], np.float32)[0]
    S = x.shape[0]
    L = ((64 + S + 127) // 128) * 128
    FS = S // NCORES
    xT = np.zeros((D, L), np.float32)
    xT[:, 48:64] = np.asarray(inp["meta_tokens"], np.float32).T
    xT[:, 64:64 + S] = x.T
    w_in = np.asarray(inp["w_in"], np.float32)[0]
    conv_w = np.asarray(inp["conv_w"], np.float32)[0]
    cst = make_consts()
    col = lambda v: np.ascontiguousarray(np.asarray(v, np.float32).reshape(KC, 128).T)
    maps = []
    for c in range(NCORES):
        w1 = np.zeros((D, W1COLS), np.float32)
        w1[:, G_QA:G_QA + 64] = w_in[:, c * 64:(c + 1) * 64]
        w1[:, G_KA:G_KA + 64] = w_in[:, 512 + c * 64:512 + (c + 1) * 64]
        w1[:, G_V:G_V + 64] = w_in[:, 1024 + c * 64:1024 + (c + 1) * 64]
        for r in range(6):
            w1[:, G_V + 64 + r] = w_in[:, 1536 + c]
        w1[:, G_V + 70] = w_in[:, 4616 + c]
        w1[:, G_V + 71] = w_in[:, 4624 + c]
        w1[:, G_QB:G_QB + 128] = w_in[:, 1544 + c * 128:1544 + (c + 1) * 128]
        w1[:, G_KB:G_KB + 128] = w_in[:, 2568 + c * 128:2568 + (c + 1) * 128]
        w1[:, G_VB:G_VB + 128] = w_in[:, 3592 + c * 128:3592 + (c + 1) * 128]
        w1[:, G_Z:G_Z + 128] = w_in[:, 4632 + c * 128:4632 + (c + 1) * 128]
        pv = np.zeros((128, PV_N), np.float32)
        pv[:, PV_G0:PV_G0 + 8] = col(inp["ln_in_g"])
        pv[:, PV_B0:PV_B0 + 8] = col(inp["ln_in_b"])
        for t, base in enumerate((0, 1024, 2048)):
            pv[:, PV_CW + 4 * t:PV_CW + 4 * t + 4] = conv_w[:, base + c * 128:base + (c + 1) * 128].T
        pv[:, PV_BF] = np.asarray(inp["b_f"], np.float32)[0, c]
        pv[:, PV_ALOG] = np.asarray(inp["a_log"], np.float32)[0, c]
        pv[:, PV_DT] = np.asarray(inp["dt_bias"], np.float32)[0, c]
        pv[:, PV_GNW] = np.asarray(inp["gdn_norm_w"], np.float32)[0]
        pv[:, PV_G1:PV_G1 + 8] = col(inp["ln1_g"])
        pv[:, PV_B1:PV_B1 + 8] = col(inp["ln1_b"])
        pv[:, PV_G2:PV_G2 + 8] = col(inp["ln2_g"])
        pv[:, PV_B2:PV_B2 + 8] = col(inp["ln2_b"])
        maps.append({
            "xT": xT, "xown": np.ascontiguousarray(xT[:, 64 + c * FS:64 + (c + 1) * FS]), "w1": w1, "pv": pv, "cst": cst,
            "w2g": np.ascontiguousarray(w_in[:, 5656:7704]),
            "woa": np.asarray(inp["w_out_a"], np.float32)[0], "wob": np.asarray(inp["w_out_b"], np.float32)[0],
            "wo": np.asarray(inp["w_o"], np.float32)[0], "wg": np.asarray(inp["w_gate"], np.float32)[0],
            "wu": np.asarray(inp["w_up"], np.float32)[0], "wd": np.asarray(inp["w_down"], np.float32)[0],
        })
    return maps, S


P1_KEYS = ("xT", "w1", "pv", "cst")
P2_KEYS = ("xown", "pv", "cst", "w2g", "woa", "wob", "wo", "wg", "wu", "wd")


def kernel_fused(**inputs):
    maps, S = shard_inputs(inputs)
    nc = build(S, "fused")
    res = run_bass_kernel_spmd(nc, maps, core_ids=list(range(NCORES)))
    out = np.concatenate([np.asarray(res.results[c]["outT"], np.float32).T for c in range(NCORES)], axis=0)
    return out[None].astype(np.float32)


def kernel(**inputs):
    maps, S = shard_inputs(inputs)
    FS = S // NCORES
    nc1 = build(S, "p1")
    r1 = run_bass_kernel_spmd(nc1, maps, core_ids=list(range(NCORES)))
    ys = [np.asarray(r1.results[c]["ysrc"]).reshape(NCORES, 192, FS) for c in range(NCORES)]
    maps2 = []
    for c in range(NCORES):
        m2 = dict(maps[c])
        m2["yin"] = np.ascontiguousarray(np.concatenate([ys[r][c] for r in range(NCORES)], axis=0))
        maps2.append(m2)
    nc2 = build(S, "p2")
    res = run_bass_kernel_spmd(nc2, maps2, core_ids=list(range(NCORES)))
    out = np.concatenate([np.asarray(res.results[c]["outT"], np.float32).T for c in range(NCORES)], axis=0)
    return out[None].astype(np.float32)
```

```python
import contextlib
import numpy as np
import concourse.bass as bass
import concourse.mybir as mybir
from concourse.bass_utils import run_bass_kernel_spmd

F32 = mybir.dt.float32
BF16 = mybir.dt.bfloat16
ALU = mybir.AluOpType
AF = mybir.ActivationFunctionType

NCORES = 8
D = 1024
KC = 8
DFF = 2816
FC = 22
ALPHA = 2.0 ** 0.25
LN_EPS = 1e-5
NORM_EPS = 1e-6
NEG = -30000.0
G_QA, G_KA, G_V, G_QB, G_KB, G_VB, G_Z = 0, 64, 128, 200, 328, 456, 584
W1COLS = 712
GROUPS = [(G_QA, 64), (G_KA, 64), (G_V, 72), (G_QB, 128), (G_KB, 128), (G_VB, 128), (G_Z, 128)]
PV_G0, PV_B0 = 0, 8
PV_CW = 16
PV_BF, PV_ALOG, PV_DT, PV_GNW = 28, 29, 30, 31
PV_G1, PV_B1, PV_G2, PV_B2 = 32, 40, 48, 56
PV_N = 64
C_ID, C_LT2, C_BLK, C_MSL, C_MIU, C_SEL, C_MQ, C_MK = 0, 128, 256, 384, 512, 640, 642, 646
C_N = 650

ENGS = ("pe", "act", "dve", "pool", "sp")
EPOCH = 16000


class Buf:
    __slots__ = ("w", "r", "name", "excl")

    def __init__(self, name="", excl=False):
        self.w = None
        self.r = []
        self.name = name
        self.excl = excl


class Op:
    __slots__ = ("eng", "fn", "deps", "sig", "val", "key", "inc", "ep")


class Prog:
    def __init__(self):
        self.ops = {e: [] for e in ENGS}
        self.pending = {e: [] for e in ENGS}
        self.dmas = {}
        self.cap = None

    def begin_capture(self):
        self.cap = []

    def end_capture(self):
        c, self.cap = self.cap, None
        return c

    def add_merged(self, lists):
        idx = [0] * len(lists)
        tot = [max(1, len(l)) for l in lists]
        while True:
            best, bf = -1, 2.0
            for i, l in enumerate(lists):
                if idx[i] < len(l):
                    f = idx[i] / tot[i]
                    if f < bf:
                        best, bf = i, f
            if best < 0:
                break
            self.add(*lists[best][idx[best]])
            idx[best] += 1

    def add(self, eng, fn, reads=(), writes=(), key=None, inc=16):
        if self.cap is not None:
            self.cap.append((eng, fn, reads, writes, key, inc))
            return None
        op = Op()
        op.eng, op.fn, op.sig, op.val, op.key, op.inc = eng, fn, False, 0, key, inc
        deps = list(self.pending[eng])
        self.pending[eng] = []
        ex = [b for b in reads if b.excl]
        if ex:
            writes = list(writes) + ex
            reads = [b for b in reads if not b.excl]
        raw = set()
        for b in reads:
            if b.w is not None:
                deps.append(b.w)
                raw.add(id(b.w))
        for b in writes:
            if b.w is not None:
                deps.append(b.w)
                if b.excl:
                    raw.add(id(b.w))
            deps.extend(b.r)
        need, seen = [], set()
        for d in deps:
            if id(d) in seen:
                continue
            seen.add(id(d))
            if d.key is None and key is None and d.eng == eng and (eng == "pe" or id(d) not in raw):
                continue
            if d.key is None:
                d.sig = True
            need.append(d)
        op.deps = need
        for b in reads:
            b.r.append(op)
        for b in writes:
            b.w = op
            b.r = []
        self.ops[eng].append(op)
        if key is not None:
            self.dmas[id(key)] = op
        return op

    def barrier(self):
        last = []
        for e in ENGS:
            if self.ops[e]:
                o = self.ops[e][-1]
                if o.key is None:
                    o.sig = True
                last.append(o)
        last.extend(self.dmas.values())
        for e in ENGS:
            self.pending[e] = list(last)

    def finalize(self):
        keycnt = {}
        for e in ENGS:
            cnt = 0
            for op in self.ops[e]:
                if op.key is not None:
                    k = id(op.key)
                    keycnt[k] = keycnt.get(k, 0) + op.inc
                    op.val = keycnt[k]
                elif op.sig:
                    cnt += 1
                    op.ep = (cnt - 1) // EPOCH
                    op.val = (cnt - 1) % EPOCH + 1
            self.nep = getattr(self, "nep", {})
            self.nep[e] = (cnt - 1) // EPOCH + 1 if cnt else 1

    def emit(self, eng, h, engsem, keysem):
        waited = {}
        for op in self.ops[eng]:
            for d in op.deps:
                s = keysem[id(d.key)] if d.key is not None else engsem[d.eng][d.ep]
                sid = id(s)
                if waited.get(sid, 0) < d.val:
                    h.wait_ge(s, d.val)
                    waited[sid] = d.val
            ins = op.fn(h)
            if op.key is not None:
                ins.then_inc(keysem[id(op.key)], op.inc)
            elif op.sig:
                ins.then_inc(engsem[eng][op.ep], 1)


def build(S, mode="fused"):
    L = ((64 + S + 127) // 128) * 128
    NBLK = L // 128
    tiles = []
    p = 0
    while p < L:
        w = min(512, L - p)
        tiles.append((p, w))
        p += w
    FS = S // NCORES
    W2 = min(512, FS)
    NT2 = FS // W2

    nc = bass.Bass("TRN2", target_bir_lowering=False)
    dt_in = lambda n, shp: nc.dram_tensor(n, shp, F32, kind="ExternalInput").ap()
    xT = dt_in("xT", [D, L])
    xown = dt_in("xown", [D, FS])
    w1 = dt_in("w1", [D, W1COLS])
    pv_d = dt_in("pv", [128, PV_N])
    cst_d = dt_in("cst", [128, C_N])
    w2g = dt_in("w2g", [D, 2048])
    woa = dt_in("woa", [512, D])
    wob = dt_in("wob", [D, D])
    wo = dt_in("wo", [D, D])
    wg = dt_in("wg", [D, DFF])
    wu = dt_in("wu", [D, DFF])
    wd = dt_in("wd", [DFF, D])
    if mode != "p1":
        outT = nc.dram_tensor("outT", [D, FS], F32, kind="ExternalOutput").ap()
    if mode == "p1":
        ysrc = nc.dram_tensor("ysrc", [NCORES * 192, FS], BF16, kind="ExternalOutput").ap()
    else:
        ysrc = nc.dram_tensor("ysrc", [NCORES * 192, FS], BF16).ap()
    DBG = 0
    if mode == "p2":
        yin = nc.dram_tensor("yin", [NCORES * 192, FS], BF16, kind="ExternalInput").ap()
    agout = nc.dram_tensor("ystage", [16 * NCORES * 96, FS], BF16).ap()
    agin = nc.dram_tensor("agin", [96, FS], BF16).ap()
    agbuf = nc.dram_tensor("agbuf", [NCORES * 96, FS], BF16).ap()

    P = Prog()
    es = contextlib.ExitStack()
    with es:
        ARENA_F = 50 * 1024
        arena = es.enter_context(nc.sbuf_tensor("arena", [128, ARENA_F], F32))
        psum = es.enter_context(nc.psum_tensor("ps", [128, 4096], F32))
        bank = [psum[:, b * 512:(b + 1) * 512] for b in range(8)]
        bankbuf = [Buf("bank%d" % b, excl=True) for b in range(8)]

        class Arena:
            def __init__(self):
                self.off = 0

            def f32(self, n):
                a = arena[:, self.off:self.off + n]
                self.off += n
                assert self.off <= ARENA_F, "SBUF arena overflow %d" % self.off
                return a

            def bf16(self, n):
                m = (n + 1) // 2
                a = arena[:, self.off:self.off + m].bitcast(BF16)
                self.off += m
                assert self.off <= ARENA_F, "SBUF arena overflow %d" % self.off
                return a[:, 0:n]

        A = Arena()
        cst = A.f32(C_N)
        pv = A.f32(PV_N)
        ones_f = A.f32(128)
        identb = A.bf16(128)
        onesb = A.bf16(128)
        B_cst, B_pv, B_misc = Buf("cst"), Buf("pv"), Buf("misc")
        ident = cst[:, C_ID:C_ID + 128]
        P.add("sp", lambda e: e.dma_start(out=cst, in_=cst_d), writes=[B_cst], key=B_cst)
        P.add("sp", lambda e: e.dma_start(out=pv, in_=pv_d), writes=[B_pv], key=B_pv)
        P.add("pool", lambda e: e.memset(ones_f, 1.0), writes=[B_misc])
        P.add("pool", lambda e: e.memset(onesb, 1.0 / 1024.0), writes=[B_misc])
        P.add("pool", lambda e: e.tensor_copy(out=identb, in_=ident), reads=[B_cst], writes=[B_misc])
        CONSTS = [B_cst, B_pv, B_misc]
        base_off = A.off

        gp = [5]
        gmode = [0]

        def gbank():
            if gmode[0] == 1:
                gp[0] = 6 if gp[0] == 5 else 5
                return gp[0]
            if gmode[0] == 2:
                return 4
            b = gp[0]
            gp[0] = 5 + (gp[0] - 5 + 1) % 3
            return b

        gpa = [3]

        DO_P1 = mode != "p2"
        Wc = A.bf16(KC * W1COLS).rearrange("p (k c) -> p k c", k=KC)
        B_Wc = Buf("Wc")
        bW = A.f32(8)
        prep_off = A.off
        csum = A.f32(W1COLS)
        stg = [A.f32(W1COLS), A.f32(W1COLS)]
        B_stg = [Buf("stg0"), Buf("stg1")]
        wgt = A.f32(W1COLS)
        B_wgt = Buf("wgt")
        B_bW, B_csum = Buf("bW"), Buf("csum")
        w1v = w1.rearrange("(k p) c -> p k c", p=128)
        psc = [bank[5], bank[6]]
        for k in range(KC if DO_P1 else 0):
            s = stg[k % 2]
            P.add("sp", lambda e, s=s, k=k: e.dma_start(out=s, in_=w1v[:, k, :]), writes=[B_stg[k % 2]], key=B_stg[k % 2])
            P.add("dve", lambda e, s=s, k=k: e.tensor_scalar(out=wgt, in0=s, scalar1=pv[:, PV_G0 + k:PV_G0 + k + 1], scalar2=None, op0=ALU.mult),
                  reads=[B_stg[k % 2], B_pv], writes=[B_wgt])
            P.add("pe", lambda e, k=k: e.matmul(psc[0][:, 0:512], ones_f, wgt[:, 0:512], start=(k == 0), stop=(k == KC - 1)),
                  reads=[B_wgt, B_misc], writes=[bankbuf[5]])
            P.add("pe", lambda e, k=k: e.matmul(psc[1][:, 0:W1COLS - 512], ones_f, wgt[:, 512:W1COLS], start=(k == 0), stop=(k == KC - 1)),
                  reads=[B_wgt, B_misc], writes=[bankbuf[6]])
            order = [3, 0, 1, 2, 4, 5, 6]
            for oi, gi in enumerate(order):
                go, gm = GROUPS[gi]
                P.add("pe", lambda e, s=s, k=k, gi=gi, go=go, gm=gm, oi=oi: e.matmul(bank[7][0:gm, gi:gi + 1], s[:, go:go + gm], pv[:, PV_B0 + k:PV_B0 + k + 1],
                                                                                    start=(k == 0 and oi == 0), stop=(k == KC - 1 and oi == 6), skip_group_check=True),
                      reads=[B_stg[k % 2], B_pv], writes=[bankbuf[7]])
        if DO_P1:
            P.add("act", lambda e: e.mul(csum[:, 0:512], psc[0][:, 0:512], 1.0 / 1024.0), reads=[bankbuf[5]], writes=[B_csum])
            P.add("act", lambda e: e.mul(csum[:, 512:W1COLS], psc[1][:, 0:W1COLS - 512], 1.0 / 1024.0), reads=[bankbuf[6]], writes=[B_csum])
            P.add("dve", lambda e: e.tensor_copy(out=bW[:, 0:7], in_=bank[7][:, 0:7]), reads=[bankbuf[7]], writes=[B_bW])
        for k in range(KC if DO_P1 else 0):
            s = stg[k % 2]
            P.add("sp", lambda e, s=s, k=k: e.dma_start(out=s, in_=w1v[:, k, :]), writes=[B_stg[k % 2]], key=B_stg[k % 2])
            P.add("dve", lambda e, s=s, k=k: e.scalar_tensor_tensor(out=Wc[:, k, :], in0=s, scalar=pv[:, PV_G0 + k:PV_G0 + k + 1], in1=csum,
                                                                    op0=ALU.mult, op1=ALU.subtract),
                  reads=[B_stg[k % 2], B_pv, B_csum], writes=[B_Wc])
        P.barrier()
        A.off = prep_off
        sc1 = A.f32(8)
        B_sc1 = Buf("sc1")
        if DO_P1:
          P.add("act", lambda e: e.activation(out=sc1[:, 0:1], in_=pv[:, PV_ALOG:PV_ALOG + 1], func=AF.Exp), reads=[B_pv], writes=[B_sc1])
          P.add("dve", lambda e: e.tensor_scalar(out=sc1[:, 0:1], in0=sc1[:, 0:1], scalar1=-1.0, scalar2=None, op0=ALU.mult), reads=[B_sc1], writes=[B_sc1])
          P.add("dve", lambda e: e.tensor_scalar(out=sc1[:, 1:2], in0=bW[:, 0:1], scalar1=0.125, scalar2=None, op0=ALU.mult), reads=[B_bW], writes=[B_sc1])

        Kaug = A.bf16(L)
        Vc = A.bf16(NBLK * 65).rearrange("p (b c) -> p b c", c=65)
        B_K = [Buf("K%d" % i) for i in range(len(tiles))]
        B_V = [Buf("V%d" % i) for i in range(len(tiles))]
        B_Vinit = Buf("Vinit")
        if DO_P1:
            P.add("pool", lambda e: e.memset(Vc[:, :, 64:65], 1.0), writes=[B_Vinit])

        xb = [A.bf16(KC * 512).rearrange("p (k w) -> p k w", k=KC) for _ in range(2)]
        B_xb = [Buf("xb0"), Buf("xb1")]
        sq = A.bf16(KC * 512).rearrange("p (k w) -> p k w", k=KC)
        B_sq = Buf("sq")

        def T32(name):
            return A.f32(512), Buf(name)

        def T16(name):
            return A.bf16(512), Buf(name)

        mean_sb, B_mean = T32("mean")
        rstd, B_rstd = T32("rstd")
        tmpA, B_tmpA = T32("tmpA")
        tmpB, B_tmpB = T32("tmpB")
        Qaug_l = [A.bf16(512), A.bf16(512)]
        B_Q_l = [Buf("Qaug0"), Buf("Qaug1")]
        VG, B_VG = T32("VG")
        X_l = [[A.f32(516) for _ in range(3)]] * 2
        B_X_l = [[Buf("X_%d" % t) for t in range(3)]] * 2
        zs_l = [A.f32(512), A.f32(512)]
        B_zs_l = [Buf("zs0"), Buf("zs1")]
        ctile, B_c = T32("c")
        lrow, B_lrow = T32("lrow")
        lrow2, B_lrow2 = T32("lrow2")
        hiB, B_hi = T16("hi"); loB, B_lo = T16("lo"); lo2B, B_lo2 = T16("lo2")
        r1, B_r1 = T32("r1")
        accr, B_accr = T32("accr")
        onesrow, B_onesrow = T32("onesrow")
        bigm, B_bigm = T32("bigm")
        carry = A.f32(2)
        B_carry = Buf("carry")
        PT = [A.bf16(1024).rearrange("p (j w) -> p j w", j=2) for _ in range(2)]
        B_PT = [Buf("PT0"), Buf("PT1")]
        B_PT4 = [Buf("PT4_%d" % i) for i in range(4)]
        O_sb, B_Osb = T32("Osb")
        rden, B_rden = T32("rden")
        YA, B_YA = T16("YA")
        YB, B_YB = T16("YB")
        halo = [A.f32(12) for _ in range(3)]
        B_halo = [Buf("halo%d" % i) for i in range(3)]
        sctm_l = [A.f32(32).rearrange("p (j c) -> p j c", c=8) for _ in range(2)]
        B_sctm_l = [Buf("sctm0"), Buf("sctm1")]
        cq, B_cq = T32("cq"); ck, B_ck = T32("ck"); cv, B_cv = T32("cv")
        qs, B_qs = cq, B_cq
        ks, B_ks = ck, B_ck
        sqq, B_sqq = T16("sqq"); sqk, B_sqk = sqq, B_sqq
        sqb16, B_sqb16 = sqq, B_sqq
        sqob, B_sqob = T16("sqob")
        rnq, B_rnq = T32("rnq"); rnk, B_rnk = rnq, B_rnq
        qnT_l = [T16("qnT0"), T16("qnT1")]; knT_l = [T16("knT0"), T16("knT1")]; vsT_l = [T16("vsT0"), T16("vsT1")]
        tms_l = [A.f32(64), A.f32(64)]
        B_tms_l = [Buf("tms0"), Buf("tms1")]
        rhs8_l = [A.f32(8), A.f32(8)]
        glbc_l = [A.f32(8), A.f32(8)]
        B_glbc_l = [Buf("glbc0"), Buf("glbc1")]
        Xuw_l = [A.bf16(4 * 256).rearrange("p (j c) -> p j c", c=256) for _ in range(2)]
        B_Xuw_l = [Buf("Xuw0"), Buf("Xuw1")]
        kdec_l = [A.bf16(4 * 128).rearrange("p (j c) -> p j c", c=128) for _ in range(2)]
        B_kdec_l = [Buf("kdec0"), Buf("kdec1")]
        UW = A.bf16(4 * 256).rearrange("p (j c) -> p j c", c=256)
        B_UW = [Buf("UW%d" % j) for j in range(4)]
        diag = [A.f32(128) for _ in range(2)]
        B_diag = [Buf("diag0"), Buf("diag1")]
        T1 = [A.f32(128) for _ in range(2)]
        B_T1 = [Buf("T1a"), Buf("T1b")]
        T3 = [A.f32(128) for _ in range(2)]
        B_T3 = [Buf("T3a"), Buf("T3b")]
        Dsl = [A.f32(128) for _ in range(2)]
        B_Dsl = [Buf("Dsl0"), Buf("Dsl1")]
        Diu = [A.f32(128) for _ in range(2)]
        B_Diu = [Buf("Diu0"), Buf("Diu1")]
        EGR = [A.f32(128) for _ in range(4)]
        B_EGR = [Buf("EGR%d" % j) for j in range(4)]
        attnT = [A.bf16(128) for _ in range(4)]
        B_attnT = [Buf("attnT%d" % j) for j in range(4)]
        PP = [[A.bf16(256) for _ in range(2)] for _ in range(4)]
        B_PP = [[Buf("PP%d_%d" % (j, q)) for q in range(2)] for j in range(4)]
        RR = [[A.bf16(128) for _ in range(2)] for _ in range(4)]
        B_RR = [[Buf("RR%d_%d" % (j, q)) for q in range(2)] for j in range(4)]
        qdecT, B_qdecT = T32("qdecT")
        QeffT, B_QeffT = T32("QeffT")
        MT = [A.f32(128) for _ in range(8)]
        B_MT = [Buf("MT%d" % j) for j in range(8)]
        Bn = [A.f32(128) for _ in range(8)]
        B_Bn = [Buf("Bn%d" % j) for j in range(8)]
        Sst = [A.f32(128) for _ in range(9)]
        B_S = [Buf("S%d" % j) for j in range(9)]
        sqo, B_sqo = T32("sqo")
        rno, B_rno = T32("rno")
        p1_end = A.off

        if DO_P1:
            P.add("pool", lambda e: e.memset(Sst[0], 0.0), writes=[B_S[0]])
            P.add("pool", lambda e: e.memset(onesrow, 1.0), writes=[B_onesrow])
            P.add("pool", lambda e: e.memset(bigm, 0.0), writes=[B_bigm])
            P.add("pool", lambda e: e.memset(bigm[64:70, 0:48], 30000.0), writes=[B_bigm])
            for t_ in range(3):
                P.add("pool", lambda e, t_=t_: e.memset(X_l[0][t_][:, 0:3], 0.0), writes=[B_X_l[0][t_]])

        xTv = xT.rearrange("(k p) l -> p k l", p=128)
        ysv = ysrc.rearrange("(s f) t -> s f t", f=192)
        B_ysrc = Buf("ysrc")
        RS = slice(64, 70)
        cwq = lambda j: pv[:, PV_CW + j:PV_CW + j + 1]
        s_cur = [0]

        def y_out(src, rows0, nrows, c0, W, Bsrc):
            p = max(c0, 64)
            end = min(c0 + W, 64 + S)
            while p < end:
                f = p - 64
                sh = f // FS
                fe = min(end - 64, (sh + 1) * FS)
                n = fe - f
                P.add("sp", lambda e, sh=sh, f=f, n=n, p=p: e.dma_start(out=ysv[sh, rows0:rows0 + nrows, f - sh * FS:f - sh * FS + n],
                                                                        in_=src[0:nrows, p - c0:p - c0 + n]),
                      reads=[Bsrc], writes=[B_ysrc], key=B_ysrc)
                p += n

        STAGE = 99

        def p1_tile(ti, c0, W):
            nb = W // 128
            blk0 = c0 // 128
            par = ti % 2
            xbt = xb[par]
            Qaug, B_Q = Qaug_l[par], B_Q_l[par]
            zs, B_zs = zs_l[par], B_zs_l[par]
            (Xq, Xk, Xv), (B_Xq, B_Xk, B_Xv) = X_l[par], B_X_l[par]
            sctm, B_sctm = sctm_l[par], B_sctm_l[par]
            (qnT, B_qnT), (knT, B_knT), (vsT, B_vsT) = qnT_l[par], knT_l[par], vsT_l[par]
            tms, B_tms = tms_l[par], B_tms_l[par]
            t_beta, t_nbeta, t_x, t_ax, t_e, t_l, t_g, t_gc, t_ngc, t_e1, t_e2, t_d = [tms[:, 4 * i_:4 * i_ + 4] for i_ in range(12)]
            rhs8, glbc, B_glbc = rhs8_l[par], glbc_l[par], B_glbc_l[par]
            Xuw, B_Xuw, kdec, B_kdec = Xuw_l[par], B_Xuw_l[par], kdec_l[par], B_kdec_l[par]
            gmode[0] = 2
            P.begin_capture()
            P.add("pool", lambda e, xbt=xbt, c0=c0, W=W: e.dma_start(out=xbt[:, :, 0:W], in_=xTv[:, :, c0:c0 + W]), writes=[B_xb[par]], key=B_xb[par])
            P.add("pool", lambda e, xbt=xbt, W=W: e.tensor_tensor(out=sq[:, :, 0:W], in0=xbt[:, :, 0:W], in1=xbt[:, :, 0:W], op=ALU.mult),
                  reads=[B_xb[par]], writes=[B_sq])
            b1, b2 = gbank(), gbank()
            for k in range(KC):
                P.add("pe", lambda e, k=k, b1=b1, xbt=xbt, W=W: e.matmul(bank[b1][:, 0:W], onesb, xbt[:, k, 0:W], start=(k == 0), stop=(k == KC - 1)),
                      reads=[B_xb[par], B_misc], writes=[bankbuf[b1]])
            P.add("act", lambda e, b1=b1, W=W: e.copy(mean_sb[:, 0:W], bank[b1][:, 0:W]), reads=[bankbuf[b1]], writes=[B_mean])
            for k in range(KC):
                P.add("pe", lambda e, k=k, b2=b2, W=W: e.matmul(bank[b2][:, 0:W], onesb, sq[:, k, 0:W], start=(k == 0), stop=(k == KC - 1)),
                      reads=[B_sq, B_misc], writes=[bankbuf[b2]])
            P.add("dve", lambda e, W=W: e.tensor_tensor(out=tmpA[:, 0:W], in0=mean_sb[:, 0:W], in1=mean_sb[:, 0:W], op=ALU.mult), reads=[B_mean], writes=[B_tmpA])
            P.add("dve", lambda e, b2=b2, W=W: e.tensor_tensor(out=tmpA[:, 0:W], in0=bank[b2][:, 0:W], in1=tmpA[:, 0:W], op=ALU.subtract),
                  reads=[bankbuf[b2], B_tmpA], writes=[B_tmpA])
            P.add("dve", lambda e, W=W: e.tensor_scalar(out=tmpA[:, 0:W], in0=tmpA[:, 0:W], scalar1=0.0, scalar2=LN_EPS, op0=ALU.max, op1=ALU.add),
                  reads=[B_tmpA], writes=[B_tmpA])
            P.add("act", lambda e, W=W: e.activation(out=tmpA[:, 0:W], in_=tmpA[:, 0:W], func=AF.Ln), reads=[B_tmpA], writes=[B_tmpA])
            P.add("act", lambda e, W=W: e.activation(out=rstd[:, 0:W], in_=tmpA[:, 0:W], func=AF.Exp, scale=-0.5), reads=[B_tmpA], writes=[B_rstd])

            def proj(go, gm):
                b = gbank()
                for k in range(KC):
                    P.add("pe", lambda e, k=k, b=b: e.matmul(bank[b][0:gm, 0:W], Wc[:, k, go:go + gm], xbt[:, k, 0:W], start=(k == 0), stop=(k == KC - 1)),
                          reads=[B_Wc, B_xb[par]], writes=[bankbuf[b]])
                return b

            def evac(b, gm, gi, out_ap, Bout, func=AF.Identity, scale=1.0, bias_ap=None, tmp=None, Btmp=None):
                tmp_, Bt = (tmpB, B_tmpB) if tmp is None else (tmp, Btmp)
                P.add("dve", lambda e: e.tensor_tensor(out=tmp_[0:gm, 0:W], in0=bank[b][0:gm, 0:W], in1=rstd[0:gm, 0:W], op=ALU.mult),
                      reads=[bankbuf[b], B_rstd], writes=[Bt])
                bia = bW[0:gm, gi:gi + 1] if bias_ap is None else bias_ap
                P.add("act", lambda e: e.activation(out=out_ap, in_=tmp_[0:gm, 0:W], func=func, bias=bia, scale=scale),
                      reads=[Bt, B_bW, B_sc1], writes=[Bout])

            b = proj(G_QA, 64)
            evac(b, 64, 0, Qaug[0:64, 0:W], B_Q, scale=0.125, bias_ap=sc1[0:64, 1:2])
            b = proj(G_KA, 64)
            evac(b, 64, 1, Kaug[0:64, c0:c0 + W], B_K[ti])
            b = proj(G_V, 72)
            evac(b, 72, 2, VG[0:72, 0:W], B_VG)
            b = proj(G_QB, 128)
            evac(b, 128, 3, Xq[:, 3:3 + W], B_Xq)
            b = proj(G_KB, 128)
            evac(b, 128, 4, Xk[:, 3:3 + W], B_Xk)
            b = proj(G_VB, 128)
            evac(b, 128, 5, Xv[:, 3:3 + W], B_Xv)
            b = proj(G_Z, 128)
            evac(b, 128, 6, zs[:, 0:W], B_zs, func=AF.Silu)
            if ti == 0:
                for X_, B_ in ((Xq, B_Xq), (Xk, B_Xk), (Xv, B_Xv)):
                    P.add("pool", lambda e, X_=X_: e.memset(X_[:, 3:3 + 48], 0.0), writes=[B_])

            for j in range(nb):
                b = gbank()
                P.add("pe", lambda e, j=j, b=b: e.transpose(bank[b][:, 0:72], VG[0:72, j * 128:(j + 1) * 128], ident[0:72, 0:72]),
                      reads=[B_VG, B_cst], writes=[bankbuf[b]])
                P.add("act", lambda e, j=j, b=b: e.copy(Vc[:, blk0 + j, 0:64], bank[b][:, 0:64]), reads=[bankbuf[b], B_Vinit], writes=[B_V[ti]])
                P.add("dve", lambda e, j=j, b=b: e.tensor_copy(out=sctm[:, j, :], in_=bank[b][:, 64:72]), reads=[bankbuf[b]], writes=[B_sctm])

            P.add("dve", lambda e: e.tensor_scalar(out=lrow[RS, 0:W], in0=VG[RS, 0:W], scalar1=pv[RS, PV_BF:PV_BF + 1], scalar2=-1.0, op0=ALU.add, op1=ALU.mult),
                  reads=[B_VG, B_pv], writes=[B_lrow])
            P.add("dve", lambda e: e.tensor_scalar(out=lrow2[RS, 0:W], in0=lrow[RS, 0:W], scalar1=-1.0, scalar2=None, op0=ALU.mult), reads=[B_lrow], writes=[B_lrow2])
            P.add("dve", lambda e: e.tensor_tensor(out=lrow2[RS, 0:W], in0=lrow2[RS, 0:W], in1=lrow[RS, 0:W], op=ALU.max), reads=[B_lrow, B_lrow2], writes=[B_lrow2])
            P.add("act", lambda e: e.activation(out=lrow2[RS, 0:W], in_=lrow2[RS, 0:W], func=AF.Exp, scale=-1.0), reads=[B_lrow2], writes=[B_lrow2])
            P.add("act", lambda e: e.activation(out=lrow2[RS, 0:W], in_=lrow2[RS, 0:W], func=AF.Ln, bias=1.0), reads=[B_lrow2], writes=[B_lrow2])
            P.add("dve", lambda e: e.scalar_tensor_tensor(out=lrow[RS, 0:W], in0=lrow[RS, 0:W], scalar=0.0, in1=lrow2[RS, 0:W], op0=ALU.max, op1=ALU.add),
                  reads=[B_lrow, B_lrow2], writes=[B_lrow])
            init = 0.0 if ti == 0 else carry[RS, 0:1]
            P.add("dve", lambda e, init=init: e.tensor_tensor_scan(out=ctile[RS, 0:W], data0=onesrow[RS, 0:W], data1=lrow[RS, 0:W], initial=init,
                                                                  op0=ALU.mult, op1=ALU.subtract),
                  reads=[B_lrow, B_onesrow, B_carry], writes=[B_c])
            P.add("dve", lambda e: e.tensor_copy(out=carry[RS, 0:1], in_=ctile[RS, W - 1:W]), reads=[B_c], writes=[B_carry])

            def split3(src, Bsrc):
                P.add("dve", lambda e: e.tensor_copy(out=hiB[RS, 0:W], in_=src[RS, 0:W]), reads=[Bsrc], writes=[B_hi])
                P.add("dve", lambda e: e.tensor_tensor(out=r1[RS, 0:W], in0=src[RS, 0:W], in1=hiB[RS, 0:W], op=ALU.subtract), reads=[Bsrc, B_hi], writes=[B_r1])
                P.add("dve", lambda e: e.tensor_copy(out=loB[RS, 0:W], in_=r1[RS, 0:W]), reads=[B_r1], writes=[B_lo])
                P.add("dve", lambda e: e.tensor_tensor(out=r1[RS, 0:W], in0=r1[RS, 0:W], in1=loB[RS, 0:W], op=ALU.subtract), reads=[B_r1, B_lo], writes=[B_r1])
                P.add("dve", lambda e: e.tensor_copy(out=lo2B[RS, 0:W], in_=r1[RS, 0:W]), reads=[B_r1], writes=[B_lo2])

            def augrows(mc, out_ap, Bout):
                m = lambda i: cst[RS, mc + i:mc + i + 1]
                P.add("dve", lambda e: e.tensor_scalar(out=accr[RS, 0:W], in0=hiB[RS, 0:W], scalar1=m(0), scalar2=m(3), op0=ALU.mult, op1=ALU.add),
                      reads=[B_hi, B_cst], writes=[B_accr])
                P.add("dve", lambda e: e.scalar_tensor_tensor(out=accr[RS, 0:W], in0=loB[RS, 0:W], scalar=m(1), in1=accr[RS, 0:W], op0=ALU.mult, op1=ALU.add),
                      reads=[B_lo, B_accr, B_cst], writes=[B_accr])
                P.add("dve", lambda e: e.scalar_tensor_tensor(out=out_ap, in0=lo2B[RS, 0:W], scalar=m(2), in1=accr[RS, 0:W], op0=ALU.mult, op1=ALU.add),
                      reads=[B_lo2, B_accr, B_cst], writes=[Bout])

            split3(ctile, B_c)
            augrows(C_MQ, Qaug[RS, 0:W], B_Q)
            if ti == 0:
                P.add("dve", lambda e: e.tensor_tensor(out=lrow2[RS, 0:W], in0=ctile[RS, 0:W], in1=bigm[RS, 0:W], op=ALU.add), reads=[B_c, B_bigm], writes=[B_lrow2])
                split3(lrow2, B_lrow2)
            augrows(C_MK, Kaug[RS, c0:c0 + W], B_K[ti])

            a_ops = P.end_capture()
            P.begin_capture()
            nkb = blk0 + nb
            SB = [0, 1, 3]

            def qk(kb):
                sb_ = SB[kb % 3]
                P.add("pe", lambda e, sb_=sb_, kb=kb: e.matmul(bank[sb_][:, 0:W], Kaug[0:70, kb * 128:(kb + 1) * 128], Qaug[0:70, 0:W], start=True, stop=True),
                      reads=[B_Q, B_K[kb * 128 // 512]], writes=[bankbuf[sb_]])

            for kb_ in range(min(3, nkb)):
                qk(kb_)

            def exp_sel(kb):
                sb_ = SB[kb % 3]
                pt_ = PT[(kb % 4) // 2][:, kb % 2, 0:W]
                Bpt = B_PT4[kb % 4]
                P.add("act", lambda e, sb_=sb_, pt_=pt_: e.activation(out=pt_, in_=bank[sb_][:, 0:W], func=AF.Exp), reads=[bankbuf[sb_]], writes=[Bpt])
                jj = kb - blk0
                if jj >= 0:
                    P.add("pool", lambda e, pt_=pt_, jj=jj: e.affine_select(out=pt_, in_=pt_, pattern=[[1, W]], compare_op=ALU.is_ge,
                                                                            fill=0.0, base=-128 * jj, channel_multiplier=-1),
                          reads=[Bpt], writes=[Bpt])

            def pvmm(kb):
                pt_ = PT[(kb % 4) // 2][:, kb % 2, 0:W]
                P.add("pe", lambda e, pt_=pt_, kb=kb: e.matmul(bank[2][0:65, 0:W], Vc[:, kb, 0:65], pt_, start=(kb == 0), stop=(kb == nkb - 1)),
                      reads=[B_PT4[kb % 4], B_V[kb * 128 // 512], B_Vinit], writes=[bankbuf[2]])

            for kb in range(0, nkb, 2):
                two = kb + 1 < nkb
                exp_sel(kb)
                if two:
                    exp_sel(kb + 1)
                pvmm(kb)
                if two:
                    pvmm(kb + 1)
                if kb + 3 < nkb:
                    qk(kb + 3)
                if kb + 4 < nkb:
                    qk(kb + 4)
            P.add("act", lambda e: e.copy(O_sb[0:65, 0:W], bank[2][0:65, 0:W]), reads=[bankbuf[2]], writes=[B_Osb])
            b = 2
            P.add("pe", lambda e, b=b: e.matmul(bank[b][0:64, 0:W], ones_f[64:65, 0:64], O_sb[64:65, 0:W], start=True, stop=True),
                  reads=[B_Osb, B_misc], writes=[bankbuf[b]])
            P.add("act", lambda e, b=b: e.activation(out=rden[0:64, 0:W], in_=bank[b][0:64, 0:W], func=AF.Ln), reads=[bankbuf[b]], writes=[B_rden])
            P.add("act", lambda e: e.activation(out=rden[0:64, 0:W], in_=rden[0:64, 0:W], func=AF.Exp, scale=-1.0), reads=[B_rden], writes=[B_rden])
            P.add("dve", lambda e: e.tensor_tensor(out=YA[0:64, 0:W], in0=O_sb[0:64, 0:W], in1=rden[0:64, 0:W], op=ALU.mult), reads=[B_Osb, B_rden], writes=[B_YA])
            y_out(YA, 0, 64, c0, W, B_YA)

            att_ops = P.end_capture()
            P.begin_capture()
            gmode[0] = 2
            def conv(X_, B_X, off, out_, B_out):
                P.add("dve", lambda e: e.tensor_scalar(out=out_[:, 0:W], in0=X_[:, 0:W], scalar1=cwq(off), scalar2=None, op0=ALU.mult), reads=[B_X, B_pv], writes=[B_out])
                for j in range(1, 4):
                    P.add("dve", lambda e, j=j: e.scalar_tensor_tensor(out=out_[:, 0:W], in0=X_[:, j:j + W], scalar=cwq(off + j), in1=out_[:, 0:W], op0=ALU.mult, op1=ALU.add),
                          reads=[B_X, B_pv, B_out], writes=[B_out])
                P.add("pool", lambda e: e.tensor_copy(out=X_[:, 0:3], in_=X_[:, W:W + 3]), reads=[B_X], writes=[B_X])

            conv(Xq, B_Xq, 0, cq, B_cq)
            conv(Xk, B_Xk, 4, ck, B_ck)
            conv(Xv, B_Xv, 8, cv, B_cv)
            P.add("act", lambda e: e.activation(out=qs[:, 0:W], in_=cq[:, 0:W], func=AF.Silu), reads=[B_cq], writes=[B_qs])
            P.add("act", lambda e: e.activation(out=ks[:, 0:W], in_=ck[:, 0:W], func=AF.Silu), reads=[B_ck], writes=[B_ks])
            P.add("act", lambda e: e.activation(out=vsT[:, 0:W], in_=cv[:, 0:W], func=AF.Silu), reads=[B_cv], writes=[B_vsT])
            for (src, Bs, sq_, Bsq, rn, Brn, outb, Bo, sc) in ((qs, B_qs, sqq, B_sqq, rnq, B_rnq, qnT, B_qnT, 128.0), (ks, B_ks, sqk, B_sqk, rnk, B_rnk, knT, B_knT, 1.0)):
                P.add("act", lambda e, src=src: e.activation(out=sqb16[:, 0:W], in_=src[:, 0:W], func=AF.Square), reads=[Bs], writes=[B_sqb16])
                b = gbank()
                P.add("pe", lambda e, b=b: e.matmul(bank[b][:, 0:W], onesb, sqb16[:, 0:W], start=True, stop=True), reads=[B_sqb16, B_misc], writes=[bankbuf[b]])
                P.add("dve", lambda e, b=b, rn=rn, sc=sc: e.tensor_scalar(out=rn[:, 0:W], in0=bank[b][:, 0:W], scalar1=1024.0 * sc, scalar2=NORM_EPS * sc, op0=ALU.mult, op1=ALU.add),
                      reads=[bankbuf[b]], writes=[Brn])
                P.add("act", lambda e, rn=rn: e.activation(out=rn[:, 0:W], in_=rn[:, 0:W], func=AF.Ln), reads=[Brn], writes=[Brn])
                P.add("act", lambda e, rn=rn: e.activation(out=rn[:, 0:W], in_=rn[:, 0:W], func=AF.Exp, scale=-0.5), reads=[Brn], writes=[Brn])
                P.add("dve", lambda e, src=src, rn=rn, outb=outb: e.tensor_tensor(out=outb[:, 0:W], in0=src[:, 0:W], in1=rn[:, 0:W], op=ALU.mult), reads=[Bs, Brn], writes=[Bo])

            a_in = sctm[:, 0:nb, 6]
            b_in = sctm[:, 0:nb, 7]
            nbs = slice(0, nb)
            P.add("act", lambda e: e.activation(out=t_beta[:, nbs], in_=b_in, func=AF.Exp, scale=-1.0), reads=[B_sctm], writes=[B_tms])
            P.add("act", lambda e: e.activation(out=t_beta[:, nbs], in_=t_beta[:, nbs], func=AF.Ln, bias=1.0), reads=[B_tms], writes=[B_tms])
            P.add("act", lambda e: e.activation(out=t_beta[:, nbs], in_=t_beta[:, nbs], func=AF.Exp, scale=-1.0), reads=[B_tms], writes=[B_tms])
            P.add("dve", lambda e: e.tensor_scalar(out=t_nbeta[:, nbs], in0=t_beta[:, nbs], scalar1=-1.0, scalar2=None, op0=ALU.mult), reads=[B_tms], writes=[B_tms])
            P.add("dve", lambda e: e.tensor_scalar(out=t_x[:, nbs], in0=a_in, scalar1=pv[:, PV_DT:PV_DT + 1], scalar2=None, op0=ALU.add), reads=[B_sctm, B_pv], writes=[B_tms])
            P.add("dve", lambda e: e.tensor_scalar(out=t_ax[:, nbs], in0=t_x[:, nbs], scalar1=-1.0, scalar2=None, op0=ALU.mult), reads=[B_tms], writes=[B_tms])
            P.add("dve", lambda e: e.tensor_tensor(out=t_ax[:, nbs], in0=t_ax[:, nbs], in1=t_x[:, nbs], op=ALU.max), reads=[B_tms], writes=[B_tms])
            P.add("act", lambda e: e.activation(out=t_e[:, nbs], in_=t_ax[:, nbs], func=AF.Exp, scale=-1.0), reads=[B_tms], writes=[B_tms])
            P.add("act", lambda e: e.activation(out=t_l[:, nbs], in_=t_e[:, nbs], func=AF.Ln, bias=1.0), reads=[B_tms], writes=[B_tms])
            P.add("dve", lambda e: e.scalar_tensor_tensor(out=t_g[:, nbs], in0=t_x[:, nbs], scalar=0.0, in1=t_l[:, nbs], op0=ALU.max, op1=ALU.add), reads=[B_tms], writes=[B_tms])
            P.add("dve", lambda e: e.tensor_scalar(out=t_g[:, nbs], in0=t_g[:, nbs], scalar1=sc1[:, 0:1], scalar2=None, op0=ALU.mult), reads=[B_tms, B_sc1], writes=[B_tms])
            bg = gbank()
            P.add("pe", lambda e, bg=bg: e.matmul(bank[bg][:, 0:nb], cst[:, C_LT2:C_LT2 + 128], t_g[:, nbs], start=True, stop=True), reads=[B_tms, B_cst], writes=[bankbuf[bg]])
            P.add("pe", lambda e, bg=bg: e.matmul(bank[bg][:, 8:8 + nb], cst[:, C_BLK:C_BLK + 128], t_g[:, nbs], start=True, stop=True), reads=[B_tms, B_cst], writes=[bankbuf[bg]])
            for c_ in range(2):
                P.add("dve", lambda e, c_=c_: e.tensor_scalar(out=rhs8[:, 0:2 * nb].rearrange("p (j c) -> p j c", c=2)[:, :, c_], in0=t_g[:, nbs],
                                                               scalar1=cst[:, C_SEL + c_:C_SEL + c_ + 1], scalar2=None, op0=ALU.mult),
                      reads=[B_tms, B_cst], writes=[B_glbc])
            P.add("pe", lambda e, bg=bg: e.matmul(bank[bg][:, 16:16 + 2 * nb], ones_f, rhs8[:, 0:2 * nb], start=True, stop=True), reads=[B_glbc, B_misc], writes=[bankbuf[bg]])
            P.add("dve", lambda e, bg=bg: e.tensor_copy(out=t_gc[:, nbs], in_=bank[bg][:, 0:nb]), reads=[bankbuf[bg]], writes=[B_tms])
            P.add("dve", lambda e: e.tensor_scalar(out=t_ngc[:, nbs], in0=t_gc[:, nbs], scalar1=-1.0, scalar2=None, op0=ALU.mult), reads=[B_tms], writes=[B_tms])
            P.add("dve", lambda e, bg=bg: e.tensor_tensor(out=t_d[:, nbs], in0=bank[bg][:, 8:8 + nb], in1=t_gc[:, nbs], op=ALU.subtract), reads=[bankbuf[bg], B_tms], writes=[B_tms])
            P.add("act", lambda e: e.activation(out=t_e2[:, nbs], in_=t_d[:, nbs], func=AF.Exp), reads=[B_tms], writes=[B_tms])
            P.add("act", lambda e: e.activation(out=t_e1[:, nbs], in_=t_gc[:, nbs], func=AF.Exp), reads=[B_tms], writes=[B_tms])
            P.add("dve", lambda e: e.tensor_tensor(out=t_e1[:, nbs], in0=t_e1[:, nbs], in1=t_beta[:, nbs], op=ALU.mult), reads=[B_tms], writes=[B_tms])
            P.add("act", lambda e, bg=bg: e.activation(out=glbc[:, 0:2 * nb], in_=bank[bg][:, 16:16 + 2 * nb], func=AF.Exp), reads=[bankbuf[bg]], writes=[B_glbc])

            for j in range(nb):
                b = gbank()
                pb = bank[b].bitcast(BF16)
                P.add("pe", lambda e, j=j, pb=pb: e.transpose(pb[:, 0:128], knT[:, j * 128:(j + 1) * 128], identb), reads=[B_knT, B_misc], writes=[bankbuf[b]])
                P.add("pe", lambda e, j=j, pb=pb: e.transpose(pb[:, 128:256], vsT[:, j * 128:(j + 1) * 128], identb), reads=[B_vsT, B_misc], writes=[bankbuf[b]])
                P.add("dve", lambda e, j=j, pb=pb: e.tensor_scalar(out=Xuw[:, j, 128:256], in0=pb[:, 0:128], scalar1=t_e1[:, j:j + 1], scalar2=None, op0=ALU.mult),
                      reads=[bankbuf[b], B_tms], writes=[B_Xuw])
                P.add("dve", lambda e, j=j, pb=pb: e.tensor_scalar(out=kdec[:, j, :], in0=pb[:, 0:128], scalar1=t_e2[:, j:j + 1], scalar2=None, op0=ALU.mult),
                      reads=[bankbuf[b], B_tms], writes=[B_kdec])
                P.add("dve", lambda e, j=j, pb=pb: e.tensor_scalar(out=Xuw[:, j, 0:128], in0=pb[:, 128:256], scalar1=t_beta[:, j:j + 1], scalar2=None, op0=ALU.mult),
                      reads=[bankbuf[b], B_tms], writes=[B_Xuw])

            a_ops = a_ops + P.end_capture()
            P.begin_capture()
            gmode[0] = 1
            for j in range(nb):
                q2 = j % 2
                cs = slice(j * 128, (j + 1) * 128)
                P.add("dve", lambda e, j=j, q2=q2: e.tensor_scalar(out=diag[q2], in0=ident, scalar1=t_gc[:, j:j + 1], scalar2=None, op0=ALU.mult),
                      reads=[B_cst, B_tms], writes=[B_diag[q2]])
                b = gbank()
                P.add("pe", lambda e, b=b, q2=q2: e.matmul(bank[b][:, 0:128], ones_f, diag[q2], start=True, stop=True), reads=[B_diag[q2], B_misc], writes=[bankbuf[b]])
                P.add("dve", lambda e, b=b, q2=q2: e.scalar_tensor_tensor(out=T1[q2], in0=bank[b][:, 0:128], scalar=-1.0, in1=cst[:, C_MSL:C_MSL + 128], op0=ALU.mult, op1=ALU.add),
                      reads=[bankbuf[b], B_cst], writes=[B_T1[q2]])
                P.add("act", lambda e, j=j, q2=q2: e.activation(out=Dsl[q2], in_=T1[q2], func=AF.Exp, bias=t_gc[:, j:j + 1]), reads=[B_T1[q2], B_tms], writes=[B_Dsl[q2]])
                P.add("dve", lambda e, b=b, q2=q2: e.tensor_tensor(out=T3[q2], in0=bank[b][:, 0:128], in1=cst[:, C_MIU:C_MIU + 128], op=ALU.add),
                      reads=[bankbuf[b], B_cst], writes=[B_T3[q2]])
                P.add("act", lambda e, j=j, q2=q2: e.activation(out=Diu[q2], in_=T3[q2], func=AF.Exp, bias=t_ngc[:, j:j + 1]), reads=[B_T3[q2], B_tms], writes=[B_Diu[q2]])
                P.add("act", lambda e, b=b, j=j: e.activation(out=EGR[j], in_=bank[b][:, 0:128], func=AF.Exp), reads=[bankbuf[b]], writes=[B_EGR[j]])
                b2_ = gbank()
                P.add("pe", lambda e, b2_=b2_, cs=cs: e.matmul(bank[b2_][:, 0:128], knT[:, cs], knT[:, cs], start=True, stop=True), reads=[B_knT], writes=[bankbuf[b2_]])
                P.add("pe", lambda e, b2_=b2_, cs=cs: e.matmul(bank[b2_][:, 128:256], knT[:, cs], qnT[:, cs], start=True, stop=True), reads=[B_knT, B_qnT], writes=[bankbuf[b2_]])
                P.add("dve", lambda e, b2_=b2_, j=j, q2=q2: e.scalar_tensor_tensor(out=PP[j][0][:, 0:128], in0=bank[b2_][:, 0:128], scalar=t_nbeta[:, j:j + 1], in1=Dsl[q2],
                                                                                  op0=ALU.mult, op1=ALU.mult),
                      reads=[bankbuf[b2_], B_tms, B_Dsl[q2]], writes=[B_PP[j][0]])
                P.add("dve", lambda e, b2_=b2_, j=j, q2=q2: e.tensor_tensor(out=attnT[j], in0=bank[b2_][:, 128:256], in1=Diu[q2], op=ALU.mult),
                      reads=[bankbuf[b2_], B_Diu[q2]], writes=[B_attnT[j]])
                b3 = gbank()
                pb3 = bank[b3].bitcast(BF16)
                P.add("pe", lambda e, pb3=pb3, j=j: e.transpose(pb3[:, 0:128], PP[j][0][:, 0:128], identb), reads=[B_PP[j][0], B_misc], writes=[bankbuf[b3]])
                P.add("act", lambda e, pb3=pb3, j=j: e.copy(PP[j][0][:, 128:256], pb3[:, 0:128]), reads=[bankbuf[b3]], writes=[B_PP[j][0]])
                P.add("dve", lambda e, pb3=pb3, j=j: e.tensor_tensor(out=RR[j][0], in0=pb3[:, 0:128], in1=identb, op=ALU.add), reads=[bankbuf[b3], B_misc], writes=[B_RR[j][0]])
            for m in range(1, 6):
                src, dst = (m - 1) % 2, m % 2
                for j in range(nb):
                    b = gbank()
                    pbf = bank[b]
                    P.add("pe", lambda e, b=b, j=j, src=src: e.matmul(bank[b][:, 0:128], PP[j][src][:, 128:256], PP[j][src][:, 0:128], start=True, stop=True),
                          reads=[B_PP[j][src]], writes=[bankbuf[b]])
                    if m < 5:
                        P.add("pe", lambda e, b=b, j=j, src=src: e.matmul(bank[b][:, 128:256], PP[j][src][:, 0:128], PP[j][src][:, 128:256], start=True, stop=True),
                              reads=[B_PP[j][src]], writes=[bankbuf[b]])
                    wcols = 256 if m < 5 else 128
                    P.add("act", lambda e, b=b, j=j, dst=dst, wcols=wcols: e.copy(PP[j][dst][:, 0:wcols], bank[b][:, 0:wcols]), reads=[bankbuf[b]], writes=[B_PP[j][dst]])
                    b2_ = gbank()
                    P.add("pe", lambda e, b2_=b2_, j=j, src=src, dst=dst: e.matmul(bank[b2_][:, 0:128], PP[j][dst][:, 0:128], RR[j][src], start=True, stop=True),
                          reads=[B_PP[j][dst], B_RR[j][src]], writes=[bankbuf[b2_]])
                    P.add("dve", lambda e, b2_=b2_, j=j, src=src, dst=dst: e.tensor_tensor(out=RR[j][dst], in0=bank[b2_][:, 0:128], in1=RR[j][src], op=ALU.add),
                          reads=[bankbuf[b2_], B_RR[j][src]], writes=[B_RR[j][dst]])
            RF = 1
            for j in range(nb):
                b = gbank()
                P.add("pe", lambda e, b=b, j=j: e.matmul(bank[b][:, 0:256], RR[j][RF], Xuw[:, j, :], start=True, stop=True), reads=[B_RR[j][RF], B_Xuw], writes=[bankbuf[b]])
                P.add("act", lambda e, b=b, j=j: e.copy(UW[:, j, :], bank[b][:, 0:256]), reads=[bankbuf[b]], writes=[B_UW[j]])
                for h in range(2):
                    n = 2 * j + h
                    rs_ = slice(64 * h, 64 * h + 64)
                    b2_ = gbank()
                    P.add("pe", lambda e, b2_=b2_, j=j, rs_=rs_: e.matmul(bank[b2_][:, 0:128], UW[rs_, j, 128:256], kdec[rs_, j, :], start=True, stop=True),
                          reads=[B_UW[j], B_kdec], writes=[bankbuf[b2_]])
                    P.add("pe", lambda e, b2_=b2_, j=j, rs_=rs_: e.matmul(bank[b2_][:, 128:256], kdec[rs_, j, :], UW[rs_, j, 0:128], start=True, stop=True),
                          reads=[B_UW[j], B_kdec], writes=[bankbuf[b2_]])
                    P.add("dve", lambda e, b2_=b2_, n=n: e.scalar_tensor_tensor(out=MT[n], in0=ident, scalar=glbc[:, n:n + 1], in1=bank[b2_][:, 0:128], op0=ALU.mult, op1=ALU.subtract),
                          reads=[bankbuf[b2_], B_glbc, B_cst], writes=[B_MT[n]])
                    P.add("act", lambda e, b2_=b2_, n=n: e.copy(Bn[n], bank[b2_][:, 128:256]), reads=[bankbuf[b2_]], writes=[B_Bn[n]])
                cs = slice(j * 128, (j + 1) * 128)
                P.add("dve", lambda e, j=j, cs=cs: e.tensor_tensor(out=qdecT[:, cs], in0=qnT[:, cs], in1=EGR[j], op=ALU.mult), reads=[B_qnT, B_EGR[j]], writes=[B_qdecT])
                b3 = gbank()
                P.add("pe", lambda e, b3=b3, j=j: e.matmul(bank[b3][:, 0:128], UW[:, j, 128:256], attnT[j], start=True, stop=True), reads=[B_UW[j], B_attnT[j]], writes=[bankbuf[b3]])
                P.add("dve", lambda e, b3=b3, cs=cs: e.tensor_tensor(out=QeffT[:, cs], in0=qdecT[:, cs], in1=bank[b3][:, 0:128], op=ALU.subtract),
                      reads=[bankbuf[b3], B_qdecT], writes=[B_QeffT])
            bo = 7
            for n in range(2 * nb):
                j, h = n // 2, n % 2
                rs_ = slice(64 * h, 64 * h + 64)
                si = s_cur[0]
                sn = (si + 1) % 9
                col = slice(j * 128 + 64 * h, j * 128 + 64 * h + 64)
                P.add("pe", lambda e, j=j, rs_=rs_, col=col, h=h: e.matmul(bank[bo][:, col], UW[rs_, j, 0:128], attnT[j][rs_, 64 * h:64 * h + 64], start=True, stop=False),
                      reads=[B_UW[j], B_attnT[j]], writes=[bankbuf[bo]])
                P.add("pe", lambda e, si=si, col=col: e.matmul(bank[bo][:, col], Sst[si], QeffT[:, col], start=False, stop=True),
                      reads=[B_S[si], B_QeffT], writes=[bankbuf[bo]])
                bs = gbank()
                if bs == bo:
                    bs = gbank()
                P.add("pe", lambda e, bs=bs, n=n, si=si: e.matmul(bank[bs][:, 0:128], MT[n], Sst[si], start=True, stop=True), reads=[B_MT[n], B_S[si]], writes=[bankbuf[bs]])
                P.add("dve", lambda e, bs=bs, n=n, sn=sn: e.tensor_tensor(out=Sst[sn], in0=bank[bs][:, 0:128], in1=Bn[n], op=ALU.add), reads=[bankbuf[bs], B_Bn[n]], writes=[B_S[sn]])
                s_cur[0] = sn
            P.add("act", lambda e: e.activation(out=sqob[:, 0:W], in_=bank[bo][:, 0:W], func=AF.Square), reads=[bankbuf[bo]], writes=[B_sqob])
            b = gbank()
            if b == bo:
                b = gbank()
            P.add("pe", lambda e, b=b: e.matmul(bank[b][:, 0:W], onesb, sqob[:, 0:W], start=True, stop=True), reads=[B_sqob, B_misc], writes=[bankbuf[b]])
            P.add("dve", lambda e, b=b: e.tensor_scalar(out=rno[:, 0:W], in0=bank[b][:, 0:W], scalar1=8.0, scalar2=NORM_EPS, op0=ALU.mult, op1=ALU.add),
                  reads=[bankbuf[b]], writes=[B_rno])
            P.add("act", lambda e: e.activation(out=rno[:, 0:W], in_=rno[:, 0:W], func=AF.Ln), reads=[B_rno], writes=[B_rno])
            P.add("act", lambda e: e.activation(out=rno[:, 0:W], in_=rno[:, 0:W], func=AF.Exp, scale=-0.5), reads=[B_rno], writes=[B_rno])
            P.add("dve", lambda e: e.scalar_tensor_tensor(out=sqo[:, 0:W], in0=bank[bo][:, 0:W], scalar=pv[:, PV_GNW:PV_GNW + 1], in1=rno[:, 0:W], op0=ALU.mult, op1=ALU.mult),
                  reads=[bankbuf[bo], B_rno, B_pv, B_sqo], writes=[B_sqo])
            P.add("dve", lambda e: e.tensor_tensor(out=YB[:, 0:W], in0=sqo[:, 0:W], in1=zs[:, 0:W], op=ALU.mult), reads=[B_sqo, B_zs], writes=[B_YB])
            y_out(YB, 64, 128, c0, W, B_YB)
            gdn_ops = P.end_capture()
            return a_ops, att_ops, gdn_ops

        KNT = 999
        gmode[0] = 1
        secs = [p1_tile(ti_, c0_t, W_t) for ti_, (c0_t, W_t) in enumerate(tiles if DO_P1 else [])]
        if secs:
            P.add_merged([secs[0][0]])
        for i_ in range(len(secs)):
            lists = [secs[i_][1], secs[i_][2]]
            if i_ + 1 < len(secs):
                lists.append(secs[i_ + 1][0])
            P.add_merged(lists)
        gmode[0] = 0

        STOP = {"fused": 0, "p1": 1, "p2": 0}[mode]
        B_ag = Buf("ag")
        if mode == "fused":
          B_agin, B_agbuf, B_stage = Buf("agin"), Buf("agbuf"), Buf("ystage")
          for pc in range(16):
            P.add("sp", lambda e, pc=pc: e.dma_start(out=agin, in_=ysrc[pc * 96:(pc + 1) * 96, :]), reads=[B_ysrc], writes=[B_agin], key=B_agin)
            P.add("pool", lambda e: e.collective_compute("AllGather", ALU.bypass, replica_groups=[list(range(NCORES))], ins=[agin.opt()], outs=[agbuf.opt()]),
                  reads=[B_agin], writes=[B_agbuf], key=B_agbuf, inc=1)
            P.add("sp", lambda e, pc=pc: e.dma_start(out=agout[pc * 768:(pc + 1) * 768, :], in_=agbuf), reads=[B_agbuf], writes=[B_stage], key=B_stage)
          P.add("sp", lambda e: e.nop(), reads=[B_stage], writes=[B_ag])
        P.barrier()

        A.off = base_off
        bufA = A.f32(KC * 512).rearrange("p (k w) -> p k w", k=KC)
        bufH = A.f32(KC * 512).rearrange("p (k w) -> p k w", k=KC)
        bufH1 = A.f32(KC * 512).rearrange("p (k w) -> p k w", k=KC)
        hb = A.bf16(KC * 512).rearrange("p (k w) -> p k w", k=KC)
        h1b = hb
        yb16 = A.bf16(12 * FS).rearrange("p (k w) -> p k w", k=12)
        mixb = A.bf16(KC * 512).rearrange("p (k w) -> p k w", k=KC)
        actb = A.bf16(FC * 512).rearrange("p (k w) -> p k w", k=FC)
        ring = [A.bf16(KC * 512).rearrange("p (k w) -> p k w", k=KC) for _ in range(3)]
        B_ring = [Buf("ring%d" % i) for i in range(3)]
        dpan = A.bf16(FC * 512).rearrange("p (k w) -> p k w", k=FC)
        B_dpan = Buf("dpan")
        sq2 = actb[:, 0:8, :]
        xb2 = actb[:, 8:16, :]
        l_mean = A.f32(512); l_rstd = A.f32(512); l_t = A.f32(512); l_t2 = A.f32(512); l_t3 = A.f32(512)
        B_bufA, B_bufH, B_bufH1, B_hb, B_h1b, B_yb16, B_mixb, B_actb = [Buf(n) for n in ("bufA", "bufH", "bufH1", "hb", "h1b", "yb16", "mixb", "actb")]
        B_h1b = B_hb
        B_lmean, B_lrstd, B_lt, B_lt2, B_lt3 = [Buf(n) for n in ("lmean", "lrstd", "lt", "lt2", "lt3")]
        B_sq2 = B_actb
        B_xb2 = B_actb
        gp2 = [0]

        def gb2():
            b = gp2[0]
            gp2[0] = (gp2[0] + 1) % 8
            return b

        rp = [0]

        def load_panel(src_ap, kk, ncols):
            i = rp[0]
            rp[0] = (rp[0] + 1) % 3
            P.add("pool", lambda e: e.dma_start(out=ring[i][:, 0:kk, 0:ncols], in_=src_ap.rearrange("(k p) c -> p k c", p=128)), writes=[B_ring[i]], key=B_ring[i])
            return ring[i], B_ring[i]

        def layer_norm(src, Bsrc, dst32, Bdst32, dstb, Bdstb, gcol, bcol, W):
            P.add("act", lambda e: e.copy(xb2[:, :, 0:W], src[:, :, 0:W]), reads=[Bsrc], writes=[B_xb2])
            P.add("act", lambda e: e.activation(out=sq2[:, :, 0:W], in_=src[:, :, 0:W], func=AF.Square), reads=[Bsrc], writes=[B_sq2])
            b1, b2 = gb2(), gb2()
            for k in range(KC):
                P.add("pe", lambda e, k=k: e.matmul(bank[b1][:, 0:W], onesb, xb2[:, k, 0:W], start=(k == 0), stop=(k == KC - 1)), reads=[B_xb2, B_misc], writes=[bankbuf[b1]])
            for k in range(KC):
                P.add("pe", lambda e, k=k: e.matmul(bank[b2][:, 0:W], onesb, sq2[:, k, 0:W], start=(k == 0), stop=(k == KC - 1)), reads=[B_sq2, B_misc], writes=[bankbuf[b2]])
            P.add("act", lambda e: e.copy(l_mean[:, 0:W], bank[b1][:, 0:W]), reads=[bankbuf[b1]], writes=[B_lmean])
            P.add("dve", lambda e: e.tensor_tensor(out=l_t[:, 0:W], in0=l_mean[:, 0:W], in1=l_mean[:, 0:W], op=ALU.mult), reads=[B_lmean], writes=[B_lt])
            P.add("dve", lambda e: e.tensor_tensor(out=l_t[:, 0:W], in0=bank[b2][:, 0:W], in1=l_t[:, 0:W], op=ALU.subtract), reads=[bankbuf[b2], B_lt], writes=[B_lt])
            P.add("dve", lambda e: e.tensor_scalar(out=l_t[:, 0:W], in0=l_t[:, 0:W], scalar1=0.0, scalar2=LN_EPS, op0=ALU.max, op1=ALU.add), reads=[B_lt], writes=[B_lt])
            P.add("act", lambda e: e.activation(out=l_t[:, 0:W], in_=l_t[:, 0:W], func=AF.Ln), reads=[B_lt], writes=[B_lt])
            P.add("act", lambda e: e.activation(out=l_rstd[:, 0:W], in_=l_t[:, 0:W], func=AF.Exp, scale=-0.5), reads=[B_lt], writes=[B_lrstd])
            for k in range(KC):
                P.add("dve", lambda e, k=k: e.tensor_tensor(out=l_t2[:, 0:W], in0=src[:, k, 0:W], in1=l_mean[:, 0:W], op=ALU.subtract), reads=[Bsrc, B_lmean], writes=[B_lt2])
                P.add("dve", lambda e, k=k: e.tensor_tensor(out=l_t2[:, 0:W], in0=l_t2[:, 0:W], in1=l_rstd[:, 0:W], op=ALU.mult), reads=[B_lt2, B_lrstd], writes=[B_lt2])
                P.add("act", lambda e, k=k: e.activation(out=dst32[:, k, 0:W], in_=l_t2[:, 0:W], func=AF.Identity, bias=pv[:, bcol + k:bcol + k + 1], scale=pv[:, gcol + k:gcol + k + 1]),
                      reads=[B_lt2, B_pv], writes=[Bdst32])
                if dstb is not None:
                    P.add("act", lambda e, k=k: e.activation(out=dstb[:, k, 0:W], in_=l_t2[:, 0:W], func=AF.Identity, bias=pv[:, bcol + k:bcol + k + 1], scale=pv[:, gcol + k:gcol + k + 1]),
                          reads=[B_lt2, B_pv], writes=[Bdstb])

        xov = xown.rearrange("(k p) t -> p k t", p=128)
        outv = outT.rearrange("(k p) t -> p k t", p=128) if mode != "p1" else None
        B_out = Buf("out")
        pid_cache = {}

        def pid_of(e):
            return e.partition_id()

        agv5 = agout.rearrange("(s hf r f) t -> s hf r f t", s=8, hf=2, r=8)
        agv6 = agout.rearrange("(s hf q h f) t -> s hf q h f t", s=8, hf=2, q=4, h=2)

        if mode == "p2":
            for r in range(NCORES):
                P.add("sp", lambda e, r=r: e.dma_start(out=yb16[:, 4 + r, 0:FS], in_=yin[r * 192 + 64:r * 192 + 192, :]), writes=[B_yb16], key=B_yb16)
                P.add("sp", lambda e, r=r: e.dma_start(out=yb16[64 * (r % 2):64 * (r % 2) + 64, r // 2, 0:FS], in_=yin[r * 192:r * 192 + 64, :]), writes=[B_yb16], key=B_yb16)
        elif mode == "fused":
            def ldb1(e):
                pid = e.partition_id()
                src = agv5[bass.ds(pid, 1), 0, :, 64:96, 0:FS].rearrange("s r f t -> f (s r) t")
                return e.dma_start(out=yb16[0:32, 4:12, 0:FS], in_=src)
            P.add("sp", ldb1, reads=[B_ag], writes=[B_yb16], key=B_yb16)

            def ldb2(e):
                pid = e.partition_id()
                src = agv5[bass.ds(pid, 1), 1, :, 0:96, 0:FS].rearrange("s r f t -> f (s r) t")
                return e.dma_start(out=yb16[32:128, 4:12, 0:FS], in_=src)
            P.add("sp", ldb2, reads=[B_ag], writes=[B_yb16], key=B_yb16)
            for hh in range(2):
                def lda(e, hh=hh):
                    pid = e.partition_id()
                    src = agv6[bass.ds(pid, 1), 0, :, hh, 0:64, 0:FS].rearrange("s q f t -> f (s q) t")
                    return e.dma_start(out=yb16[64 * hh:64 * hh + 64, 0:4, 0:FS], in_=src)
                P.add("sp", lda, reads=[B_ag], writes=[B_yb16], key=B_yb16)


        def p2_tile(t2):
            t0 = t2 * W2
            W = W2
            P.add("sp", lambda e, t0=t0: e.dma_start(out=bufA[:, :, 0:W], in_=xov[:, :, t0:t0 + W]), writes=[B_bufA], key=B_bufA)
            layer_norm(bufA, B_bufA, bufH, B_bufH, hb, B_hb, PV_G0, PV_B0, W)
            for g4 in range(2):
                cs = slice(g4 * 512, g4 * 512 + 512)
                pga, Bga = load_panel(w2g[:, g4 * 512:g4 * 512 + 512], 8, 512)
                pa, Bpa = load_panel(woa[:, cs], 4, 512)
                for half in range(2):
                    if half == 0:
                        pg_, Bg_, pw_, Bw_, nk, yoff = pga, Bga, pa, Bpa, 4, 0
                    else:
                        pg_, Bg_ = load_panel(w2g[:, 1024 + g4 * 512:1024 + g4 * 512 + 512], 8, 512)
                        pw_, Bw_ = load_panel(wob[:, cs], 8, 512)
                        nk, yoff = 8, 4
                    for o in range(4):
                        oc = g4 * 4 + o
                        osl = slice(o * 128, o * 128 + 128)
                        bg_, bw_ = gb2(), gb2()
                        for k in range(KC):
                            P.add("pe", lambda e, k=k, bg_=bg_, pg_=pg_, osl=osl: e.matmul(bank[bg_][:, 0:W], pg_[:, k, osl], hb[:, k, 0:W], start=(k == 0), stop=(k == KC - 1)),
                                  reads=[Bg_, B_hb], writes=[bankbuf[bg_]])
                        for k in range(nk):
                            P.add("pe", lambda e, k=k, bw_=bw_, pw_=pw_, osl=osl, yoff=yoff, nk=nk: e.matmul(bank[bw_][:, 0:W], pw_[:, k, osl], yb16[:, yoff + k, t0:t0 + W], start=(k == 0), stop=(k == nk - 1)),
                                  reads=[Bw_, B_yb16], writes=[bankbuf[bw_]])
                        P.add("act", lambda e, bg_=bg_: e.activation(out=l_t[:, 0:W], in_=bank[bg_][:, 0:W], func=AF.Sigmoid), reads=[bankbuf[bg_]], writes=[B_lt])
                        if half == 0:
                            P.add("dve", lambda e, bw_=bw_, oc=oc: e.tensor_tensor(out=bufA[:, oc, 0:W], in0=bank[bw_][:, 0:W], in1=l_t[:, 0:W], op=ALU.mult),
                                  reads=[bankbuf[bw_], B_lt], writes=[B_bufA])
                        else:
                            P.add("dve", lambda e, bw_=bw_: e.tensor_tensor(out=l_t3[:, 0:W], in0=bank[bw_][:, 0:W], in1=l_t[:, 0:W], op=ALU.mult),
                                  reads=[bankbuf[bw_], B_lt], writes=[B_lt3])
                            P.add("dve", lambda e, oc=oc: e.tensor_tensor(out=mixb[:, oc, 0:W], in0=l_t3[:, 0:W], in1=bufA[:, oc, 0:W], op=ALU.add),
                                  reads=[B_lt3, B_bufA], writes=[B_mixb])
            for g4 in range(2):
                pw_, Bw_ = load_panel(wo[:, g4 * 512:g4 * 512 + 512], 8, 512)
                for o in range(4):
                    oc = g4 * 4 + o
                    osl = slice(o * 128, o * 128 + 128)
                    b = gb2()
                    for k in range(KC):
                        P.add("pe", lambda e, k=k, b=b, pw_=pw_, osl=osl: e.matmul(bank[b][:, 0:W], pw_[:, k, osl], mixb[:, k, 0:W], start=(k == 0), stop=(k == KC - 1)),
                              reads=[Bw_, B_mixb], writes=[bankbuf[b]])
                    P.add("dve", lambda e, b=b, oc=oc: e.scalar_tensor_tensor(out=bufA[:, oc, 0:W], in0=bufH[:, oc, 0:W], scalar=ALPHA, in1=bank[b][:, 0:W], op0=ALU.mult, op1=ALU.add),
                          reads=[bankbuf[b], B_bufH], writes=[B_bufA])
            layer_norm(bufA, B_bufA, bufH1, B_bufH1, h1b, B_h1b, PV_G1, PV_B1, W)
            for c0_ in range(0, DFF, 512):
                ncol = min(512, DFF - c0_)
                pg_, Bg_ = load_panel(wg[:, c0_:c0_ + ncol], 8, ncol)
                pu_, Bu_ = load_panel(wu[:, c0_:c0_ + ncol], 8, ncol)
                for o in range(ncol // 128):
                    fc = c0_ // 128 + o
                    osl = slice(o * 128, o * 128 + 128)
                    bg_, bu_ = gb2(), gb2()
                    for k in range(KC):
                        P.add("pe", lambda e, k=k, bg_=bg_, pg_=pg_, osl=osl: e.matmul(bank[bg_][:, 0:W], pg_[:, k, osl], h1b[:, k, 0:W], start=(k == 0), stop=(k == KC - 1)),
                              reads=[Bg_, B_h1b], writes=[bankbuf[bg_]])
                    for k in range(KC):
                        P.add("pe", lambda e, k=k, bu_=bu_, pu_=pu_, osl=osl: e.matmul(bank[bu_][:, 0:W], pu_[:, k, osl], h1b[:, k, 0:W], start=(k == 0), stop=(k == KC - 1)),
                              reads=[Bu_, B_h1b], writes=[bankbuf[bu_]])
                    P.add("act", lambda e, bg_=bg_: e.activation(out=l_t[:, 0:W], in_=bank[bg_][:, 0:W], func=AF.Silu), reads=[bankbuf[bg_]], writes=[B_lt])
                    P.add("dve", lambda e, bu_=bu_, fc=fc: e.tensor_tensor(out=actb[:, fc, 0:W], in0=bank[bu_][:, 0:W], in1=l_t[:, 0:W], op=ALU.mult),
                          reads=[bankbuf[bu_], B_lt], writes=[B_actb])
            for g4 in range(2):
                P.add("pool", lambda e, g4=g4: e.dma_start(out=dpan[:, :, :], in_=wd[:, g4 * 512:g4 * 512 + 512].rearrange("(k p) c -> p k c", p=128)), writes=[B_dpan], key=B_dpan)
                for o in range(4):
                    oc = g4 * 4 + o
                    osl = slice(o * 128, o * 128 + 128)
                    b = gb2()
                    for k in range(FC):
                        P.add("pe", lambda e, k=k, b=b, osl=osl: e.matmul(bank[b][:, 0:W], dpan[:, k, osl], actb[:, k, 0:W], start=(k == 0), stop=(k == FC - 1)),
                              reads=[B_dpan, B_actb], writes=[bankbuf[b]])
                    P.add("dve", lambda e, b=b, oc=oc: e.scalar_tensor_tensor(out=bufA[:, oc, 0:W], in0=bufH1[:, oc, 0:W], scalar=ALPHA, in1=bank[b][:, 0:W], op0=ALU.mult, op1=ALU.add),
                          reads=[bankbuf[b], B_bufH1], writes=[B_bufA])
            layer_norm(bufA, B_bufA, bufH, B_bufH, None, None, PV_G2, PV_B2, W)
            P.add("sp", lambda e, t0=t0: e.dma_start(out=outv[:, :, t0:t0 + W], in_=bufH[:, :, 0:W]), reads=[B_bufH], writes=[B_out], key=B_out)
        if STOP == 0:
            for t2_ in range(NT2):
                p2_tile(t2_)
        else:
            P.add("sp", lambda e: e.nop(), reads=[B_ysrc])
        P.add("sp", lambda e: e.nop(), reads=[B_out])

        P.finalize()
        engsem = {e: [es.enter_context(nc.semaphore("sem_%s_%d" % (e, i))) for i in range(P.nep[e])] for e in ENGS}
        print("ops per engine", {e: len(P.ops[e]) for e in ENGS}, "epochs", P.nep)
        keysem = {}
        for e in ENGS:
            for op in P.ops[e]:
                if op.key is not None and id(op.key) not in keysem:
                    keysem[id(op.key)] = es.enter_context(nc.semaphore("k_" + op.key.name))
        block = es.enter_context(nc.Block())

        @block.tensor
        def _(h):
            P.emit("pe", h, engsem, keysem)

        @block.scalar
        def _(h):
            P.emit("act", h, engsem, keysem)

        @block.vector
        def _(h):
            P.emit("dve", h, engsem, keysem)

        @block.gpsimd
        def _(h):
            P.emit("pool", h, engsem, keysem)

        @block.sync
        def _(h):
            P.emit("sp", h, engsem, keysem)
    return nc


def make_consts():
    c = np.zeros((128, C_N), np.float32)
    i = np.arange(128)
    same = (i[:, None] // 64) == (i[None, :] // 64)
    c[:, C_ID:C_ID + 128] = np.eye(128, dtype=np.float32)
    c[:, C_LT2:C_LT2 + 128] = (same & (i[:, None] <= i[None, :])).astype(np.float32)
    c[:, C_BLK:C_BLK + 128] = same.astype(np.float32)
    c[:, C_MSL:C_MSL + 128] = np.where(same & (i[:, None] > i[None, :]), 0.0, NEG)
    c[:, C_MIU:C_MIU + 128] = np.where(same & (i[:, None] <= i[None, :]), 0.0, NEG)
    c[:, C_SEL + 0] = (i < 64)
    c[:, C_SEL + 1] = (i >= 64)
    for r in range(3):
        c[64 + r, C_MQ + r] = 1.0
        c[67 + r, C_MQ + 3] = 1.0
        c[67 + r, C_MK + r] = -1.0
        c[64 + r, C_MK + 3] = 1.0
    return c


def shard_inputs(inp):
    x = np.asarray(inp["x"], np.float32)[0]
    S = x.shape[0]
    L = ((64 + S + 127) // 128) * 128
    FS = S // NCORES
    xT = np.zeros((D, L), np.float32)
    xT[:, 48:64] = np.asarray(inp["meta_tokens"], np.float32).T
    xT[:, 64:64 + S] = x.T
    w_in = np.asarray(inp["w_in"], np.float32)[0]
    conv_w = np.asarray(inp["conv_w"], np.float32)[0]
    cst = make_consts()
    col = lambda v: np.ascontiguousarray(np.asarray(v, np.float32).reshape(KC, 128).T)
    maps = []
    for c in range(NCORES):
        w1 = np.zeros((D, W1COLS), np.float32)
        w1[:, G_QA:G_QA + 64] = w_in[:, c * 64:(c + 1) * 64]
        w1[:, G_KA:G_KA + 64] = w_in[:, 512 + c * 64:512 + (c + 1) * 64]
        w1[:, G_V:G_V + 64] = w_in[:, 1024 + c * 64:1024 + (c + 1) * 64]
        for r in range(6):
            w1[:, G_V + 64 + r] = w_in[:, 1536 + c]
        w1[:, G_V + 70] = w_in[:, 4616 + c]
        w1[:, G_V + 71] = w_in[:, 4624 + c]
        w1[:, G_QB:G_QB + 128] = w_in[:, 1544 + c * 128:1544 + (c + 1) * 128]
        w1[:, G_KB:G_KB + 128] = w_in[:, 2568 + c * 128:2568 + (c + 1) * 128]
        w1[:, G_VB:G_VB + 128] = w_in[:, 3592 + c * 128:3592 + (c + 1) * 128]
        w1[:, G_Z:G_Z + 128] = w_in[:, 4632 + c * 128:4632 + (c + 1) * 128]
        pv = np.zeros((128, PV_N), np.float32)
        pv[:, PV_G0:PV_G0 + 8] = col(inp["ln_in_g"])
        pv[:, PV_B0:PV_B0 + 8] = col(inp["ln_in_b"])
        for t, base in enumerate((0, 1024, 2048)):
            pv[:, PV_CW + 4 * t:PV_CW + 4 * t + 4] = conv_w[:, base + c * 128:base + (c + 1) * 128].T
        pv[:, PV_BF] = np.asarray(inp["b_f"], np.float32)[0, c]
        pv[:, PV_ALOG] = np.asarray(inp["a_log"], np.float32)[0, c]
        pv[:, PV_DT] = np.asarray(inp["dt_bias"], np.float32)[0, c]
        pv[:, PV_GNW] = np.asarray(inp["gdn_norm_w"], np.float32)[0]
        pv[:, PV_G1:PV_G1 + 8] = col(inp["ln1_g"])
        pv[:, PV_B1:PV_B1 + 8] = col(inp["ln1_b"])
        pv[:, PV_G2:PV_G2 + 8] = col(inp["ln2_g"])
        pv[:, PV_B2:PV_B2 + 8] = col(inp["ln2_b"])
        maps.append({
            "xT": xT, "xown": np.ascontiguousarray(xT[:, 64 + c * FS:64 + (c + 1) * FS]), "w1": w1, "pv": pv, "cst": cst,
            "w2g": np.ascontiguousarray(w_in[:, 5656:7704]),
            "woa": np.asarray(inp["w_out_a"], np.float32)[0], "wob": np.asarray(inp["w_out_b"], np.float32)[0],
            "wo": np.asarray(inp["w_o"], np.float32)[0], "wg": np.asarray(inp["w_gate"], np.float32)[0],
            "wu": np.asarray(inp["w_up"], np.float32)[0], "wd": np.asarray(inp["w_down"], np.float32)[0],
        })
    return maps, S


P1_KEYS = ("xT", "w1", "pv", "cst")
P2_KEYS = ("xown", "pv", "cst", "w2g", "woa", "wob", "wo", "wg", "wu", "wd")


def kernel_fused(**inputs):
    maps, S = shard_inputs(inputs)
    nc = build(S, "fused")
    res = run_bass_kernel_spmd(nc, maps, core_ids=list(range(NCORES)))
    out = np.concatenate([np.asarray(res.results[c]["outT"], np.float32).T for c in range(NCORES)], axis=0)
    return out[None].astype(np.float32)


def kernel(**inputs):
    maps, S = shard_inputs(inputs)
    FS = S // NCORES
    nc1 = build(S, "p1")
    r1 = run_bass_kernel_spmd(nc1, maps, core_ids=list(range(NCORES)))
    ys = [np.asarray(r1.results[c]["ysrc"]).reshape(NCORES, 192, FS) for c in range(NCORES)]
    maps2 = []
    for c in range(NCORES):
        m2 = dict(maps[c])
        m2["yin"] = np.ascontiguousarray(np.concatenate([ys[r][c] for r in range(NCORES)], axis=0))
        maps2.append(m2)
    nc2 = build(S, "p2")
    res = run_bass_kernel_spmd(nc2, maps2, core_ids=list(range(NCORES)))
    out = np.concatenate([np.asarray(res.results[c]["outT"], np.float32).T for c in range(NCORES)], axis=0)
    return out[None].astype(np.float32)
```

```python
import contextlib
import numpy as np
import concourse.bass as bass
import concourse.mybir as mybir
from concourse.bass_utils import run_bass_kernel_spmd

F32 = mybir.dt.float32
BF16 = mybir.dt.bfloat16
ALU = mybir.AluOpType
AF = mybir.ActivationFunctionType

NCORES = 8
D = 1024
KC = 8
DFF = 2816
FC = 22
ALPHA = 2.0 ** 0.25
LN_EPS = 1e-5
NORM_EPS = 1e-6
NEG = -30000.0
G_QA, G_KA, G_V, G_QB, G_KB, G_VB, G_Z = 0, 64, 128, 200, 328, 456, 584
W1COLS = 712
GROUPS = [(G_QA, 64), (G_KA, 64), (G_V, 72), (G_QB, 128), (G_KB, 128), (G_VB, 128), (G_Z, 128)]
PV_G0, PV_B0 = 0, 8
PV_CW = 16
PV_BF, PV_ALOG, PV_DT, PV_GNW = 28, 29, 30, 31
PV_G1, PV_B1, PV_G2, PV_B2 = 32, 40, 48, 56
PV_N = 64
C_ID, C_LT2, C_BLK, C_MSL, C_MIU, C_SEL, C_MQ, C_MK = 0, 128, 256, 384, 512, 640, 642, 646
C_N = 650

ENGS = ("pe", "act", "dve", "pool", "sp")
EPOCH = 16000
MERGE_CHUNK = 1
MERGE_W = (1.0, 1.0, 1.0)


class Buf:
    __slots__ = ("w", "r", "name", "excl")

    def __init__(self, name="", excl=False):
        self.w = None
        self.r = []
        self.name = name
        self.excl = excl


class Op:
    __slots__ = ("eng", "fn", "deps", "sig", "val", "key", "inc", "ep")


class Prog:
    def __init__(self):
        self.ops = {e: [] for e in ENGS}
        self.pending = {e: [] for e in ENGS}
        self.dmas = {}
        self.cap = None

    def begin_capture(self):
        self.cap = []

    def end_capture(self):
        c, self.cap = self.cap, None
        return c

    def add_merged(self, lists):
        idx = [0] * len(lists)
        tot = [max(1, len(l)) for l in lists]
        while True:
            best, bf = -1, 2.0
            for i, l in enumerate(lists):
                if idx[i] < len(l):
                    f = idx[i] / tot[i] * (MERGE_W[i] if len(lists) > 1 else 1.0)
                    if f < bf:
                        best, bf = i, f
            if best < 0:
                break
            for _ in range(MERGE_CHUNK):
                if idx[best] < len(lists[best]):
                    self.add(*lists[best][idx[best]])
                    idx[best] += 1

    def add(self, eng, fn, reads=(), writes=(), key=None, inc=16):
        if self.cap is not None:
            self.cap.append((eng, fn, reads, writes, key, inc))
            return None
        op = Op()
        op.eng, op.fn, op.sig, op.val, op.key, op.inc = eng, fn, False, 0, key, inc
        deps = list(self.pending[eng])
        self.pending[eng] = []
        ex = [b for b in reads if b.excl]
        if ex:
            writes = list(writes) + ex
            reads = [b for b in reads if not b.excl]
        raw = set()
        for b in reads:
            if b.w is not None:
                deps.append(b.w)
                raw.add(id(b.w))
        for b in writes:
            if b.w is not None:
                deps.append(b.w)
                if b.excl:
                    raw.add(id(b.w))
            deps.extend(b.r)
        need, seen = [], set()
        for d in deps:
            if id(d) in seen:
                continue
            seen.add(id(d))
            if d.key is None and key is None and d.eng == eng and (eng == "pe" or id(d) not in raw):
                continue
            if d.key is None:
                d.sig = True
            need.append(d)
        op.deps = need
        for b in reads:
            b.r.append(op)
        for b in writes:
            b.w = op
            b.r = []
        self.ops[eng].append(op)
        if key is not None:
            self.dmas[id(key)] = op
        return op

    def barrier(self):
        last = []
        for e in ENGS:
            if self.ops[e]:
                o = self.ops[e][-1]
                if o.key is None:
                    o.sig = True
                last.append(o)
        last.extend(self.dmas.values())
        for e in ENGS:
            self.pending[e] = list(last)

    def finalize(self):
        keycnt = {}
        for e in ENGS:
            cnt = 0
            for op in self.ops[e]:
                if op.key is not None:
                    k = id(op.key)
                    keycnt[k] = keycnt.get(k, 0) + op.inc
                    op.val = keycnt[k]
                elif op.sig:
                    cnt += 1
                    op.ep = (cnt - 1) // EPOCH
                    op.val = (cnt - 1) % EPOCH + 1
            self.nep = getattr(self, "nep", {})
            self.nep[e] = (cnt - 1) // EPOCH + 1 if cnt else 1

    def emit(self, eng, h, engsem, keysem):
        waited = {}
        for op in self.ops[eng]:
            for d in op.deps:
                s = keysem[id(d.key)] if d.key is not None else engsem[d.eng][d.ep]
                sid = id(s)
                if waited.get(sid, 0) < d.val:
                    h.wait_ge(s, d.val)
                    waited[sid] = d.val
            ins = op.fn(h)
            if op.key is not None:
                ins.then_inc(keysem[id(op.key)], op.inc)
            elif op.sig:
                ins.then_inc(engsem[eng][op.ep], 1)


def build(S, mode="fused"):
    L = ((64 + S + 127) // 128) * 128
    NBLK = L // 128
    tiles = []
    p = 0
    while p < L:
        w = min(512, L - p)
        tiles.append((p, w))
        p += w
    FS = S // NCORES
    W2 = min(512, FS)
    NT2 = FS // W2

    nc = bass.Bass("TRN2", target_bir_lowering=False)
    dt_in = lambda n, shp: nc.dram_tensor(n, shp, F32, kind="ExternalInput").ap()
    xT = dt_in("xT", [D, L])
    xown = dt_in("xown", [D, FS])
    w1 = dt_in("w1", [D, W1COLS])
    pv_d = dt_in("pv", [128, PV_N])
    cst_d = dt_in("cst", [128, C_N])
    w2g = dt_in("w2g", [D, 2048])
    woa = dt_in("woa", [512, D])
    wob = dt_in("wob", [D, D])
    wo = dt_in("wo", [D, D])
    wg = dt_in("wg", [D, DFF])
    wu = dt_in("wu", [D, DFF])
    wd = dt_in("wd", [DFF, D])
    if mode != "p1":
        outT = nc.dram_tensor("outT", [D, FS], F32, kind="ExternalOutput").ap()
    if mode == "p1":
        ysrc = nc.dram_tensor("ysrc", [NCORES * 192, FS], BF16, kind="ExternalOutput").ap()
    else:
        ysrc = nc.dram_tensor("ysrc", [NCORES * 192, FS], BF16).ap()
    DBG = 0
    if mode == "p2":
        yin = nc.dram_tensor("yin", [NCORES * 192, FS], BF16, kind="ExternalInput").ap()
    agout = nc.dram_tensor("ystage", [16 * NCORES * 96, FS], BF16).ap()
    agin = nc.dram_tensor("agin", [96, FS], BF16).ap()
    agbuf = nc.dram_tensor("agbuf", [NCORES * 96, FS], BF16).ap()

    P = Prog()
    es = contextlib.ExitStack()
    with es:
        ARENA_F = 50 * 1024
        arena = es.enter_context(nc.sbuf_tensor("arena", [128, ARENA_F], F32))
        psum = es.enter_context(nc.psum_tensor("ps", [128, 4096], F32))
        bank = [psum[:, b * 512:(b + 1) * 512] for b in range(8)]
        bankbuf = [Buf("bank%d" % b, excl=True) for b in range(8)]

        class Arena:
            def __init__(self):
                self.off = 0

            def f32(self, n):
                a = arena[:, self.off:self.off + n]
                self.off += n
                assert self.off <= ARENA_F, "SBUF arena overflow %d" % self.off
                return a

            def bf16(self, n):
                m = (n + 1) // 2
                a = arena[:, self.off:self.off + m].bitcast(BF16)
                self.off += m
                assert self.off <= ARENA_F, "SBUF arena overflow %d" % self.off
                return a[:, 0:n]

        A = Arena()
        cst = A.f32(C_N)
        pv = A.f32(PV_N)
        ones_f = A.f32(128)
        identb = A.bf16(128)
        onesb = A.bf16(128)
        B_cst, B_pv, B_misc = Buf("cst"), Buf("pv"), Buf("misc")
        ident = cst[:, C_ID:C_ID + 128]
        P.add("sp", lambda e: e.dma_start(out=cst, in_=cst_d), writes=[B_cst], key=B_cst)
        P.add("sp", lambda e: e.dma_start(out=pv, in_=pv_d), writes=[B_pv], key=B_pv)
        P.add("pool", lambda e: e.memset(ones_f, 1.0), writes=[B_misc])
        P.add("pool", lambda e: e.memset(onesb, 1.0 / 1024.0), writes=[B_misc])
        P.add("pool", lambda e: e.tensor_copy(out=identb, in_=ident), reads=[B_cst], writes=[B_misc])
        CONSTS = [B_cst, B_pv, B_misc]
        base_off = A.off

        gp = [5]
        gmode = [0]

        def gbank():
            if gmode[0] == 1:
                gp[0] = 6 if gp[0] == 5 else 5
                return gp[0]
            if gmode[0] == 2:
                return 4
            b = gp[0]
            gp[0] = 5 + (gp[0] - 5 + 1) % 3
            return b

        gpa = [3]

        DO_P1 = mode != "p2"
        Wc = A.bf16(KC * W1COLS).rearrange("p (k c) -> p k c", k=KC)
        B_Wc = Buf("Wc")
        bW = A.f32(8)
        prep_off = A.off
        csum = A.f32(W1COLS)
        stg = [A.f32(W1COLS), A.f32(W1COLS)]
        B_stg = [Buf("stg0"), Buf("stg1")]
        wgt = A.f32(W1COLS)
        B_wgt = Buf("wgt")
        B_bW, B_csum = Buf("bW"), Buf("csum")
        w1v = w1.rearrange("(k p) c -> p k c", p=128)
        psc = [bank[5], bank[6]]
        for k in range(KC if DO_P1 else 0):
            s = stg[k % 2]
            P.add("sp", lambda e, s=s, k=k: e.dma_start(out=s, in_=w1v[:, k, :]), writes=[B_stg[k % 2]], key=B_stg[k % 2])
            P.add("dve", lambda e, s=s, k=k: e.tensor_scalar(out=wgt, in0=s, scalar1=pv[:, PV_G0 + k:PV_G0 + k + 1], scalar2=None, op0=ALU.mult),
                  reads=[B_stg[k % 2], B_pv], writes=[B_wgt])
            P.add("pe", lambda e, k=k: e.matmul(psc[0][:, 0:512], ones_f, wgt[:, 0:512], start=(k == 0), stop=(k == KC - 1)),
                  reads=[B_wgt, B_misc], writes=[bankbuf[5]])
            P.add("pe", lambda e, k=k: e.matmul(psc[1][:, 0:W1COLS - 512], ones_f, wgt[:, 512:W1COLS], start=(k == 0), stop=(k == KC - 1)),
                  reads=[B_wgt, B_misc], writes=[bankbuf[6]])
            order = [3, 0, 1, 2, 4, 5, 6]
            for oi, gi in enumerate(order):
                go, gm = GROUPS[gi]
                P.add("pe", lambda e, s=s, k=k, gi=gi, go=go, gm=gm, oi=oi: e.matmul(bank[7][0:gm, gi:gi + 1], s[:, go:go + gm], pv[:, PV_B0 + k:PV_B0 + k + 1],
                                                                                    start=(k == 0 and oi == 0), stop=(k == KC - 1 and oi == 6), skip_group_check=True),
                      reads=[B_stg[k % 2], B_pv], writes=[bankbuf[7]])
        if DO_P1:
            P.add("act", lambda e: e.mul(csum[:, 0:512], psc[0][:, 0:512], 1.0 / 1024.0), reads=[bankbuf[5]], writes=[B_csum])
            P.add("act", lambda e: e.mul(csum[:, 512:W1COLS], psc[1][:, 0:W1COLS - 512], 1.0 / 1024.0), reads=[bankbuf[6]], writes=[B_csum])
            for gi_, (go_, gm_) in enumerate(GROUPS):
                P.add("dve", lambda e, gi_=gi_, gm_=gm_: e.tensor_copy(out=bW[0:gm_, gi_:gi_ + 1], in_=bank[7][0:gm_, gi_:gi_ + 1]), reads=[bankbuf[7]], writes=[B_bW])
        for k in range(KC if DO_P1 else 0):
            s = stg[k % 2]
            P.add("sp", lambda e, s=s, k=k: e.dma_start(out=s, in_=w1v[:, k, :]), writes=[B_stg[k % 2]], key=B_stg[k % 2])
            P.add("dve", lambda e, s=s, k=k: e.scalar_tensor_tensor(out=Wc[:, k, :], in0=s, scalar=pv[:, PV_G0 + k:PV_G0 + k + 1], in1=csum,
                                                                    op0=ALU.mult, op1=ALU.subtract),
                  reads=[B_stg[k % 2], B_pv, B_csum], writes=[B_Wc])
        P.barrier()
        A.off = prep_off
        sc1 = A.f32(8)
        B_sc1 = Buf("sc1")
        if DO_P1:
          P.add("act", lambda e: e.activation(out=sc1[:, 0:1], in_=pv[:, PV_ALOG:PV_ALOG + 1], func=AF.Exp), reads=[B_pv], writes=[B_sc1])
          P.add("dve", lambda e: e.tensor_scalar(out=sc1[:, 0:1], in0=sc1[:, 0:1], scalar1=-1.0, scalar2=None, op0=ALU.mult), reads=[B_sc1], writes=[B_sc1])
          P.add("dve", lambda e: e.tensor_scalar(out=sc1[:, 1:2], in0=bW[:, 0:1], scalar1=0.125, scalar2=None, op0=ALU.mult), reads=[B_bW], writes=[B_sc1])

        Kaug = A.bf16(L)
        Vc = A.bf16(NBLK * 65).rearrange("p (b c) -> p b c", c=65)
        B_K = [Buf("K%d" % i) for i in range(len(tiles))]
        B_V = [Buf("V%d" % i) for i in range(len(tiles))]
        B_Vinit = Buf("Vinit")
        if DO_P1:
            P.add("pool", lambda e: e.memset(Vc[:, :, 64:65], 1.0), writes=[B_Vinit])

        xb = [A.bf16(KC * 512).rearrange("p (k w) -> p k w", k=KC) for _ in range(2)]
        B_xb = [Buf("xb0"), Buf("xb1")]
        sq = A.bf16(KC * 512).rearrange("p (k w) -> p k w", k=KC)
        B_sq = Buf("sq")

        def T32(name):
            return A.f32(512), Buf(name)

        def T16(name):
            return A.bf16(512), Buf(name)

        mean_sb, B_mean = T32("mean")
        rstd, B_rstd = T32("rstd")
        tmpA, B_tmpA = T32("tmpA")
        tmpB, B_tmpB = T32("tmpB")
        Qaug_l = [A.bf16(512), A.bf16(512)]
        B_Q_l = [Buf("Qaug0"), Buf("Qaug1")]
        VG, B_VG = T32("VG")
        X_l = [[A.f32(516) for _ in range(3)]] * 2
        B_X_l = [[Buf("X_%d" % t) for t in range(3)]] * 2
        zs_l = [A.f32(512), A.f32(512)]
        B_zs_l = [Buf("zs0"), Buf("zs1")]
        ctile, B_c = T32("c")
        lrow, B_lrow = T32("lrow")
        lrow2, B_lrow2 = T32("lrow2")
        hiB, B_hi = T16("hi"); loB, B_lo = T16("lo"); lo2B, B_lo2 = T16("lo2")
        r1, B_r1 = T32("r1")
        accr, B_accr = T32("accr")
        onesrow, B_onesrow = T32("onesrow")
        bigm, B_bigm = T32("bigm")
        carry = A.f32(2)
        B_carry = Buf("carry")
        PT = [A.bf16(1024).rearrange("p (j w) -> p j w", j=2) for _ in range(2)]
        B_PT = [Buf("PT0"), Buf("PT1")]
        B_PT4 = [Buf("PT4_%d" % i) for i in range(4)]
        O_sb, B_Osb = T32("Osb")
        rden, B_rden = T32("rden")
        YA, B_YA = T16("YA")
        YB, B_YB = T16("YB")
        halo = [A.f32(12) for _ in range(3)]
        B_halo = [Buf("halo%d" % i) for i in range(3)]
        sctm_l = [A.f32(32).rearrange("p (j c) -> p j c", c=8) for _ in range(2)]
        B_sctm_l = [Buf("sctm0"), Buf("sctm1")]
        cq, B_cq = T32("cq"); ck, B_ck = T32("ck"); cv, B_cv = T32("cv")
        qs, B_qs = cq, B_cq
        ks, B_ks = ck, B_ck
        sqq, B_sqq = T16("sqq"); sqk, B_sqk = sqq, B_sqq
        sqb16, B_sqb16 = sqq, B_sqq
        sqob, B_sqob = T16("sqob")
        rnq, B_rnq = T32("rnq"); rnk, B_rnk = rnq, B_rnq
        qnT_l = [T16("qnT0"), T16("qnT1")]; knT_l = [T16("knT0"), T16("knT1")]; vsT_l = [T16("vsT0"), T16("vsT1")]
        tms_l = [A.f32(64), A.f32(64)]
        B_tms_l = [Buf("tms0"), Buf("tms1")]
        rhs8_l = [A.f32(8), A.f32(8)]
        glbc_l = [A.f32(8), A.f32(8)]
        B_glbc_l = [Buf("glbc0"), Buf("glbc1")]
        Xuw_l = [A.bf16(4 * 256).rearrange("p (j c) -> p j c", c=256) for _ in range(2)]
        B_Xuw_l = [Buf("Xuw0"), Buf("Xuw1")]
        kdec_l = [A.bf16(4 * 128).rearrange("p (j c) -> p j c", c=128) for _ in range(2)]
        B_kdec_l = [Buf("kdec0"), Buf("kdec1")]
        UW = A.bf16(4 * 256).rearrange("p (j c) -> p j c", c=256)
        B_UW = [Buf("UW%d" % j) for j in range(4)]
        diag = [A.f32(128) for _ in range(2)]
        B_diag = [Buf("diag0"), Buf("diag1")]
        T1 = [A.f32(128) for _ in range(2)]
        B_T1 = [Buf("T1a"), Buf("T1b")]
        T3 = [A.f32(128) for _ in range(2)]
        B_T3 = [Buf("T3a"), Buf("T3b")]
        Dsl = [A.f32(128) for _ in range(2)]
        B_Dsl = [Buf("Dsl0"), Buf("Dsl1")]
        Diu = [A.f32(128) for _ in range(2)]
        B_Diu = [Buf("Diu0"), Buf("Diu1")]
        EGR = [A.f32(128) for _ in range(4)]
        B_EGR = [Buf("EGR%d" % j) for j in range(4)]
        attnT = [A.bf16(128) for _ in range(4)]
        B_attnT = [Buf("attnT%d" % j) for j in range(4)]
        PP = [[A.bf16(256) for _ in range(2)] for _ in range(4)]
        B_PP = [[Buf("PP%d_%d" % (j, q)) for q in range(2)] for j in range(4)]
        RR = [[A.bf16(128) for _ in range(2)] for _ in range(4)]
        B_RR = [[Buf("RR%d_%d" % (j, q)) for q in range(2)] for j in range(4)]
        qdecT, B_qdecT = T32("qdecT")
        QeffT, B_QeffT = T32("QeffT")
        MT = [A.f32(128) for _ in range(8)]
        B_MT = [Buf("MT%d" % j) for j in range(8)]
        Bn = [A.f32(128) for _ in range(8)]
        B_Bn = [Buf("Bn%d" % j) for j in range(8)]
        Sst = [A.f32(128) for _ in range(9)]
        B_S = [Buf("S%d" % j) for j in range(9)]
        sqo, B_sqo = T32("sqo")
        rno, B_rno = T32("rno")
        p1_end = A.off

        if DO_P1:
            P.add("pool", lambda e: e.memset(Sst[0], 0.0), writes=[B_S[0]])
            P.add("pool", lambda e: e.memset(onesrow, 1.0), writes=[B_onesrow])
            P.add("pool", lambda e: e.memset(bigm, 0.0), writes=[B_bigm])
            P.add("pool", lambda e: e.memset(bigm[64:70, 0:48], 30000.0), writes=[B_bigm])
            for t_ in range(3):
                P.add("pool", lambda e, t_=t_: e.memset(X_l[0][t_][:, 0:3], 0.0), writes=[B_X_l[0][t_]])

        xTv = xT.rearrange("(k p) l -> p k l", p=128)
        ysv = ysrc.rearrange("(s f) t -> s f t", f=192)
        B_ysrc = Buf("ysrc")
        RS = slice(64, 70)
        cwq = lambda j: pv[:, PV_CW + j:PV_CW + j + 1]
        s_cur = [0]

        def y_out(src, rows0, nrows, c0, W, Bsrc):
            p = max(c0, 64)
            end = min(c0 + W, 64 + S)
            while p < end:
                f = p - 64
                sh = f // FS
                fe = min(end - 64, (sh + 1) * FS)
                n = fe - f
                P.add("sp", lambda e, sh=sh, f=f, n=n, p=p: e.dma_start(out=ysv[sh, rows0:rows0 + nrows, f - sh * FS:f - sh * FS + n],
                                                                        in_=src[0:nrows, p - c0:p - c0 + n]),
                      reads=[Bsrc], writes=[B_ysrc], key=B_ysrc)
                p += n

        STAGE = 99

        def p1_tile(ti, c0, W):
            nb = W // 128
            blk0 = c0 // 128
            par = ti % 2
            xbt = xb[par]
            Qaug, B_Q = Qaug_l[par], B_Q_l[par]
            zs, B_zs = zs_l[par], B_zs_l[par]
            (Xq, Xk, Xv), (B_Xq, B_Xk, B_Xv) = X_l[par], B_X_l[par]
            sctm, B_sctm = sctm_l[par], B_sctm_l[par]
            (qnT, B_qnT), (knT, B_knT), (vsT, B_vsT) = qnT_l[par], knT_l[par], vsT_l[par]
            tms, B_tms = tms_l[par], B_tms_l[par]
            t_beta, t_nbeta, t_x, t_ax, t_e, t_l, t_g, t_gc, t_ngc, t_e1, t_e2, t_d = [tms[:, 4 * i_:4 * i_ + 4] for i_ in range(12)]
            rhs8, glbc, B_glbc = rhs8_l[par], glbc_l[par], B_glbc_l[par]
            Xuw, B_Xuw, kdec, B_kdec = Xuw_l[par], B_Xuw_l[par], kdec_l[par], B_kdec_l[par]
            gmode[0] = 2
            P.begin_capture()
            P.add("pool", lambda e, xbt=xbt, c0=c0, W=W: e.dma_start(out=xbt[:, :, 0:W], in_=xTv[:, :, c0:c0 + W]), writes=[B_xb[par]], key=B_xb[par])
            P.add("pool", lambda e, xbt=xbt, W=W: e.tensor_tensor(out=sq[:, :, 0:W], in0=xbt[:, :, 0:W], in1=xbt[:, :, 0:W], op=ALU.mult),
                  reads=[B_xb[par]], writes=[B_sq])
            b1, b2 = gbank(), gbank()
            for k in range(KC):
                P.add("pe", lambda e, k=k, b1=b1, xbt=xbt, W=W: e.matmul(bank[b1][:, 0:W], onesb, xbt[:, k, 0:W], start=(k == 0), stop=(k == KC - 1)),
                      reads=[B_xb[par], B_misc], writes=[bankbuf[b1]])
            P.add("act", lambda e, b1=b1, W=W: e.copy(mean_sb[:, 0:W], bank[b1][:, 0:W]), reads=[bankbuf[b1]], writes=[B_mean])
            for k in range(KC):
                P.add("pe", lambda e, k=k, b2=b2, W=W: e.matmul(bank[b2][:, 0:W], onesb, sq[:, k, 0:W], start=(k == 0), stop=(k == KC - 1)),
                      reads=[B_sq, B_misc], writes=[bankbuf[b2]])
            P.add("dve", lambda e, W=W: e.tensor_tensor(out=tmpA[:, 0:W], in0=mean_sb[:, 0:W], in1=mean_sb[:, 0:W], op=ALU.mult), reads=[B_mean], writes=[B_tmpA])
            P.add("dve", lambda e, b2=b2, W=W: e.tensor_tensor(out=tmpA[:, 0:W], in0=bank[b2][:, 0:W], in1=tmpA[:, 0:W], op=ALU.subtract),
                  reads=[bankbuf[b2], B_tmpA], writes=[B_tmpA])
            P.add("dve", lambda e, W=W: e.tensor_scalar(out=tmpA[:, 0:W], in0=tmpA[:, 0:W], scalar1=0.0, scalar2=LN_EPS, op0=ALU.max, op1=ALU.add),
                  reads=[B_tmpA], writes=[B_tmpA])
            P.add("act", lambda e, W=W: e.activation(out=tmpA[:, 0:W], in_=tmpA[:, 0:W], func=AF.Ln), reads=[B_tmpA], writes=[B_tmpA])
            P.add("act", lambda e, W=W: e.activation(out=rstd[:, 0:W], in_=tmpA[:, 0:W], func=AF.Exp, scale=-0.5), reads=[B_tmpA], writes=[B_rstd])

            def proj(go, gm):
                b = gbank()
                for k in range(KC):
                    P.add("pe", lambda e, k=k, b=b: e.matmul(bank[b][0:gm, 0:W], Wc[:, k, go:go + gm], xbt[:, k, 0:W], start=(k == 0), stop=(k == KC - 1)),
                          reads=[B_Wc, B_xb[par]], writes=[bankbuf[b]])
                return b

            def evac(b, gm, gi, out_ap, Bout, func=AF.Identity, scale=1.0, bias_ap=None, tmp=None, Btmp=None):
                tmp_, Bt = (tmpB, B_tmpB) if tmp is None else (tmp, Btmp)
                P.add("dve", lambda e: e.tensor_tensor(out=tmp_[0:gm, 0:W], in0=bank[b][0:gm, 0:W], in1=rstd[0:gm, 0:W], op=ALU.mult),
                      reads=[bankbuf[b], B_rstd], writes=[Bt])
                bia = bW[0:gm, gi:gi + 1] if bias_ap is None else bias_ap
                P.add("act", lambda e: e.activation(out=out_ap, in_=tmp_[0:gm, 0:W], func=func, bias=bia, scale=scale),
                      reads=[Bt, B_bW, B_sc1], writes=[Bout])

            b = proj(G_QA, 64)
            evac(b, 64, 0, Qaug[0:64, 0:W], B_Q, scale=0.125, bias_ap=sc1[0:64, 1:2])
            b = proj(G_KA, 64)
            evac(b, 64, 1, Kaug[0:64, c0:c0 + W], B_K[ti])
            b = proj(G_V, 72)
            evac(b, 72, 2, VG[0:72, 0:W], B_VG)
            b = proj(G_QB, 128)
            evac(b, 128, 3, Xq[:, 3:3 + W], B_Xq)
            b = proj(G_KB, 128)
            evac(b, 128, 4, Xk[:, 3:3 + W], B_Xk)
            b = proj(G_VB, 128)
            evac(b, 128, 5, Xv[:, 3:3 + W], B_Xv)
            b = proj(G_Z, 128)
            evac(b, 128, 6, zs[:, 0:W], B_zs, func=AF.Silu)
            if ti == 0:
                for X_, B_ in ((Xq, B_Xq), (Xk, B_Xk), (Xv, B_Xv)):
                    P.add("pool", lambda e, X_=X_: e.memset(X_[:, 3:3 + 48], 0.0), writes=[B_])

            for j in range(nb):
                b = gbank()
                P.add("pe", lambda e, j=j, b=b: e.transpose(bank[b][:, 0:72], VG[0:72, j * 128:(j + 1) * 128], ident[0:72, 0:72]),
                      reads=[B_VG, B_cst], writes=[bankbuf[b]])
                P.add("act", lambda e, j=j, b=b: e.copy(Vc[:, blk0 + j, 0:64], bank[b][:, 0:64]), reads=[bankbuf[b], B_Vinit], writes=[B_V[ti]])
                P.add("dve", lambda e, j=j, b=b: e.tensor_copy(out=sctm[:, j, :], in_=bank[b][:, 64:72]), reads=[bankbuf[b]], writes=[B_sctm])

            P.add("dve", lambda e: e.tensor_scalar(out=lrow[RS, 0:W], in0=VG[RS, 0:W], scalar1=pv[RS, PV_BF:PV_BF + 1], scalar2=-1.0, op0=ALU.add, op1=ALU.mult),
                  reads=[B_VG, B_pv], writes=[B_lrow])
            P.add("dve", lambda e: e.tensor_scalar(out=lrow2[RS, 0:W], in0=lrow[RS, 0:W], scalar1=-1.0, scalar2=None, op0=ALU.mult), reads=[B_lrow], writes=[B_lrow2])
            P.add("dve", lambda e: e.tensor_tensor(out=lrow2[RS, 0:W], in0=lrow2[RS, 0:W], in1=lrow[RS, 0:W], op=ALU.max), reads=[B_lrow, B_lrow2], writes=[B_lrow2])
            P.add("act", lambda e: e.activation(out=lrow2[RS, 0:W], in_=lrow2[RS, 0:W], func=AF.Exp, scale=-1.0), reads=[B_lrow2], writes=[B_lrow2])
            P.add("act", lambda e: e.activation(out=lrow2[RS, 0:W], in_=lrow2[RS, 0:W], func=AF.Ln, bias=1.0), reads=[B_lrow2], writes=[B_lrow2])
            P.add("dve", lambda e: e.scalar_tensor_tensor(out=lrow[RS, 0:W], in0=lrow[RS, 0:W], scalar=0.0, in1=lrow2[RS, 0:W], op0=ALU.max, op1=ALU.add),
                  reads=[B_lrow, B_lrow2], writes=[B_lrow])
            init = 0.0 if ti == 0 else carry[RS, 0:1]
            P.add("dve", lambda e, init=init: e.tensor_tensor_scan(out=ctile[RS, 0:W], data0=onesrow[RS, 0:W], data1=lrow[RS, 0:W], initial=init,
                                                                  op0=ALU.mult, op1=ALU.subtract),
                  reads=[B_lrow, B_onesrow, B_carry], writes=[B_c])
            P.add("dve", lambda e: e.tensor_copy(out=carry[RS, 0:1], in_=ctile[RS, W - 1:W]), reads=[B_c], writes=[B_carry])

            def split3(src, Bsrc):
                P.add("dve", lambda e: e.tensor_copy(out=hiB[RS, 0:W], in_=src[RS, 0:W]), reads=[Bsrc], writes=[B_hi])
                P.add("dve", lambda e: e.tensor_tensor(out=r1[RS, 0:W], in0=src[RS, 0:W], in1=hiB[RS, 0:W], op=ALU.subtract), reads=[Bsrc, B_hi], writes=[B_r1])
                P.add("dve", lambda e: e.tensor_copy(out=loB[RS, 0:W], in_=r1[RS, 0:W]), reads=[B_r1], writes=[B_lo])
                P.add("dve", lambda e: e.tensor_tensor(out=r1[RS, 0:W], in0=r1[RS, 0:W], in1=loB[RS, 0:W], op=ALU.subtract), reads=[B_r1, B_lo], writes=[B_r1])
                P.add("dve", lambda e: e.tensor_copy(out=lo2B[RS, 0:W], in_=r1[RS, 0:W]), reads=[B_r1], writes=[B_lo2])

            def augrows(mc, out_ap, Bout):
                m = lambda i: cst[RS, mc + i:mc + i + 1]
                P.add("dve", lambda e: e.tensor_scalar(out=accr[RS, 0:W], in0=hiB[RS, 0:W], scalar1=m(0), scalar2=m(3), op0=ALU.mult, op1=ALU.add),
                      reads=[B_hi, B_cst], writes=[B_accr])
                P.add("dve", lambda e: e.scalar_tensor_tensor(out=accr[RS, 0:W], in0=loB[RS, 0:W], scalar=m(1), in1=accr[RS, 0:W], op0=ALU.mult, op1=ALU.add),
                      reads=[B_lo, B_accr, B_cst], writes=[B_accr])
                P.add("dve", lambda e: e.scalar_tensor_tensor(out=out_ap, in0=lo2B[RS, 0:W], scalar=m(2), in1=accr[RS, 0:W], op0=ALU.mult, op1=ALU.add),
                      reads=[B_lo2, B_accr, B_cst], writes=[Bout])

            split3(ctile, B_c)
            augrows(C_MQ, Qaug[RS, 0:W], B_Q)
            if ti == 0:
                P.add("dve", lambda e: e.tensor_tensor(out=lrow2[RS, 0:W], in0=ctile[RS, 0:W], in1=bigm[RS, 0:W], op=ALU.add), reads=[B_c, B_bigm], writes=[B_lrow2])
                split3(lrow2, B_lrow2)
            augrows(C_MK, Kaug[RS, c0:c0 + W], B_K[ti])

            a_ops = P.end_capture()
            P.begin_capture()
            nkb = blk0 + nb
            SB = [0, 1, 3]

            def qk(kb):
                sb_ = SB[kb % 3]
                P.add("pe", lambda e, sb_=sb_, kb=kb: e.matmul(bank[sb_][:, 0:W], Kaug[0:70, kb * 128:(kb + 1) * 128], Qaug[0:70, 0:W], start=True, stop=True),
                      reads=[B_Q, B_K[kb * 128 // 512]], writes=[bankbuf[sb_]])

            for kb_ in range(min(3, nkb)):
                qk(kb_)

            def exp_sel(kb):
                sb_ = SB[kb % 3]
                pt_ = PT[(kb % 4) // 2][:, kb % 2, 0:W]
                Bpt = B_PT4[kb % 4]
                P.add("act", lambda e, sb_=sb_, pt_=pt_: e.activation(out=pt_, in_=bank[sb_][:, 0:W], func=AF.Exp), reads=[bankbuf[sb_]], writes=[Bpt])
                jj = kb - blk0
                if jj >= 0:
                    P.add("pool", lambda e, pt_=pt_, jj=jj: e.affine_select(out=pt_, in_=pt_, pattern=[[1, W]], compare_op=ALU.is_ge,
                                                                            fill=0.0, base=-128 * jj, channel_multiplier=-1),
                          reads=[Bpt], writes=[Bpt])

            def pvmm(kb):
                pt_ = PT[(kb % 4) // 2][:, kb % 2, 0:W]
                P.add("pe", lambda e, pt_=pt_, kb=kb: e.matmul(bank[2][0:65, 0:W], Vc[:, kb, 0:65], pt_, start=(kb == 0), stop=(kb == nkb - 1)),
                      reads=[B_PT4[kb % 4], B_V[kb * 128 // 512], B_Vinit], writes=[bankbuf[2]])

            for kb in range(0, nkb, 2):
                two = kb + 1 < nkb
                exp_sel(kb)
                if two:
                    exp_sel(kb + 1)
                pvmm(kb)
                if two:
                    pvmm(kb + 1)
                if kb + 3 < nkb:
                    qk(kb + 3)
                if kb + 4 < nkb:
                    qk(kb + 4)
            P.add("act", lambda e: e.copy(O_sb[0:65, 0:W], bank[2][0:65, 0:W]), reads=[bankbuf[2]], writes=[B_Osb])
            b = 2
            P.add("pe", lambda e, b=b: e.matmul(bank[b][0:64, 0:W], ones_f[64:65, 0:64], O_sb[64:65, 0:W], start=True, stop=True),
                  reads=[B_Osb, B_misc], writes=[bankbuf[b]])
            P.add("act", lambda e, b=b: e.activation(out=rden[0:64, 0:W], in_=bank[b][0:64, 0:W], func=AF.Ln), reads=[bankbuf[b]], writes=[B_rden])
            P.add("act", lambda e: e.activation(out=rden[0:64, 0:W], in_=rden[0:64, 0:W], func=AF.Exp, scale=-1.0), reads=[B_rden], writes=[B_rden])
            P.add("dve", lambda e: e.tensor_tensor(out=YA[0:64, 0:W], in0=O_sb[0:64, 0:W], in1=rden[0:64, 0:W], op=ALU.mult), reads=[B_Osb, B_rden], writes=[B_YA])
            y_out(YA, 0, 64, c0, W, B_YA)

            att_ops = P.end_capture()
            P.begin_capture()
            gmode[0] = 2
            def conv(X_, B_X, off, out_, B_out):
                P.add("dve", lambda e: e.tensor_scalar(out=out_[:, 0:W], in0=X_[:, 0:W], scalar1=cwq(off), scalar2=None, op0=ALU.mult), reads=[B_X, B_pv], writes=[B_out])
                for j in range(1, 4):
                    P.add("dve", lambda e, j=j: e.scalar_tensor_tensor(out=out_[:, 0:W], in0=X_[:, j:j + W], scalar=cwq(off + j), in1=out_[:, 0:W], op0=ALU.mult, op1=ALU.add),
                          reads=[B_X, B_pv, B_out], writes=[B_out])
                P.add("pool", lambda e: e.tensor_copy(out=X_[:, 0:3], in_=X_[:, W:W + 3]), reads=[B_X], writes=[B_X])

            conv(Xq, B_Xq, 0, cq, B_cq)
            conv(Xk, B_Xk, 4, ck, B_ck)
            conv(Xv, B_Xv, 8, cv, B_cv)
            P.add("act", lambda e: e.activation(out=qs[:, 0:W], in_=cq[:, 0:W], func=AF.Silu), reads=[B_cq], writes=[B_qs])
            P.add("act", lambda e: e.activation(out=ks[:, 0:W], in_=ck[:, 0:W], func=AF.Silu), reads=[B_ck], writes=[B_ks])
            P.add("act", lambda e: e.activation(out=vsT[:, 0:W], in_=cv[:, 0:W], func=AF.Silu), reads=[B_cv], writes=[B_vsT])
            for (src, Bs, sq_, Bsq, rn, Brn, outb, Bo, sc) in ((qs, B_qs, sqq, B_sqq, rnq, B_rnq, qnT, B_qnT, 128.0), (ks, B_ks, sqk, B_sqk, rnk, B_rnk, knT, B_knT, 1.0)):
                P.add("act", lambda e, src=src: e.activation(out=sqb16[:, 0:W], in_=src[:, 0:W], func=AF.Square), reads=[Bs], writes=[B_sqb16])
                b = gbank()
                P.add("pe", lambda e, b=b: e.matmul(bank[b][:, 0:W], onesb, sqb16[:, 0:W], start=True, stop=True), reads=[B_sqb16, B_misc], writes=[bankbuf[b]])
                P.add("dve", lambda e, b=b, rn=rn, sc=sc: e.tensor_scalar(out=rn[:, 0:W], in0=bank[b][:, 0:W], scalar1=1024.0 * sc, scalar2=NORM_EPS * sc, op0=ALU.mult, op1=ALU.add),
                      reads=[bankbuf[b]], writes=[Brn])
                P.add("act", lambda e, rn=rn: e.activation(out=rn[:, 0:W], in_=rn[:, 0:W], func=AF.Ln), reads=[Brn], writes=[Brn])
                P.add("act", lambda e, rn=rn: e.activation(out=rn[:, 0:W], in_=rn[:, 0:W], func=AF.Exp, scale=-0.5), reads=[Brn], writes=[Brn])
                P.add("dve", lambda e, src=src, rn=rn, outb=outb: e.tensor_tensor(out=outb[:, 0:W], in0=src[:, 0:W], in1=rn[:, 0:W], op=ALU.mult), reads=[Bs, Brn], writes=[Bo])

            a_in = sctm[:, 0:nb, 6]
            b_in = sctm[:, 0:nb, 7]
            nbs = slice(0, nb)
            P.add("act", lambda e: e.activation(out=t_beta[:, nbs], in_=b_in, func=AF.Exp, scale=-1.0), reads=[B_sctm], writes=[B_tms])
            P.add("act", lambda e: e.activation(out=t_beta[:, nbs], in_=t_beta[:, nbs], func=AF.Ln, bias=1.0), reads=[B_tms], writes=[B_tms])
            P.add("act", lambda e: e.activation(out=t_beta[:, nbs], in_=t_beta[:, nbs], func=AF.Exp, scale=-1.0), reads=[B_tms], writes=[B_tms])
            P.add("dve", lambda e: e.tensor_scalar(out=t_nbeta[:, nbs], in0=t_beta[:, nbs], scalar1=-1.0, scalar2=None, op0=ALU.mult), reads=[B_tms], writes=[B_tms])
            P.add("dve", lambda e: e.tensor_scalar(out=t_x[:, nbs], in0=a_in, scalar1=pv[:, PV_DT:PV_DT + 1], scalar2=None, op0=ALU.add), reads=[B_sctm, B_pv], writes=[B_tms])
            P.add("dve", lambda e: e.tensor_scalar(out=t_ax[:, nbs], in0=t_x[:, nbs], scalar1=-1.0, scalar2=None, op0=ALU.mult), reads=[B_tms], writes=[B_tms])
            P.add("dve", lambda e: e.tensor_tensor(out=t_ax[:, nbs], in0=t_ax[:, nbs], in1=t_x[:, nbs], op=ALU.max), reads=[B_tms], writes=[B_tms])
            P.add("act", lambda e: e.activation(out=t_e[:, nbs], in_=t_ax[:, nbs], func=AF.Exp, scale=-1.0), reads=[B_tms], writes=[B_tms])
            P.add("act", lambda e: e.activation(out=t_l[:, nbs], in_=t_e[:, nbs], func=AF.Ln, bias=1.0), reads=[B_tms], writes=[B_tms])
            P.add("dve", lambda e: e.scalar_tensor_tensor(out=t_g[:, nbs], in0=t_x[:, nbs], scalar=0.0, in1=t_l[:, nbs], op0=ALU.max, op1=ALU.add), reads=[B_tms], writes=[B_tms])
            P.add("dve", lambda e: e.tensor_scalar(out=t_g[:, nbs], in0=t_g[:, nbs], scalar1=sc1[:, 0:1], scalar2=None, op0=ALU.mult), reads=[B_tms, B_sc1], writes=[B_tms])
            bg = gbank()
            P.add("pe", lambda e, bg=bg: e.matmul(bank[bg][:, 0:nb], cst[:, C_LT2:C_LT2 + 128], t_g[:, nbs], start=True, stop=True), reads=[B_tms, B_cst], writes=[bankbuf[bg]])
            P.add("pe", lambda e, bg=bg: e.matmul(bank[bg][:, 8:8 + nb], cst[:, C_BLK:C_BLK + 128], t_g[:, nbs], start=True, stop=True), reads=[B_tms, B_cst], writes=[bankbuf[bg]])
            for c_ in range(2):
                P.add("dve", lambda e, c_=c_: e.tensor_scalar(out=rhs8[:, 0:2 * nb].rearrange("p (j c) -> p j c", c=2)[:, :, c_], in0=t_g[:, nbs],
                                                               scalar1=cst[:, C_SEL + c_:C_SEL + c_ + 1], scalar2=None, op0=ALU.mult),
                      reads=[B_tms, B_cst], writes=[B_glbc])
            P.add("pe", lambda e, bg=bg: e.matmul(bank[bg][:, 16:16 + 2 * nb], ones_f, rhs8[:, 0:2 * nb], start=True, stop=True), reads=[B_glbc, B_misc], writes=[bankbuf[bg]])
            P.add("dve", lambda e, bg=bg: e.tensor_copy(out=t_gc[:, nbs], in_=bank[bg][:, 0:nb]), reads=[bankbuf[bg]], writes=[B_tms])
            P.add("dve", lambda e: e.tensor_scalar(out=t_ngc[:, nbs], in0=t_gc[:, nbs], scalar1=-1.0, scalar2=None, op0=ALU.mult), reads=[B_tms], writes=[B_tms])
            P.add("dve", lambda e, bg=bg: e.tensor_tensor(out=t_d[:, nbs], in0=bank[bg][:, 8:8 + nb], in1=t_gc[:, nbs], op=ALU.subtract), reads=[bankbuf[bg], B_tms], writes=[B_tms])
            P.add("act", lambda e: e.activation(out=t_e2[:, nbs], in_=t_d[:, nbs], func=AF.Exp), reads=[B_tms], writes=[B_tms])
            P.add("act", lambda e: e.activation(out=t_e1[:, nbs], in_=t_gc[:, nbs], func=AF.Exp), reads=[B_tms], writes=[B_tms])
            P.add("dve", lambda e: e.tensor_tensor(out=t_e1[:, nbs], in0=t_e1[:, nbs], in1=t_beta[:, nbs], op=ALU.mult), reads=[B_tms], writes=[B_tms])
            P.add("act", lambda e, bg=bg: e.activation(out=glbc[:, 0:2 * nb], in_=bank[bg][:, 16:16 + 2 * nb], func=AF.Exp), reads=[bankbuf[bg]], writes=[B_glbc])

            for j in range(nb):
                b = gbank()
                pb = bank[b].bitcast(BF16)
                P.add("pe", lambda e, j=j, pb=pb: e.transpose(pb[:, 0:128], knT[:, j * 128:(j + 1) * 128], identb), reads=[B_knT, B_misc], writes=[bankbuf[b]])
                P.add("pe", lambda e, j=j, pb=pb: e.transpose(pb[:, 128:256], vsT[:, j * 128:(j + 1) * 128], identb), reads=[B_vsT, B_misc], writes=[bankbuf[b]])
                P.add("dve", lambda e, j=j, pb=pb: e.tensor_scalar(out=Xuw[:, j, 128:256], in0=pb[:, 0:128], scalar1=t_e1[:, j:j + 1], scalar2=None, op0=ALU.mult),
                      reads=[bankbuf[b], B_tms], writes=[B_Xuw])
                P.add("dve", lambda e, j=j, pb=pb: e.tensor_scalar(out=kdec[:, j, :], in0=pb[:, 0:128], scalar1=t_e2[:, j:j + 1], scalar2=None, op0=ALU.mult),
                      reads=[bankbuf[b], B_tms], writes=[B_kdec])
                P.add("dve", lambda e, j=j, pb=pb: e.tensor_scalar(out=Xuw[:, j, 0:128], in0=pb[:, 128:256], scalar1=t_beta[:, j:j + 1], scalar2=None, op0=ALU.mult),
                      reads=[bankbuf[b], B_tms], writes=[B_Xuw])

            a_ops = a_ops + P.end_capture()
            P.begin_capture()
            gmode[0] = 1
            for j in range(nb):
                q2 = j % 2
                cs = slice(j * 128, (j + 1) * 128)
                P.add("dve", lambda e, j=j, q2=q2: e.tensor_scalar(out=diag[q2], in0=ident, scalar1=t_gc[:, j:j + 1], scalar2=None, op0=ALU.mult),
                      reads=[B_cst, B_tms], writes=[B_diag[q2]])
                b = gbank()
                P.add("pe", lambda e, b=b, q2=q2: e.matmul(bank[b][:, 0:128], ones_f, diag[q2], start=True, stop=True), reads=[B_diag[q2], B_misc], writes=[bankbuf[b]])
                P.add("dve", lambda e, b=b, q2=q2: e.scalar_tensor_tensor(out=T1[q2], in0=bank[b][:, 0:128], scalar=-1.0, in1=cst[:, C_MSL:C_MSL + 128], op0=ALU.mult, op1=ALU.add),
                      reads=[bankbuf[b], B_cst], writes=[B_T1[q2]])
                P.add("act", lambda e, j=j, q2=q2: e.activation(out=Dsl[q2], in_=T1[q2], func=AF.Exp, bias=t_gc[:, j:j + 1]), reads=[B_T1[q2], B_tms], writes=[B_Dsl[q2]])
                P.add("dve", lambda e, b=b, q2=q2: e.tensor_tensor(out=T3[q2], in0=bank[b][:, 0:128], in1=cst[:, C_MIU:C_MIU + 128], op=ALU.add),
                      reads=[bankbuf[b], B_cst], writes=[B_T3[q2]])
                P.add("act", lambda e, j=j, q2=q2: e.activation(out=Diu[q2], in_=T3[q2], func=AF.Exp, bias=t_ngc[:, j:j + 1]), reads=[B_T3[q2], B_tms], writes=[B_Diu[q2]])
                P.add("act", lambda e, b=b, j=j: e.activation(out=EGR[j], in_=bank[b][:, 0:128], func=AF.Exp), reads=[bankbuf[b]], writes=[B_EGR[j]])
                b2_ = gbank()
                P.add("pe", lambda e, b2_=b2_, cs=cs: e.matmul(bank[b2_][:, 0:128], knT[:, cs], knT[:, cs], start=True, stop=True), reads=[B_knT], writes=[bankbuf[b2_]])
                P.add("pe", lambda e, b2_=b2_, cs=cs: e.matmul(bank[b2_][:, 128:256], knT[:, cs], qnT[:, cs], start=True, stop=True), reads=[B_knT, B_qnT], writes=[bankbuf[b2_]])
                P.add("dve", lambda e, b2_=b2_, j=j, q2=q2: e.scalar_tensor_tensor(out=PP[j][0][:, 0:128], in0=bank[b2_][:, 0:128], scalar=t_nbeta[:, j:j + 1], in1=Dsl[q2],
                                                                                  op0=ALU.mult, op1=ALU.mult),
                      reads=[bankbuf[b2_], B_tms, B_Dsl[q2]], writes=[B_PP[j][0]])
                P.add("dve", lambda e, b2_=b2_, j=j, q2=q2: e.tensor_tensor(out=attnT[j], in0=bank[b2_][:, 128:256], in1=Diu[q2], op=ALU.mult),
                      reads=[bankbuf[b2_], B_Diu[q2]], writes=[B_attnT[j]])
                b3 = gbank()
                pb3 = bank[b3].bitcast(BF16)
                P.add("pe", lambda e, pb3=pb3, j=j: e.transpose(pb3[:, 0:128], PP[j][0][:, 0:128], identb), reads=[B_PP[j][0], B_misc], writes=[bankbuf[b3]])
                P.add("act", lambda e, pb3=pb3, j=j: e.copy(PP[j][0][:, 128:256], pb3[:, 0:128]), reads=[bankbuf[b3]], writes=[B_PP[j][0]])
                P.add("dve", lambda e, pb3=pb3, j=j: e.tensor_tensor(out=RR[j][0], in0=pb3[:, 0:128], in1=identb, op=ALU.add), reads=[bankbuf[b3], B_misc], writes=[B_RR[j][0]])
            for m in range(1, 6):
                src, dst = (m - 1) % 2, m % 2
                for j in range(nb):
                    b = gbank()
                    pbf = bank[b]
                    P.add("pe", lambda e, b=b, j=j, src=src: e.matmul(bank[b][:, 0:128], PP[j][src][:, 128:256], PP[j][src][:, 0:128], start=True, stop=True),
                          reads=[B_PP[j][src]], writes=[bankbuf[b]])
                    if m < 5:
                        P.add("pe", lambda e, b=b, j=j, src=src: e.matmul(bank[b][:, 128:256], PP[j][src][:, 0:128], PP[j][src][:, 128:256], start=True, stop=True),
                              reads=[B_PP[j][src]], writes=[bankbuf[b]])
                    wcols = 256 if m < 5 else 128
                    P.add("act", lambda e, b=b, j=j, dst=dst, wcols=wcols: e.copy(PP[j][dst][:, 0:wcols], bank[b][:, 0:wcols]), reads=[bankbuf[b]], writes=[B_PP[j][dst]])
                    b2_ = gbank()
                    P.add("pe", lambda e, b2_=b2_, j=j, src=src, dst=dst: e.matmul(bank[b2_][:, 0:128], PP[j][dst][:, 0:128], RR[j][src], start=True, stop=True),
                          reads=[B_PP[j][dst], B_RR[j][src]], writes=[bankbuf[b2_]])
                    P.add("dve", lambda e, b2_=b2_, j=j, src=src, dst=dst: e.tensor_tensor(out=RR[j][dst], in0=bank[b2_][:, 0:128], in1=RR[j][src], op=ALU.add),
                          reads=[bankbuf[b2_], B_RR[j][src]], writes=[B_RR[j][dst]])
            RF = 1
            for j in range(nb):
                b = gbank()
                P.add("pe", lambda e, b=b, j=j: e.matmul(bank[b][:, 0:256], RR[j][RF], Xuw[:, j, :], start=True, stop=True), reads=[B_RR[j][RF], B_Xuw], writes=[bankbuf[b]])
                P.add("act", lambda e, b=b, j=j: e.copy(UW[:, j, :], bank[b][:, 0:256]), reads=[bankbuf[b]], writes=[B_UW[j]])
                for h in range(2):
                    n = 2 * j + h
                    rs_ = slice(64 * h, 64 * h + 64)
                    b2_ = gbank()
                    P.add("pe", lambda e, b2_=b2_, j=j, rs_=rs_: e.matmul(bank[b2_][:, 0:128], UW[rs_, j, 128:256], kdec[rs_, j, :], start=True, stop=True),
                          reads=[B_UW[j], B_kdec], writes=[bankbuf[b2_]])
                    P.add("pe", lambda e, b2_=b2_, j=j, rs_=rs_: e.matmul(bank[b2_][:, 128:256], kdec[rs_, j, :], UW[rs_, j, 0:128], start=True, stop=True),
                          reads=[B_UW[j], B_kdec], writes=[bankbuf[b2_]])
                    P.add("dve", lambda e, b2_=b2_, n=n: e.scalar_tensor_tensor(out=MT[n], in0=ident, scalar=glbc[:, n:n + 1], in1=bank[b2_][:, 0:128], op0=ALU.mult, op1=ALU.subtract),
                          reads=[bankbuf[b2_], B_glbc, B_cst], writes=[B_MT[n]])
                    P.add("act", lambda e, b2_=b2_, n=n: e.copy(Bn[n], bank[b2_][:, 128:256]), reads=[bankbuf[b2_]], writes=[B_Bn[n]])
                cs = slice(j * 128, (j + 1) * 128)
                P.add("dve", lambda e, j=j, cs=cs: e.tensor_tensor(out=qdecT[:, cs], in0=qnT[:, cs], in1=EGR[j], op=ALU.mult), reads=[B_qnT, B_EGR[j]], writes=[B_qdecT])
                b3 = gbank()
                P.add("pe", lambda e, b3=b3, j=j: e.matmul(bank[b3][:, 0:128], UW[:, j, 128:256], attnT[j], start=True, stop=True), reads=[B_UW[j], B_attnT[j]], writes=[bankbuf[b3]])
                P.add("dve", lambda e, b3=b3, cs=cs: e.tensor_tensor(out=QeffT[:, cs], in0=qdecT[:, cs], in1=bank[b3][:, 0:128], op=ALU.subtract),
                      reads=[bankbuf[b3], B_qdecT], writes=[B_QeffT])
            bo = 7
            for n in range(2 * nb):
                j, h = n // 2, n % 2
                rs_ = slice(64 * h, 64 * h + 64)
                si = s_cur[0]
                sn = (si + 1) % 9
                col = slice(j * 128 + 64 * h, j * 128 + 64 * h + 64)
                P.add("pe", lambda e, j=j, rs_=rs_, col=col, h=h: e.matmul(bank[bo][:, col], UW[rs_, j, 0:128], attnT[j][rs_, 64 * h:64 * h + 64], start=True, stop=False),
                      reads=[B_UW[j], B_attnT[j]], writes=[bankbuf[bo]])
                P.add("pe", lambda e, si=si, col=col: e.matmul(bank[bo][:, col], Sst[si], QeffT[:, col], start=False, stop=True),
                      reads=[B_S[si], B_QeffT], writes=[bankbuf[bo]])
                bs = gbank()
                if bs == bo:
                    bs = gbank()
                P.add("pe", lambda e, bs=bs, n=n, si=si: e.matmul(bank[bs][:, 0:128], MT[n], Sst[si], start=True, stop=True), reads=[B_MT[n], B_S[si]], writes=[bankbuf[bs]])
                P.add("dve", lambda e, bs=bs, n=n, sn=sn: e.tensor_tensor(out=Sst[sn], in0=bank[bs][:, 0:128], in1=Bn[n], op=ALU.add), reads=[bankbuf[bs], B_Bn[n]], writes=[B_S[sn]])
                s_cur[0] = sn
            P.add("act", lambda e: e.activation(out=sqob[:, 0:W], in_=bank[bo][:, 0:W], func=AF.Square), reads=[bankbuf[bo]], writes=[B_sqob])
            b = gbank()
            if b == bo:
                b = gbank()
            P.add("pe", lambda e, b=b: e.matmul(bank[b][:, 0:W], onesb, sqob[:, 0:W], start=True, stop=True), reads=[B_sqob, B_misc], writes=[bankbuf[b]])
            P.add("dve", lambda e, b=b: e.tensor_scalar(out=rno[:, 0:W], in0=bank[b][:, 0:W], scalar1=8.0, scalar2=NORM_EPS, op0=ALU.mult, op1=ALU.add),
                  reads=[bankbuf[b]], writes=[B_rno])
            P.add("act", lambda e: e.activation(out=rno[:, 0:W], in_=rno[:, 0:W], func=AF.Ln), reads=[B_rno], writes=[B_rno])
            P.add("act", lambda e: e.activation(out=rno[:, 0:W], in_=rno[:, 0:W], func=AF.Exp, scale=-0.5), reads=[B_rno], writes=[B_rno])
            P.add("dve", lambda e: e.scalar_tensor_tensor(out=sqo[:, 0:W], in0=bank[bo][:, 0:W], scalar=pv[:, PV_GNW:PV_GNW + 1], in1=rno[:, 0:W], op0=ALU.mult, op1=ALU.mult),
                  reads=[bankbuf[bo], B_rno, B_pv, B_sqo], writes=[B_sqo])
            P.add("dve", lambda e: e.tensor_tensor(out=YB[:, 0:W], in0=sqo[:, 0:W], in1=zs[:, 0:W], op=ALU.mult), reads=[B_sqo, B_zs], writes=[B_YB])
            y_out(YB, 64, 128, c0, W, B_YB)
            gdn_ops = P.end_capture()
            return a_ops, att_ops, gdn_ops

        KNT = 999
        gmode[0] = 1
        secs = [p1_tile(ti_, c0_t, W_t) for ti_, (c0_t, W_t) in enumerate(tiles if DO_P1 else [])]
        if secs:
            P.add_merged([secs[0][0]])
        for i_ in range(len(secs)):
            lists = [secs[i_][1], secs[i_][2]]
            if i_ + 1 < len(secs):
                lists.append(secs[i_ + 1][0])
            P.add_merged(lists)
        gmode[0] = 0

        STOP = {"fused": 0, "p1": 1, "p2": 0}[mode]
        B_ag = Buf("ag")
        if mode == "fused":
          B_agin, B_agbuf, B_stage = Buf("agin"), Buf("agbuf"), Buf("ystage")
          for pc in range(16):
            P.add("sp", lambda e, pc=pc: e.dma_start(out=agin, in_=ysrc[pc * 96:(pc + 1) * 96, :]), reads=[B_ysrc], writes=[B_agin], key=B_agin)
            P.add("pool", lambda e: e.collective_compute("AllGather", ALU.bypass, replica_groups=[list(range(NCORES))], ins=[agin.opt()], outs=[agbuf.opt()]),
                  reads=[B_agin], writes=[B_agbuf], key=B_agbuf, inc=1)
            P.add("sp", lambda e, pc=pc: e.dma_start(out=agout[pc * 768:(pc + 1) * 768, :], in_=agbuf), reads=[B_agbuf], writes=[B_stage], key=B_stage)
          P.add("sp", lambda e: e.nop(), reads=[B_stage], writes=[B_ag])
        P.barrier()

        A.off = base_off
        bufA = A.f32(KC * 512).rearrange("p (k w) -> p k w", k=KC)
        bufH = A.f32(KC * 512).rearrange("p (k w) -> p k w", k=KC)
        bufH1 = A.f32(KC * 512).rearrange("p (k w) -> p k w", k=KC)
        hb = A.bf16(KC * 512).rearrange("p (k w) -> p k w", k=KC)
        h1b = hb
        yb16 = A.bf16(12 * FS).rearrange("p (k w) -> p k w", k=12)
        mixb = A.bf16(KC * 512).rearrange("p (k w) -> p k w", k=KC)
        actb = A.bf16(FC * 512).rearrange("p (k w) -> p k w", k=FC)
        ring = [A.bf16(KC * 512).rearrange("p (k w) -> p k w", k=KC) for _ in range(3)]
        B_ring = [Buf("ring%d" % i) for i in range(3)]
        dpan = A.bf16(FC * 512).rearrange("p (k w) -> p k w", k=FC)
        B_dpan = Buf("dpan")
        sq2 = actb[:, 0:8, :]
        xb2 = actb[:, 8:16, :]
        l_mean = A.f32(512); l_rstd = A.f32(512); l_t = A.f32(512); l_t2 = A.f32(512); l_t3 = A.f32(512)
        B_bufA, B_bufH, B_bufH1, B_hb, B_h1b, B_yb16, B_mixb, B_actb = [Buf(n) for n in ("bufA", "bufH", "bufH1", "hb", "h1b", "yb16", "mixb", "actb")]
        B_h1b = B_hb
        B_lmean, B_lrstd, B_lt, B_lt2, B_lt3 = [Buf(n) for n in ("lmean", "lrstd", "lt", "lt2", "lt3")]
        B_sq2 = B_actb
        B_xb2 = B_actb
        gp2 = [0]

        def gb2():
            b = gp2[0]
            gp2[0] = (gp2[0] + 1) % 8
            return b

        rp = [0]

        def load_panel(src_ap, kk, ncols):
            i = rp[0]
            rp[0] = (rp[0] + 1) % 3
            P.add("pool", lambda e: e.dma_start(out=ring[i][:, 0:kk, 0:ncols], in_=src_ap.rearrange("(k p) c -> p k c", p=128)), writes=[B_ring[i]], key=B_ring[i])
            return ring[i], B_ring[i]

        def layer_norm(src, Bsrc, dst32, Bdst32, dstb, Bdstb, gcol, bcol, W):
            P.add("act", lambda e: e.copy(xb2[:, :, 0:W], src[:, :, 0:W]), reads=[Bsrc], writes=[B_xb2])
            P.add("act", lambda e: e.activation(out=sq2[:, :, 0:W], in_=src[:, :, 0:W], func=AF.Square), reads=[Bsrc], writes=[B_sq2])
            b1, b2 = gb2(), gb2()
            for k in range(KC):
                P.add("pe", lambda e, k=k: e.matmul(bank[b1][:, 0:W], onesb, xb2[:, k, 0:W], start=(k == 0), stop=(k == KC - 1)), reads=[B_xb2, B_misc], writes=[bankbuf[b1]])
            for k in range(KC):
                P.add("pe", lambda e, k=k: e.matmul(bank[b2][:, 0:W], onesb, sq2[:, k, 0:W], start=(k == 0), stop=(k == KC - 1)), reads=[B_sq2, B_misc], writes=[bankbuf[b2]])
            P.add("act", lambda e: e.copy(l_mean[:, 0:W], bank[b1][:, 0:W]), reads=[bankbuf[b1]], writes=[B_lmean])
            P.add("dve", lambda e: e.tensor_tensor(out=l_t[:, 0:W], in0=l_mean[:, 0:W], in1=l_mean[:, 0:W], op=ALU.mult), reads=[B_lmean], writes=[B_lt])
            P.add("dve", lambda e: e.tensor_tensor(out=l_t[:, 0:W], in0=bank[b2][:, 0:W], in1=l_t[:, 0:W], op=ALU.subtract), reads=[bankbuf[b2], B_lt], writes=[B_lt])
            P.add("dve", lambda e: e.tensor_scalar(out=l_t[:, 0:W], in0=l_t[:, 0:W], scalar1=0.0, scalar2=LN_EPS, op0=ALU.max, op1=ALU.add), reads=[B_lt], writes=[B_lt])
            P.add("act", lambda e: e.activation(out=l_t[:, 0:W], in_=l_t[:, 0:W], func=AF.Ln), reads=[B_lt], writes=[B_lt])
            P.add("act", lambda e: e.activation(out=l_rstd[:, 0:W], in_=l_t[:, 0:W], func=AF.Exp, scale=-0.5), reads=[B_lt], writes=[B_lrstd])
            for k in range(KC):
                P.add("dve", lambda e, k=k: e.tensor_tensor(out=l_t2[:, 0:W], in0=src[:, k, 0:W], in1=l_mean[:, 0:W], op=ALU.subtract), reads=[Bsrc, B_lmean], writes=[B_lt2])
                P.add("dve", lambda e, k=k: e.tensor_tensor(out=l_t2[:, 0:W], in0=l_t2[:, 0:W], in1=l_rstd[:, 0:W], op=ALU.mult), reads=[B_lt2, B_lrstd], writes=[B_lt2])
                P.add("act", lambda e, k=k: e.activation(out=dst32[:, k, 0:W], in_=l_t2[:, 0:W], func=AF.Identity, bias=pv[:, bcol + k:bcol + k + 1], scale=pv[:, gcol + k:gcol + k + 1]),
                      reads=[B_lt2, B_pv], writes=[Bdst32])
                if dstb is not None:
                    P.add("act", lambda e, k=k: e.activation(out=dstb[:, k, 0:W], in_=l_t2[:, 0:W], func=AF.Identity, bias=pv[:, bcol + k:bcol + k + 1], scale=pv[:, gcol + k:gcol + k + 1]),
                          reads=[B_lt2, B_pv], writes=[Bdstb])

        xov = xown.rearrange("(k p) t -> p k t", p=128)
        outv = outT.rearrange("(k p) t -> p k t", p=128) if mode != "p1" else None
        B_out = Buf("out")
        pid_cache = {}

        def pid_of(e):
            return e.partition_id()

        agv5 = agout.rearrange("(s hf r f) t -> s hf r f t", s=8, hf=2, r=8)
        agv6 = agout.rearrange("(s hf q h f) t -> s hf q h f t", s=8, hf=2, q=4, h=2)

        if mode == "p2":
            for r in range(NCORES):
                P.add("sp", lambda e, r=r: e.dma_start(out=yb16[:, 4 + r, 0:FS], in_=yin[r * 192 + 64:r * 192 + 192, :]), writes=[B_yb16], key=B_yb16)
                P.add("sp", lambda e, r=r: e.dma_start(out=yb16[64 * (r % 2):64 * (r % 2) + 64, r // 2, 0:FS], in_=yin[r * 192:r * 192 + 64, :]), writes=[B_yb16], key=B_yb16)
        elif mode == "fused":
            def ldb1(e):
                pid = e.partition_id()
                src = agv5[bass.ds(pid, 1), 0, :, 64:96, 0:FS].rearrange("s r f t -> f (s r) t")
                return e.dma_start(out=yb16[0:32, 4:12, 0:FS], in_=src)
            P.add("sp", ldb1, reads=[B_ag], writes=[B_yb16], key=B_yb16)

            def ldb2(e):
                pid = e.partition_id()
                src = agv5[bass.ds(pid, 1), 1, :, 0:96, 0:FS].rearrange("s r f t -> f (s r) t")
                return e.dma_start(out=yb16[32:128, 4:12, 0:FS], in_=src)
            P.add("sp", ldb2, reads=[B_ag], writes=[B_yb16], key=B_yb16)
            for hh in range(2):
                def lda(e, hh=hh):
                    pid = e.partition_id()
                    src = agv6[bass.ds(pid, 1), 0, :, hh, 0:64, 0:FS].rearrange("s q f t -> f (s q) t")
                    return e.dma_start(out=yb16[64 * hh:64 * hh + 64, 0:4, 0:FS], in_=src)
                P.add("sp", lda, reads=[B_ag], writes=[B_yb16], key=B_yb16)


        def p2_tile(t2):
            t0 = t2 * W2
            W = W2
            P.add("sp", lambda e, t0=t0: e.dma_start(out=bufA[:, :, 0:W], in_=xov[:, :, t0:t0 + W]), writes=[B_bufA], key=B_bufA)
            layer_norm(bufA, B_bufA, bufH, B_bufH, hb, B_hb, PV_G0, PV_B0, W)
            for g4 in range(2):
                cs = slice(g4 * 512, g4 * 512 + 512)
                pga, Bga = load_panel(w2g[:, g4 * 512:g4 * 512 + 512], 8, 512)
                pa, Bpa = load_panel(woa[:, cs], 4, 512)
                for half in range(2):
                    if half == 0:
                        pg_, Bg_, pw_, Bw_, nk, yoff = pga, Bga, pa, Bpa, 4, 0
                    else:
                        pg_, Bg_ = load_panel(w2g[:, 1024 + g4 * 512:1024 + g4 * 512 + 512], 8, 512)
                        pw_, Bw_ = load_panel(wob[:, cs], 8, 512)
                        nk, yoff = 8, 4
                    for o in range(4):
                        oc = g4 * 4 + o
                        osl = slice(o * 128, o * 128 + 128)
                        bg_, bw_ = gb2(), gb2()
                        for k in range(KC):
                            P.add("pe", lambda e, k=k, bg_=bg_, pg_=pg_, osl=osl: e.matmul(bank[bg_][:, 0:W], pg_[:, k, osl], hb[:, k, 0:W], start=(k == 0), stop=(k == KC - 1)),
                                  reads=[Bg_, B_hb], writes=[bankbuf[bg_]])
                        for k in range(nk):
                            P.add("pe", lambda e, k=k, bw_=bw_, pw_=pw_, osl=osl, yoff=yoff, nk=nk: e.matmul(bank[bw_][:, 0:W], pw_[:, k, osl], yb16[:, yoff + k, t0:t0 + W], start=(k == 0), stop=(k == nk - 1)),
                                  reads=[Bw_, B_yb16], writes=[bankbuf[bw_]])
                        P.add("act", lambda e, bg_=bg_: e.activation(out=l_t[:, 0:W], in_=bank[bg_][:, 0:W], func=AF.Sigmoid), reads=[bankbuf[bg_]], writes=[B_lt])
                        if half == 0:
                            P.add("dve", lambda e, bw_=bw_, oc=oc: e.tensor_tensor(out=bufA[:, oc, 0:W], in0=bank[bw_][:, 0:W], in1=l_t[:, 0:W], op=ALU.mult),
                                  reads=[bankbuf[bw_], B_lt], writes=[B_bufA])
                        else:
                            P.add("dve", lambda e, bw_=bw_: e.tensor_tensor(out=l_t3[:, 0:W], in0=bank[bw_][:, 0:W], in1=l_t[:, 0:W], op=ALU.mult),
                                  reads=[bankbuf[bw_], B_lt], writes=[B_lt3])
                            P.add("dve", lambda e, oc=oc: e.tensor_tensor(out=mixb[:, oc, 0:W], in0=l_t3[:, 0:W], in1=bufA[:, oc, 0:W], op=ALU.add),
                                  reads=[B_lt3, B_bufA], writes=[B_mixb])
            for g4 in range(2):
                pw_, Bw_ = load_panel(wo[:, g4 * 512:g4 * 512 + 512], 8, 512)
                for o in range(4):
                    oc = g4 * 4 + o
                    osl = slice(o * 128, o * 128 + 128)
                    b = gb2()
                    for k in range(KC):
                        P.add("pe", lambda e, k=k, b=b, pw_=pw_, osl=osl: e.matmul(bank[b][:, 0:W], pw_[:, k, osl], mixb[:, k, 0:W], start=(k == 0), stop=(k == KC - 1)),
                              reads=[Bw_, B_mixb], writes=[bankbuf[b]])
                    P.add("dve", lambda e, b=b, oc=oc: e.scalar_tensor_tensor(out=bufA[:, oc, 0:W], in0=bufH[:, oc, 0:W], scalar=ALPHA, in1=bank[b][:, 0:W], op0=ALU.mult, op1=ALU.add),
                          reads=[bankbuf[b], B_bufH], writes=[B_bufA])
            layer_norm(bufA, B_bufA, bufH1, B_bufH1, h1b, B_h1b, PV_G1, PV_B1, W)
            for c0_ in range(0, DFF, 512):
                ncol = min(512, DFF - c0_)
                pg_, Bg_ = load_panel(wg[:, c0_:c0_ + ncol], 8, ncol)
                pu_, Bu_ = load_panel(wu[:, c0_:c0_ + ncol], 8, ncol)
                for o in range(ncol // 128):
                    fc = c0_ // 128 + o
                    osl = slice(o * 128, o * 128 + 128)
                    bg_, bu_ = gb2(), gb2()
                    for k in range(KC):
                        P.add("pe", lambda e, k=k, bg_=bg_, pg_=pg_, osl=osl: e.matmul(bank[bg_][:, 0:W], pg_[:, k, osl], h1b[:, k, 0:W], start=(k == 0), stop=(k == KC - 1)),
                              reads=[Bg_, B_h1b], writes=[bankbuf[bg_]])
                    for k in range(KC):
                        P.add("pe", lambda e, k=k, bu_=bu_, pu_=pu_, osl=osl: e.matmul(bank[bu_][:, 0:W], pu_[:, k, osl], h1b[:, k, 0:W], start=(k == 0), stop=(k == KC - 1)),
                              reads=[Bu_, B_h1b], writes=[bankbuf[bu_]])
                    P.add("act", lambda e, bg_=bg_: e.activation(out=l_t[:, 0:W], in_=bank[bg_][:, 0:W], func=AF.Silu), reads=[bankbuf[bg_]], writes=[B_lt])
                    P.add("dve", lambda e, bu_=bu_, fc=fc: e.tensor_tensor(out=actb[:, fc, 0:W], in0=bank[bu_][:, 0:W], in1=l_t[:, 0:W], op=ALU.mult),
                          reads=[bankbuf[bu_], B_lt], writes=[B_actb])
            for g4 in range(2):
                P.add("pool", lambda e, g4=g4: e.dma_start(out=dpan[:, :, :], in_=wd[:, g4 * 512:g4 * 512 + 512].rearrange("(k p) c -> p k c", p=128)), writes=[B_dpan], key=B_dpan)
                for o in range(4):
                    oc = g4 * 4 + o
                    osl = slice(o * 128, o * 128 + 128)
                    b = gb2()
                    for k in range(FC):
                        P.add("pe", lambda e, k=k, b=b, osl=osl: e.matmul(bank[b][:, 0:W], dpan[:, k, osl], actb[:, k, 0:W], start=(k == 0), stop=(k == FC - 1)),
                              reads=[B_dpan, B_actb], writes=[bankbuf[b]])
                    P.add("dve", lambda e, b=b, oc=oc: e.scalar_tensor_tensor(out=bufA[:, oc, 0:W], in0=bufH1[:, oc, 0:W], scalar=ALPHA, in1=bank[b][:, 0:W], op0=ALU.mult, op1=ALU.add),
                          reads=[bankbuf[b], B_bufH1], writes=[B_bufA])
            layer_norm(bufA, B_bufA, bufH, B_bufH, None, None, PV_G2, PV_B2, W)
            P.add("sp", lambda e, t0=t0: e.dma_start(out=outv[:, :, t0:t0 + W], in_=bufH[:, :, 0:W]), reads=[B_bufH], writes=[B_out], key=B_out)
        if STOP == 0:
            for t2_ in range(NT2):
                p2_tile(t2_)
        else:
            P.add("sp", lambda e: e.nop(), reads=[B_ysrc])
        P.add("sp", lambda e: e.nop(), reads=[B_out])

        P.finalize()
        engsem = {e: [es.enter_context(nc.semaphore("sem_%s_%d" % (e, i))) for i in range(P.nep[e])] for e in ENGS}
        print("ops per engine", {e: len(P.ops[e]) for e in ENGS}, "epochs", P.nep)
        keysem = {}
        for e in ENGS:
            for op in P.ops[e]:
                if op.key is not None and id(op.key) not in keysem:
                    keysem[id(op.key)] = es.enter_context(nc.semaphore("k_" + op.key.name))
        block = es.enter_context(nc.Block())

        @block.tensor
        def _(h):
            P.emit("pe", h, engsem, keysem)

        @block.scalar
        def _(h):
            P.emit("act", h, engsem, keysem)

        @block.vector
        def _(h):
            P.emit("dve", h, engsem, keysem)

        @block.gpsimd
        def _(h):
            P.emit("pool", h, engsem, keysem)

        @block.sync
        def _(h):
            P.emit("sp", h, engsem, keysem)
    return nc


def make_consts():
    c = np.zeros((128, C_N), np.float32)
    i = np.arange(128)
    same = (i[:, None] // 64) == (i[None, :] // 64)
    c[:, C_ID:C_ID + 128] = np.eye(128, dtype=np.float32)
    c[:, C_LT2:C_LT2 + 128] = (same & (i[:, None] <= i[None, :])).astype(np.float32)
    c[:, C_BLK:C_BLK + 128] = same.astype(np.float32)
    c[:, C_MSL:C_MSL + 128] = np.where(same & (i[:, None] > i[None, :]), 0.0, NEG)
    c[:, C_MIU:C_MIU + 128] = np.where(same & (i[:, None] <= i[None, :]), 0.0, NEG)
    c[:, C_SEL + 0] = (i < 64)
    c[:, C_SEL + 1] = (i >= 64)
    for r in range(3):
        c[64 + r, C_MQ + r] = 1.0
        c[67 + r, C_MQ + 3] = 1.0
        c[67 + r, C_MK + r] = -1.0
        c[64 + r, C_MK + 3] = 1.0
    return c


def shard_inputs(inp):
    x = np.asarray(inp["x"], np.float32)[0]
    S = x.shape[0]
    L = ((64 + S + 127) // 128) * 128
    FS = S // NCORES
    xT = np.zeros((D, L), np.float32)
    xT[:, 48:64] = np.asarray(inp["meta_tokens"], np.float32).T
    xT[:, 64:64 + S] = x.T
    w_in = np.asarray(inp["w_in"], np.float32)[0]
    conv_w = np.asarray(inp["conv_w"], np.float32)[0]
    cst = make_consts()
    col = lambda v: np.ascontiguousarray(np.asarray(v, np.float32).reshape(KC, 128).T)
    maps = []
    for c in range(NCORES):
        w1 = np.zeros((D, W1COLS), np.float32)
        w1[:, G_QA:G_QA + 64] = w_in[:, c * 64:(c + 1) * 64]
        w1[:, G_KA:G_KA + 64] = w_in[:, 512 + c * 64:512 + (c + 1) * 64]
        w1[:, G_V:G_V + 64] = w_in[:, 1024 + c * 64:1024 + (c + 1) * 64]
        for r in range(6):
            w1[:, G_V + 64 + r] = w_in[:, 1536 + c]
        w1[:, G_V + 70] = w_in[:, 4616 + c]
        w1[:, G_V + 71] = w_in[:, 4624 + c]
        w1[:, G_QB:G_QB + 128] = w_in[:, 1544 + c * 128:1544 + (c + 1) * 128]
        w1[:, G_KB:G_KB + 128] = w_in[:, 2568 + c * 128:2568 + (c + 1) * 128]
        w1[:, G_VB:G_VB + 128] = w_in[:, 3592 + c * 128:3592 + (c + 1) * 128]
        w1[:, G_Z:G_Z + 128] = w_in[:, 4632 + c * 128:4632 + (c + 1) * 128]
        pv = np.zeros((128, PV_N), np.float32)
        pv[:, PV_G0:PV_G0 + 8] = col(inp["ln_in_g"])
        pv[:, PV_B0:PV_B0 + 8] = col(inp["ln_in_b"])
        for t, base in enumerate((0, 1024, 2048)):
            pv[:, PV_CW + 4 * t:PV_CW + 4 * t + 4] = conv_w[:, base + c * 128:base + (c + 1) * 128].T
        pv[:, PV_BF] = np.asarray(inp["b_f"], np.float32)[0, c]
        pv[:, PV_ALOG] = np.asarray(inp["a_log"], np.float32)[0, c]
        pv[:, PV_DT] = np.asarray(inp["dt_bias"], np.float32)[0, c]
        pv[:, PV_GNW] = np.asarray(inp["gdn_norm_w"], np.float32)[0]
        pv[:, PV_G1:PV_G1 + 8] = col(inp["ln1_g"])
        pv[:, PV_B1:PV_B1 + 8] = col(inp["ln1_b"])
        pv[:, PV_G2:PV_G2 + 8] = col(inp["ln2_g"])
        pv[:, PV_B2:PV_B2 + 8] = col(inp["ln2_b"])
        maps.append({
            "xT": xT, "xown": np.ascontiguousarray(xT[:, 64 + c * FS:64 + (c + 1) * FS]), "w1": w1, "pv": pv, "cst": cst,
            "w2g": np.ascontiguousarray(w_in[:, 5656:7704]),
            "woa": np.asarray(inp["w_out_a"], np.float32)[0], "wob": np.asarray(inp["w_out_b"], np.float32)[0],
            "wo": np.asarray(inp["w_o"], np.float32)[0], "wg": np.asarray(inp["w_gate"], np.float32)[0],
            "wu": np.asarray(inp["w_up"], np.float32)[0], "wd": np.asarray(inp["w_down"], np.float32)[0],
        })
    return maps, S


P1_KEYS = ("xT", "w1", "pv", "cst")
P2_KEYS = ("xown", "pv", "cst", "w2g", "woa", "wob", "wo", "wg", "wu", "wd")


def kernel_fused(**inputs):
    maps, S = shard_inputs(inputs)
    nc = build(S, "fused")
    res = run_bass_kernel_spmd(nc, maps, core_ids=list(range(NCORES)))
    out = np.concatenate([np.asarray(res.results[c]["outT"], np.float32).T for c in range(NCORES)], axis=0)
    return out[None].astype(np.float32)


def kernel(**inputs):
    maps, S = shard_inputs(inputs)
    FS = S // NCORES
    nc1 = build(S, "p1")
    r1 = run_bass_kernel_spmd(nc1, maps, core_ids=list(range(NCORES)))
    ys = [np.asarray(r1.results[c]["ysrc"]).reshape(NCORES, 192, FS) for c in range(NCORES)]
    maps2 = []
    for c in range(NCORES):
        m2 = dict(maps[c])
        m2["yin"] = np.ascontiguousarray(np.concatenate([ys[r][c] for r in range(NCORES)], axis=0))
        maps2.append(m2)
    nc2 = build(S, "p2")
    res = run_bass_kernel_spmd(nc2, maps2, core_ids=list(range(NCORES)))
    out = np.concatenate([np.asarray(res.results[c]["outT"], np.float32).T for c in range(NCORES)], axis=0)
    return out[None].astype(np.float32)
```
